# Optimizing a Trainium2 kernel written in Bass

```python
import math
import jax, jax.numpy as jnp
from jax import lax
import numpy as np

D_MODEL = 1024
BATCH = 2
SEQ = 8192
DEPTH = 2

GRID_W = 64
Q_BLOCK = 128
EPS = 1e-6
ROPE_THETA = 10000.0

A_HEADS = 8
A_KV_HEADS = 2
A_HEAD_DIM = 64
A_WIDTH = A_HEADS * A_HEAD_DIM
A_KV_WIDTH = A_KV_HEADS * A_HEAD_DIM

B_WIDTH = D_MODEL // 2
HY_ORDER = 2
HY_SHORT = 3
HY_EMB_BANDS = 16
HY_EMB_DIM = 1 + 2 * HY_EMB_BANDS
HY_FILTER_HIDDEN = 64
HY_DECAY_TARGET = 1e-2
HY_FAST_DECAY = 0.3
HY_SLOW_DECAY = 1.5

EVEN_IN = A_WIDTH + 2 * A_KV_WIDTH + A_WIDTH + (HY_ORDER + 1) * B_WIDTH + B_WIDTH

C_HEADS = 16
C_NOPE = 64
C_ROPE = 32
C_V = 64
C_Q_RANK = 384
C_KV_RANK = 256
C_WIDTH = C_HEADS * C_V
ODD_IN = C_Q_RANK + C_KV_RANK + C_ROPE + C_WIDTH

kernel_name = 'hybrid_gqa_hyena_mla_encoder'

F32 = jnp.float32


def rmsnorm(x, g):
    xf = x.astype(F32)
    y = xf * lax.rsqrt(jnp.mean(xf * xf, axis=-1, keepdims=True) + EPS)
    return (y * g.astype(F32)).astype(x.dtype)


def split_cols(p, widths):
    out, start = [], 0
    for wd in widths:
        out.append(p[..., start:start + wd])
        start += wd
    return out


def axial_rope_angles(L, dim):
    rows = L // GRID_W
    row = jnp.repeat(jnp.arange(rows), GRID_W).astype(F32)
    col = jnp.tile(jnp.arange(GRID_W), rows).astype(F32)
    n = dim // 4
    inv = ROPE_THETA ** (-jnp.arange(n, dtype=F32) / n)
    ang = jnp.concatenate([row[:, None] * inv, col[:, None] * inv], axis=-1)
    return jnp.cos(ang), jnp.sin(ang)


def apply_rope(x, cos, sin):
    xf = x.astype(F32)
    half = x.shape[-1] // 2
    x1, x2 = xf[..., :half], xf[..., half:]
    return jnp.concatenate([x1 * cos - x2 * sin, x1 * sin + x2 * cos], axis=-1).astype(x.dtype)


def blocked_attention(q, k, v, scale):
    b, hk, g, L, dq = q.shape
    nb = L // Q_BLOCK
    qb = jnp.moveaxis(q.reshape(b, hk, g, nb, Q_BLOCK, dq), 3, 0)

    def one_block(qblk):
        s = jnp.einsum('bhgqd,bhkd->bhgqk', qblk, k, preferred_element_type=F32) * scale
        p = jax.nn.softmax(s, axis=-1)
        return jnp.einsum('bhgqk,bhkd->bhgqd', p.astype(v.dtype), v)

    o = lax.map(one_block, qb)
    return jnp.moveaxis(o, 0, 3).reshape(b, hk, g, L, v.shape[-1])


def short_conv(u, w, bias):
    y = lax.conv_general_dilated(u, w[:, None, :].astype(u.dtype), window_strides=(1,),
                                 padding=((HY_SHORT // 2, HY_SHORT // 2),),
                                 dimension_numbers=('NWC', 'WIO', 'NWC'),
                                 feature_group_count=u.shape[-1])
    return y + bias.astype(u.dtype)


def hyena_filters(L, w1, b1, f1, w2, b2, f2, w3):
    t = jnp.linspace(0.0, 1.0, L, dtype=F32)[:, None]
    w = 2.0 * math.pi * jnp.arange(L, dtype=F32)[:, None] / L
    bands = jnp.linspace(1e-4, HY_EMB_BANDS - 1, HY_EMB_BANDS, dtype=F32)
    z = jnp.concatenate([t, jnp.cos(bands * w), -jnp.sin(bands * w)], axis=-1)
    h = jnp.sin(f1.astype(F32) * (z @ w1.astype(F32) + b1.astype(F32)))
    h = jnp.sin(f2.astype(F32) * (h @ w2.astype(F32) + b2.astype(F32)))
    h = h @ w3.astype(F32)
    min_decay = math.log(HY_DECAY_TARGET) / HY_SLOW_DECAY
    max_decay = math.log(HY_DECAY_TARGET) / HY_FAST_DECAY
    deltas = jnp.abs(jnp.linspace(min_decay, max_decay, B_WIDTH, dtype=F32))
    decay = jnp.exp(-t * deltas)
    h_fwd = h[:, :B_WIDTH] * decay
    h_bwd = h[:, B_WIDTH:] * decay
    l1 = jnp.sum(jnp.abs(h_fwd), axis=0, keepdims=True) + jnp.sum(jnp.abs(h_bwd[1:]), axis=0, keepdims=True)
    return h_fwd / l1, h_bwd / l1


def bidir_fftconv(u, h_fwd, h_bwd):
    L, C = h_fwd.shape
    n = 2 * L
    k = jnp.concatenate([h_fwd, jnp.zeros((1, C), F32), h_bwd[1:][::-1]], axis=0)
    U = jnp.fft.rfft(u, n=n, axis=1)
    K = jnp.fft.rfft(k, n=n, axis=0)
    return jnp.fft.irfft(U * K[None], n=n, axis=1)[:, :L]


def hyena_mixer(p, conv_w, conv_b, fw1, fb1, ff1, fw2, fb2, ff2, fw3, hy_d):
    L = p.shape[1]
    u = short_conv(p, conv_w, conv_b)
    x0, x1, v = split_cols(u, [B_WIDTH, B_WIDTH, B_WIDTH])
    z = (v * x1).astype(F32)
    hf, hb = hyena_filters(L, fw1, fb1, ff1, fw2, fb2, ff2, fw3)
    y = bidir_fftconv(z, hf, hb) + z * hy_d.astype(F32)
    return (y * x0.astype(F32)).astype(p.dtype)


def even_mixer(h, w_in, q_norm, k_norm, conv_w, conv_b, fw1, fb1, ff1, fw2, fb2, ff2, fw3, hy_d, w_out):
    b, L, _ = h.shape
    p = jnp.einsum('bld,de->ble', h, w_in)
    q, k, v, ga, hy, gb = split_cols(p, [A_WIDTH, A_KV_WIDTH, A_KV_WIDTH, A_WIDTH,
                                         (HY_ORDER + 1) * B_WIDTH, B_WIDTH])
    cos, sin = axial_rope_angles(L, A_HEAD_DIM)
    cos4, sin4 = cos[None, :, None], sin[None, :, None]
    q = apply_rope(rmsnorm(q.reshape(b, L, A_HEADS, A_HEAD_DIM), q_norm), cos4, sin4)
    k = apply_rope(rmsnorm(k.reshape(b, L, A_KV_HEADS, A_HEAD_DIM), k_norm), cos4, sin4)
    v = v.reshape(b, L, A_KV_HEADS, A_HEAD_DIM)
    grp = A_HEADS // A_KV_HEADS
    qg = q.reshape(b, L, A_KV_HEADS, grp, A_HEAD_DIM).transpose(0, 2, 3, 1, 4)
    o = blocked_attention(qg, k.transpose(0, 2, 1, 3), v.transpose(0, 2, 1, 3), A_HEAD_DIM ** -0.5)
    ya = o.transpose(0, 3, 1, 2, 4).reshape(b, L, A_WIDTH) * jax.nn.silu(ga)
    yb = hyena_mixer(hy, conv_w, conv_b, fw1, fb1, ff1, fw2, fb2, ff2, fw3, hy_d) * jax.nn.silu(gb)
    return jnp.einsum('ble,ed->bld', jnp.concatenate([ya, yb], axis=-1), w_out)


def odd_mixer(h, w_in, q_a_norm, w_qb, kv_a_norm, w_kvb, w_out):
    b, L, _ = h.shape
    p = jnp.einsum('bld,de->ble', h, w_in)
    cq, ckv, kr, gc = split_cols(p, [C_Q_RANK, C_KV_RANK, C_ROPE, C_WIDTH])
    cos, sin = axial_rope_angles(L, C_ROPE)
    q = jnp.einsum('blr,re->ble', rmsnorm(cq, q_a_norm), w_qb).reshape(b, L, C_HEADS, C_NOPE + C_ROPE)
    q = jnp.concatenate([q[..., :C_NOPE],
                         apply_rope(q[..., C_NOPE:], cos[None, :, None], sin[None, :, None])], axis=-1)
    kv = jnp.einsum('blr,re->ble', rmsnorm(ckv, kv_a_norm), w_kvb).reshape(b, L, C_HEADS, C_NOPE + C_V)
    kr = apply_rope(kr, cos[None], sin[None])
    k = jnp.concatenate([kv[..., :C_NOPE],
                         jnp.broadcast_to(kr[:, :, None, :], (b, L, C_HEADS, C_ROPE))], axis=-1)
    v = kv[..., C_NOPE:]
    o = blocked_attention(q.transpose(0, 2, 1, 3)[:, :, None], k.transpose(0, 2, 1, 3),
                          v.transpose(0, 2, 1, 3), (C_NOPE + C_ROPE) ** -0.5)
    yc = o[:, :, 0].transpose(0, 2, 1, 3).reshape(b, L, C_WIDTH) * jax.nn.silu(gc)
    return jnp.einsum('ble,ed->bld', yc, w_out)


def setup_inputs(seed: int = 0) -> dict:
    key = jax.random.key(seed)
    ks = iter(jax.random.split(key, 40))
    ne = (DEPTH + 1) // 2
    no = DEPTH // 2

    def w(shape, fan_in):
        return jax.random.normal(next(ks), shape, F32) * (fan_in ** -0.5)

    def gain(shape, s=0.02):
        return 1.0 + s * jax.random.normal(next(ks), shape, F32)

    def small(shape, s=0.02):
        return s * jax.random.normal(next(ks), shape, F32)

    C = B_WIDTH
    return {
        'x': jax.random.normal(next(ks), (BATCH, SEQ, D_MODEL), F32),
        'e_norm': gain((ne, D_MODEL)),
        'e_w_in': w((ne, D_MODEL, EVEN_IN), D_MODEL),
        'e_q_norm': gain((ne, A_HEAD_DIM)),
        'e_k_norm': gain((ne, A_HEAD_DIM)),
        'e_conv_w': w((ne, HY_SHORT, (HY_ORDER + 1) * C), HY_SHORT),
        'e_conv_b': small((ne, (HY_ORDER + 1) * C)),
        'e_filt_w1': w((ne, HY_EMB_DIM, HY_FILTER_HIDDEN), HY_EMB_DIM),
        'e_filt_b1': small((ne, HY_FILTER_HIDDEN)),
        'e_filt_f1': gain((ne, HY_FILTER_HIDDEN), 0.1),
        'e_filt_w2': w((ne, HY_FILTER_HIDDEN, HY_FILTER_HIDDEN), HY_FILTER_HIDDEN),
        'e_filt_b2': small((ne, HY_FILTER_HIDDEN)),
        'e_filt_f2': gain((ne, HY_FILTER_HIDDEN), 0.1),
        'e_filt_w3': w((ne, HY_FILTER_HIDDEN, 2 * C), HY_FILTER_HIDDEN),
        'e_hy_d': small((ne, C), 0.1),
        'e_w_out': w((ne, A_WIDTH + B_WIDTH, D_MODEL), A_WIDTH + B_WIDTH),
        'o_norm': gain((no, D_MODEL)),
        'o_w_in': w((no, D_MODEL, ODD_IN), D_MODEL),
        'o_q_a_norm': gain((no, C_Q_RANK)),
        'o_w_qb': w((no, C_Q_RANK, C_HEADS * (C_NOPE + C_ROPE)), C_Q_RANK),
        'o_kv_a_norm': gain((no, C_KV_RANK)),
        'o_w_kvb': w((no, C_KV_RANK, C_HEADS * (C_NOPE + C_V)), C_KV_RANK),
        'o_w_out': w((no, C_WIDTH, D_MODEL), C_WIDTH),
        'final_norm': gain((D_MODEL,)),
    }


def reference(x, e_norm, e_w_in, e_q_norm, e_k_norm, e_conv_w, e_conv_b, e_filt_w1, e_filt_b1,
              e_filt_f1, e_filt_w2, e_filt_b2, e_filt_f2, e_filt_w3, e_hy_d, e_w_out,
              o_norm, o_w_in, o_q_a_norm, o_w_qb, o_kv_a_norm, o_w_kvb, o_w_out, final_norm):
    for i in range(DEPTH):
        j = i // 2
        if i % 2 == 0:
            x = x + even_mixer(rmsnorm(x, e_norm[j]), e_w_in[j], e_q_norm[j], e_k_norm[j],
                               e_conv_w[j], e_conv_b[j], e_filt_w1[j], e_filt_b1[j], e_filt_f1[j],
                               e_filt_w2[j], e_filt_b2[j], e_filt_f2[j], e_filt_w3[j], e_hy_d[j],
                               e_w_out[j])
        else:
            x = x + odd_mixer(rmsnorm(x, o_norm[j]), o_w_in[j], o_q_a_norm[j], o_w_qb[j],
                              o_kv_a_norm[j], o_w_kvb[j], o_w_out[j])
    return rmsnorm(x, final_norm)
```

```python
import math
import ml_dtypes

import numpy as np
from contextlib import ExitStack
import concourse.bass as bass
import concourse.mybir as mybir
from concourse.bass_utils import run_bass_kernel_spmd

F32 = mybir.dt.float32
BF16 = mybir.dt.bfloat16
AF = mybir.ActivationFunctionType
ALU = mybir.AluOpType
AX = mybir.AxisListType

N_DMA_SEMS = 24


class Ctx:
    def __init__(self, nc, es):
        self.nc = nc
        self.es = es
        self.es_root = es
        self.eng = {'pe': nc.tensor, 'act': nc.scalar, 'dve': nc.vector,
                    'pool': nc.gpsimd, 'sp': nc.sync}
        self.semobj = {}
        for e in ['pe', 'act', 'dve', 'pool']:
            self.semobj[e] = es.enter_context(nc.semaphore("s_" + e))
        self.cnt = {e: 0 for e in ['pe', 'act', 'dve', 'pool']}
        self.seen = {e: {} for e in self.eng}
        self.dma_use = []
        for i in range(N_DMA_SEMS):
            k = "d%d" % i
            self.semobj[k] = es.enter_context(nc.semaphore("s_" + k))
            self.dma_use.append(0)
        self.dma_rr = 0
        self.last_w = {}
        self.readers = {}
        self.nwaits = 0
        self.ninst = 0
        self.excl = set()
        self.uid = 0

    def sb(self, name, shape, dt):
        self.uid += 1
        return self.es.enter_context(self.nc.sbuf_tensor("%s_u%d" % (name, self.uid), list(shape), dt))

    def ps(self, name, shape, dt=F32):
        self.excl.add(name)
        self.uid += 1
        return self.es.enter_context(self.nc.psum_tensor("%s_u%d" % (name, self.uid), list(shape), dt))

    def _wait(self, e, sk, val):
        if self.seen[e].get(sk, 0) < val:
            self.eng[e].wait_ge(self.semobj[sk], val)
            self.seen[e][sk] = val
            self.nwaits += 1

    def _deps(self, e, reads, writes):
        need = {}
        ex = [k for k in reads if k in self.excl]
        if ex:
            writes = list(writes) + ex
            reads = [k for k in reads if k not in self.excl]

        def add(tok, kind):
            sk, val = tok
            if sk == 'pe' and e == 'pe':
                return
            if sk == e and kind != 'raw':
                return
            if need.get(sk, 0) < val:
                need[sk] = val

        for k in reads:
            if k in self.last_w:
                add(self.last_w[k], 'raw')
        for k in writes:
            if k in self.last_w:
                add(self.last_w[k], 'waw')
            for sk, val in self.readers.get(k, {}).items():
                add((sk, val), 'war')
        for sk, val in need.items():
            self._wait(e, sk, val)

    def _done(self, tok, reads, writes):
        ex = [k for k in reads if k in self.excl]
        if ex:
            writes = list(writes) + ex
            reads = [k for k in reads if k not in self.excl]
        for k in writes:
            self.last_w[k] = tok
            self.readers[k] = {}
        for k in reads:
            r = self.readers.setdefault(k, {})
            if r.get(tok[0], 0) < tok[1]:
                r[tok[0]] = tok[1]

    def op(self, e, reads, writes, build):
        self._deps(e, reads, writes)
        inst = build(self.eng[e])
        self.cnt[e] += 1
        inst.then_inc(self.semobj[e], 1)
        self._done((e, self.cnt[e]), reads, writes)
        self.ninst += 1
        return inst

    def dma(self, q, reads, writes, out, in_, **kw):
        self._deps(q, reads, writes)
        i = self.dma_rr
        self.dma_rr = (self.dma_rr + 1) % N_DMA_SEMS
        sk = "d%d" % i
        self._wait(q, sk, 16 * self.dma_use[i])
        self.dma_use[i] += 1
        inst = self.eng[q].dma_start(out=out, in_=in_, **kw)
        inst.then_inc(self.semobj[sk], 16)
        self._done((sk, 16 * self.dma_use[i]), reads, writes)
        self.ninst += 1
        return inst

    def collective(self, kind, reads, writes, ins, outs, groups):
        self._deps('pool', reads, writes)
        self.ncoll = getattr(self, 'ncoll', 0) + 1
        sk = "cc%d" % self.ncoll
        self.semobj[sk] = self.es_root.enter_context(self.nc.semaphore("s_" + sk))
        inst = self.nc.gpsimd.collective_compute(kind, ALU.bypass, replica_groups=groups, ins=ins, outs=outs)
        inst.then_inc(self.semobj[sk], 16)
        self._done((sk, 16), reads, writes)
        self.ninst += 1
        return inst

    def barrier(self):
        for e in self.eng:
            for sk in ['pe', 'act', 'dve', 'pool']:
                if sk != e and self.cnt[sk]:
                    self._wait(e, sk, self.cnt[sk])
            for i in range(N_DMA_SEMS):
                if self.dma_use[i]:
                    self._wait(e, "d%d" % i, 16 * self.dma_use[i])
        self.nbar = getattr(self, 'nbar', 0) + 1
        for sk in ['pe', 'act', 'dve', 'pool']:
            if self.cnt[sk] > 20000:
                self.semobj[sk] = self.es_root.enter_context(self.nc.semaphore("s_%s_b%d" % (sk, self.nbar)))
                self.cnt[sk] = 0
                for e in self.eng:
                    self.seen[e].pop(sk, None)
                for k in list(self.last_w):
                    if self.last_w[k][0] == sk:
                        del self.last_w[k]
                for k in self.readers:
                    self.readers[k].pop(sk, None)

    def finish(self, keys=(), e='sp'):
        for i in range(N_DMA_SEMS):
            if self.dma_use[i]:
                self._wait(e, "d%d" % i, 16 * self.dma_use[i])
        for k in keys:
            if k in self.last_w:
                sk, val = self.last_w[k]
                self._wait(e, sk, val)

import numpy as np
F=np.float32
BF=ml_dtypes.bfloat16
L=8192
def angles(dim):
    rows=L//64; row=np.repeat(np.arange(rows),64).astype(F); col=np.tile(np.arange(64),rows).astype(F)
    n=dim//4; inv=(10000.0**(-np.arange(n,dtype=F)/n)).astype(F)
    ang=np.concatenate([row[:,None]*inv,col[:,None]*inv],-1)
    return np.cos(ang).astype(F),np.sin(ang).astype(F)
def consts():
    sw=np.zeros((128,128),F)
    for p in range(128): sw[p,(p+64)%128]=1
    c64,s64=angles(64)
    tm=lambda t: np.ascontiguousarray(t.reshape(64,128,-1).transpose(1,0,2).reshape(128,-1))
    return dict(swap=sw, ident=np.eye(128,dtype=F).astype(BF), cos64=tm(c64), sin64=tm(s64))
def l1a_inputs(d, b, g, C):
    W=d['e_w_in'][0]
    hA=2*g; hB=2*g+1; kv=g//2
    heads=[W[:,hA*64:(hA+1)*64], W[:,hB*64:(hB+1)*64], W[:,512+kv*64:512+(kv+1)*64], W[:,512+kv*64:512+(kv+1)*64]]
    cols=[]
    for a in range(2):
        for h in range(4):
            cols.append(heads[h][:,a*32:(a+1)*32])
    cols.append(W[:,640+kv*64:640+(kv+1)*64])
    w_tm=np.ascontiguousarray(np.concatenate(cols,1))
    w_ga=np.ascontiguousarray(W[:,768+hA*64:768+hA*64+128])
    gq=d['e_q_norm'][0]; gk=d['e_k_norm'][0]; gs=[gq,gq,gk,gk]
    G=np.concatenate([gs[h][a*32:(a+1)*32] for a in range(2) for h in range(4)])[None,:].astype(F)
    return dict(xT=np.ascontiguousarray(d['x'][b].T), w_tm=w_tm, w_ga=w_ga,
                enorm=np.ascontiguousarray(d['e_norm'][0].reshape(8,128).T), qkgain=G,
                cos_tm=C['cos64'], sin_tm=C['sin64'], swapd=C['swap'], identd=C['ident'])

def hy_consts():
    n=np.arange(128)
    ang=2*np.pi*np.outer(n,n)/128.0
    cs,sn=np.cos(ang),np.sin(ang)
    FA=np.concatenate([cs,-sn],1).astype(BF)
    Fd=np.stack([cs,sn,-sn],1).astype(BF)
    Gd=np.stack([np.concatenate([cs,sn],1),np.concatenate([-sn,cs],1)],1).astype(BF)
    a2=2*np.pi*np.outer(n,n)/16384.0
    tw=np.stack([np.cos(a2),np.sin(a2)],1).astype(F)
    t=np.linspace(0,1,L,dtype=F)[:,None]; w=(2*math.pi*np.arange(L,dtype=F)[:,None]/L).astype(F)
    bands=np.linspace(1e-4,15,16,dtype=F)
    zf=np.concatenate([t,np.cos(bands*w),-np.sin(bands*w)],-1).astype(F)
    zTf=np.ascontiguousarray(zf.T)
    zTr=np.zeros_like(zTf); zTr[:,1:]=zTf[:,:0:-1]
    tr=np.zeros(L,F); tr[1:]=t[:0:-1,0]
    trow=np.stack([t[:,0],tr],0).astype(F)
    mind=math.log(1e-2)/1.5; maxd=math.log(1e-2)/0.3
    deltas=np.abs(np.linspace(mind,maxd,512,dtype=F)).astype(F)
    return dict(FA=FA,Fd=Fd,Gd=Gd,tw=tw,zTf=zTf,zTr=zTr,trow=trow,deltas=deltas)
def l1b_inputs(d,b,g,H):
    W=d['e_w_in'][0]; cs=slice(g*128,(g+1)*128)
    w_hy=np.ascontiguousarray(np.concatenate([W[:,1280:1792][:,cs],W[:,1792:2304][:,cs],W[:,2304:2816][:,cs],W[:,2816:3328][:,cs]],1))
    cw=d['e_conv_w'][0]; cb=d['e_conv_b'][0]
    convw=np.zeros((128,12),F)
    for gi in range(3):
        ch=gi*512+np.arange(g*128,(g+1)*128)
        convw[:,gi*4+0]=cw[0,ch]; convw[:,gi*4+1]=cw[1,ch]; convw[:,gi*4+2]=cw[2,ch]; convw[:,gi*4+3]=cb[ch]
    fparams=np.stack([d['e_filt_f1'][0],d['e_filt_b1'][0],d['e_filt_f2'][0],d['e_filt_b2'][0]],1).astype(F)
    w3=d['e_filt_w3'][0]
    w3d=np.ascontiguousarray(np.concatenate([w3[:,cs],w3[:,512:][:,cs]],1))
    return dict(xT=np.ascontiguousarray(d['x'][b].T), w_hy=w_hy, enorm=np.ascontiguousarray(d['e_norm'][0].reshape(8,128).T),
                convw=convw, fparams=fparams, w1d=d['e_filt_w1'][0], w2d=d['e_filt_w2'][0], w3d=w3d,
                zTf=H['zTf'], zTr=H['zTr'], trow=H['trow'], negdel=(-H['deltas'][cs])[:,None].astype(F), hyd=d['e_hy_d'][0][cs][:,None].astype(F),
                FAd=H['FA'], Fd=H['Fd'], Gd=H['Gd'], twd=H['tw'])

def tm16(t):
    return np.ascontiguousarray(t.reshape(16,128,-1).transpose(1,0,2).reshape(128,-1))
def l2_inputs(d, b, ts, y0T_b, C32):
    sl=slice(ts*2048,(ts+1)*2048)
    return dict(y0T=np.ascontiguousarray(y0T_b[:, sl]), xs=np.ascontiguousarray(d['x'][b, sl]), wout=d['e_w_out'][0], win=d['o_w_in'][0],
                onorm=np.ascontiguousarray(d['o_norm'][0].reshape(8,128).T), cos2=tm16(C32[0][sl]), sin2=tm16(C32[1][sl]),
                identd=np.eye(128,dtype=F).astype(BF))
def l4_inputs(d, b, ts, ocT_b, gcT_bts, x1_bts):
    sl=slice(ts*2048,(ts+1)*2048)
    return dict(ocT=np.ascontiguousarray(ocT_b[:, sl]), gcT=gcT_bts, x1=x1_bts, wout=d['o_w_out'][0], fnorm=d['final_norm'][None,:].astype(F))

def l3_consts():
    c,s=angles(32)
    return dict(cosT=np.ascontiguousarray(np.tile(c.T,(2,1))), sinT=np.ascontiguousarray(np.tile(s.T,(2,1))))
def l3_inputs(d, b, g, latT_b, C3, swap):
    wqb=d['o_w_qb'][0]; wkvb=d['o_w_kvb'][0]
    wq=np.zeros((384,4,96),F); wk=np.zeros((256,4,64),F); wv=np.zeros((256,4,64),F)
    for i in range(4):
        h=4*g+i
        wq[:,i,:]=wqb[:,h*96:h*96+96]
        wk[:,i,:]=wkvb[:,h*128:h*128+64]; wv[:,i,:]=wkvb[:,h*128+64:h*128+128]
    return dict(latT=latT_b, wq=wq, wk=wk, wv=wv, qgain=np.ascontiguousarray(d['o_q_a_norm'][0].reshape(3,128).T),
                kvgain=np.ascontiguousarray(d['o_kv_a_norm'][0].reshape(2,128).T), cosT=C3['cosT'], sinT=C3['sinT'], swapd=swap)

import numpy as np

L = 8192
D = 1024
NBLK = 16
TB = 512
EPS = 1e-6


def load_weights_scaled(c, wd, gain_t, gkey, wb, ncols, tag):
    stg = [c.sb("wstg%s%d" % (tag, i), [128, ncols], F32) for i in range(2)]
    for ch in range(8):
        s = stg[ch % 2]
        k = "wstg%s%d" % (tag, ch % 2)
        c.dma('sp', [], [k], s[:], wd[ch * 128:(ch + 1) * 128, :])
        c.op('dve', [k, gkey], ['wb' + tag],
             lambda e, s=s, ch=ch: e.tensor_scalar(out=wb[:, ch, :], in0=s[:], scalar1=gain_t[:, ch:ch + 1],
                                                   scalar2=None, op0=ALU.mult))


class Sweep:
    def __init__(self, c, xT, tag):
        self.c = c
        self.tag = tag
        self.xTv = xT.rearrange("(ch p) t -> p ch t", p=128)
        self.xf = [c.sb("xf%s%d" % (tag, i), [128, 8, TB], F32) for i in range(2)]
        self.sq = c.sb("sq" + tag, [128, 8, TB], BF16)
        self.hT = [c.sb("hT%s%d" % (tag, i), [128, 8, TB], BF16) for i in range(2)]
        self.sd = c.sb("sd" + tag, [128, TB], F32)
        self.R = c.sb("R" + tag, [128, TB], F32)
        self.ones = c.sb("ones" + tag, [128, 128], BF16)
        c.op('pool', [], ['ones' + tag], lambda e: e.memset(self.ones[:], 1.0))

    def load(self, j):
        c, t = self.c, self.tag
        s = j % 2
        c.dma('sp', [], ['xf%s%d' % (t, s)], self.xf[s][:], self.xTv[:, :, j * TB:(j + 1) * TB])

    def norm(self, j, ss_ps, ss_key):
        c, t = self.c, self.tag
        s = j % 2
        xf, hT = self.xf[s], self.hT[s]
        kx, kh = 'xf%s%d' % (t, s), 'hT%s%d' % (t, s)
        c.op('pool', [kx], ['sq' + t], lambda e: e.tensor_tensor(out=self.sq[:], in0=xf[:], in1=xf[:], op=ALU.mult))
        for ch in range(8):
            c.op('pe', ['sq' + t, 'ones' + t], [ss_key],
                 lambda e, ch=ch: e.matmul(ss_ps, lhsT=self.ones[:], rhs=self.sq[:, ch, :], start=(ch == 0), stop=(ch == 7)))
        c.op('act', [ss_key], ['sd' + t],
             lambda e: e.activation(out=self.sd[:], in_=ss_ps, func=AF.Sqrt, scale=1.0 / D, bias=EPS))
        c.op('dve', ['sd' + t], ['R' + t], lambda e: e.reciprocal(out=self.R[:], in_=self.sd[:]))
        c.op('dve', [kx, 'R' + t], [kh],
             lambda e: e.tensor_tensor(out=hT[:], in0=xf[:], in1=self.R[:].unsqueeze(1).broadcast_to([128, 8, TB]), op=ALU.mult))
        return hT, kh


def emit_attn_L0(c, es, xT, w_tm, w_ga, enorm, qkgain, cos_tm, sin_tm, swapd, identd, ya_out, dbg=None):
    nc = c.nc
    QTp = [c.sb("QTA", [128, L], BF16), c.sb("QTB", [128, L], BF16)]
    c.op('pool', [], ['QTA'], lambda e: e.memset(QTp[0][64:128, :], 0.0))
    c.op('pool', [], ['QTB'], lambda e: e.memset(QTp[1][0:64, :], 0.0))
    KT = c.sb("KT", [128, L], BF16)
    VA = c.sb("VA", [128, 64, 128], BF16)
    VB = c.sb("VB", [128, 64, 128], BF16)
    gaT = c.sb("gaT", [128, L], BF16)
    c.op('pool', [], ['VA'], lambda e: e.memset(VA[:, :, 64:128], 1.0))
    c.op('pool', [], ['VB'], lambda e: e.memset(VB[:, :, 0:64], 1.0))
    ident = c.sb("ident", [128, 128], BF16)
    c.dma('sp', [], ['ident'], ident[:], identd)
    with ExitStack() as es1:
        c.es = es1
        gain = c.sb("gainA", [128, 8], F32)
        c.dma('sp', [], ['gainA'], gain[:], enorm)
        wtm = c.sb("wbA", [128, 8, 320], BF16)
        wga = c.sb("wbG", [128, 8, 128], BF16)
        load_weights_scaled(c, w_tm, gain, 'gainA', wtm, 320, 'A')
        load_weights_scaled(c, w_ga, gain, 'gainA', wga, 128, 'G')
        G = c.sb("G", [128, 256], F32)
        c.dma('sp', [], ['G'], G[:], qkgain.partition_broadcast(128))
        cosS = c.sb("cosS", [128, 64, 32], F32)
        sinS = c.sb("sinS", [128, 64, 32], F32)
        c.dma('sp', [], ['cosS'], cosS[:], cos_tm.rearrange("p (t j) -> p t j", j=32))
        c.dma('sp', [], ['sinS'], sinS[:], sin_tm.rearrange("p (t j) -> p t j", j=32))
        sw = Sweep(c, xT, 'A')
        ss = c.ps("ssA", [128, TB])
        pga = c.ps("pga", [128, TB])
        pt = c.ps("ptA", [128, 4, 512])
        ptr = c.ps("ptrA", [128, 4, 2, 128], BF16)
        sqq = c.sb("sqq", [128, 4, 256], F32)
        ssq = c.sb("ssq", [128, 4, 4], F32)
        sdq = c.sb("sdq", [128, 4, 4], F32)
        rsq = c.sb("rsq", [128, 4, 4], F32)
        qn = c.sb("qn", [128, 4, 256], F32)
        t1 = c.sb("t1", [128, 4, 256], F32)
        t2 = c.sb("t2", [128, 4, 256], F32)
        qk = c.sb("qk", [128, 4, 4, 64], BF16)
        sw.load(0)
        for j in range(NBLK):
            if j + 1 < NBLK:
                sw.load(j + 1)
            hT, kh = sw.norm(j, ss[:], 'ssA')
            for ch in range(8):
                c.op('pe', ['wbG', kh], ['pga'],
                     lambda e, ch=ch: e.matmul(pga[:], lhsT=wga[:, ch, :], rhs=hT[:, ch, :], start=(ch == 0), stop=(ch == 7)))
            c.op('act', ['pga'], ['gaT'],
                 lambda e: e.activation(out=gaT[:, j * TB:(j + 1) * TB], in_=pga[:], func=AF.Silu))
            for tt in range(4):
                for ch in range(8):
                    c.op('pe', ['wbA', kh], ['ptA'],
                         lambda e, ch=ch, tt=tt: e.matmul(pt[:, tt, 0:320], lhsT=hT[:, ch, tt * 128:(tt + 1) * 128],
                                                          rhs=wtm[:, ch, :], start=(ch == 0), stop=(ch == 7)))
            c.op('act', ['ptA'], ['sqq'], lambda e: e.activation(out=sqq[:], in_=pt[:, :, 0:256], func=AF.Square))
            c.op('dve', ['sqq'], ['ssq'],
                 lambda e: e.tensor_reduce(out=ssq[:], in_=sqq[:].rearrange("p t (a h j) -> p t h a j", a=2, h=4),
                                           axis=AX.XY, op=ALU.add))
            c.op('act', ['ssq'], ['sdq'], lambda e: e.activation(out=sdq[:], in_=ssq[:], func=AF.Sqrt, scale=1.0 / 64, bias=EPS))
            c.op('dve', ['sdq'], ['rsq'], lambda e: e.reciprocal(out=rsq[:], in_=sdq[:]))
            for a in range(2):
                c.op('dve', ['ptA', 'rsq'], ['qn'],
                     lambda e, a=a: e.tensor_tensor(
                         out=qn[:, :, a * 128:(a + 1) * 128].rearrange("p t (h j) -> p t h j", h=4),
                         in0=pt[:, :, a * 128:(a + 1) * 128].rearrange("p t (h j) -> p t h j", h=4),
                         in1=rsq[:].unsqueeze(3).broadcast_to([128, 4, 4, 32]), op=ALU.mult))
            c.op('dve', ['qn', 'G'], ['qn'],
                 lambda e: e.tensor_tensor(out=qn[:], in0=qn[:], in1=G[:].unsqueeze(1).broadcast_to([128, 4, 256]), op=ALU.mult))
            qv = qn[:].rearrange("p t (g j) -> p t g j", j=32)
            cb = cosS[:, j * 4:(j + 1) * 4, :].unsqueeze(2).broadcast_to([128, 4, 8, 32])
            sb_ = sinS[:, j * 4:(j + 1) * 4, :].unsqueeze(2).broadcast_to([128, 4, 8, 32])
            c.op('dve', ['qn', 'cosS'], ['t1'],
                 lambda e: e.tensor_tensor(out=t1[:].rearrange("p t (g j) -> p t g j", j=32), in0=qv, in1=cb, op=ALU.mult))
            c.op('pool', ['qn', 'sinS'], ['t2'],
                 lambda e: e.tensor_tensor(out=t2[:].rearrange("p t (g j) -> p t g j", j=32), in0=qv, in1=sb_, op=ALU.mult))
            t1v = t1[:].rearrange("p t (a h j) -> p t a h j", a=2, h=4)
            t2v = t2[:].rearrange("p t (a h j) -> p t a h j", a=2, h=4)
            c.op('dve', ['t1', 't2'], ['qk'],
                 lambda e: e.tensor_tensor(out=qk[:, :, :, 0:32], in0=t1v[:, :, 0, :, :], in1=t2v[:, :, 1, :, :], op=ALU.subtract))
            c.op('pool', ['t1', 't2'], ['qk'],
                 lambda e: e.tensor_tensor(out=qk[:, :, :, 32:64], in0=t2v[:, :, 0, :, :], in1=t1v[:, :, 1, :, :], op=ALU.add))
            for tt in range(4):
                c.op('pe', ['qk', 'ident'], ['ptrA'],
                     lambda e, tt=tt: e.transpose(out=ptr[:, tt, 0, :], in_=qk[:, tt, 0:2, :].rearrange("p h j -> p (h j)"), identity=ident[:]))
                c.op('pe', ['qk', 'ident'], ['ptrA'],
                     lambda e, tt=tt: e.transpose(out=ptr[:, tt, 1, :], in_=qk[:, tt, 2:4, :].rearrange("p h j -> p (h j)"), identity=ident[:]))
            c.op('act', ['ptrA'], ['QTA'],
                 lambda e: e.copy(out=QTp[0][0:64, j * TB:(j + 1) * TB].rearrange("p (t q) -> p t q", t=4), in_=ptr[0:64, :, 0, :]))
            c.op('act', ['ptrA'], ['QTB'],
                 lambda e: e.copy(out=QTp[1][64:128, j * TB:(j + 1) * TB].rearrange("p (t q) -> p t q", t=4), in_=ptr[64:128, :, 0, :]))
            c.op('dve', ['ptrA'], ['KT'],
                 lambda e: e.tensor_copy(out=KT[:, j * TB:(j + 1) * TB].rearrange("p (t q) -> p t q", t=4), in_=ptr[:, :, 1, :]))
            c.op('act', ['ptA'], ['VA'], lambda e: e.copy(out=VA[:, j * 4:(j + 1) * 4, 0:64], in_=pt[:, :, 256:320]))
            c.op('pool', ['VA'], ['VB'], lambda e: e.tensor_copy(out=VB[:, j * 4:(j + 1) * 4, 64:128], in_=VA[:, j * 4:(j + 1) * 4, 0:64]))
    c.barrier()
    c.es = es
    if dbg is not None:
        c.dma('sp', ['QTA'], ['dbgQ'], dbg['QT'][0:64, :], QTp[0][0:64, :])
        c.dma('sp', ['QTB'], ['dbgQ2'], dbg['QT'][64:128, :], QTp[1][64:128, :])
        c.dma('sp', ['KT'], ['dbgK'], dbg['KT'], KT[:])
        c.dma('sp', ['gaT'], ['dbgG'], dbg['gaT'], gaT[:])
        c.dma('sp', ['VA'], ['dbgV'], dbg['VA'], VA[:].rearrange("p t j -> p (t j)"))
    with ExitStack() as es2:
        c.es = es2
        swp = c.sb("swp", [128, 128], F32)
        c.dma('sp', [], ['swp'], swp[:], swapd)
        PS = [c.ps("PS%d" % i, [128, 2, 512]) for i in range(3)]
        PT = [c.sb("PT%d" % i, [128, 2, 512], BF16) for i in range(3)]
        PO = c.ps("PO", [128, 2, 512])
        PS2 = PS[2][:, 0, :]
        S_sb = c.sb("S_sb", [128, 512], F32)
        O_sb = c.sb("O_sb", [128, 512], F32)
        rinv = c.sb("rinv", [128, 512], F32)
        yo = [c.sb("yo%d" % i, [128, 512], BF16) for i in range(2)]

        def S(qb, kt):
            s = kt % 3
            for h in range(2):
                c.op('pe', ['KT', 'QT' + 'AB'[h]], ['PS%d' % s],
                     lambda e, h=h: e.matmul(PS[s][:, h, :], lhsT=KT[:, kt * 128:(kt + 1) * 128],
                                             rhs=QTp[h][:, qb * 512:(qb + 1) * 512], start=True, stop=True))

        S(0, 0)
        S(0, 1)
        for qb in range(NBLK):
            for kt in range(64):
                s = kt % 3
                if kt + 2 < 64:
                    S(qb, kt + 2)
                c.op('act', ['PS%d' % s], ['PT%d' % s],
                     lambda e: e.activation(out=PT[s][:], in_=PS[s][:], func=AF.Exp, scale=0.125))
                c.op('pe', ['VA', 'PT%d' % s], ['PO'],
                     lambda e: e.matmul(PO[:, 0, :], lhsT=VA[:, kt, :], rhs=PT[s][:, 0, :], start=(kt == 0), stop=(kt == 63)))
                c.op('pe', ['VB', 'PT%d' % s], ['PO'],
                     lambda e: e.matmul(PO[:, 1, :], lhsT=VB[:, kt, :], rhs=PT[s][:, 1, :], start=(kt == 0), stop=(kt == 63)))
            if qb + 1 < NBLK:
                S(qb + 1, 0)
                S(qb + 1, 1)
            c.op('dve', ['PO'], ['S_sb'], lambda e: e.tensor_copy(out=S_sb[64:128, :], in_=PO[64:128, 0, :]))
            c.op('dve', ['PO'], ['S_sb'], lambda e: e.tensor_copy(out=S_sb[0:64, :], in_=PO[0:64, 1, :]))
            c.op('dve', ['PO'], ['O_sb'], lambda e: e.tensor_copy(out=O_sb[0:64, :], in_=PO[0:64, 0, :]))
            c.op('dve', ['PO'], ['O_sb'], lambda e: e.tensor_copy(out=O_sb[64:128, :], in_=PO[64:128, 1, :]))
            c.op('pe', ['swp', 'S_sb'], ['PS2'], lambda e: e.matmul(PS2, lhsT=swp[:], rhs=S_sb[:], start=True, stop=True))
            c.op('dve', ['PS2'], ['rinv'], lambda e: e.reciprocal(out=rinv[:], in_=PS2))
            c.op('dve', ['O_sb', 'rinv'], ['O_sb'], lambda e: e.tensor_tensor(out=O_sb[:], in0=O_sb[:], in1=rinv[:], op=ALU.mult))
            y = yo[qb % 2]
            ky = 'yo%d' % (qb % 2)
            c.op('pool', ['O_sb', 'gaT'], [ky],
                 lambda e: e.tensor_tensor(out=y[:], in0=O_sb[:], in1=gaT[:, qb * 512:(qb + 1) * 512], op=ALU.mult))
            c.dma('sp', [ky], ['ya_out'], ya_out[:, qb * 512:(qb + 1) * 512], y[:])
    c.barrier()
    c.es = es

import numpy as np

I32 = mybir.dt.int32
PI_LO = 3.1415925
TWO_PI = 2.0 * np.pi
CG = 4
NG = 128 // CG


def lockstep(*chains):
    n = max(len(ch) for ch in chains)
    for i in range(n):
        for ch in chains:
            if i < len(ch):
                ch[i]()


def emit_hyena_L0(c, es, xT, w_hy, enorm, convw, fparams, w1d, w2d, w3d, zTf, zTr, trow, negdel, hyd,
                  FAd, Fd, Gd, twd, kscr, zscr, yscr, yb_out, dbg=None):
    nc = c.nc
    zT = c.sb("zT", [128, L], BF16)
    g0 = c.sb("g0", [128, L], BF16)
    with ExitStack() as es1:
        c.es = es1
        gain = c.sb("gainH", [128, 8], F32)
        c.dma('sp', [], ['gainH'], gain[:], enorm)
        wb = c.sb("wbH", [128, 8, 512], BF16)
        load_weights_scaled(c, w_hy, gain, 'gainH', wb, 512, 'H')
        cw = c.sb("cw", [128, 12], F32)
        c.dma('sp', [], ['cw'], cw[:], convw)
        sw = Sweep(c, xT, 'H')
        ss = c.ps("ssH", [128, TB])
        pf = c.ps("pfH", [128, 4, 512])
        raw = [c.sb("raw%d" % i, [128, 3, 514], F32) for i in range(3)]
        sg = [c.sb("sg%d" % i, [128, 512], F32) for i in range(3)]
        u = c.sb("u", [128, 3, 512], F32)
        c.op('pool', [], ['raw0'], lambda e: e.memset(raw[0][:, :, 0:1], 0.0))

        def conv(jj):
            r = raw[jj % 3]
            kr = 'raw%d' % (jj % 3)
            for g in range(3):
                c.op('dve', [kr, 'cw'], ['u'],
                     lambda e, g=g: e.tensor_scalar(out=u[:, g, :], in0=r[:, g, 1:513], scalar1=cw[:, g * 4 + 1:g * 4 + 2],
                                                    scalar2=cw[:, g * 4 + 3:g * 4 + 4], op0=ALU.mult, op1=ALU.add))
                c.op('dve', [kr, 'cw', 'u'], ['u'],
                     lambda e, g=g: e.scalar_tensor_tensor(out=u[:, g, :], in0=r[:, g, 0:512], scalar=cw[:, g * 4:g * 4 + 1],
                                                           in1=u[:, g, :], op0=ALU.mult, op1=ALU.add))
                c.op('dve', [kr, 'cw', 'u'], ['u'],
                     lambda e, g=g: e.scalar_tensor_tensor(out=u[:, g, :], in0=r[:, g, 2:514], scalar=cw[:, g * 4 + 2:g * 4 + 3],
                                                           in1=u[:, g, :], op0=ALU.mult, op1=ALU.add))
            c.op('dve', ['u'], ['zT'],
                 lambda e: e.tensor_tensor(out=zT[:, jj * TB:(jj + 1) * TB], in0=u[:, 2, :], in1=u[:, 1, :], op=ALU.mult))
            c.op('pool', ['u', 'sg%d' % (jj % 3)], ['g0'],
                 lambda e: e.tensor_tensor(out=g0[:, jj * TB:(jj + 1) * TB], in0=u[:, 0, :], in1=sg[jj % 3][:], op=ALU.mult))

        sw.load(0)
        for j in range(NBLK):
            if j + 1 < NBLK:
                sw.load(j + 1)
            hT, kh = sw.norm(j, ss[:], 'ssH')
            for m in range(4):
                for ch in range(8):
                    c.op('pe', ['wbH', kh], ['pfH'],
                         lambda e, m=m, ch=ch: e.matmul(pf[:, m, :], lhsT=wb[:, ch, m * 128:(m + 1) * 128], rhs=hT[:, ch, :],
                                                        start=(ch == 0), stop=(ch == 7)))
            r = raw[j % 3]
            kr = 'raw%d' % (j % 3)
            c.op('act', ['pfH'], [kr], lambda e: e.copy(out=r[:, :, 1:513], in_=pf[:, 0:3, :]))
            c.op('act', ['pfH'], ['sg%d' % (j % 3)], lambda e: e.activation(out=sg[j % 3][:], in_=pf[:, 3, :], func=AF.Silu))
            if j > 0:
                rp = raw[(j - 1) % 3]
                kp = 'raw%d' % ((j - 1) % 3)
                c.op('pool', [kp], [kr], lambda e: e.tensor_copy(out=r[:, :, 0:1], in_=rp[:, :, 512:513]))
                c.op('pool', [kr], [kp], lambda e: e.tensor_copy(out=rp[:, :, 513:514], in_=r[:, :, 1:2]))
                conv(j - 1)
        c.op('pool', [], ['raw%d' % ((NBLK - 1) % 3)], lambda e: e.memset(raw[(NBLK - 1) % 3][:, :, 513:514], 0.0))
        conv(NBLK - 1)
    c.barrier()
    c.es = es
    if dbg is not None:
        c.dma('sp', ['zT'], ['dbgz'], dbg['zT'], zT[:])
        c.dma('sp', ['g0'], ['dbgg'], dbg['g0'], g0[:])
    c.dma('sp', ['zT'], ['zscr'], zscr, zT[:])
    kT = c.sb("kT", [128, 2 * L], BF16)
    l1p = c.sb("l1p", [128, 32], F32)
    rl1 = c.sb("rl1", [128, 1], F32)
    with ExitStack() as es2:
        c.es = es2
        fp = c.sb("fp", [64, 4], F32)
        c.dma('sp', [], ['fp'], fp[:], fparams)
        fb = c.sb("fb", [64, 2], F32)
        c.op('dve', ['fp'], ['fb'], lambda e: e.tensor_tensor(out=fb[:, 0:1], in0=fp[:, 0:1], in1=fp[:, 1:2], op=ALU.mult))
        c.op('dve', ['fp'], ['fb'], lambda e: e.tensor_tensor(out=fb[:, 1:2], in0=fp[:, 2:3], in1=fp[:, 3:4], op=ALU.mult))
        w1 = c.sb("w1", [33, 64], F32)
        w2 = c.sb("w2", [64, 64], F32)
        w3 = c.sb("w3", [64, 256], F32)
        c.dma('sp', [], ['w1'], w1[:], w1d)
        c.dma('sp', [], ['w2'], w2[:], w2d)
        c.dma('sp', [], ['w3'], w3[:], w3d)
        nd = c.sb("nd", [128, 1], F32)
        c.dma('sp', [], ['nd'], nd[:], negdel)
        def mk(name, shape, dt=F32):
            return [c.sb("%s_%d" % (name, dr), shape, dt) for dr in range(2)]
        zb = [[c.sb("zb%d_%d" % (dr, i), [33, 512], F32) for i in range(2)] for dr in range(2)]
        tb = [[c.sb("tb%d_%d" % (dr, i), [128, 512], F32) for i in range(2)] for dr in range(2)]
        P1 = [c.ps("P1_%d" % dr, [64, 512]) for dr in range(2)]
        P2 = [c.ps("P2_%d" % dr, [64, 512]) for dr in range(2)]
        P3 = [c.ps("P3_%d" % dr, [128, 512]) for dr in range(2)]
        a1 = mk("a1", [64, 512]); ki = mk("ki", [64, 512], I32); rr = mk("rr", [64, 512])
        h1 = mk("h1", [64, 512]); h2 = mk("h2", [64, 512]); dec = mk("dec", [128, 512]); hk = mk("hk", [128, 512])

        def sin_stages(dr, P, pk, col, hout, hk_):
            A, KI, RR = a1[dr], ki[dr], rr[dr]
            ka, kk, kr_ = 'a1_%d' % dr, 'ki_%d' % dr, 'rr_%d' % dr
            return [
                lambda: c.op('act', [pk, 'fp', 'fb'], [ka],
                             lambda e: e.activation(out=A[:], in_=P[:], func=AF.Identity, scale=fp[:, 2 * col:2 * col + 1], bias=fb[:, col:col + 1])),
                lambda: c.op('dve', [ka], [kk], lambda e: e.tensor_scalar(out=KI[:], in0=A[:], scalar1=1.0 / TWO_PI, scalar2=None, op0=ALU.mult)),
                lambda: c.op('dve', [kk, ka], [kr_],
                             lambda e: e.scalar_tensor_tensor(out=RR[:], in0=KI[:], scalar=-TWO_PI, in1=A[:], op0=ALU.mult, op1=ALU.add)),
                lambda: c.op('dve', [kr_], [kr_],
                             lambda e: e.tensor_scalar(out=RR[:], in0=RR[:], scalar1=-PI_LO, scalar2=PI_LO, op0=ALU.max, op1=ALU.min)),
                lambda: c.op('act', [kr_], [hk_], lambda e: e.activation(out=hout[:], in_=RR[:], func=AF.Sin)),
            ]

        def filt_chain(dr, j):
            s = j % 2
            zsrc = zTf if dr == 0 else zTr
            Z, T = zb[dr][s], tb[dr][s]
            kz, kt_ = 'zb%d_%d' % (dr, s), 'tb%d_%d' % (dr, s)
            p1, p2, p3 = P1[dr], P2[dr], P3[dr]
            H1, H2, DEC, HK = h1[dr], h2[dr], dec[dr], hk[dr]
            it = dr * NBLK + j
            st = [
                lambda: (c.dma('sp', [], [kz], Z[:], zsrc[:, j * TB:(j + 1) * TB]),
                         c.dma('sp', [], [kt_], T[:], trow[dr:dr + 1, j * TB:(j + 1) * TB].partition_broadcast(128))),
                lambda: c.op('pe', ['w1', kz], ['P1_%d' % dr], lambda e: e.matmul(p1[:], lhsT=w1[:], rhs=Z[:], start=True, stop=True)),
            ]
            st += sin_stages(dr, p1, 'P1_%d' % dr, 0, H1, 'h1_%d' % dr)
            st += [lambda: c.op('pe', ['w2', 'h1_%d' % dr], ['P2_%d' % dr], lambda e: e.matmul(p2[:], lhsT=w2[:], rhs=H1[:], start=True, stop=True))]
            st += sin_stages(dr, p2, 'P2_%d' % dr, 1, H2, 'h2_%d' % dr)
            st += [
                lambda: c.op('pe', ['w3', 'h2_%d' % dr], ['P3_%d' % dr],
                             lambda e: e.matmul(p3[:], lhsT=w3[:, dr * 128:(dr + 1) * 128], rhs=H2[:], start=True, stop=True)),
                lambda: c.op('act', [kt_, 'nd'], ['dec_%d' % dr], lambda e: e.activation(out=DEC[:], in_=T[:], func=AF.Exp, scale=nd[:, 0:1])),
                lambda: c.op('dve', ['P3_%d' % dr, 'dec_%d' % dr], ['hk_%d' % dr], lambda e: e.tensor_tensor(out=HK[:], in0=p3[:], in1=DEC[:], op=ALU.mult)),
            ]
            if dr == 1 and j == 0:
                st += [lambda: c.op('dve', ['hk_%d' % dr], ['hk_%d' % dr], lambda e: e.memset(HK[:, 0:1], 0.0))]
            st += [
                lambda: c.op('dve', ['hk_%d' % dr], ['l1p'],
                             lambda e: e.tensor_reduce(out=l1p[:, it:it + 1], in_=HK[:], axis=AX.X, op=ALU.add, apply_absolute_value=True)),
                lambda: c.op('pool', ['hk_%d' % dr], ['kT'], lambda e: e.tensor_copy(out=kT[:, dr * L + j * TB: dr * L + (j + 1) * TB], in_=HK[:])),
            ]
            return st

        for j in range(NBLK):
            lockstep(filt_chain(0, j), filt_chain(1, j))
        l1s = c.sb("l1s", [128, 1], F32)
        c.op('dve', ['l1p'], ['l1s'], lambda e: e.tensor_reduce(out=l1s[:], in_=l1p[:], axis=AX.X, op=ALU.add))
        c.op('dve', ['l1s'], ['rl1'], lambda e: e.reciprocal(out=rl1[:], in_=l1s[:]))
    c.barrier()
    c.es = es
    if dbg is not None:
        c.dma('sp', ['kT'], ['dbgk'], dbg['kT'], kT[:])
    c.dma('sp', ['kT'], ['kscr'], kscr, kT[:])
    with ExitStack() as es3:
        c.es = es3
        Kc = c.sb("Kc", [128, 128, 128], BF16)
        Xc = c.sb("Xc", [128, 128, 128], BF16)
        kv = kscr.rearrange("c (a b) -> a c b", b=128)
        zv = zscr.rearrange("c (a b) -> a c b", b=128)
        for i in range(16):
            c.dma('sp', ['kscr'], ['Kc'], Kc[:, i * 8:(i + 1) * 8, :], kv[:, i * 8:(i + 1) * 8, :])
        for i in range(16):
            c.dma('sp', ['zscr'], ['Xc'], Xc[0:64, i * 8:(i + 1) * 8, :], zv[:, i * 8:(i + 1) * 8, :])
        FA = c.sb("FA", [128, 256], BF16)
        Fm = c.sb("Fm", [128, 3, 128], BF16)
        Gm = c.sb("Gm", [128, 2, 256], BF16)
        tw = c.sb("tw", [128, 2, 128], F32)
        c.dma('sp', [], ['FA'], FA[:], FAd)
        c.dma('sp', [], ['Fm'], Fm[:], Fd)
        c.dma('sp', [], ['Gm'], Gm[:], Gd)
        c.dma('sp', [], ['tw'], tw[:], twd)
        PA_f = c.ps("PA_f", [128, CG, 256]); PXr_f = c.ps("PXr_f", [128, CG, 128]); PXi_f = c.ps("PXi_f", [128, CG, 128])
        PA_d = c.ps("PA_d", [128, CG, 256]); PXr_d = c.ps("PXr_d", [128, CG, 128]); PXi_d = c.ps("PXi_d", [128, CG, 128])
        mf = [c.sb("mf%d" % i, [128, CG, 128], F32) for i in range(4)]
        md = [c.sb("md%d" % i, [128, CG, 128], F32) for i in range(4)]
        Br_f = c.sb("Br_f", [128, CG, 128], BF16); Bi_f = c.sb("Bi_f", [128, CG, 128], BF16)
        Br_d = c.sb("Br_d", [128, CG, 128], BF16); Bi_d = c.sb("Bi_d", [128, CG, 128], BF16)
        Kr = [c.sb("Kr%d" % i, [128, CG, 128], F32) for i in range(2)]
        Ki = [c.sb("Ki%d" % i, [128, CG, 128], F32) for i in range(2)]
        Yr = c.sb("Yr", [128, CG, 128], BF16); Yi = c.sb("Yi", [128, CG, 128], BF16)
        Dr = c.sb("Dr", [128, CG, 128], BF16); Di = c.sb("Di", [128, CG, 128], BF16)
        Yo = [c.sb("Yo%d" % i, [64, CG, 128], F32) for i in range(2)]
        twc = tw[:, 0, :].unsqueeze(1).broadcast_to([128, CG, 128])
        tws = tw[:, 1, :].unsqueeze(1).broadcast_to([128, CG, 128])

        def cmul_st(m, mk_, ar, ai, ak, br, bi, bk, outr, outi, okr, oki, conj):
            m1, m2, m3, m4 = m
            k1, k2, k3, k4 = [mk_ + str(i) for i in range(4)]
            st = [
                lambda: c.op('dve', ak + bk, [k1], lambda e: e.tensor_tensor(out=m1[:], in0=ar, in1=br, op=ALU.mult)),
                lambda: c.op('dve', ak + bk, [k2], lambda e: e.tensor_tensor(out=m2[:], in0=ai, in1=bi, op=ALU.mult)),
                lambda: c.op('dve', ak + bk, [k3], lambda e: e.tensor_tensor(out=m3[:], in0=ar, in1=bi, op=ALU.mult)),
                lambda: c.op('dve', ak + bk, [k4], lambda e: e.tensor_tensor(out=m4[:], in0=ai, in1=br, op=ALU.mult)),
            ]
            if not conj:
                st += [lambda: c.op('pool', [k1, k2], okr, lambda e: e.tensor_tensor(out=outr, in0=m1[:], in1=m2[:], op=ALU.subtract)),
                       lambda: c.op('pool', [k3, k4], oki, lambda e: e.tensor_tensor(out=outi, in0=m3[:], in1=m4[:], op=ALU.add))]
            else:
                st += [lambda: c.op('pool', [k1, k2], okr, lambda e: e.tensor_tensor(out=outr, in0=m1[:], in1=m2[:], op=ALU.add)),
                       lambda: c.op('pool', [k3, k4], oki, lambda e: e.tensor_tensor(out=outi, in0=m4[:], in1=m3[:], op=ALU.subtract))]
            return st

        def fwd_st(src, ksrc, K, gi, PA, kpa, PXr, kxr, PXi, kxi, m, mk_, Br, kbr, Bi, kbi):
            def stageA():
                for cc in range(CG):
                    ch = gi * CG + cc
                    c.op('pe', [ksrc, 'FA'], [kpa],
                         lambda e, cc=cc, ch=ch: e.matmul(PA[:, cc, :], lhsT=src[0:K, ch, :], rhs=FA[0:K, :], start=True, stop=True))
            Bf = Br[:].rearrange("p c k -> p (c k)")
            Bg = Bi[:].rearrange("p c k -> p (c k)")
            xr = PXr[:].rearrange("p c k -> p (c k)")
            xi = PXi[:].rearrange("p c k -> p (c k)")

            def stageB():
                c.op('pe', ['Fm', kbr], [kxr], lambda e: e.matmul(xr, lhsT=Fm[:, 0, :], rhs=Bf, start=True, stop=False))
                c.op('pe', ['Fm', kbi], [kxr], lambda e: e.matmul(xr, lhsT=Fm[:, 1, :], rhs=Bg, start=False, stop=True))
                c.op('pe', ['Fm', kbi], [kxi], lambda e: e.matmul(xi, lhsT=Fm[:, 0, :], rhs=Bg, start=True, stop=False))
                c.op('pe', ['Fm', kbr], [kxi], lambda e: e.matmul(xi, lhsT=Fm[:, 2, :], rhs=Bf, start=False, stop=True))
            return ([stageA] + cmul_st(m, mk_, PA[:, :, 0:128], PA[:, :, 128:256], [kpa], twc, tws, ['tw'], Br[:], Bi[:], [kbr], [kbi], True)
                    + [stageB])

        def filt_fft(gi):
            s = gi % 2
            st = fwd_st(Kc, 'Kc', 128, gi, PA_f, 'PA_f', PXr_f, 'PXr_f', PXi_f, 'PXi_f', mf, 'mf', Br_f, 'Br_f', Bi_f, 'Bi_f')
            st += [lambda: c.op('act', ['PXr_f'], ['Kr%d' % s], lambda e: e.copy(out=Kr[s][:], in_=PXr_f[:])),
                   lambda: c.op('act', ['PXi_f'], ['Ki%d' % s], lambda e: e.copy(out=Ki[s][:], in_=PXi_f[:]))]
            return st

        yv = yscr.rearrange("c (a b) -> a c b", b=128)

        def data_fft(gi):
            s = gi % 2
            st = fwd_st(Xc, 'Xc', 64, gi, PA_d, 'PA_d', PXr_d, 'PXr_d', PXi_d, 'PXi_d', md, 'md', Br_d, 'Br_d', Bi_d, 'Bi_d')
            st += cmul_st(md, 'md', PXr_d[:], PXi_d[:], ['PXr_d', 'PXi_d'], Kr[s][:], Ki[s][:], ['Kr%d' % s, 'Ki%d' % s], Yr[:], Yi[:], ['Yr'], ['Yi'], False)
            PC = PA_d

            def stageC():
                for cc in range(CG):
                    c.op('pe', ['Yr', 'Gm'], ['PA_d'],
                         lambda e, cc=cc: e.matmul(PC[:, cc, :], lhsT=Yr[:, cc, :], rhs=Gm[:, 0, :], start=True, stop=False))
                    c.op('pe', ['Yi', 'Gm'], ['PA_d'],
                         lambda e, cc=cc: e.matmul(PC[:, cc, :], lhsT=Yi[:, cc, :], rhs=Gm[:, 1, :], start=False, stop=True))
            st += [stageC]
            st += cmul_st(md, 'md', PC[:, :, 0:128], PC[:, :, 128:256], ['PA_d'], twc, tws, ['tw'], Dr[:], Di[:], ['Dr'], ['Di'], False)
            pd = PXr_d[0:64].rearrange("p c k -> p (c k)")
            yo = Yo[s]
            ky = 'Yo%d' % s

            def stageD():
                c.op('pe', ['Fm', 'Dr'], ['PXr_d'],
                     lambda e: e.matmul(pd, lhsT=Fm[:, 0, 0:64], rhs=Dr[:].rearrange("p c k -> p (c k)"), start=True, stop=False))
                c.op('pe', ['Fm', 'Di'], ['PXr_d'],
                     lambda e: e.matmul(pd, lhsT=Fm[:, 2, 0:64], rhs=Di[:].rearrange("p c k -> p (c k)"), start=False, stop=True))
            st += [stageD,
                   lambda: c.op('act', ['PXr_d'], [ky], lambda e: e.activation(out=yo[:], in_=PXr_d[0:64], func=AF.Copy, scale=1.0 / 16384.0)),
                   lambda: c.dma('sp', [ky], ['yscr%d' % gi], yv[:, gi * CG:(gi + 1) * CG, :], yo[:])]
            return st

        lockstep(filt_fft(0))
        for gi in range(NG):
            if gi + 1 < NG:
                lockstep(data_fft(gi), filt_fft(gi + 1))
            else:
                lockstep(data_fft(gi))
    c.barrier()
    c.es = es
    with ExitStack() as es4:
        c.es = es4
        dd = c.sb("dd", [128, 1], F32)
        c.dma('sp', [], ['dd'], dd[:], hyd)
        yt = [c.sb("yt%d" % i, [128, 2048], F32) for i in range(2)]
        zd = c.sb("zd", [128, 2048], F32)
        ob = [c.sb("ob%d" % i, [128, 2048], BF16) for i in range(2)]
        allscr = ['yscr%d' % gi for gi in range(NG)]
        for q in range(4):
            s = q % 2
            sl = slice(q * 2048, (q + 1) * 2048)
            c.dma('sp', allscr, ['yt%d' % s], yt[s][:], yscr[:, sl])
            if dbg is not None:
                c.dma('sp', ['yt%d' % s], ['dbgy%d' % q], dbg['yc'][:, sl], yt[s][:])
            c.op('pool', ['zT', 'dd'], ['zd'], lambda e: e.tensor_scalar(out=zd[:], in0=zT[:, sl], scalar1=dd[:, 0:1], scalar2=None, op0=ALU.mult))
            c.op('dve', ['yt%d' % s, 'rl1', 'zd'], ['yt%d' % s],
                 lambda e: e.scalar_tensor_tensor(out=yt[s][:], in0=yt[s][:], scalar=rl1[:, 0:1], in1=zd[:], op0=ALU.mult, op1=ALU.add))
            c.op('dve', ['yt%d' % s, 'g0'], ['ob%d' % s], lambda e: e.tensor_tensor(out=ob[s][:], in0=yt[s][:], in1=g0[:, sl], op=ALU.mult))
            c.dma('sp', ['ob%d' % s], ['yb_out'], yb_out[:, sl], ob[s][:])
    c.barrier()
    c.es = es

import numpy as np

NT = 16
TS = 2048


def load_weights_plain(c, wd, wb, ncols, tag):
    stg = [c.sb("wstg%s%d" % (tag, i), [128, ncols], F32) for i in range(2)]
    for ch in range(8):
        s = stg[ch % 2]
        k = "wstg%s%d" % (tag, ch % 2)
        c.dma('sp', [], [k], s[:], wd[ch * 128:(ch + 1) * 128, :])
        c.op('dve', [k], ['wb' + tag], lambda e, s=s, ch=ch: e.tensor_copy(out=wb[:, ch, :], in_=s[:]))


def outproj_tile(c, i, yT, ykey, wo, po, xin_d, xt, kx):
    for half in range(2):
        for ech in range(8):
            c.op('pe', [ykey, 'wbO'], ['po'],
                 lambda e, half=half, ech=ech: e.matmul(po[:, half * 512:(half + 1) * 512], lhsT=yT[:, ech, i * 128:(i + 1) * 128],
                                                        rhs=wo[:, ech, half * 512:(half + 1) * 512], start=(ech == 0), stop=(ech == 7)))
    c.dma('sp', [], [kx], xt[:], xin_d[i * 128:(i + 1) * 128, :])
    c.op('dve', ['po', kx], [kx], lambda e: e.tensor_tensor(out=xt[:], in0=po[:], in1=xt[:], op=ALU.add))


def rstd_tile(c, xt, kx, junk, ssq, sd, rs, n):
    c.op('act', [kx], ['junk', 'ssq'], lambda e: e.activation(out=junk[:], in_=xt[:], func=AF.Square, accum_out=ssq[:, 0:1]))
    c.op('act', ['ssq'], ['sd'], lambda e: e.activation(out=sd[:], in_=ssq[:], func=AF.Sqrt, scale=1.0 / n, bias=EPS))
    c.op('dve', ['sd'], ['rs'], lambda e: e.reciprocal(out=rs[:], in_=sd[:]))


def emit_L2(c, es, y0T_d, xs_d, wout_d, win_d, onorm_d, cos_d, sin_d, identd, x1_d, latT_d, gcT_d, do_gate=True):
    with ExitStack() as es1:
        c.es = es1
        ident = c.sb("ident2", [128, 128], BF16)
        c.dma('sp', [], ['ident2'], ident[:], identd)
        yT = c.sb("y0T", [128, 8, TS], BF16)
        c.dma('sp', [], ['y0T'], yT[:], y0T_d.rearrange("(ch p) t -> p ch t", p=128))
        wo = c.sb("wbO", [128, 8, 1024], BF16)
        load_weights_plain(c, wout_d, wo, 1024, 'O')
        gain = c.sb("gainI", [128, 8], F32)
        c.dma('sp', [], ['gainI'], gain[:], onorm_d)
        wi = c.sb("wbI", [128, 8, 1696], BF16)
        load_weights_scaled(c, win_d, gain, 'gainI', wi, 1696, 'I')
        cosS = c.sb("cos2", [128, NT, 16], F32)
        sinS = c.sb("sin2", [128, NT, 16], F32)
        c.dma('sp', [], ['cos2'], cosS[:], cos_d.rearrange("p (t j) -> p t j", j=16))
        c.dma('sp', [], ['sin2'], sinS[:], sin_d.rearrange("p (t j) -> p t j", j=16))
        h1T = c.sb("h1T", [128, 8, TS], BF16)
        po = c.ps("po", [128, 1024])
        ptr = c.ps("ptr2", [128, 8, 128], BF16)
        pl = c.ps("pl", [128, 1024])
        pg = c.ps("pg", [128, 512])
        plt = c.ps("plt", [128, 6, 128], BF16)
        ltT = [c.sb("ltT%d" % i, [128, 6, 128], BF16) for i in range(2)]
        xt = [c.sb("xt%d" % i, [128, 1024], F32) for i in range(2)]
        junk = c.sb("junk", [128, 1024], F32)
        ssq = c.sb("ssq", [128, 1], F32)
        sd = c.sb("sd", [128, 1], F32)
        rs = c.sb("rs", [128, 1], F32)
        h1 = c.sb("h1", [128, 1024], BF16)
        ss2 = c.sb("ss2", [128, 2], F32)
        sd2 = c.sb("sd2", [128, 2], F32)
        rs2 = c.sb("rs2", [128, 2], F32)
        latn = [c.sb("latn%d" % i, [128, 672], BF16) for i in range(2)]
        k1 = c.sb("k1", [128, 32], F32)
        k2 = c.sb("k2", [128, 32], F32)
        gt = [c.sb("gt%d" % i, [128, 512], BF16) for i in range(2)]
        for i in range(NT):
            s = i % 2
            kx = 'xt%d' % s
            outproj_tile(c, i, yT, 'y0T', wo, po, xs_d, xt[s], kx)
            c.dma('sp', [kx], ['x1_d'], x1_d[i * 128:(i + 1) * 128, :], xt[s][:])
            rstd_tile(c, xt[s], kx, junk, ssq, sd, rs, D)
            c.op('act', [kx, 'rs'], ['h1'], lambda e: e.activation(out=h1[:], in_=xt[s][:], func=AF.Identity, scale=rs[:, 0:1]))
            for ch in range(8):
                c.op('pe', ['h1', 'ident2'], ['ptr2'],
                     lambda e, ch=ch: e.transpose(out=ptr[:, ch, :], in_=h1[:, ch * 128:(ch + 1) * 128], identity=ident[:]))
            c.op('dve', ['ptr2'], ['h1T'], lambda e: e.tensor_copy(out=h1T[:, :, i * 128:(i + 1) * 128], in_=ptr[:]))
            for (lo, hi) in ((0, 512), (512, 672)):
                for ch in range(8):
                    c.op('pe', ['h1T', 'wbI'], ['pl'],
                         lambda e, ch=ch, lo=lo, hi=hi: e.matmul(pl[:, lo:hi], lhsT=h1T[:, ch, i * 128:(i + 1) * 128], rhs=wi[:, ch, lo:hi],
                                                                 start=(ch == 0), stop=(ch == 7)))
            c.op('act', ['pl'], ['junk', 'ss2'], lambda e: e.activation(out=junk[:, 0:384], in_=pl[:, 0:384], func=AF.Square, accum_out=ss2[:, 0:1]))
            c.op('act', ['pl'], ['junk', 'ss2'], lambda e: e.activation(out=junk[:, 384:640], in_=pl[:, 384:640], func=AF.Square, accum_out=ss2[:, 1:2]))
            c.op('act', ['ss2'], ['sd2'], lambda e: e.activation(out=sd2[:, 0:1], in_=ss2[:, 0:1], func=AF.Sqrt, scale=1.0 / 384, bias=EPS))
            c.op('act', ['ss2'], ['sd2'], lambda e: e.activation(out=sd2[:, 1:2], in_=ss2[:, 1:2], func=AF.Sqrt, scale=1.0 / 256, bias=EPS))
            c.op('dve', ['sd2'], ['rs2'], lambda e: e.reciprocal(out=rs2[:], in_=sd2[:]))
            ln = latn[s]
            kl = 'latn%d' % s
            c.op('act', ['pl', 'rs2'], [kl], lambda e: e.activation(out=ln[:, 0:384], in_=pl[:, 0:384], func=AF.Identity, scale=rs2[:, 0:1]))
            c.op('act', ['pl', 'rs2'], [kl], lambda e: e.activation(out=ln[:, 384:640], in_=pl[:, 384:640], func=AF.Identity, scale=rs2[:, 1:2]))
            krv = pl[:, 640:672].rearrange("p (a j) -> p a j", a=2)
            cb = cosS[:, i, :].unsqueeze(1).broadcast_to([128, 2, 16])
            sb_ = sinS[:, i, :].unsqueeze(1).broadcast_to([128, 2, 16])
            c.op('dve', ['pl', 'cos2'], ['k1'], lambda e: e.tensor_tensor(out=k1[:].rearrange("p (a j) -> p a j", a=2), in0=krv, in1=cb, op=ALU.mult))
            c.op('dve', ['pl', 'sin2'], ['k2'], lambda e: e.tensor_tensor(out=k2[:].rearrange("p (a j) -> p a j", a=2), in0=krv, in1=sb_, op=ALU.mult))
            c.op('dve', ['k1', 'k2'], [kl], lambda e: e.tensor_tensor(out=ln[:, 640:656], in0=k1[:, 0:16], in1=k2[:, 16:32], op=ALU.subtract))
            c.op('dve', ['k1', 'k2'], [kl], lambda e: e.tensor_tensor(out=ln[:, 656:672], in0=k2[:, 0:16], in1=k1[:, 16:32], op=ALU.add))
            for ch in range(6):
                wdt = 128 if ch < 5 else 32
                c.op('pe', [kl, 'ident2'], ['plt'],
                     lambda e, ch=ch, wdt=wdt: e.transpose(out=plt[0:wdt, ch, :], in_=ln[:, ch * 128:ch * 128 + wdt], identity=ident[:]))
            lt = ltT[s]
            klt = 'ltT%d' % s
            c.op('dve', ['plt'], [klt], lambda e: e.tensor_copy(out=lt[:, 0:5, :], in_=plt[:, 0:5, :]))
            c.op('dve', ['plt'], [klt], lambda e: e.tensor_copy(out=lt[0:32, 5, :], in_=plt[0:32, 5, :]))
            c.dma('sp', [klt], ['latT_d'], latT_d[0:640, i * 128:(i + 1) * 128].rearrange("(ch p) t -> p ch t", p=128), lt[:, 0:5, :])
            c.dma('sp', [klt], ['latT_d'], latT_d[640:672, i * 128:(i + 1) * 128], lt[0:32, 5, :])
        n = 0
        for ec in (range(8) if do_gate else []):
            for blk in range(4):
                for ch in range(8):
                    c.op('pe', ['h1T', 'wbI'], ['pg'],
                         lambda e, ch=ch: e.matmul(pg[:], lhsT=wi[:, ch, 672 + ec * 128:672 + (ec + 1) * 128], rhs=h1T[:, ch, blk * 512:(blk + 1) * 512],
                                                   start=(ch == 0), stop=(ch == 7)))
                g = gt[n % 2]
                kg = 'gt%d' % (n % 2)
                c.op('act', ['pg'], [kg], lambda e: e.activation(out=g[:], in_=pg[:], func=AF.Silu))
                c.dma('sp', [kg], ['gcT_d'], gcT_d[ec * 128:(ec + 1) * 128, blk * 512:(blk + 1) * 512], g[:])
                n += 1
    c.barrier()
    c.es = es


def emit_L4(c, es, ocT_d, gcT_d, x1_d, wout_d, fnorm_d, out_d):
    with ExitStack() as es1:
        c.es = es1
        yT = c.sb("ycT", [128, 8, TS], BF16)
        gT = c.sb("gcT", [128, 8, TS], BF16)
        c.dma('sp', [], ['ycT'], yT[:], ocT_d.rearrange("(ch p) t -> p ch t", p=128))
        c.dma('sp', [], ['gcT'], gT[:], gcT_d.rearrange("(ch p) t -> p ch t", p=128))
        for ch in range(8):
            eng = 'dve' if ch % 2 == 0 else 'pool'
            c.op(eng, ['ycT', 'gcT'], ['ycT'], lambda e, ch=ch: e.tensor_tensor(out=yT[:, ch, :], in0=yT[:, ch, :], in1=gT[:, ch, :], op=ALU.mult))
        wo = c.sb("wbO", [128, 8, 1024], BF16)
        load_weights_plain(c, wout_d, wo, 1024, 'O')
        fn = c.sb("fn", [128, 1024], F32)
        c.dma('sp', [], ['fn'], fn[:], fnorm_d.partition_broadcast(128))
        po = c.ps("po", [128, 1024])
        xt = [c.sb("xt%d" % i, [128, 1024], F32) for i in range(2)]
        ot = [c.sb("ot%d" % i, [128, 1024], F32) for i in range(2)]
        junk = c.sb("junk", [128, 1024], F32)
        ssq = c.sb("ssq", [128, 1], F32)
        sd = c.sb("sd", [128, 1], F32)
        rs = c.sb("rs", [128, 1], F32)
        for i in range(NT):
            s = i % 2
            kx = 'xt%d' % s
            outproj_tile(c, i, yT, 'ycT', wo, po, x1_d, xt[s], kx)
            rstd_tile(c, xt[s], kx, junk, ssq, sd, rs, D)
            c.op('dve', [kx, 'rs', 'fn'], ['ot%d' % s],
                 lambda e: e.scalar_tensor_tensor(out=ot[s][:], in0=xt[s][:], scalar=rs[:, 0:1], in1=fn[:], op0=ALU.mult, op1=ALU.mult))
            c.dma('sp', ['ot%d' % s], ['out_d'], out_d[i * 128:(i + 1) * 128, :], ot[s][:])
    c.barrier()
    c.es = es

import numpy as np

L = 8192
NQB = 16
SCALE3 = 96.0 ** -0.5


def emit_attn_L1(c, es, latT_d, wq_d, wk_d, wv_d, qgain_d, kvgain_d, cosT_d, sinT_d, swapd, oc_d, nqb=NQB, dbg=None, cq_d=None):
    if cq_d is None:
        cq_d = latT_d[0:384, :]
    ckT = c.sb("ckT", [128, 2, L], BF16)
    c.dma('sp', [], ['ckT'], ckT[:], latT_d[384:640, :].rearrange("(ch p) t -> p ch t", p=128))
    qg = c.sb("qg", [128, 3], F32)
    kg = c.sb("kg", [128, 2], F32)
    c.dma('sp', [], ['qg'], qg[:], qgain_d)
    c.dma('sp', [], ['kg'], kg[:], kvgain_d)
    wq = c.sb("wq", [128, 3, 4, 96], BF16)
    wqr = c.sb("wqr", [128, 3, 4, 96], BF16)
    c.op('pool', [], ['wqr'], lambda e: e.memset(wqr[:], 0.0))
    wk = c.sb("wk", [128, 2, 4, 64], BF16)
    wv = c.sb("wv", [128, 2, 4, 64], BF16)
    swp = c.sb("swp3", [128, 128], F32)
    c.dma('sp', [], ['swp3'], swp[:], swapd)
    with ExitStack() as es0:
        c.es = es0
        st = c.sb("wst3", [128, 4 * 96], F32)
        for ch in range(3):
            c.dma('sp', [], ['wst3'], st[:], wq_d[ch * 128:(ch + 1) * 128].rearrange("p h j -> p (h j)"))
            sv = st[:].rearrange("p (h j) -> p h j", h=4)
            c.op('dve', ['wst3', 'qg'], ['wq'], lambda e, ch=ch: e.tensor_scalar(out=wq[:, ch], in0=sv, scalar1=qg[:, ch:ch + 1], scalar2=None, op0=ALU.mult))
            c.op('dve', ['wst3', 'qg'], ['wqr'],
                 lambda e, ch=ch: e.tensor_scalar(out=wqr[:, ch, :, 64:80], in0=sv[:, :, 80:96], scalar1=qg[:, ch:ch + 1], scalar2=-1.0, op0=ALU.mult, op1=ALU.mult))
            c.op('dve', ['wst3', 'qg'], ['wqr'],
                 lambda e, ch=ch: e.tensor_scalar(out=wqr[:, ch, :, 80:96], in0=sv[:, :, 64:80], scalar1=qg[:, ch:ch + 1], scalar2=None, op0=ALU.mult))
        for ch in range(2):
            c.dma('sp', [], ['wst3'], st[:, 0:256], wk_d[ch * 128:(ch + 1) * 128].rearrange("p h j -> p (h j)"))
            sv = st[:, 0:256].rearrange("p (h j) -> p h j", h=4)
            c.op('dve', ['wst3', 'kg'], ['wk'], lambda e, ch=ch: e.tensor_scalar(out=wk[:, ch], in0=sv, scalar1=kg[:, ch:ch + 1], scalar2=None, op0=ALU.mult))
        for ch in range(2):
            c.dma('sp', [], ['wst3'], st[:, 0:256], wv_d[ch * 128:(ch + 1) * 128].rearrange("p h j -> p (h j)"))
            sv = st[:, 0:256].rearrange("p (h j) -> p h j", h=4)
            c.op('dve', ['wst3', 'kg'], ['wv'], lambda e, ch=ch: e.tensor_scalar(out=wv[:, ch], in0=sv, scalar1=kg[:, ch:ch + 1], scalar2=None, op0=ALU.mult))
    c.barrier()
    c.es = es
    KT = [c.sb("KT3%d" % h, [96, L], BF16) for h in range(2)]
    VA = c.sb("VA3", [128, 64, 128], BF16)
    VB = c.sb("VB3", [128, 64, 128], BF16)
    c.op('pool', [], ['VA3'], lambda e: e.memset(VA[:, :, 64:128], 1.0))
    c.op('pool', [], ['VB3'], lambda e: e.memset(VB[:, :, 0:64], 1.0))
    for h in range(2):
        c.dma('sp', [], ['KT3%d' % h], KT[h][64:96, :], latT_d[640:672, :])
    for pair in range(2):
        with ExitStack() as esA:
            c.es = esA
            pk = c.ps("pk", [128, 512])
            pv = c.ps("pv", [128, 4, 128])
            for h in range(2):
                hh = pair * 2 + h
                for blk in range(16):
                    for ch in range(2):
                        c.op('pe', ['wk', 'ckT'], ['pk'],
                             lambda e, ch=ch: e.matmul(pk[0:64, :], lhsT=wk[:, ch, hh, :], rhs=ckT[:, ch, blk * 512:(blk + 1) * 512],
                                                       start=(ch == 0), stop=(ch == 1)))
                    eng = 'act' if blk % 2 == 0 else 'dve'
                    if eng == 'act':
                        c.op('act', ['pk'], ['KT3%d' % h], lambda e: e.copy(out=KT[h][0:64, blk * 512:(blk + 1) * 512], in_=pk[0:64, :]))
                    else:
                        c.op('dve', ['pk'], ['KT3%d' % h], lambda e: e.tensor_copy(out=KT[h][0:64, blk * 512:(blk + 1) * 512], in_=pk[0:64, :]))
            for g4 in range(16):
                for tt in range(4):
                    kt = g4 * 4 + tt
                    for ch in range(2):
                        c.op('pe', ['wv', 'ckT'], ['pv'],
                             lambda e, ch=ch, tt=tt, kt=kt: e.matmul(pv[:, tt, :], lhsT=ckT[:, ch, kt * 128:(kt + 1) * 128],
                                                                     rhs=wv[:, ch, pair * 2:pair * 2 + 2, :].rearrange("p h j -> p (h j)"),
                                                                     start=(ch == 0), stop=(ch == 1)))
                c.op('act', ['pv'], ['VA3'], lambda e: e.copy(out=VA[:, g4 * 4:(g4 + 1) * 4, 0:64], in_=pv[:, :, 0:64]))
                c.op('dve', ['pv'], ['VB3'], lambda e: e.tensor_copy(out=VB[:, g4 * 4:(g4 + 1) * 4, 64:128], in_=pv[:, :, 64:128]))
        c.barrier()
        c.es = es
        if dbg is not None and pair == 0:
            c.dma('sp', ['KT30'], ['dbgK'], dbg['KT'], KT[0][:])
            c.dma('sp', ['VB3'], ['dbgV'], dbg['VB'], VB[:].rearrange("p t j -> p (t j)"))
        with ExitStack() as esB:
            c.es = esB
            PS = [c.ps("PS%d" % i, [128, 2, 512]) for i in range(3)]
            PO = c.ps("PO", [128, 2, 512])
            PQ = PS[0][:, 0, :]
            PQR = PS[1][:, 0, :]
            PS2 = PS[2][:, 0, :]
            PT = [c.sb("PT%d" % i, [128, 2, 512], BF16) for i in range(3)]
            S_sb = c.sb("S_sb", [128, 512], F32)
            O_sb = c.sb("O_sb", [128, 512], F32)
            rinv = c.sb("rinv", [128, 512], F32)
            yo = [c.sb("yo%d" % i, [128, 512], BF16) for i in range(2)]
            cq = [c.sb("cq%d" % i, [128, 3, 512], BF16) for i in range(2)]
            cs = [c.sb("cs%d" % i, [128, 2, 512], F32) for i in range(2)]
            QT = [c.sb("QT3%d" % h, [96, nqb * 512], BF16) for h in range(2)]
            m1 = c.sb("m1", [128, 512], F32)
            m2 = c.sb("m2", [128, 512], F32)

            def loadq(qb):
                s = qb % 2
                c.dma('sp', [], ['cq%d' % s], cq[s][:], cq_d[:, qb * 512:(qb + 1) * 512].rearrange("(ch p) t -> p ch t", p=128))
                c.dma('sp', [], ['cs%d' % s], cs[s][64:96, 0, :], cosT_d[:, qb * 512:(qb + 1) * 512])
                c.dma('sp', [], ['cs%d' % s], cs[s][64:96, 1, :], sinT_d[:, qb * 512:(qb + 1) * 512])

            def projq(qb):
                s = qb % 2
                for h in range(2):
                    hh = pair * 2 + h
                    q = QT[h][:, qb * 512:(qb + 1) * 512]
                    kq = 'QT3%d' % h
                    for ch in range(3):
                        c.op('pe', ['wq', 'cq%d' % s], ['PS0'],
                             lambda e, ch=ch: e.matmul(PQ[0:96, :], lhsT=wq[:, ch, hh, :], rhs=cq[s][:, ch, :], start=(ch == 0), stop=(ch == 2)))
                    for ch in range(3):
                        c.op('pe', ['wqr', 'cq%d' % s], ['PS1'],
                             lambda e, ch=ch: e.matmul(PQR[0:96, :], lhsT=wqr[:, ch, hh, :], rhs=cq[s][:, ch, :], start=(ch == 0), stop=(ch == 2)))
                    c.op('dve', ['PS0'], [kq], lambda e: e.tensor_copy(out=q[0:64, :], in_=PQ[0:64, :]))
                    c.op('dve', ['PS0', 'cs%d' % s], ['m1'], lambda e: e.tensor_tensor(out=m1[64:96, :], in0=PQ[64:96, :], in1=cs[s][64:96, 0, :], op=ALU.mult))
                    c.op('dve', ['PS1', 'cs%d' % s], ['m2'], lambda e: e.tensor_tensor(out=m2[64:96, :], in0=PQR[64:96, :], in1=cs[s][64:96, 1, :], op=ALU.mult))
                    c.op('pool', ['m1', 'm2'], [kq], lambda e: e.tensor_tensor(out=q[64:96, :], in0=m1[64:96, :], in1=m2[64:96, :], op=ALU.add))

            def S(qb, kt):
                s = kt % 3
                for h in range(2):
                    c.op('pe', ['KT3%d' % h, 'QT3%d' % h], ['PS%d' % s],
                         lambda e, h=h: e.matmul(PS[s][:, h, :], lhsT=KT[h][:, kt * 128:(kt + 1) * 128], rhs=QT[h][:, qb * 512:(qb + 1) * 512],
                                                 start=True, stop=True))

            loadq(0)
            for qb in range(nqb):
                if qb + 1 < nqb:
                    loadq(qb + 1)
                projq(qb)
            S(0, 0)
            S(0, 1)
            for qb in range(nqb):
                for kt in range(64):
                    s = kt % 3
                    if kt + 2 < 64:
                        S(qb, kt + 2)
                    c.op('act', ['PS%d' % s], ['PT%d' % s], lambda e: e.activation(out=PT[s][:], in_=PS[s][:], func=AF.Exp, scale=SCALE3))
                    c.op('pe', ['VA3', 'PT%d' % s], ['PO'],
                         lambda e: e.matmul(PO[:, 0, :], lhsT=VA[:, kt, :], rhs=PT[s][:, 0, :], start=(kt == 0), stop=(kt == 63)))
                    c.op('pe', ['VB3', 'PT%d' % s], ['PO'],
                         lambda e: e.matmul(PO[:, 1, :], lhsT=VB[:, kt, :], rhs=PT[s][:, 1, :], start=(kt == 0), stop=(kt == 63)))
                if qb + 1 < nqb:
                    S(qb + 1, 0)
                    S(qb + 1, 1)
                c.op('dve', ['PO'], ['S_sb'], lambda e: e.tensor_copy(out=S_sb[64:128, :], in_=PO[64:128, 0, :]))
                c.op('dve', ['PO'], ['S_sb'], lambda e: e.tensor_copy(out=S_sb[0:64, :], in_=PO[0:64, 1, :]))
                c.op('dve', ['PO'], ['O_sb'], lambda e: e.tensor_copy(out=O_sb[0:64, :], in_=PO[0:64, 0, :]))
                c.op('dve', ['PO'], ['O_sb'], lambda e: e.tensor_copy(out=O_sb[64:128, :], in_=PO[64:128, 1, :]))
                c.op('pe', ['swp3', 'S_sb'], ['PS2'], lambda e: e.matmul(PS2, lhsT=swp[:], rhs=S_sb[:], start=True, stop=True))
                c.op('dve', ['PS2'], ['rinv'], lambda e: e.reciprocal(out=rinv[:], in_=PS2))
                y = yo[qb % 2]
                ky = 'yo%d' % (qb % 2)
                c.op('dve', ['O_sb', 'rinv'], [ky], lambda e: e.tensor_tensor(out=y[:], in0=O_sb[:], in1=rinv[:], op=ALU.mult))
                c.dma('sp', [ky], ['oc_d'], oc_d[pair, :, qb * 512:(qb + 1) * 512], y[:])
        c.barrier()
        c.es = es

import numpy as np


def emit_select(c, es, sel_d, x1s, gcT, latT, x1_own, gc_own, cq_own):
    with ExitStack() as es1:
        c.es = es1
        sel = c.sb("sel", [128, 4], F32)
        c.dma('sp', [], ['sel'], sel[:], sel_d)
        xa = [c.sb("xa%d" % i, [128, 4, 1024], F32) for i in range(2)]
        xo = [c.sb("xo%d" % i, [128, 1024], F32) for i in range(2)]
        xv = x1s.rearrange("(ts i p) d -> i p ts d", ts=4, p=128)
        for i in range(16):
            s = i % 2
            ka, ko = 'xa%d' % s, 'xo%d' % s
            c.dma('sp', [], [ka], xa[s][:], xv[i])
            c.op('dve', [ka, 'sel'], [ko], lambda e: e.tensor_scalar(out=xo[s][:], in0=xa[s][:, 0, :], scalar1=sel[:, 0:1], scalar2=None, op0=ALU.mult))
            for ts in range(1, 4):
                c.op('dve', [ka, 'sel', ko], [ko],
                     lambda e, ts=ts: e.scalar_tensor_tensor(out=xo[s][:], in0=xa[s][:, ts, :], scalar=sel[:, ts:ts + 1], in1=xo[s][:], op0=ALU.mult, op1=ALU.add))
            c.dma('sp', [ko], ['x1_own'], x1_own[i * 128:(i + 1) * 128, :], xo[s][:])
        ga = [c.sb("ga%d" % i, [128, 4, 2048], BF16) for i in range(2)]
        go = [c.sb("go%d" % i, [128, 2048], BF16) for i in range(2)]
        jobs = [(gcT[ch * 128:(ch + 1) * 128, :], gc_own[ch * 128:(ch + 1) * 128, :]) for ch in range(8)]
        jobs += [(latT[ch * 128:(ch + 1) * 128, :], cq_own[ch * 128:(ch + 1) * 128, :]) for ch in range(3)]
        for n, (src, dst) in enumerate(jobs):
            s = n % 2
            ka, ko = 'ga%d' % s, 'go%d' % s
            c.dma('sp', [], [ka], ga[s][:], src.rearrange("p (ts t) -> p ts t", ts=4))
            c.op('dve', [ka, 'sel'], [ko], lambda e: e.tensor_scalar(out=go[s][:], in0=ga[s][:, 0, :], scalar1=sel[:, 0:1], scalar2=None, op0=ALU.mult))
            for ts in range(1, 4):
                c.op('dve', [ka, 'sel', ko], [ko],
                     lambda e, ts=ts: e.scalar_tensor_tensor(out=go[s][:], in0=ga[s][:, ts, :], scalar=sel[:, ts:ts + 1], in1=go[s][:], op0=ALU.mult, op1=ALU.add))
            c.dma('sp', [ko], ['own%d' % n], dst, go[s][:])
    c.barrier()
    c.es = es


_CACHE = {}


def _dram(nc, kind, n, shp, dt=F32):
    return nc.dram_tensor(n, list(shp), dt, kind=kind).ap()


def build_fused():
    nc = bass.Bass("TRN2", target_bir_lowering=False)
    I = lambda n, s, dt=F32: _dram(nc, "ExternalInput", n, s, dt)
    O = lambda n, s, dt=F32: _dram(nc, "ExternalOutput", n, s, dt)
    S = lambda n, s, dt=F32: _dram(nc, "Internal", n, s, dt)
    xT = I("xT", [1024, 8192]); xtm = I("xtm", [8192, 1024]); enorm = I("enorm", [128, 8])
    w_tm = I("w_tm", [4, 1024, 320]); w_ga = I("w_ga", [4, 1024, 128]); qkgain = I("qkgain", [1, 256])
    cos_tm = I("cos_tm", [128, 2048]); sin_tm = I("sin_tm", [128, 2048]); swapd = I("swapd", [128, 128]); identd = I("identd", [128, 128], BF16)
    w_hy = I("w_hy", [4, 1024, 512]); convw = I("convw", [4, 128, 12]); fparams = I("fparams", [64, 4])
    w1d = I("w1d", [33, 64]); w2d = I("w2d", [64, 64]); w3d = I("w3d", [4, 64, 256]); zTf = I("zTf", [33, 8192]); zTr = I("zTr", [33, 8192]); trow = I("trow", [2, 8192])
    negdel = I("negdel", [4, 128, 1]); hyd = I("hyd", [4, 128, 1]); FAd = I("FAd", [128, 256], BF16); Fd = I("Fd", [128, 3, 128], BF16)
    Gd = I("Gd", [128, 2, 256], BF16); twd = I("twd", [128, 2, 128])
    wout = I("wout", [1024, 1024]); win = I("win", [1024, 1696]); onorm = I("onorm", [128, 8]); cos2 = I("cos2", [4, 128, 256]); sin2 = I("sin2", [4, 128, 256])
    wq = I("wq", [4, 384, 4, 96]); wk = I("wk", [4, 256, 4, 64]); wv = I("wv", [4, 256, 4, 64]); qgain = I("qgain", [128, 3]); kvgain = I("kvgain", [128, 2])
    cosT = I("cosT", [32, 2048]); sinT = I("sinT", [32, 2048]); sel = I("sel", [128, 4])
    wout2 = I("wout2", [1024, 1024]); fnorm = I("fnorm", [1, 1024])
    kscr = S("kscr", [128, 16384], BF16); zscr = S("zscr", [128, 8192], BF16); yscr = S("yscr", [128, 8192])
    y0T = S("y0Ts", [1024, 8192], BF16); x1s = S("x1s", [8192, 1024]); latT = S("latTs", [672, 8192], BF16)
    gcT = S("gcTs", [1024, 8192], BF16)
    x1o = S("x1own", [2048, 1024]); gco = S("gcown", [1024, 2048], BF16); cqo = S("cqown", [384, 2048], BF16); oco = S("ocown", [1024, 2048], BF16)
    out = O("out", [2048, 1024])
    with ExitStack() as es:
        c = Ctx(nc, es)

        def scoped(fn):
            with ExitStack() as e1:
                c.es = e1
                fn(e1)
            c.barrier()
            c.es = es

        for g in range(4):
            scoped(lambda e1: emit_attn_L0(c, e1, xT, w_tm[g], w_ga[g], enorm, qkgain, cos_tm, sin_tm, swapd, identd, y0T[g * 128:(g + 1) * 128, :]))
            scoped(lambda e1: emit_hyena_L0(c, e1, xT, w_hy[g], enorm, convw[g], fparams, w1d, w2d, w3d[g], zTf, zTr, trow, negdel[g], hyd[g],
                                            FAd, Fd, Gd, twd, kscr, zscr, yscr, y0T[512 + g * 128:512 + (g + 1) * 128, :]))
        for ts in range(4):
            sl = slice(ts * 2048, (ts + 1) * 2048)
            scoped(lambda e1: emit_L2(c, e1, y0T[:, sl], xtm[sl, :], wout, win, onorm, cos2[ts], sin2[ts], identd, x1s[sl, :], latT[:, sl], gcT[:, sl]))
        emit_select(c, es, sel, x1s, gcT, latT, x1o, gco, cqo)
        for g in range(4):
            scoped(lambda e1: emit_attn_L1(c, e1, latT, wq[g], wk[g], wv[g], qgain, kvgain, cosT, sinT, swapd,
                                           oco[g * 256:(g + 1) * 256, :].rearrange("(a p) t -> a p t", p=128), nqb=4, cq_d=cqo))
        scoped(lambda e1: emit_L4(c, e1, oco, gco, x1o, wout2, fnorm, out))
        c.finish()
        print("fused program: inst", c.ninst, "waits", c.nwaits, dict(c.cnt))
    return nc


def _get(name, fn):
    if name not in _CACHE:
        _CACHE[name] = fn()
    return _CACHE[name]


def fused_inputs(d, b, C, H, C32, C3):
    m = {}
    a = [l1a_inputs(d, b, g, C) for g in range(4)]
    h = [l1b_inputs(d, b, g, H) for g in range(4)]
    m.update(xT=a[0]['xT'], xtm=np.ascontiguousarray(d['x'][b]), enorm=a[0]['enorm'], qkgain=a[0]['qkgain'],
             cos_tm=C['cos64'], sin_tm=C['sin64'], swapd=C['swap'], identd=C['ident'],
             w_tm=np.stack([x['w_tm'] for x in a]), w_ga=np.stack([x['w_ga'] for x in a]),
             w_hy=np.stack([x['w_hy'] for x in h]), convw=np.stack([x['convw'] for x in h]), fparams=h[0]['fparams'],
             w1d=h[0]['w1d'], w2d=h[0]['w2d'], w3d=np.stack([x['w3d'] for x in h]), zTf=H['zTf'], zTr=H['zTr'], trow=H['trow'],
             negdel=np.stack([x['negdel'] for x in h]), hyd=np.stack([x['hyd'] for x in h]), FAd=H['FA'], Fd=H['Fd'], Gd=H['Gd'], twd=H['tw'],
             wout=d['e_w_out'][0], win=d['o_w_in'][0], onorm=np.ascontiguousarray(d['o_norm'][0].reshape(8, 128).T),
             cos2=np.stack([tm16(C32[0][ts * 2048:(ts + 1) * 2048]) for ts in range(4)]),
             sin2=np.stack([tm16(C32[1][ts * 2048:(ts + 1) * 2048]) for ts in range(4)]))
    l3 = [l3_inputs(d, b, g, None, C3, C['swap']) for g in range(4)]
    m.update(wq=np.stack([x['wq'] for x in l3]), wk=np.stack([x['wk'] for x in l3]), wv=np.stack([x['wv'] for x in l3]),
             qgain=l3[0]['qgain'], kvgain=l3[0]['kvgain'], cosT=C3['cosT'], sinT=C3['sinT'],
             wout2=d['o_w_out'][0], fnorm=d['final_norm'][None, :].astype(np.float32))
    return m


def kernel(**inputs):
    d = {k: np.asarray(v) for k, v in inputs.items()}
    C = consts(); H = hy_consts(); C32 = angles(32); C3 = l3_consts()
    per_batch = [fused_inputs(d, b, C, H, C32, C3) for b in range(2)]
    ins = []
    for i in range(8):
        q = i % 4
        m = dict(per_batch[i // 4])
        onehot = np.zeros((128, 4), np.float32); onehot[:, q] = 1.0
        m.update(sel=onehot, cosT=np.ascontiguousarray(C3['cosT'][:, q * 2048:(q + 1) * 2048]), sinT=np.ascontiguousarray(C3['sinT'][:, q * 2048:(q + 1) * 2048]))
        ins.append(m)
    res = run_bass_kernel_spmd(_get('F', build_fused), ins, core_ids=list(range(8))).results
    out = np.empty((2, 8192, 1024), np.float32)
    for i in range(8):
        b, q = i // 4, i % 4
        out[b, q * 2048:(q + 1) * 2048] = np.asarray(res[i]['out'])
    return out
```

```python
import math
import ml_dtypes

import numpy as np
from contextlib import ExitStack
import concourse.bass as bass
import concourse.mybir as mybir
from concourse.bass_utils import run_bass_kernel_spmd

F32 = mybir.dt.float32
BF16 = mybir.dt.bfloat16
AF = mybir.ActivationFunctionType
ALU = mybir.AluOpType
AX = mybir.AxisListType

N_DMA_SEMS = 24


class Ctx:
    def __init__(self, nc, es):
        self.nc = nc
        self.es = es
        self.es_root = es
        self.eng = {'pe': nc.tensor, 'act': nc.scalar, 'dve': nc.vector,
                    'pool': nc.gpsimd, 'sp': nc.sync}
        self.semobj = {}
        for e in ['pe', 'act', 'dve', 'pool']:
            self.semobj[e] = es.enter_context(nc.semaphore("s_" + e))
        self.cnt = {e: 0 for e in ['pe', 'act', 'dve', 'pool']}
        self.seen = {e: {} for e in self.eng}
        self.dma_use = []
        for i in range(N_DMA_SEMS):
            k = "d%d" % i
            self.semobj[k] = es.enter_context(nc.semaphore("s_" + k))
            self.dma_use.append(0)
        self.dma_rr = 0
        self.last_w = {}
        self.readers = {}
        self.nwaits = 0
        self.ninst = 0
        self.excl = set()
        self.uid = 0

    def sb(self, name, shape, dt):
        self.uid += 1
        return self.es.enter_context(self.nc.sbuf_tensor("%s_u%d" % (name, self.uid), list(shape), dt))

    def ps(self, name, shape, dt=F32):
        self.excl.add(name)
        self.uid += 1
        return self.es.enter_context(self.nc.psum_tensor("%s_u%d" % (name, self.uid), list(shape), dt))

    def _wait(self, e, sk, val):
        if self.seen[e].get(sk, 0) < val:
            self.eng[e].wait_ge(self.semobj[sk], val)
            self.seen[e][sk] = val
            self.nwaits += 1

    def _deps(self, e, reads, writes):
        need = {}
        ex = [k for k in reads if k in self.excl]
        if ex:
            writes = list(writes) + ex
            reads = [k for k in reads if k not in self.excl]

        def add(tok, kind):
            sk, val = tok
            if sk == 'pe' and e == 'pe':
                return
            if sk == e and kind != 'raw':
                return
            if need.get(sk, 0) < val:
                need[sk] = val

        for k in reads:
            if k in self.last_w:
                add(self.last_w[k], 'raw')
        for k in writes:
            if k in self.last_w:
                add(self.last_w[k], 'waw')
            for sk, val in self.readers.get(k, {}).items():
                add((sk, val), 'war')
        for sk, val in need.items():
            self._wait(e, sk, val)

    def _done(self, tok, reads, writes):
        ex = [k for k in reads if k in self.excl]
        if ex:
            writes = list(writes) + ex
            reads = [k for k in reads if k not in self.excl]
        for k in writes:
            self.last_w[k] = tok
            self.readers[k] = {}
        for k in reads:
            r = self.readers.setdefault(k, {})
            if r.get(tok[0], 0) < tok[1]:
                r[tok[0]] = tok[1]

    def op(self, e, reads, writes, build):
        self._deps(e, reads, writes)
        inst = build(self.eng[e])
        self.cnt[e] += 1
        inst.then_inc(self.semobj[e], 1)
        self._done((e, self.cnt[e]), reads, writes)
        self.ninst += 1
        return inst

    def dma(self, q, reads, writes, out, in_, **kw):
        self._deps(q, reads, writes)
        i = self.dma_rr
        self.dma_rr = (self.dma_rr + 1) % N_DMA_SEMS
        sk = "d%d" % i
        self._wait(q, sk, 16 * self.dma_use[i])
        self.dma_use[i] += 1
        inst = self.eng[q].dma_start(out=out, in_=in_, **kw)
        inst.then_inc(self.semobj[sk], 16)
        self._done((sk, 16 * self.dma_use[i]), reads, writes)
        self.ninst += 1
        return inst

    def collective(self, kind, reads, writes, ins, outs, groups):
        self._deps('pool', reads, writes)
        self.ncoll = getattr(self, 'ncoll', 0) + 1
        sk = "cc%d" % self.ncoll
        self.semobj[sk] = self.es_root.enter_context(self.nc.semaphore("s_" + sk))
        inst = self.nc.gpsimd.collective_compute(kind, ALU.bypass, replica_groups=groups, ins=ins, outs=outs)
        inst.then_inc(self.semobj[sk], 16)
        self._done((sk, 16), reads, writes)
        self.ninst += 1
        return inst

    def barrier(self):
        for e in self.eng:
            for sk in ['pe', 'act', 'dve', 'pool']:
                if sk != e and self.cnt[sk]:
                    self._wait(e, sk, self.cnt[sk])
            for i in range(N_DMA_SEMS):
                if self.dma_use[i]:
                    self._wait(e, "d%d" % i, 16 * self.dma_use[i])
        self.nbar = getattr(self, 'nbar', 0) + 1
        for sk in ['pe', 'act', 'dve', 'pool']:
            if self.cnt[sk] > 20000:
                self.semobj[sk] = self.es_root.enter_context(self.nc.semaphore("s_%s_b%d" % (sk, self.nbar)))
                self.cnt[sk] = 0
                for e in self.eng:
                    self.seen[e].pop(sk, None)
                for k in list(self.last_w):
                    if self.last_w[k][0] == sk:
                        del self.last_w[k]
                for k in self.readers:
                    self.readers[k].pop(sk, None)

    def finish(self, keys=(), e='sp'):
        for i in range(N_DMA_SEMS):
            if self.dma_use[i]:
                self._wait(e, "d%d" % i, 16 * self.dma_use[i])
        for k in keys:
            if k in self.last_w:
                sk, val = self.last_w[k]
                self._wait(e, sk, val)

import numpy as np
F=np.float32
BF=ml_dtypes.bfloat16
L=8192
def angles(dim):
    rows=L//64; row=np.repeat(np.arange(rows),64).astype(F); col=np.tile(np.arange(64),rows).astype(F)
    n=dim//4; inv=(10000.0**(-np.arange(n,dtype=F)/n)).astype(F)
    ang=np.concatenate([row[:,None]*inv,col[:,None]*inv],-1)
    return np.cos(ang).astype(F),np.sin(ang).astype(F)
def consts():
    sw=np.zeros((128,128),F)
    for p in range(128): sw[p,(p+64)%128]=1
    c64,s64=angles(64)
    tm=lambda t: np.ascontiguousarray(t.reshape(64,128,-1).transpose(1,0,2).reshape(128,-1))
    return dict(swap=sw, ident=np.eye(128,dtype=F).astype(BF), cos64=tm(c64), sin64=tm(s64))
def l1a_inputs(d, b, g, C):
    W=d['e_w_in'][0]
    hA=2*g; hB=2*g+1; kv=g//2
    heads=[W[:,hA*64:(hA+1)*64], W[:,hB*64:(hB+1)*64], W[:,512+kv*64:512+(kv+1)*64], W[:,512+kv*64:512+(kv+1)*64]]
    cols=[]
    for a in range(2):
        for h in range(4):
            cols.append(heads[h][:,a*32:(a+1)*32])
    cols.append(W[:,640+kv*64:640+(kv+1)*64])
    w_tm=np.ascontiguousarray(np.concatenate(cols,1))
    w_ga=np.ascontiguousarray(W[:,768+hA*64:768+hA*64+128])
    gq=d['e_q_norm'][0]; gk=d['e_k_norm'][0]; gs=[gq,gq,gk,gk]
    G=np.concatenate([gs[h][a*32:(a+1)*32] for a in range(2) for h in range(4)])[None,:].astype(F)
    return dict(xT=np.ascontiguousarray(d['x'][b].T), w_tm=w_tm, w_ga=w_ga,
                enorm=np.ascontiguousarray(d['e_norm'][0].reshape(8,128).T), qkgain=G,
                cos_tm=C['cos64'], sin_tm=C['sin64'], swapd=C['swap'], identd=C['ident'])

def hy_consts():
    n=np.arange(128)
    ang=2*np.pi*np.outer(n,n)/128.0
    cs,sn=np.cos(ang),np.sin(ang)
    FA=np.concatenate([cs,-sn],1).astype(BF)
    Fd=np.stack([cs,sn,-sn],1).astype(BF)
    Gd=np.stack([np.concatenate([cs,sn],1),np.concatenate([-sn,cs],1)],1).astype(BF)
    a2=2*np.pi*np.outer(n,n)/16384.0
    tw=np.stack([np.cos(a2),np.sin(a2)],1).astype(F)
    t=np.linspace(0,1,L,dtype=F)[:,None]; w=(2*math.pi*np.arange(L,dtype=F)[:,None]/L).astype(F)
    bands=np.linspace(1e-4,15,16,dtype=F)
    zf=np.concatenate([t,np.cos(bands*w),-np.sin(bands*w)],-1).astype(F)
    zTf=np.ascontiguousarray(zf.T)
    zTr=np.zeros_like(zTf); zTr[:,1:]=zTf[:,:0:-1]
    tr=np.zeros(L,F); tr[1:]=t[:0:-1,0]
    trow=np.stack([t[:,0],tr],0).astype(F)
    mind=math.log(1e-2)/1.5; maxd=math.log(1e-2)/0.3
    deltas=np.abs(np.linspace(mind,maxd,512,dtype=F)).astype(F)
    return dict(FA=FA,Fd=Fd,Gd=Gd,tw=tw,zTf=zTf,zTr=zTr,trow=trow,deltas=deltas)
def l1b_inputs(d,b,g,H):
    W=d['e_w_in'][0]; cs=slice(g*128,(g+1)*128)
    w_hy=np.ascontiguousarray(np.concatenate([W[:,1280:1792][:,cs],W[:,1792:2304][:,cs],W[:,2304:2816][:,cs],W[:,2816:3328][:,cs]],1))
    cw=d['e_conv_w'][0]; cb=d['e_conv_b'][0]
    convw=np.zeros((128,12),F)
    for gi in range(3):
        ch=gi*512+np.arange(g*128,(g+1)*128)
        convw[:,gi*4+0]=cw[0,ch]; convw[:,gi*4+1]=cw[1,ch]; convw[:,gi*4+2]=cw[2,ch]; convw[:,gi*4+3]=cb[ch]
    fparams=np.stack([d['e_filt_f1'][0],d['e_filt_b1'][0],d['e_filt_f2'][0],d['e_filt_b2'][0]],1).astype(F)
    w3=d['e_filt_w3'][0]
    w3d=np.ascontiguousarray(np.concatenate([w3[:,cs],w3[:,512:][:,cs]],1))
    return dict(xT=np.ascontiguousarray(d['x'][b].T), w_hy=w_hy, enorm=np.ascontiguousarray(d['e_norm'][0].reshape(8,128).T),
                convw=convw, fparams=fparams, w1d=d['e_filt_w1'][0], w2d=d['e_filt_w2'][0], w3d=w3d,
                zTf=H['zTf'], zTr=H['zTr'], trow=H['trow'], negdel=(-H['deltas'][cs])[:,None].astype(F), hyd=d['e_hy_d'][0][cs][:,None].astype(F),
                FAd=H['FA'], Fd=H['Fd'], Gd=H['Gd'], twd=H['tw'])

def tm16(t):
    return np.ascontiguousarray(t.reshape(16,128,-1).transpose(1,0,2).reshape(128,-1))
def l2_inputs(d, b, ts, y0T_b, C32):
    sl=slice(ts*2048,(ts+1)*2048)
    return dict(y0T=np.ascontiguousarray(y0T_b[:, sl]), xs=np.ascontiguousarray(d['x'][b, sl]), wout=d['e_w_out'][0], win=d['o_w_in'][0],
                onorm=np.ascontiguousarray(d['o_norm'][0].reshape(8,128).T), cos2=tm16(C32[0][sl]), sin2=tm16(C32[1][sl]),
                identd=np.eye(128,dtype=F).astype(BF))
def l4_inputs(d, b, ts, ocT_b, gcT_bts, x1_bts):
    sl=slice(ts*2048,(ts+1)*2048)
    return dict(ocT=np.ascontiguousarray(ocT_b[:, sl]), gcT=gcT_bts, x1=x1_bts, wout=d['o_w_out'][0], fnorm=d['final_norm'][None,:].astype(F))

def l3_consts():
    c,s=angles(32)
    return dict(cosT=np.ascontiguousarray(np.tile(c.T,(2,1))), sinT=np.ascontiguousarray(np.tile(s.T,(2,1))))
def l3_inputs(d, b, g, latT_b, C3, swap):
    wqb=d['o_w_qb'][0]; wkvb=d['o_w_kvb'][0]
    wq=np.zeros((384,4,96),F); wk=np.zeros((256,4,64),F); wv=np.zeros((256,4,64),F)
    for i in range(4):
        h=4*g+i
        wq[:,i,:]=wqb[:,h*96:h*96+96]
        wk[:,i,:]=wkvb[:,h*128:h*128+64]; wv[:,i,:]=wkvb[:,h*128+64:h*128+128]
    return dict(latT=latT_b, wq=wq, wk=wk, wv=wv, qgain=np.ascontiguousarray(d['o_q_a_norm'][0].reshape(3,128).T),
                kvgain=np.ascontiguousarray(d['o_kv_a_norm'][0].reshape(2,128).T), cosT=C3['cosT'], sinT=C3['sinT'], swapd=swap)

import numpy as np

L = 8192
D = 1024
NBLK = 16
TB = 512
EPS = 1e-6


def load_weights_scaled(c, wd, gain_t, gkey, wb, ncols, tag):
    stg = [c.sb("wstg%s%d" % (tag, i), [128, ncols], F32) for i in range(2)]
    for ch in range(8):
        s = stg[ch % 2]
        k = "wstg%s%d" % (tag, ch % 2)
        c.dma('sp', [], [k], s[:], wd[ch * 128:(ch + 1) * 128, :])
        c.op('dve', [k, gkey], ['wb' + tag],
             lambda e, s=s, ch=ch: e.tensor_scalar(out=wb[:, ch, :], in0=s[:], scalar1=gain_t[:, ch:ch + 1],
                                                   scalar2=None, op0=ALU.mult))


class Sweep:
    def __init__(self, c, xT, tag):
        self.c = c
        self.tag = tag
        self.xTv = xT.rearrange("(ch p) t -> p ch t", p=128)
        self.xf = [c.sb("xf%s%d" % (tag, i), [128, 8, TB], F32) for i in range(2)]
        self.sq = c.sb("sq" + tag, [128, 8, TB], BF16)
        self.hT = [c.sb("hT%s%d" % (tag, i), [128, 8, TB], BF16) for i in range(2)]
        self.sd = [c.sb("sd%s%d" % (tag, i), [128, TB], F32) for i in range(2)]
        self.R = [c.sb("R%s%d" % (tag, i), [128, TB], F32) for i in range(2)]
        self.ones = c.sb("ones" + tag, [128, 128], BF16)
        c.op('pool', [], ['ones' + tag], lambda e: e.memset(self.ones[:], 1.0))

    def load(self, j):
        c, t = self.c, self.tag
        s = j % 2
        c.dma('sp', [], ['xf%s%d' % (t, s)], self.xf[s][:], self.xTv[:, :, j * TB:(j + 1) * TB])

    def stats(self, j, ss_ps, ss_key):
        self.stats_a(j, ss_ps, ss_key)
        self.stats_b(j)

    def stats_b(self, j):
        c, t = self.c, self.tag
        s = j % 2
        sd, R = self.sd[s], self.R[s]
        c.op('dve', ['sd%s%d' % (t, s)], ['R%s%d' % (t, s)], lambda e: e.reciprocal(out=R[:], in_=sd[:]))

    def stats_a(self, j, ss_ps, ss_key):
        c, t = self.c, self.tag
        s = j % 2
        xf = self.xf[s]
        kx = 'xf%s%d' % (t, s)
        sd, R = self.sd[s], self.R[s]
        c.op('pool', [kx], ['sq' + t], lambda e: e.tensor_tensor(out=self.sq[:], in0=xf[:], in1=xf[:], op=ALU.mult))
        for ch in range(8):
            c.op('pe', ['sq' + t, 'ones' + t], [ss_key],
                 lambda e, ch=ch: e.matmul(ss_ps, lhsT=self.ones[:], rhs=self.sq[:, ch, :], start=(ch == 0), stop=(ch == 7)))
        c.op('act', [ss_key], ['sd%s%d' % (t, s)],
             lambda e: e.activation(out=sd[:], in_=ss_ps, func=AF.Sqrt, scale=1.0 / D, bias=EPS))

    def apply(self, j):
        c, t = self.c, self.tag
        s = j % 2
        xf, hT, R = self.xf[s], self.hT[s], self.R[s]
        kx, kh = 'xf%s%d' % (t, s), 'hT%s%d' % (t, s)
        c.op('dve', [kx, 'R%s%d' % (t, s)], [kh],
             lambda e: e.tensor_tensor(out=hT[:], in0=xf[:], in1=R[:].unsqueeze(1).broadcast_to([128, 8, TB]), op=ALU.mult))
        return hT, kh


def emit_attn_L0(c, es, xT, w_tm, w_ga, enorm, qkgain, cos_tm, sin_tm, swapd, identd, ya_out, dbg=None):
    nc = c.nc
    QTp = [c.sb("QTA", [128, L], BF16), c.sb("QTB", [128, L], BF16)]
    c.op('pool', [], ['QTA'], lambda e: e.memset(QTp[0][64:128, :], 0.0))
    c.op('pool', [], ['QTB'], lambda e: e.memset(QTp[1][0:64, :], 0.0))
    KT = c.sb("KT", [128, L], BF16)
    VA = c.sb("VA", [128, 64, 128], BF16)
    VB = c.sb("VB", [128, 64, 128], BF16)
    gaT = c.sb("gaT", [128, L], BF16)
    c.op('pool', [], ['VA'], lambda e: e.memset(VA[:, :, 64:128], 1.0))
    c.op('pool', [], ['VB'], lambda e: e.memset(VB[:, :, 0:64], 1.0))
    ident = c.sb("ident", [128, 128], BF16)
    c.dma('sp', [], ['ident'], ident[:], identd)
    with ExitStack() as es1:
        c.es = es1
        gain = c.sb("gainA", [128, 8], F32)
        c.dma('sp', [], ['gainA'], gain[:], enorm)
        wtm = c.sb("wbA", [128, 8, 320], BF16)
        wga = c.sb("wbG", [128, 8, 128], BF16)
        load_weights_scaled(c, w_tm, gain, 'gainA', wtm, 320, 'A')
        load_weights_scaled(c, w_ga, gain, 'gainA', wga, 128, 'G')
        G = c.sb("G", [128, 256], F32)
        c.dma('sp', [], ['G'], G[:], qkgain.partition_broadcast(128))
        cosS = c.sb("cosS", [128, 64, 32], F32)
        sinS = c.sb("sinS", [128, 64, 32], F32)
        c.dma('sp', [], ['cosS'], cosS[:], cos_tm.rearrange("p (t j) -> p t j", j=32))
        c.dma('sp', [], ['sinS'], sinS[:], sin_tm.rearrange("p (t j) -> p t j", j=32))
        sw = Sweep(c, xT, 'A')
        ss = c.ps("ssA", [128, TB])
        pga = c.ps("pga", [128, TB])
        pt = c.ps("ptA", [128, 4, 512])
        ptr = c.ps("ptrA", [128, 4, 2, 128], BF16)
        sqq = c.sb("sqq", [128, 4, 256], F32)
        ssq = c.sb("ssq", [128, 4, 4], F32)
        sdq = c.sb("sdq", [128, 4, 4], F32)
        rsq = c.sb("rsq", [128, 4, 4], F32)
        qn = c.sb("qn", [128, 4, 256], F32)
        t1 = c.sb("t1", [128, 4, 256], F32)
        t2 = c.sb("t2", [128, 4, 256], F32)
        qk = c.sb("qk", [128, 4, 4, 64], BF16)
        sw.load(0)
        sw.load(1)
        sw.stats(0, ss[:], 'ssA')
        cur = sw.apply(0)
        for j in range(NBLK):
            hT, kh = cur
            if j + 1 < NBLK:
                sw.stats_a(j + 1, ss[:], 'ssA')
            for ch in range(8):
                c.op('pe', ['wbG', kh], ['pga'],
                     lambda e, ch=ch: e.matmul(pga[:], lhsT=wga[:, ch, :], rhs=hT[:, ch, :], start=(ch == 0), stop=(ch == 7)))
            c.op('act', ['pga'], ['gaT'],
                 lambda e: e.activation(out=gaT[:, j * TB:(j + 1) * TB], in_=pga[:], func=AF.Silu))
            for tt in range(4):
                for ch in range(8):
                    c.op('pe', ['wbA', kh], ['ptA'],
                         lambda e, ch=ch, tt=tt: e.matmul(pt[:, tt, 0:320], lhsT=hT[:, ch, tt * 128:(tt + 1) * 128],
                                                          rhs=wtm[:, ch, :], start=(ch == 0), stop=(ch == 7)))
            if j + 1 < NBLK:
                sw.stats_b(j + 1)
                cur = sw.apply(j + 1)
            if j + 2 < NBLK:
                sw.load(j + 2)
            c.op('act', ['ptA'], ['sqq'], lambda e: e.activation(out=sqq[:], in_=pt[:, :, 0:256], func=AF.Square))
            c.op('dve', ['sqq'], ['ssq'],
                 lambda e: e.tensor_reduce(out=ssq[:], in_=sqq[:].rearrange("p t (a h j) -> p t h a j", a=2, h=4),
                                           axis=AX.XY, op=ALU.add))
            c.op('act', ['ssq'], ['sdq'], lambda e: e.activation(out=sdq[:], in_=ssq[:], func=AF.Sqrt, scale=1.0 / 64, bias=EPS))
            c.op('dve', ['sdq'], ['rsq'], lambda e: e.reciprocal(out=rsq[:], in_=sdq[:]))
            for a in range(2):
                c.op('dve', ['ptA', 'rsq'], ['qn'],
                     lambda e, a=a: e.tensor_tensor(
                         out=qn[:, :, a * 128:(a + 1) * 128].rearrange("p t (h j) -> p t h j", h=4),
                         in0=pt[:, :, a * 128:(a + 1) * 128].rearrange("p t (h j) -> p t h j", h=4),
                         in1=rsq[:].unsqueeze(3).broadcast_to([128, 4, 4, 32]), op=ALU.mult))
            c.op('dve', ['qn', 'G'], ['qn'],
                 lambda e: e.tensor_tensor(out=qn[:], in0=qn[:], in1=G[:].unsqueeze(1).broadcast_to([128, 4, 256]), op=ALU.mult))
            qv = qn[:].rearrange("p t (g j) -> p t g j", j=32)
            cb = cosS[:, j * 4:(j + 1) * 4, :].unsqueeze(2).broadcast_to([128, 4, 8, 32])
            sb_ = sinS[:, j * 4:(j + 1) * 4, :].unsqueeze(2).broadcast_to([128, 4, 8, 32])
            c.op('dve', ['qn', 'cosS'], ['t1'],
                 lambda e: e.tensor_tensor(out=t1[:].rearrange("p t (g j) -> p t g j", j=32), in0=qv, in1=cb, op=ALU.mult))
            c.op('pool', ['qn', 'sinS'], ['t2'],
                 lambda e: e.tensor_tensor(out=t2[:].rearrange("p t (g j) -> p t g j", j=32), in0=qv, in1=sb_, op=ALU.mult))
            t1v = t1[:].rearrange("p t (a h j) -> p t a h j", a=2, h=4)
            t2v = t2[:].rearrange("p t (a h j) -> p t a h j", a=2, h=4)
            c.op('dve', ['t1', 't2'], ['qk'],
                 lambda e: e.tensor_tensor(out=qk[:, :, :, 0:32], in0=t1v[:, :, 0, :, :], in1=t2v[:, :, 1, :, :], op=ALU.subtract))
            c.op('pool', ['t1', 't2'], ['qk'],
                 lambda e: e.tensor_tensor(out=qk[:, :, :, 32:64], in0=t2v[:, :, 0, :, :], in1=t1v[:, :, 1, :, :], op=ALU.add))
            for tt in range(4):
                c.op('pe', ['qk', 'ident'], ['ptrA'],
                     lambda e, tt=tt: e.transpose(out=ptr[:, tt, 0, :], in_=qk[:, tt, 0:2, :].rearrange("p h j -> p (h j)"), identity=ident[:]))
                c.op('pe', ['qk', 'ident'], ['ptrA'],
                     lambda e, tt=tt: e.transpose(out=ptr[:, tt, 1, :], in_=qk[:, tt, 2:4, :].rearrange("p h j -> p (h j)"), identity=ident[:]))
            c.op('act', ['ptrA'], ['QTA'],
                 lambda e: e.copy(out=QTp[0][0:64, j * TB:(j + 1) * TB].rearrange("p (t q) -> p t q", t=4), in_=ptr[0:64, :, 0, :]))
            c.op('act', ['ptrA'], ['QTB'],
                 lambda e: e.copy(out=QTp[1][64:128, j * TB:(j + 1) * TB].rearrange("p (t q) -> p t q", t=4), in_=ptr[64:128, :, 0, :]))
            c.op('dve', ['ptrA'], ['KT'],
                 lambda e: e.tensor_copy(out=KT[:, j * TB:(j + 1) * TB].rearrange("p (t q) -> p t q", t=4), in_=ptr[:, :, 1, :]))
            c.op('act', ['ptA'], ['VA'], lambda e: e.copy(out=VA[:, j * 4:(j + 1) * 4, 0:64], in_=pt[:, :, 256:320]))
            c.op('pool', ['VA'], ['VB'], lambda e: e.tensor_copy(out=VB[:, j * 4:(j + 1) * 4, 64:128], in_=VA[:, j * 4:(j + 1) * 4, 0:64]))
    c.barrier()
    c.es = es
    if dbg is not None:
        c.dma('sp', ['QTA'], ['dbgQ'], dbg['QT'][0:64, :], QTp[0][0:64, :])
        c.dma('sp', ['QTB'], ['dbgQ2'], dbg['QT'][64:128, :], QTp[1][64:128, :])
        c.dma('sp', ['KT'], ['dbgK'], dbg['KT'], KT[:])
        c.dma('sp', ['gaT'], ['dbgG'], dbg['gaT'], gaT[:])
        c.dma('sp', ['VA'], ['dbgV'], dbg['VA'], VA[:].rearrange("p t j -> p (t j)"))
    with ExitStack() as es2:
        c.es = es2
        swp = c.sb("swp", [128, 128], F32)
        c.dma('sp', [], ['swp'], swp[:], swapd)
        PS = [c.ps("PS%d" % i, [128, 2, 512]) for i in range(3)]
        PT = [c.sb("PT%d" % i, [128, 2, 512], BF16) for i in range(3)]
        PO = c.ps("PO", [128, 2, 512])
        PS2 = PS[2][:, 0, :]
        S_sb = c.sb("S_sb", [128, 512], F32)
        O_sb = c.sb("O_sb", [128, 512], F32)
        rinv = c.sb("rinv", [128, 512], F32)
        yo = [c.sb("yo%d" % i, [128, 512], BF16) for i in range(2)]

        def S(qb, kt):
            s = kt % 3
            for h in range(2):
                c.op('pe', ['KT', 'QT' + 'AB'[h]], ['PS%d' % s],
                     lambda e, h=h: e.matmul(PS[s][:, h, :], lhsT=KT[:, kt * 128:(kt + 1) * 128],
                                             rhs=QTp[h][:, qb * 512:(qb + 1) * 512], start=True, stop=True))

        S(0, 0)
        S(0, 1)
        for qb in range(NBLK):
            for kt in range(64):
                s = kt % 3
                if kt + 2 < 64:
                    S(qb, kt + 2)
                c.op('act', ['PS%d' % s], ['PT%d' % s],
                     lambda e: e.activation(out=PT[s][:], in_=PS[s][:], func=AF.Exp, scale=0.125))
                c.op('pe', ['VA', 'PT%d' % s], ['PO'],
                     lambda e: e.matmul(PO[:, 0, :], lhsT=VA[:, kt, :], rhs=PT[s][:, 0, :], start=(kt == 0), stop=(kt == 63)))
                c.op('pe', ['VB', 'PT%d' % s], ['PO'],
                     lambda e: e.matmul(PO[:, 1, :], lhsT=VB[:, kt, :], rhs=PT[s][:, 1, :], start=(kt == 0), stop=(kt == 63)))
            if qb + 1 < NBLK:
                S(qb + 1, 0)
                S(qb + 1, 1)
            c.op('dve', ['PO'], ['S_sb'], lambda e: e.tensor_copy(out=S_sb[64:128, :], in_=PO[64:128, 0, :]))
            c.op('dve', ['PO'], ['S_sb'], lambda e: e.tensor_copy(out=S_sb[0:64, :], in_=PO[0:64, 1, :]))
            c.op('dve', ['PO'], ['O_sb'], lambda e: e.tensor_copy(out=O_sb[0:64, :], in_=PO[0:64, 0, :]))
            c.op('dve', ['PO'], ['O_sb'], lambda e: e.tensor_copy(out=O_sb[64:128, :], in_=PO[64:128, 1, :]))
            c.op('pe', ['swp', 'S_sb'], ['PS2'], lambda e: e.matmul(PS2, lhsT=swp[:], rhs=S_sb[:], start=True, stop=True))
            c.op('dve', ['PS2'], ['rinv'], lambda e: e.reciprocal(out=rinv[:], in_=PS2))
            c.op('dve', ['O_sb', 'rinv'], ['O_sb'], lambda e: e.tensor_tensor(out=O_sb[:], in0=O_sb[:], in1=rinv[:], op=ALU.mult))
            y = yo[qb % 2]
            ky = 'yo%d' % (qb % 2)
            c.op('pool', ['O_sb', 'gaT'], [ky],
                 lambda e: e.tensor_tensor(out=y[:], in0=O_sb[:], in1=gaT[:, qb * 512:(qb + 1) * 512], op=ALU.mult))
            c.dma('sp', [ky], ['ya_out'], ya_out[:, qb * 512:(qb + 1) * 512], y[:])
    c.barrier()
    c.es = es

import numpy as np

I32 = mybir.dt.int32
PI_LO = 3.1415925
TWO_PI = 2.0 * np.pi
CG = 4
NG = 128 // CG


def lockstep(*chains):
    n = max(len(ch) for ch in chains)
    for i in range(n):
        for ch in chains:
            if i < len(ch):
                ch[i]()


def emit_hyena_L0(c, es, xT, w_hy, enorm, convw, fparams, w1d, w2d, w3d, zTf, zTr, trow, negdel, hyd,
                  FAd, Fd, Gd, twd, kscr, zscr, yscr, yb_out, dbg=None):
    nc = c.nc
    zT = c.sb("zT", [128, L], BF16)
    g0 = c.sb("g0", [128, L], BF16)
    with ExitStack() as es1:
        c.es = es1
        gain = c.sb("gainH", [128, 8], F32)
        c.dma('sp', [], ['gainH'], gain[:], enorm)
        wb = c.sb("wbH", [128, 8, 512], BF16)
        load_weights_scaled(c, w_hy, gain, 'gainH', wb, 512, 'H')
        cw = c.sb("cw", [128, 12], F32)
        c.dma('sp', [], ['cw'], cw[:], convw)
        sw = Sweep(c, xT, 'H')
        ss = c.ps("ssH", [128, TB])
        pf = c.ps("pfH", [128, 4, 512])
        raw = [c.sb("raw%d" % i, [128, 3, 514], F32) for i in range(3)]
        sg = [c.sb("sg%d" % i, [128, 512], F32) for i in range(3)]
        u = c.sb("u", [128, 3, 512], F32)
        c.op('pool', [], ['raw0'], lambda e: e.memset(raw[0][:, :, 0:1], 0.0))

        def conv(jj):
            r = raw[jj % 3]
            kr = 'raw%d' % (jj % 3)
            for g in range(3):
                c.op('dve', [kr, 'cw'], ['u'],
                     lambda e, g=g: e.tensor_scalar(out=u[:, g, :], in0=r[:, g, 1:513], scalar1=cw[:, g * 4 + 1:g * 4 + 2],
                                                    scalar2=cw[:, g * 4 + 3:g * 4 + 4], op0=ALU.mult, op1=ALU.add))
                c.op('dve', [kr, 'cw', 'u'], ['u'],
                     lambda e, g=g: e.scalar_tensor_tensor(out=u[:, g, :], in0=r[:, g, 0:512], scalar=cw[:, g * 4:g * 4 + 1],
                                                           in1=u[:, g, :], op0=ALU.mult, op1=ALU.add))
                c.op('dve', [kr, 'cw', 'u'], ['u'],
                     lambda e, g=g: e.scalar_tensor_tensor(out=u[:, g, :], in0=r[:, g, 2:514], scalar=cw[:, g * 4 + 2:g * 4 + 3],
                                                           in1=u[:, g, :], op0=ALU.mult, op1=ALU.add))
            c.op('dve', ['u'], ['zT'],
                 lambda e: e.tensor_tensor(out=zT[:, jj * TB:(jj + 1) * TB], in0=u[:, 2, :], in1=u[:, 1, :], op=ALU.mult))
            c.op('pool', ['u', 'sg%d' % (jj % 3)], ['g0'],
                 lambda e: e.tensor_tensor(out=g0[:, jj * TB:(jj + 1) * TB], in0=u[:, 0, :], in1=sg[jj % 3][:], op=ALU.mult))

        sw.load(0)
        sw.load(1)
        sw.stats(0, ss[:], 'ssH')
        cur = sw.apply(0)
        for j in range(NBLK):
            hT, kh = cur
            if j + 1 < NBLK:
                sw.stats_a(j + 1, ss[:], 'ssH')
            for m in range(4):
                for ch in range(8):
                    c.op('pe', ['wbH', kh], ['pfH'],
                         lambda e, m=m, ch=ch: e.matmul(pf[:, m, :], lhsT=wb[:, ch, m * 128:(m + 1) * 128], rhs=hT[:, ch, :],
                                                        start=(ch == 0), stop=(ch == 7)))
            if j + 1 < NBLK:
                sw.stats_b(j + 1)
                cur = sw.apply(j + 1)
            if j + 2 < NBLK:
                sw.load(j + 2)
            r = raw[j % 3]
            kr = 'raw%d' % (j % 3)
            c.op('act', ['pfH'], [kr], lambda e: e.copy(out=r[:, :, 1:513], in_=pf[:, 0:3, :]))
            c.op('act', ['pfH'], ['sg%d' % (j % 3)], lambda e: e.activation(out=sg[j % 3][:], in_=pf[:, 3, :], func=AF.Silu))
            if j > 0:
                rp = raw[(j - 1) % 3]
                kp = 'raw%d' % ((j - 1) % 3)
                c.op('pool', [kp], [kr], lambda e: e.tensor_copy(out=r[:, :, 0:1], in_=rp[:, :, 512:513]))
                c.op('pool', [kr], [kp], lambda e: e.tensor_copy(out=rp[:, :, 513:514], in_=r[:, :, 1:2]))
                conv(j - 1)
        c.op('pool', [], ['raw%d' % ((NBLK - 1) % 3)], lambda e: e.memset(raw[(NBLK - 1) % 3][:, :, 513:514], 0.0))
        conv(NBLK - 1)
    c.barrier()
    c.es = es
    if dbg is not None:
        c.dma('sp', ['zT'], ['dbgz'], dbg['zT'], zT[:])
        c.dma('sp', ['g0'], ['dbgg'], dbg['g0'], g0[:])
    c.dma('sp', ['zT'], ['zscr'], zscr, zT[:])
    kT = c.sb("kT", [128, 2 * L], BF16)
    l1p = c.sb("l1p", [128, 32], F32)
    rl1 = c.sb("rl1", [128, 1], F32)
    with ExitStack() as es2:
        c.es = es2
        fp = c.sb("fp", [64, 4], F32)
        c.dma('sp', [], ['fp'], fp[:], fparams)
        fb = c.sb("fb", [64, 2], F32)
        c.op('dve', ['fp'], ['fb'], lambda e: e.tensor_tensor(out=fb[:, 0:1], in0=fp[:, 0:1], in1=fp[:, 1:2], op=ALU.mult))
        c.op('dve', ['fp'], ['fb'], lambda e: e.tensor_tensor(out=fb[:, 1:2], in0=fp[:, 2:3], in1=fp[:, 3:4], op=ALU.mult))
        w1 = c.sb("w1", [33, 64], F32)
        w2 = c.sb("w2", [64, 64], F32)
        w3 = c.sb("w3", [64, 256], F32)
        c.dma('sp', [], ['w1'], w1[:], w1d)
        c.dma('sp', [], ['w2'], w2[:], w2d)
        c.dma('sp', [], ['w3'], w3[:], w3d)
        nd = c.sb("nd", [128, 1], F32)
        c.dma('sp', [], ['nd'], nd[:], negdel)
        def mk(name, shape, dt=F32):
            return [c.sb("%s_%d" % (name, dr), shape, dt) for dr in range(2)]
        zb = [[c.sb("zb%d_%d" % (dr, i), [33, 512], F32) for i in range(2)] for dr in range(2)]
        tb = [[c.sb("tb%d_%d" % (dr, i), [128, 512], F32) for i in range(2)] for dr in range(2)]
        P1 = [c.ps("P1_%d" % dr, [64, 512]) for dr in range(2)]
        P2 = [c.ps("P2_%d" % dr, [64, 512]) for dr in range(2)]
        P3 = [c.ps("P3_%d" % dr, [128, 512]) for dr in range(2)]
        a1 = mk("a1", [64, 512]); ki = mk("ki", [64, 512], I32); rr = mk("rr", [64, 512])
        h1 = mk("h1", [64, 512]); h2 = mk("h2", [64, 512]); dec = mk("dec", [128, 512]); hk = mk("hk", [128, 512])

        def sin_stages(dr, P, pk, col, hout, hk_):
            A, KI, RR = a1[dr], ki[dr], rr[dr]
            ka, kk, kr_ = 'a1_%d' % dr, 'ki_%d' % dr, 'rr_%d' % dr
            return [
                lambda: c.op('act', [pk, 'fp', 'fb'], [ka],
                             lambda e: e.activation(out=A[:], in_=P[:], func=AF.Identity, scale=fp[:, 2 * col:2 * col + 1], bias=fb[:, col:col + 1])),
                lambda: c.op('dve', [ka], [kk], lambda e: e.tensor_scalar(out=KI[:], in0=A[:], scalar1=1.0 / TWO_PI, scalar2=None, op0=ALU.mult)),
                lambda: c.op('dve', [kk, ka], [kr_],
                             lambda e: e.scalar_tensor_tensor(out=RR[:], in0=KI[:], scalar=-TWO_PI, in1=A[:], op0=ALU.mult, op1=ALU.add)),
                lambda: c.op('dve', [kr_], [kr_],
                             lambda e: e.tensor_scalar(out=RR[:], in0=RR[:], scalar1=-PI_LO, scalar2=PI_LO, op0=ALU.max, op1=ALU.min)),
                lambda: c.op('act', [kr_], [hk_], lambda e: e.activation(out=hout[:], in_=RR[:], func=AF.Sin)),
            ]

        def filt_chain(dr, j):
            s = j % 2
            zsrc = zTf if dr == 0 else zTr
            Z, T = zb[dr][s], tb[dr][s]
            kz, kt_ = 'zb%d_%d' % (dr, s), 'tb%d_%d' % (dr, s)
            p1, p2, p3 = P1[dr], P2[dr], P3[dr]
            H1, H2, DEC, HK = h1[dr], h2[dr], dec[dr], hk[dr]
            it = dr * NBLK + j
            st = [
                lambda: (c.dma('sp', [], [kz], Z[:], zsrc[:, j * TB:(j + 1) * TB]),
                         c.dma('sp', [], [kt_], T[:], trow[dr:dr + 1, j * TB:(j + 1) * TB].partition_broadcast(128))),
                lambda: c.op('pe', ['w1', kz], ['P1_%d' % dr], lambda e: e.matmul(p1[:], lhsT=w1[:], rhs=Z[:], start=True, stop=True)),
            ]
            st += sin_stages(dr, p1, 'P1_%d' % dr, 0, H1, 'h1_%d' % dr)
            st += [lambda: c.op('pe', ['w2', 'h1_%d' % dr], ['P2_%d' % dr], lambda e: e.matmul(p2[:], lhsT=w2[:], rhs=H1[:], start=True, stop=True))]
            st += sin_stages(dr, p2, 'P2_%d' % dr, 1, H2, 'h2_%d' % dr)
            st += [
                lambda: c.op('pe', ['w3', 'h2_%d' % dr], ['P3_%d' % dr],
                             lambda e: e.matmul(p3[:], lhsT=w3[:, dr * 128:(dr + 1) * 128], rhs=H2[:], start=True, stop=True)),
                lambda: c.op('act', [kt_, 'nd'], ['dec_%d' % dr], lambda e: e.activation(out=DEC[:], in_=T[:], func=AF.Exp, scale=nd[:, 0:1])),
                lambda: c.op('dve', ['P3_%d' % dr, 'dec_%d' % dr], ['hk_%d' % dr], lambda e: e.tensor_tensor(out=HK[:], in0=p3[:], in1=DEC[:], op=ALU.mult)),
            ]
            if dr == 1 and j == 0:
                st += [lambda: c.op('dve', ['hk_%d' % dr], ['hk_%d' % dr], lambda e: e.memset(HK[:, 0:1], 0.0))]
            st += [
                lambda: c.op('dve', ['hk_%d' % dr], ['l1p'],
                             lambda e: e.tensor_reduce(out=l1p[:, it:it + 1], in_=HK[:], axis=AX.X, op=ALU.add, apply_absolute_value=True)),
                lambda: c.op('pool', ['hk_%d' % dr], ['kT'], lambda e: e.tensor_copy(out=kT[:, dr * L + j * TB: dr * L + (j + 1) * TB], in_=HK[:])),
            ]
            return st

        for j in range(NBLK):
            lockstep(filt_chain(0, j), filt_chain(1, j))
        l1s = c.sb("l1s", [128, 1], F32)
        c.op('dve', ['l1p'], ['l1s'], lambda e: e.tensor_reduce(out=l1s[:], in_=l1p[:], axis=AX.X, op=ALU.add))
        c.op('dve', ['l1s'], ['rl1'], lambda e: e.reciprocal(out=rl1[:], in_=l1s[:]))
    c.barrier()
    c.es = es
    if dbg is not None:
        c.dma('sp', ['kT'], ['dbgk'], dbg['kT'], kT[:])
    c.dma('sp', ['kT'], ['kscr'], kscr, kT[:])
    with ExitStack() as es3:
        c.es = es3
        Kc = c.sb("Kc", [128, 128, 128], BF16)
        Xc = c.sb("Xc", [128, 128, 128], BF16)
        kv = kscr.rearrange("c (a b) -> a c b", b=128)
        zv = zscr.rearrange("c (a b) -> a c b", b=128)
        for i in range(16):
            c.dma('sp', ['kscr'], ['Kc'], Kc[:, i * 8:(i + 1) * 8, :], kv[:, i * 8:(i + 1) * 8, :])
        for i in range(16):
            c.dma('sp', ['zscr'], ['Xc'], Xc[0:64, i * 8:(i + 1) * 8, :], zv[:, i * 8:(i + 1) * 8, :])
        FA = c.sb("FA", [128, 256], BF16)
        Fm = c.sb("Fm", [128, 3, 128], BF16)
        Gm = c.sb("Gm", [128, 2, 256], BF16)
        tw = c.sb("tw", [128, 2, 128], F32)
        c.dma('sp', [], ['FA'], FA[:], FAd)
        c.dma('sp', [], ['Fm'], Fm[:], Fd)
        c.dma('sp', [], ['Gm'], Gm[:], Gd)
        c.dma('sp', [], ['tw'], tw[:], twd)
        PA_f = c.ps("PA_f", [128, CG, 256]); PXr_f = c.ps("PXr_f", [128, CG, 128]); PXi_f = c.ps("PXi_f", [128, CG, 128])
        PA_d = c.ps("PA_d", [128, CG, 256]); PXr_d = c.ps("PXr_d", [128, CG, 128]); PXi_d = c.ps("PXi_d", [128, CG, 128])
        mf = [c.sb("mf%d" % i, [128, CG, 128], F32) for i in range(4)]
        md = [c.sb("md%d" % i, [128, CG, 128], F32) for i in range(4)]
        Br_f = c.sb("Br_f", [128, CG, 128], BF16); Bi_f = c.sb("Bi_f", [128, CG, 128], BF16)
        Br_d = c.sb("Br_d", [128, CG, 128], BF16); Bi_d = c.sb("Bi_d", [128, CG, 128], BF16)
        Kr = [c.sb("Kr%d" % i, [128, CG, 128], F32) for i in range(2)]
        Ki = [c.sb("Ki%d" % i, [128, CG, 128], F32) for i in range(2)]
        Yr = c.sb("Yr", [128, CG, 128], BF16); Yi = c.sb("Yi", [128, CG, 128], BF16)
        Dr = c.sb("Dr", [128, CG, 128], BF16); Di = c.sb("Di", [128, CG, 128], BF16)
        Yo = [c.sb("Yo%d" % i, [64, CG, 128], F32) for i in range(2)]
        twc = tw[:, 0, :].unsqueeze(1).broadcast_to([128, CG, 128])
        tws = tw[:, 1, :].unsqueeze(1).broadcast_to([128, CG, 128])

        def cmul_st(m, mk_, ar, ai, ak, br, bi, bk, outr, outi, okr, oki, conj):
            m1, m2, m3, m4 = m
            k1, k2, k3, k4 = [mk_ + str(i) for i in range(4)]
            st = [
                lambda: c.op('dve', ak + bk, [k1], lambda e: e.tensor_tensor(out=m1[:], in0=ar, in1=br, op=ALU.mult)),
                lambda: c.op('dve', ak + bk, [k2], lambda e: e.tensor_tensor(out=m2[:], in0=ai, in1=bi, op=ALU.mult)),
                lambda: c.op('dve', ak + bk, [k3], lambda e: e.tensor_tensor(out=m3[:], in0=ar, in1=bi, op=ALU.mult)),
                lambda: c.op('dve', ak + bk, [k4], lambda e: e.tensor_tensor(out=m4[:], in0=ai, in1=br, op=ALU.mult)),
            ]
            if not conj:
                st += [lambda: c.op('pool', [k1, k2], okr, lambda e: e.tensor_tensor(out=outr, in0=m1[:], in1=m2[:], op=ALU.subtract)),
                       lambda: c.op('pool', [k3, k4], oki, lambda e: e.tensor_tensor(out=outi, in0=m3[:], in1=m4[:], op=ALU.add))]
            else:
                st += [lambda: c.op('pool', [k1, k2], okr, lambda e: e.tensor_tensor(out=outr, in0=m1[:], in1=m2[:], op=ALU.add)),
                       lambda: c.op('pool', [k3, k4], oki, lambda e: e.tensor_tensor(out=outi, in0=m4[:], in1=m3[:], op=ALU.subtract))]
            return st

        def fwd_st(src, ksrc, K, gi, PA, kpa, PXr, kxr, PXi, kxi, m, mk_, Br, kbr, Bi, kbi):
            def stageA():
                for cc in range(CG):
                    ch = gi * CG + cc
                    c.op('pe', [ksrc, 'FA'], [kpa],
                         lambda e, cc=cc, ch=ch: e.matmul(PA[:, cc, :], lhsT=src[0:K, ch, :], rhs=FA[0:K, :], start=True, stop=True))
            Bf = Br[:].rearrange("p c k -> p (c k)")
            Bg = Bi[:].rearrange("p c k -> p (c k)")
            xr = PXr[:].rearrange("p c k -> p (c k)")
            xi = PXi[:].rearrange("p c k -> p (c k)")

            def stageB():
                c.op('pe', ['Fm', kbr], [kxr], lambda e: e.matmul(xr, lhsT=Fm[:, 0, :], rhs=Bf, start=True, stop=False))
                c.op('pe', ['Fm', kbi], [kxr], lambda e: e.matmul(xr, lhsT=Fm[:, 1, :], rhs=Bg, start=False, stop=True))
                c.op('pe', ['Fm', kbi], [kxi], lambda e: e.matmul(xi, lhsT=Fm[:, 0, :], rhs=Bg, start=True, stop=False))
                c.op('pe', ['Fm', kbr], [kxi], lambda e: e.matmul(xi, lhsT=Fm[:, 2, :], rhs=Bf, start=False, stop=True))
            return ([stageA] + cmul_st(m, mk_, PA[:, :, 0:128], PA[:, :, 128:256], [kpa], twc, tws, ['tw'], Br[:], Bi[:], [kbr], [kbi], True)
                    + [stageB])

        def filt_fft(gi):
            s = gi % 2
            st = fwd_st(Kc, 'Kc', 128, gi, PA_f, 'PA_f', PXr_f, 'PXr_f', PXi_f, 'PXi_f', mf, 'mf', Br_f, 'Br_f', Bi_f, 'Bi_f')
            st += [lambda: c.op('act', ['PXr_f'], ['Kr%d' % s], lambda e: e.copy(out=Kr[s][:], in_=PXr_f[:])),
                   lambda: c.op('act', ['PXi_f'], ['Ki%d' % s], lambda e: e.copy(out=Ki[s][:], in_=PXi_f[:]))]
            return st

        yv = yscr.rearrange("c (a b) -> a c b", b=128)

        def data_fft(gi):
            s = gi % 2
            st = fwd_st(Xc, 'Xc', 64, gi, PA_d, 'PA_d', PXr_d, 'PXr_d', PXi_d, 'PXi_d', md, 'md', Br_d, 'Br_d', Bi_d, 'Bi_d')
            st += cmul_st(md, 'md', PXr_d[:], PXi_d[:], ['PXr_d', 'PXi_d'], Kr[s][:], Ki[s][:], ['Kr%d' % s, 'Ki%d' % s], Yr[:], Yi[:], ['Yr'], ['Yi'], False)
            PC = PA_d

            def stageC():
                for cc in range(CG):
                    c.op('pe', ['Yr', 'Gm'], ['PA_d'],
                         lambda e, cc=cc: e.matmul(PC[:, cc, :], lhsT=Yr[:, cc, :], rhs=Gm[:, 0, :], start=True, stop=False))
                    c.op('pe', ['Yi', 'Gm'], ['PA_d'],
                         lambda e, cc=cc: e.matmul(PC[:, cc, :], lhsT=Yi[:, cc, :], rhs=Gm[:, 1, :], start=False, stop=True))
            st += [stageC]
            st += cmul_st(md, 'md', PC[:, :, 0:128], PC[:, :, 128:256], ['PA_d'], twc, tws, ['tw'], Dr[:], Di[:], ['Dr'], ['Di'], False)
            pd = PXr_d[0:64].rearrange("p c k -> p (c k)")
            yo = Yo[s]
            ky = 'Yo%d' % s

            def stageD():
                c.op('pe', ['Fm', 'Dr'], ['PXr_d'],
                     lambda e: e.matmul(pd, lhsT=Fm[:, 0, 0:64], rhs=Dr[:].rearrange("p c k -> p (c k)"), start=True, stop=False))
                c.op('pe', ['Fm', 'Di'], ['PXr_d'],
                     lambda e: e.matmul(pd, lhsT=Fm[:, 2, 0:64], rhs=Di[:].rearrange("p c k -> p (c k)"), start=False, stop=True))
            st += [stageD,
                   lambda: c.op('act', ['PXr_d'], [ky], lambda e: e.activation(out=yo[:], in_=PXr_d[0:64], func=AF.Copy, scale=1.0 / 16384.0)),
                   lambda: c.dma('sp', [ky], ['yscr%d' % gi], yv[:, gi * CG:(gi + 1) * CG, :], yo[:])]
            return st

        lockstep(filt_fft(0))
        for gi in range(NG):
            if gi + 1 < NG:
                lockstep(data_fft(gi), filt_fft(gi + 1))
            else:
                lockstep(data_fft(gi))
    c.barrier()
    c.es = es
    with ExitStack() as es4:
        c.es = es4
        dd = c.sb("dd", [128, 1], F32)
        c.dma('sp', [], ['dd'], dd[:], hyd)
        yt = [c.sb("yt%d" % i, [128, 2048], F32) for i in range(2)]
        zd = c.sb("zd", [128, 2048], F32)
        ob = [c.sb("ob%d" % i, [128, 2048], BF16) for i in range(2)]
        allscr = ['yscr%d' % gi for gi in range(NG)]
        for q in range(4):
            s = q % 2
            sl = slice(q * 2048, (q + 1) * 2048)
            c.dma('sp', allscr, ['yt%d' % s], yt[s][:], yscr[:, sl])
            if dbg is not None:
                c.dma('sp', ['yt%d' % s], ['dbgy%d' % q], dbg['yc'][:, sl], yt[s][:])
            c.op('dve', ['zT', 'dd'], ['zd'], lambda e: e.tensor_scalar(out=zd[:], in0=zT[:, sl], scalar1=dd[:, 0:1], scalar2=None, op0=ALU.mult))
            c.op('dve', ['yt%d' % s, 'rl1', 'zd'], ['yt%d' % s],
                 lambda e: e.scalar_tensor_tensor(out=yt[s][:], in0=yt[s][:], scalar=rl1[:, 0:1], in1=zd[:], op0=ALU.mult, op1=ALU.add))
            c.op('dve', ['yt%d' % s, 'g0'], ['ob%d' % s], lambda e: e.tensor_tensor(out=ob[s][:], in0=yt[s][:], in1=g0[:, sl], op=ALU.mult))
            c.dma('sp', ['ob%d' % s], ['yb_out'], yb_out[:, sl], ob[s][:])
    c.barrier()
    c.es = es

import numpy as np

NT = 16
TS = 2048


def load_weights_plain(c, wd, wb, ncols, tag):
    stg = [c.sb("wstg%s%d" % (tag, i), [128, ncols], F32) for i in range(2)]
    for ch in range(8):
        s = stg[ch % 2]
        k = "wstg%s%d" % (tag, ch % 2)
        c.dma('sp', [], [k], s[:], wd[ch * 128:(ch + 1) * 128, :])
        c.op('dve', [k], ['wb' + tag], lambda e, s=s, ch=ch: e.tensor_copy(out=wb[:, ch, :], in_=s[:]))


def outproj_tile(c, i, yT, ykey, wo, po, xin_d, xt, kx):
    for half in range(2):
        for ech in range(8):
            c.op('pe', [ykey, 'wbO'], ['po'],
                 lambda e, half=half, ech=ech: e.matmul(po[:, half * 512:(half + 1) * 512], lhsT=yT[:, ech, i * 128:(i + 1) * 128],
                                                        rhs=wo[:, ech, half * 512:(half + 1) * 512], start=(ech == 0), stop=(ech == 7)))
    c.dma('sp', [], [kx], xt[:], xin_d[i * 128:(i + 1) * 128, :])
    c.op('dve', ['po', kx], [kx], lambda e: e.tensor_tensor(out=xt[:], in0=po[:], in1=xt[:], op=ALU.add))


def rstd_tile(c, xt, kx, junk, ssq, sd, rs, n):
    c.op('act', [kx], ['junk', 'ssq'], lambda e: e.activation(out=junk[:], in_=xt[:], func=AF.Square, accum_out=ssq[:, 0:1]))
    c.op('act', ['ssq'], ['sd'], lambda e: e.activation(out=sd[:], in_=ssq[:], func=AF.Sqrt, scale=1.0 / n, bias=EPS))
    c.op('dve', ['sd'], ['rs'], lambda e: e.reciprocal(out=rs[:], in_=sd[:]))


def emit_L2(c, es, y0T_d, xs_d, wout_d, win_d, onorm_d, cos_d, sin_d, identd, x1_d, latT_d, gcT_d, do_gate=True):
    with ExitStack() as es1:
        c.es = es1
        ident = c.sb("ident2", [128, 128], BF16)
        c.dma('sp', [], ['ident2'], ident[:], identd)
        yT = c.sb("y0T", [128, 8, TS], BF16)
        c.dma('sp', [], ['y0T'], yT[:], y0T_d.rearrange("(ch p) t -> p ch t", p=128))
        wo = c.sb("wbO", [128, 8, 1024], BF16)
        load_weights_plain(c, wout_d, wo, 1024, 'O')
        gain = c.sb("gainI", [128, 8], F32)
        c.dma('sp', [], ['gainI'], gain[:], onorm_d)
        wi = c.sb("wbI", [128, 8, 1696], BF16)
        load_weights_scaled(c, win_d, gain, 'gainI', wi, 1696, 'I')
        cosS = c.sb("cos2", [128, NT, 16], F32)
        sinS = c.sb("sin2", [128, NT, 16], F32)
        c.dma('sp', [], ['cos2'], cosS[:], cos_d.rearrange("p (t j) -> p t j", j=16))
        c.dma('sp', [], ['sin2'], sinS[:], sin_d.rearrange("p (t j) -> p t j", j=16))
        h1T = c.sb("h1T", [128, 8, TS], BF16)
        po = c.ps("po", [128, 1024])
        ptr = c.ps("ptr2", [128, 8, 128], BF16)
        pl = c.ps("pl", [128, 1024])
        pg = c.ps("pg", [128, 512])
        plt = c.ps("plt", [128, 6, 128], BF16)
        ltT = [c.sb("ltT%d" % i, [128, 6, 128], BF16) for i in range(2)]
        xt = [c.sb("xt%d" % i, [128, 1024], F32) for i in range(2)]
        junk = c.sb("junk", [128, 1024], F32)
        ssq = c.sb("ssq", [128, 1], F32)
        sd = c.sb("sd", [128, 1], F32)
        rs = c.sb("rs", [128, 1], F32)
        h1 = c.sb("h1", [128, 1024], BF16)
        ss2 = c.sb("ss2", [128, 2], F32)
        sd2 = c.sb("sd2", [128, 2], F32)
        rs2 = c.sb("rs2", [128, 2], F32)
        latn = [c.sb("latn%d" % i, [128, 672], BF16) for i in range(2)]
        k1 = c.sb("k1", [128, 32], F32)
        k2 = c.sb("k2", [128, 32], F32)
        gt = [c.sb("gt%d" % i, [128, 512], BF16) for i in range(2)]
        for i in range(NT):
            s = i % 2
            kx = 'xt%d' % s
            outproj_tile(c, i, yT, 'y0T', wo, po, xs_d, xt[s], kx)
            c.dma('sp', [kx], ['x1_d'], x1_d[i * 128:(i + 1) * 128, :], xt[s][:])
            rstd_tile(c, xt[s], kx, junk, ssq, sd, rs, D)
            c.op('act', [kx, 'rs'], ['h1'], lambda e: e.activation(out=h1[:], in_=xt[s][:], func=AF.Identity, scale=rs[:, 0:1]))
            for ch in range(8):
                c.op('pe', ['h1', 'ident2'], ['ptr2'],
                     lambda e, ch=ch: e.transpose(out=ptr[:, ch, :], in_=h1[:, ch * 128:(ch + 1) * 128], identity=ident[:]))
            c.op('dve', ['ptr2'], ['h1T'], lambda e: e.tensor_copy(out=h1T[:, :, i * 128:(i + 1) * 128], in_=ptr[:]))
            for (lo, hi) in ((0, 512), (512, 672)):
                for ch in range(8):
                    c.op('pe', ['h1T', 'wbI'], ['pl'],
                         lambda e, ch=ch, lo=lo, hi=hi: e.matmul(pl[:, lo:hi], lhsT=h1T[:, ch, i * 128:(i + 1) * 128], rhs=wi[:, ch, lo:hi],
                                                                 start=(ch == 0), stop=(ch == 7)))
            c.op('act', ['pl'], ['junk', 'ss2'], lambda e: e.activation(out=junk[:, 0:384], in_=pl[:, 0:384], func=AF.Square, accum_out=ss2[:, 0:1]))
            c.op('act', ['pl'], ['junk', 'ss2'], lambda e: e.activation(out=junk[:, 384:640], in_=pl[:, 384:640], func=AF.Square, accum_out=ss2[:, 1:2]))
            c.op('act', ['ss2'], ['sd2'], lambda e: e.activation(out=sd2[:, 0:1], in_=ss2[:, 0:1], func=AF.Sqrt, scale=1.0 / 384, bias=EPS))
            c.op('act', ['ss2'], ['sd2'], lambda e: e.activation(out=sd2[:, 1:2], in_=ss2[:, 1:2], func=AF.Sqrt, scale=1.0 / 256, bias=EPS))
            c.op('dve', ['sd2'], ['rs2'], lambda e: e.reciprocal(out=rs2[:], in_=sd2[:]))
            ln = latn[s]
            kl = 'latn%d' % s
            c.op('act', ['pl', 'rs2'], [kl], lambda e: e.activation(out=ln[:, 0:384], in_=pl[:, 0:384], func=AF.Identity, scale=rs2[:, 0:1]))
            c.op('act', ['pl', 'rs2'], [kl], lambda e: e.activation(out=ln[:, 384:640], in_=pl[:, 384:640], func=AF.Identity, scale=rs2[:, 1:2]))
            krv = pl[:, 640:672].rearrange("p (a j) -> p a j", a=2)
            cb = cosS[:, i, :].unsqueeze(1).broadcast_to([128, 2, 16])
            sb_ = sinS[:, i, :].unsqueeze(1).broadcast_to([128, 2, 16])
            c.op('dve', ['pl', 'cos2'], ['k1'], lambda e: e.tensor_tensor(out=k1[:].rearrange("p (a j) -> p a j", a=2), in0=krv, in1=cb, op=ALU.mult))
            c.op('dve', ['pl', 'sin2'], ['k2'], lambda e: e.tensor_tensor(out=k2[:].rearrange("p (a j) -> p a j", a=2), in0=krv, in1=sb_, op=ALU.mult))
            c.op('dve', ['k1', 'k2'], [kl], lambda e: e.tensor_tensor(out=ln[:, 640:656], in0=k1[:, 0:16], in1=k2[:, 16:32], op=ALU.subtract))
            c.op('dve', ['k1', 'k2'], [kl], lambda e: e.tensor_tensor(out=ln[:, 656:672], in0=k2[:, 0:16], in1=k1[:, 16:32], op=ALU.add))
            for ch in range(6):
                wdt = 128 if ch < 5 else 32
                c.op('pe', [kl, 'ident2'], ['plt'],
                     lambda e, ch=ch, wdt=wdt: e.transpose(out=plt[0:wdt, ch, :], in_=ln[:, ch * 128:ch * 128 + wdt], identity=ident[:]))
            lt = ltT[s]
            klt = 'ltT%d' % s
            c.op('dve', ['plt'], [klt], lambda e: e.tensor_copy(out=lt[:, 0:5, :], in_=plt[:, 0:5, :]))
            c.op('dve', ['plt'], [klt], lambda e: e.tensor_copy(out=lt[0:32, 5, :], in_=plt[0:32, 5, :]))
            c.dma('sp', [klt], ['latT_d'], latT_d[0:640, i * 128:(i + 1) * 128].rearrange("(ch p) t -> p ch t", p=128), lt[:, 0:5, :])
            c.dma('sp', [klt], ['latT_d'], latT_d[640:672, i * 128:(i + 1) * 128], lt[0:32, 5, :])
        n = 0
        for ec in (range(8) if do_gate else []):
            for blk in range(4):
                for ch in range(8):
                    c.op('pe', ['h1T', 'wbI'], ['pg'],
                         lambda e, ch=ch: e.matmul(pg[:], lhsT=wi[:, ch, 672 + ec * 128:672 + (ec + 1) * 128], rhs=h1T[:, ch, blk * 512:(blk + 1) * 512],
                                                   start=(ch == 0), stop=(ch == 7)))
                g = gt[n % 2]
                kg = 'gt%d' % (n % 2)
                c.op('act', ['pg'], [kg], lambda e: e.activation(out=g[:], in_=pg[:], func=AF.Silu))
                c.dma('sp', [kg], ['gcT_d'], gcT_d[ec * 128:(ec + 1) * 128, blk * 512:(blk + 1) * 512], g[:])
                n += 1
    c.barrier()
    c.es = es


def emit_L4(c, es, ocT_d, gcT_d, x1_d, wout_d, fnorm_d, out_d):
    with ExitStack() as es1:
        c.es = es1
        yT = c.sb("ycT", [128, 8, TS], BF16)
        gT = c.sb("gcT", [128, 8, TS], BF16)
        c.dma('sp', [], ['ycT'], yT[:], ocT_d.rearrange("(ch p) t -> p ch t", p=128))
        c.dma('sp', [], ['gcT'], gT[:], gcT_d.rearrange("(ch p) t -> p ch t", p=128))
        for ch in range(8):
            eng = 'dve' if ch % 2 == 0 else 'pool'
            c.op(eng, ['ycT', 'gcT'], ['ycT'], lambda e, ch=ch: e.tensor_tensor(out=yT[:, ch, :], in0=yT[:, ch, :], in1=gT[:, ch, :], op=ALU.mult))
        wo = c.sb("wbO", [128, 8, 1024], BF16)
        load_weights_plain(c, wout_d, wo, 1024, 'O')
        fn = c.sb("fn", [128, 1024], F32)
        c.dma('sp', [], ['fn'], fn[:], fnorm_d.partition_broadcast(128))
        po = c.ps("po", [128, 1024])
        xt = [c.sb("xt%d" % i, [128, 1024], F32) for i in range(2)]
        ot = [c.sb("ot%d" % i, [128, 1024], F32) for i in range(2)]
        junk = c.sb("junk", [128, 1024], F32)
        ssq = c.sb("ssq", [128, 1], F32)
        sd = c.sb("sd", [128, 1], F32)
        rs = c.sb("rs", [128, 1], F32)
        for i in range(NT):
            s = i % 2
            kx = 'xt%d' % s
            outproj_tile(c, i, yT, 'ycT', wo, po, x1_d, xt[s], kx)
            rstd_tile(c, xt[s], kx, junk, ssq, sd, rs, D)
            c.op('dve', [kx, 'rs', 'fn'], ['ot%d' % s],
                 lambda e: e.scalar_tensor_tensor(out=ot[s][:], in0=xt[s][:], scalar=rs[:, 0:1], in1=fn[:], op0=ALU.mult, op1=ALU.mult))
            c.dma('sp', ['ot%d' % s], ['out_d'], out_d[i * 128:(i + 1) * 128, :], ot[s][:])
    c.barrier()
    c.es = es

import numpy as np

L = 8192
NQB = 16
SCALE3 = 96.0 ** -0.5


def emit_attn_L1(c, es, latT_d, wq_d, wk_d, wv_d, qgain_d, kvgain_d, cosT_d, sinT_d, swapd, oc_d, nqb=NQB, dbg=None, cq_d=None):
    if cq_d is None:
        cq_d = latT_d[0:384, :]
    ckT = c.sb("ckT", [128, 2, L], BF16)
    c.dma('sp', [], ['ckT'], ckT[:], latT_d[384:640, :].rearrange("(ch p) t -> p ch t", p=128))
    qg = c.sb("qg", [128, 3], F32)
    kg = c.sb("kg", [128, 2], F32)
    c.dma('sp', [], ['qg'], qg[:], qgain_d)
    c.dma('sp', [], ['kg'], kg[:], kvgain_d)
    wq = c.sb("wq", [128, 3, 4, 96], BF16)
    wqr = c.sb("wqr", [128, 3, 4, 96], BF16)
    c.op('pool', [], ['wqr'], lambda e: e.memset(wqr[:], 0.0))
    wk = c.sb("wk", [128, 2, 4, 64], BF16)
    wv = c.sb("wv", [128, 2, 4, 64], BF16)
    swp = c.sb("swp3", [128, 128], F32)
    c.dma('sp', [], ['swp3'], swp[:], swapd)
    with ExitStack() as es0:
        c.es = es0
        st = c.sb("wst3", [128, 4 * 96], F32)
        for ch in range(3):
            c.dma('sp', [], ['wst3'], st[:], wq_d[ch * 128:(ch + 1) * 128].rearrange("p h j -> p (h j)"))
            sv = st[:].rearrange("p (h j) -> p h j", h=4)
            c.op('dve', ['wst3', 'qg'], ['wq'], lambda e, ch=ch: e.tensor_scalar(out=wq[:, ch], in0=sv, scalar1=qg[:, ch:ch + 1], scalar2=None, op0=ALU.mult))
            c.op('dve', ['wst3', 'qg'], ['wqr'],
                 lambda e, ch=ch: e.tensor_scalar(out=wqr[:, ch, :, 64:80], in0=sv[:, :, 80:96], scalar1=qg[:, ch:ch + 1], scalar2=-1.0, op0=ALU.mult, op1=ALU.mult))
            c.op('dve', ['wst3', 'qg'], ['wqr'],
                 lambda e, ch=ch: e.tensor_scalar(out=wqr[:, ch, :, 80:96], in0=sv[:, :, 64:80], scalar1=qg[:, ch:ch + 1], scalar2=None, op0=ALU.mult))
        for ch in range(2):
            c.dma('sp', [], ['wst3'], st[:, 0:256], wk_d[ch * 128:(ch + 1) * 128].rearrange("p h j -> p (h j)"))
            sv = st[:, 0:256].rearrange("p (h j) -> p h j", h=4)
            c.op('dve', ['wst3', 'kg'], ['wk'], lambda e, ch=ch: e.tensor_scalar(out=wk[:, ch], in0=sv, scalar1=kg[:, ch:ch + 1], scalar2=None, op0=ALU.mult))
        for ch in range(2):
            c.dma('sp', [], ['wst3'], st[:, 0:256], wv_d[ch * 128:(ch + 1) * 128].rearrange("p h j -> p (h j)"))
            sv = st[:, 0:256].rearrange("p (h j) -> p h j", h=4)
            c.op('dve', ['wst3', 'kg'], ['wv'], lambda e, ch=ch: e.tensor_scalar(out=wv[:, ch], in0=sv, scalar1=kg[:, ch:ch + 1], scalar2=None, op0=ALU.mult))
    c.barrier()
    c.es = es
    KT = [c.sb("KT3%d" % h, [96, L], BF16) for h in range(2)]
    VA = c.sb("VA3", [128, 64, 128], BF16)
    VB = c.sb("VB3", [128, 64, 128], BF16)
    c.op('pool', [], ['VA3'], lambda e: e.memset(VA[:, :, 64:128], 1.0))
    c.op('pool', [], ['VB3'], lambda e: e.memset(VB[:, :, 0:64], 1.0))
    for h in range(2):
        c.dma('sp', [], ['KT3%d' % h], KT[h][64:96, :], latT_d[640:672, :])
    for pair in range(2):
        with ExitStack() as esA:
            c.es = esA
            pk = c.ps("pk", [128, 512])
            pv = c.ps("pv", [128, 4, 128])
            for h in range(2):
                hh = pair * 2 + h
                for blk in range(16):
                    for ch in range(2):
                        c.op('pe', ['wk', 'ckT'], ['pk'],
                             lambda e, ch=ch: e.matmul(pk[0:64, :], lhsT=wk[:, ch, hh, :], rhs=ckT[:, ch, blk * 512:(blk + 1) * 512],
                                                       start=(ch == 0), stop=(ch == 1)))
                    eng = 'act' if blk % 2 == 0 else 'dve'
                    if eng == 'act':
                        c.op('act', ['pk'], ['KT3%d' % h], lambda e: e.copy(out=KT[h][0:64, blk * 512:(blk + 1) * 512], in_=pk[0:64, :]))
                    else:
                        c.op('dve', ['pk'], ['KT3%d' % h], lambda e: e.tensor_copy(out=KT[h][0:64, blk * 512:(blk + 1) * 512], in_=pk[0:64, :]))
            for g4 in range(16):
                for tt in range(4):
                    kt = g4 * 4 + tt
                    for ch in range(2):
                        c.op('pe', ['wv', 'ckT'], ['pv'],
                             lambda e, ch=ch, tt=tt, kt=kt: e.matmul(pv[:, tt, :], lhsT=ckT[:, ch, kt * 128:(kt + 1) * 128],
                                                                     rhs=wv[:, ch, pair * 2:pair * 2 + 2, :].rearrange("p h j -> p (h j)"),
                                                                     start=(ch == 0), stop=(ch == 1)))
                c.op('act', ['pv'], ['VA3'], lambda e: e.copy(out=VA[:, g4 * 4:(g4 + 1) * 4, 0:64], in_=pv[:, :, 0:64]))
                c.op('dve', ['pv'], ['VB3'], lambda e: e.tensor_copy(out=VB[:, g4 * 4:(g4 + 1) * 4, 64:128], in_=pv[:, :, 64:128]))
        c.barrier()
        c.es = es
        if dbg is not None and pair == 0:
            c.dma('sp', ['KT30'], ['dbgK'], dbg['KT'], KT[0][:])
            c.dma('sp', ['VB3'], ['dbgV'], dbg['VB'], VB[:].rearrange("p t j -> p (t j)"))
        with ExitStack() as esB:
            c.es = esB
            PS = [c.ps("PS%d" % i, [128, 2, 512]) for i in range(3)]
            PO = c.ps("PO", [128, 2, 512])
            PQ = PS[0][:, 0, :]
            PQR = PS[1][:, 0, :]
            PS2 = PS[2][:, 0, :]
            PT = [c.sb("PT%d" % i, [128, 2, 512], BF16) for i in range(3)]
            S_sb = c.sb("S_sb", [128, 512], F32)
            O_sb = c.sb("O_sb", [128, 512], F32)
            rinv = c.sb("rinv", [128, 512], F32)
            yo = [c.sb("yo%d" % i, [128, 512], BF16) for i in range(2)]
            cq = [c.sb("cq%d" % i, [128, 3, 512], BF16) for i in range(2)]
            cs = [c.sb("cs%d" % i, [128, 2, 512], F32) for i in range(2)]
            QT = [c.sb("QT3%d" % h, [96, nqb * 512], BF16) for h in range(2)]
            m1 = c.sb("m1", [128, 512], F32)
            m2 = c.sb("m2", [128, 512], F32)

            def loadq(qb):
                s = qb % 2
                c.dma('sp', [], ['cq%d' % s], cq[s][:], cq_d[:, qb * 512:(qb + 1) * 512].rearrange("(ch p) t -> p ch t", p=128))
                c.dma('sp', [], ['cs%d' % s], cs[s][64:96, 0, :], cosT_d[:, qb * 512:(qb + 1) * 512])
                c.dma('sp', [], ['cs%d' % s], cs[s][64:96, 1, :], sinT_d[:, qb * 512:(qb + 1) * 512])

            def projq(qb):
                s = qb % 2
                for h in range(2):
                    hh = pair * 2 + h
                    q = QT[h][:, qb * 512:(qb + 1) * 512]
                    kq = 'QT3%d' % h
                    for ch in range(3):
                        c.op('pe', ['wq', 'cq%d' % s], ['PS0'],
                             lambda e, ch=ch: e.matmul(PQ[0:96, :], lhsT=wq[:, ch, hh, :], rhs=cq[s][:, ch, :], start=(ch == 0), stop=(ch == 2)))
                    for ch in range(3):
                        c.op('pe', ['wqr', 'cq%d' % s], ['PS1'],
                             lambda e, ch=ch: e.matmul(PQR[0:96, :], lhsT=wqr[:, ch, hh, :], rhs=cq[s][:, ch, :], start=(ch == 0), stop=(ch == 2)))
                    c.op('dve', ['PS0'], [kq], lambda e: e.tensor_copy(out=q[0:64, :], in_=PQ[0:64, :]))
                    c.op('dve', ['PS0', 'cs%d' % s], ['m1'], lambda e: e.tensor_tensor(out=m1[64:96, :], in0=PQ[64:96, :], in1=cs[s][64:96, 0, :], op=ALU.mult))
                    c.op('dve', ['PS1', 'cs%d' % s], ['m2'], lambda e: e.tensor_tensor(out=m2[64:96, :], in0=PQR[64:96, :], in1=cs[s][64:96, 1, :], op=ALU.mult))
                    c.op('pool', ['m1', 'm2'], [kq], lambda e: e.tensor_tensor(out=q[64:96, :], in0=m1[64:96, :], in1=m2[64:96, :], op=ALU.add))

            def S(qb, kt):
                s = kt % 3
                for h in range(2):
                    c.op('pe', ['KT3%d' % h, 'QT3%d' % h], ['PS%d' % s],
                         lambda e, h=h: e.matmul(PS[s][:, h, :], lhsT=KT[h][:, kt * 128:(kt + 1) * 128], rhs=QT[h][:, qb * 512:(qb + 1) * 512],
                                                 start=True, stop=True))

            loadq(0)
            for qb in range(nqb):
                if qb + 1 < nqb:
                    loadq(qb + 1)
                projq(qb)
            S(0, 0)
            S(0, 1)
            for qb in range(nqb):
                for kt in range(64):
                    s = kt % 3
                    if kt + 2 < 64:
                        S(qb, kt + 2)
                    c.op('act', ['PS%d' % s], ['PT%d' % s], lambda e: e.activation(out=PT[s][:], in_=PS[s][:], func=AF.Exp, scale=SCALE3))
                    c.op('pe', ['VA3', 'PT%d' % s], ['PO'],
                         lambda e: e.matmul(PO[:, 0, :], lhsT=VA[:, kt, :], rhs=PT[s][:, 0, :], start=(kt == 0), stop=(kt == 63)))
                    c.op('pe', ['VB3', 'PT%d' % s], ['PO'],
                         lambda e: e.matmul(PO[:, 1, :], lhsT=VB[:, kt, :], rhs=PT[s][:, 1, :], start=(kt == 0), stop=(kt == 63)))
                if qb + 1 < nqb:
                    S(qb + 1, 0)
                    S(qb + 1, 1)
                c.op('dve', ['PO'], ['S_sb'], lambda e: e.tensor_copy(out=S_sb[64:128, :], in_=PO[64:128, 0, :]))
                c.op('dve', ['PO'], ['S_sb'], lambda e: e.tensor_copy(out=S_sb[0:64, :], in_=PO[0:64, 1, :]))
                c.op('dve', ['PO'], ['O_sb'], lambda e: e.tensor_copy(out=O_sb[0:64, :], in_=PO[0:64, 0, :]))
                c.op('dve', ['PO'], ['O_sb'], lambda e: e.tensor_copy(out=O_sb[64:128, :], in_=PO[64:128, 1, :]))
                c.op('pe', ['swp3', 'S_sb'], ['PS2'], lambda e: e.matmul(PS2, lhsT=swp[:], rhs=S_sb[:], start=True, stop=True))
                c.op('dve', ['PS2'], ['rinv'], lambda e: e.reciprocal(out=rinv[:], in_=PS2))
                y = yo[qb % 2]
                ky = 'yo%d' % (qb % 2)
                c.op('dve', ['O_sb', 'rinv'], [ky], lambda e: e.tensor_tensor(out=y[:], in0=O_sb[:], in1=rinv[:], op=ALU.mult))
                c.dma('sp', [ky], ['oc_d'], oc_d[pair, :, qb * 512:(qb + 1) * 512], y[:])
        c.barrier()
        c.es = es

import numpy as np


def emit_select(c, es, sel_d, x1s, gcT, latT, x1_own, gc_own, cq_own):
    with ExitStack() as es1:
        c.es = es1
        sel = c.sb("sel", [128, 4], F32)
        c.dma('sp', [], ['sel'], sel[:], sel_d)
        xa = [c.sb("xa%d" % i, [128, 4, 1024], F32) for i in range(2)]
        xo = [c.sb("xo%d" % i, [128, 1024], F32) for i in range(2)]
        xv = x1s.rearrange("(ts i p) d -> i p ts d", ts=4, p=128)
        for i in range(16):
            s = i % 2
            ka, ko = 'xa%d' % s, 'xo%d' % s
            c.dma('sp', [], [ka], xa[s][:], xv[i])
            c.op('dve', [ka, 'sel'], [ko], lambda e: e.tensor_scalar(out=xo[s][:], in0=xa[s][:, 0, :], scalar1=sel[:, 0:1], scalar2=None, op0=ALU.mult))
            for ts in range(1, 4):
                c.op('dve', [ka, 'sel', ko], [ko],
                     lambda e, ts=ts: e.scalar_tensor_tensor(out=xo[s][:], in0=xa[s][:, ts, :], scalar=sel[:, ts:ts + 1], in1=xo[s][:], op0=ALU.mult, op1=ALU.add))
            c.dma('sp', [ko], ['x1_own'], x1_own[i * 128:(i + 1) * 128, :], xo[s][:])
        ga = [c.sb("ga%d" % i, [128, 4, 2048], BF16) for i in range(2)]
        go = [c.sb("go%d" % i, [128, 2048], BF16) for i in range(2)]
        jobs = [(gcT[ch * 128:(ch + 1) * 128, :], gc_own[ch * 128:(ch + 1) * 128, :]) for ch in range(8)]
        jobs += [(latT[ch * 128:(ch + 1) * 128, :], cq_own[ch * 128:(ch + 1) * 128, :]) for ch in range(3)]
        for n, (src, dst) in enumerate(jobs):
            s = n % 2
            ka, ko = 'ga%d' % s, 'go%d' % s
            c.dma('sp', [], [ka], ga[s][:], src.rearrange("p (ts t) -> p ts t", ts=4))
            c.op('dve', [ka, 'sel'], [ko], lambda e: e.tensor_scalar(out=go[s][:], in0=ga[s][:, 0, :], scalar1=sel[:, 0:1], scalar2=None, op0=ALU.mult))
            for ts in range(1, 4):
                c.op('dve', [ka, 'sel', ko], [ko],
                     lambda e, ts=ts: e.scalar_tensor_tensor(out=go[s][:], in0=ga[s][:, ts, :], scalar=sel[:, ts:ts + 1], in1=go[s][:], op0=ALU.mult, op1=ALU.add))
            c.dma('sp', [ko], ['own%d' % n], dst, go[s][:])
    c.barrier()
    c.es = es


_CACHE = {}


def _dram(nc, kind, n, shp, dt=F32):
    return nc.dram_tensor(n, list(shp), dt, kind=kind).ap()


def build_fused():
    nc = bass.Bass("TRN2", target_bir_lowering=False)
    I = lambda n, s, dt=F32: _dram(nc, "ExternalInput", n, s, dt)
    O = lambda n, s, dt=F32: _dram(nc, "ExternalOutput", n, s, dt)
    S = lambda n, s, dt=F32: _dram(nc, "Internal", n, s, dt)
    xT = I("xT", [1024, 8192]); xtm = I("xtm", [8192, 1024]); enorm = I("enorm", [128, 8])
    w_tm = I("w_tm", [4, 1024, 320]); w_ga = I("w_ga", [4, 1024, 128]); qkgain = I("qkgain", [1, 256])
    cos_tm = I("cos_tm", [128, 2048]); sin_tm = I("sin_tm", [128, 2048]); swapd = I("swapd", [128, 128]); identd = I("identd", [128, 128], BF16)
    w_hy = I("w_hy", [4, 1024, 512]); convw = I("convw", [4, 128, 12]); fparams = I("fparams", [64, 4])
    w1d = I("w1d", [33, 64]); w2d = I("w2d", [64, 64]); w3d = I("w3d", [4, 64, 256]); zTf = I("zTf", [33, 8192]); zTr = I("zTr", [33, 8192]); trow = I("trow", [2, 8192])
    negdel = I("negdel", [4, 128, 1]); hyd = I("hyd", [4, 128, 1]); FAd = I("FAd", [128, 256], BF16); Fd = I("Fd", [128, 3, 128], BF16)
    Gd = I("Gd", [128, 2, 256], BF16); twd = I("twd", [128, 2, 128])
    wout = I("wout", [1024, 1024]); win = I("win", [1024, 1696]); onorm = I("onorm", [128, 8]); cos2 = I("cos2", [4, 128, 256]); sin2 = I("sin2", [4, 128, 256])
    wq = I("wq", [4, 384, 4, 96]); wk = I("wk", [4, 256, 4, 64]); wv = I("wv", [4, 256, 4, 64]); qgain = I("qgain", [128, 3]); kvgain = I("kvgain", [128, 2])
    cosT = I("cosT", [32, 2048]); sinT = I("sinT", [32, 2048]); sel = I("sel", [128, 4])
    wout2 = I("wout2", [1024, 1024]); fnorm = I("fnorm", [1, 1024])
    kscr = S("kscr", [128, 16384], BF16); zscr = S("zscr", [128, 8192], BF16); yscr = S("yscr", [128, 8192])
    y0T = S("y0Ts", [1024, 8192], BF16); x1s = S("x1s", [8192, 1024]); latT = S("latTs", [672, 8192], BF16)
    gcT = S("gcTs", [1024, 8192], BF16)
    x1o = S("x1own", [2048, 1024]); gco = S("gcown", [1024, 2048], BF16); cqo = S("cqown", [384, 2048], BF16); oco = S("ocown", [1024, 2048], BF16)
    out = O("out", [2048, 1024])
    with ExitStack() as es:
        c = Ctx(nc, es)

        def scoped(fn):
            with ExitStack() as e1:
                c.es = e1
                fn(e1)
            c.barrier()
            c.es = es

        for g in range(4):
            scoped(lambda e1: emit_attn_L0(c, e1, xT, w_tm[g], w_ga[g], enorm, qkgain, cos_tm, sin_tm, swapd, identd, y0T[g * 128:(g + 1) * 128, :]))
            scoped(lambda e1: emit_hyena_L0(c, e1, xT, w_hy[g], enorm, convw[g], fparams, w1d, w2d, w3d[g], zTf, zTr, trow, negdel[g], hyd[g],
                                            FAd, Fd, Gd, twd, kscr, zscr, yscr, y0T[512 + g * 128:512 + (g + 1) * 128, :]))
        for ts in range(4):
            sl = slice(ts * 2048, (ts + 1) * 2048)
            scoped(lambda e1: emit_L2(c, e1, y0T[:, sl], xtm[sl, :], wout, win, onorm, cos2[ts], sin2[ts], identd, x1s[sl, :], latT[:, sl], gcT[:, sl]))
        emit_select(c, es, sel, x1s, gcT, latT, x1o, gco, cqo)
        for g in range(4):
            scoped(lambda e1: emit_attn_L1(c, e1, latT, wq[g], wk[g], wv[g], qgain, kvgain, cosT, sinT, swapd,
                                           oco[g * 256:(g + 1) * 256, :].rearrange("(a p) t -> a p t", p=128), nqb=4, cq_d=cqo))
        scoped(lambda e1: emit_L4(c, e1, oco, gco, x1o, wout2, fnorm, out))
        c.finish()
        print("fused program: inst", c.ninst, "waits", c.nwaits, dict(c.cnt))
    return nc


def _get(name, fn):
    if name not in _CACHE:
        _CACHE[name] = fn()
    return _CACHE[name]


def fused_inputs(d, b, C, H, C32, C3):
    m = {}
    a = [l1a_inputs(d, b, g, C) for g in range(4)]
    h = [l1b_inputs(d, b, g, H) for g in range(4)]
    m.update(xT=a[0]['xT'], xtm=np.ascontiguousarray(d['x'][b]), enorm=a[0]['enorm'], qkgain=a[0]['qkgain'],
             cos_tm=C['cos64'], sin_tm=C['sin64'], swapd=C['swap'], identd=C['ident'],
             w_tm=np.stack([x['w_tm'] for x in a]), w_ga=np.stack([x['w_ga'] for x in a]),
             w_hy=np.stack([x['w_hy'] for x in h]), convw=np.stack([x['convw'] for x in h]), fparams=h[0]['fparams'],
             w1d=h[0]['w1d'], w2d=h[0]['w2d'], w3d=np.stack([x['w3d'] for x in h]), zTf=H['zTf'], zTr=H['zTr'], trow=H['trow'],
             negdel=np.stack([x['negdel'] for x in h]), hyd=np.stack([x['hyd'] for x in h]), FAd=H['FA'], Fd=H['Fd'], Gd=H['Gd'], twd=H['tw'],
             wout=d['e_w_out'][0], win=d['o_w_in'][0], onorm=np.ascontiguousarray(d['o_norm'][0].reshape(8, 128).T),
             cos2=np.stack([tm16(C32[0][ts * 2048:(ts + 1) * 2048]) for ts in range(4)]),
             sin2=np.stack([tm16(C32[1][ts * 2048:(ts + 1) * 2048]) for ts in range(4)]))
    l3 = [l3_inputs(d, b, g, None, C3, C['swap']) for g in range(4)]
    m.update(wq=np.stack([x['wq'] for x in l3]), wk=np.stack([x['wk'] for x in l3]), wv=np.stack([x['wv'] for x in l3]),
             qgain=l3[0]['qgain'], kvgain=l3[0]['kvgain'], cosT=C3['cosT'], sinT=C3['sinT'],
             wout2=d['o_w_out'][0], fnorm=d['final_norm'][None, :].astype(np.float32))
    return m


def kernel(**inputs):
    d = {k: np.asarray(v) for k, v in inputs.items()}
    C = consts(); H = hy_consts(); C32 = angles(32); C3 = l3_consts()
    per_batch = [fused_inputs(d, b, C, H, C32, C3) for b in range(2)]
    ins = []
    for i in range(8):
        q = i % 4
        m = dict(per_batch[i // 4])
        onehot = np.zeros((128, 4), np.float32); onehot[:, q] = 1.0
        m.update(sel=onehot, cosT=np.ascontiguousarray(C3['cosT'][:, q * 2048:(q + 1) * 2048]), sinT=np.ascontiguousarray(C3['sinT'][:, q * 2048:(q + 1) * 2048]))
        ins.append(m)
    res = run_bass_kernel_spmd(_get('F', build_fused), ins, core_ids=list(range(8))).results
    out = np.empty((2, 8192, 1024), np.float32)
    for i in range(8):
        b, q = i // 4, i % 4
        out[b, q * 2048:(q + 1) * 2048] = np.asarray(res[i]['out'])
    return out
```

```python
import math
import ml_dtypes

import numpy as np
from contextlib import ExitStack
import concourse.bass as bass
import concourse.mybir as mybir
from concourse.bass_utils import run_bass_kernel_spmd

F32 = mybir.dt.float32
BF16 = mybir.dt.bfloat16
AF = mybir.ActivationFunctionType
ALU = mybir.AluOpType
AX = mybir.AxisListType

N_DMA_SEMS = 24


class Ctx:
    def __init__(self, nc, es):
        self.nc = nc
        self.es = es
        self.es_root = es
        self.eng = {'pe': nc.tensor, 'act': nc.scalar, 'dve': nc.vector,
                    'pool': nc.gpsimd, 'sp': nc.sync}
        self.semobj = {}
        for e in ['pe', 'act', 'dve', 'pool']:
            self.semobj[e] = es.enter_context(nc.semaphore("s_" + e))
        self.cnt = {e: 0 for e in ['pe', 'act', 'dve', 'pool']}
        self.seen = {e: {} for e in self.eng}
        self.dma_use = []
        for i in range(N_DMA_SEMS):
            k = "d%d" % i
            self.semobj[k] = es.enter_context(nc.semaphore("s_" + k))
            self.dma_use.append(0)
        self.dma_rr = 0
        self.last_w = {}
        self.readers = {}
        self.nwaits = 0
        self.ninst = 0
        self.excl = set()
        self.uid = 0

    def sb(self, name, shape, dt):
        self.uid += 1
        return self.es.enter_context(self.nc.sbuf_tensor("%s_u%d" % (name, self.uid), list(shape), dt))

    def ps(self, name, shape, dt=F32):
        self.excl.add(name)
        self.uid += 1
        return self.es.enter_context(self.nc.psum_tensor("%s_u%d" % (name, self.uid), list(shape), dt))

    def _wait(self, e, sk, val):
        if self.seen[e].get(sk, 0) < val:
            self.eng[e].wait_ge(self.semobj[sk], val)
            self.seen[e][sk] = val
            self.nwaits += 1

    def _deps(self, e, reads, writes):
        need = {}
        ex = [k for k in reads if k in self.excl]
        if ex:
            writes = list(writes) + ex
            reads = [k for k in reads if k not in self.excl]

        def add(tok, kind):
            sk, val = tok
            if sk == 'pe' and e == 'pe':
                return
            if sk == e and kind != 'raw':
                return
            if need.get(sk, 0) < val:
                need[sk] = val

        for k in reads:
            if k in self.last_w:
                add(self.last_w[k], 'raw')
        for k in writes:
            if k in self.last_w:
                add(self.last_w[k], 'waw')
            for sk, val in self.readers.get(k, {}).items():
                add((sk, val), 'war')
        for sk, val in need.items():
            self._wait(e, sk, val)

    def _done(self, tok, reads, writes):
        ex = [k for k in reads if k in self.excl]
        if ex:
            writes = list(writes) + ex
            reads = [k for k in reads if k not in self.excl]
        for k in writes:
            self.last_w[k] = tok
            self.readers[k] = {}
        for k in reads:
            r = self.readers.setdefault(k, {})
            if r.get(tok[0], 0) < tok[1]:
                r[tok[0]] = tok[1]

    def op(self, e, reads, writes, build):
        self._deps(e, reads, writes)
        inst = build(self.eng[e])
        self.cnt[e] += 1
        inst.then_inc(self.semobj[e], 1)
        self._done((e, self.cnt[e]), reads, writes)
        self.ninst += 1
        return inst

    def dma(self, q, reads, writes, out, in_, **kw):
        self._deps(q, reads, writes)
        i = self.dma_rr
        self.dma_rr = (self.dma_rr + 1) % N_DMA_SEMS
        sk = "d%d" % i
        self._wait(q, sk, 16 * self.dma_use[i])
        self.dma_use[i] += 1
        inst = self.eng[q].dma_start(out=out, in_=in_, **kw)
        inst.then_inc(self.semobj[sk], 16)
        self._done((sk, 16 * self.dma_use[i]), reads, writes)
        self.ninst += 1
        return inst

    def collective(self, kind, reads, writes, ins, outs, groups):
        self._deps('pool', reads, writes)
        self.ncoll = getattr(self, 'ncoll', 0) + 1
        sk = "cc%d" % self.ncoll
        self.semobj[sk] = self.es_root.enter_context(self.nc.semaphore("s_" + sk))
        inst = self.nc.gpsimd.collective_compute(kind, ALU.bypass, replica_groups=groups, ins=ins, outs=outs)
        inst.then_inc(self.semobj[sk], 16)
        self._done((sk, 16), reads, writes)
        self.ninst += 1
        return inst

    def barrier(self):
        for e in self.eng:
            for sk in ['pe', 'act', 'dve', 'pool']:
                if sk != e and self.cnt[sk]:
                    self._wait(e, sk, self.cnt[sk])
            for i in range(N_DMA_SEMS):
                if self.dma_use[i]:
                    self._wait(e, "d%d" % i, 16 * self.dma_use[i])
        self.nbar = getattr(self, 'nbar', 0) + 1
        for sk in ['pe', 'act', 'dve', 'pool']:
            if self.cnt[sk] > 20000:
                self.semobj[sk] = self.es_root.enter_context(self.nc.semaphore("s_%s_b%d" % (sk, self.nbar)))
                self.cnt[sk] = 0
                for e in self.eng:
                    self.seen[e].pop(sk, None)
                for k in list(self.last_w):
                    if self.last_w[k][0] == sk:
                        del self.last_w[k]
                for k in self.readers:
                    self.readers[k].pop(sk, None)

    def finish(self, keys=(), e='sp'):
        for i in range(N_DMA_SEMS):
            if self.dma_use[i]:
                self._wait(e, "d%d" % i, 16 * self.dma_use[i])
        for k in keys:
            if k in self.last_w:
                sk, val = self.last_w[k]
                self._wait(e, sk, val)

import numpy as np
F=np.float32
BF=ml_dtypes.bfloat16
L=8192
def angles(dim):
    rows=L//64; row=np.repeat(np.arange(rows),64).astype(F); col=np.tile(np.arange(64),rows).astype(F)
    n=dim//4; inv=(10000.0**(-np.arange(n,dtype=F)/n)).astype(F)
    ang=np.concatenate([row[:,None]*inv,col[:,None]*inv],-1)
    return np.cos(ang).astype(F),np.sin(ang).astype(F)
def consts():
    sw=np.zeros((128,128),F)
    for p in range(128): sw[p,(p+64)%128]=1
    c64,s64=angles(64)
    tm=lambda t: np.ascontiguousarray(t.reshape(64,128,-1).transpose(1,0,2).reshape(128,-1))
    return dict(swap=sw, ident=np.eye(128,dtype=F).astype(BF), cos64=tm(c64), sin64=tm(s64))
def l1a_inputs(d, b, g, C):
    W=d['e_w_in'][0]
    hA=2*g; hB=2*g+1; kv=g//2
    heads=[W[:,hA*64:(hA+1)*64], W[:,hB*64:(hB+1)*64], W[:,512+kv*64:512+(kv+1)*64], W[:,512+kv*64:512+(kv+1)*64]]
    cols=[]
    for a in range(2):
        for h in range(4):
            cols.append(heads[h][:,a*32:(a+1)*32])
    cols.append(W[:,640+kv*64:640+(kv+1)*64])
    w_tm=np.ascontiguousarray(np.concatenate(cols,1))
    w_ga=np.ascontiguousarray(W[:,768+hA*64:768+hA*64+128])
    gq=d['e_q_norm'][0]; gk=d['e_k_norm'][0]; gs=[gq,gq,gk,gk]
    G=np.concatenate([gs[h][a*32:(a+1)*32] for a in range(2) for h in range(4)])[None,:].astype(F)
    return dict(xT=np.ascontiguousarray(d['x'][b].T), w_tm=w_tm, w_ga=w_ga,
                enorm=np.ascontiguousarray(d['e_norm'][0].reshape(8,128).T), qkgain=G,
                cos_tm=C['cos64'], sin_tm=C['sin64'], swapd=C['swap'], identd=C['ident'])

def hy_consts():
    n=np.arange(128)
    ang=2*np.pi*np.outer(n,n)/128.0
    cs,sn=np.cos(ang),np.sin(ang)
    FA=np.concatenate([cs,-sn],1).astype(BF)
    Fd=np.stack([cs,sn,-sn],1).astype(BF)
    Gd=np.stack([np.concatenate([cs,sn],1),np.concatenate([-sn,cs],1)],1).astype(BF)
    a2=2*np.pi*np.outer(n,n)/16384.0
    tw=np.stack([np.cos(a2),np.sin(a2)],1).astype(F)
    t=np.linspace(0,1,L,dtype=F)[:,None]; w=(2*math.pi*np.arange(L,dtype=F)[:,None]/L).astype(F)
    bands=np.linspace(1e-4,15,16,dtype=F)
    zf=np.concatenate([t,np.cos(bands*w),-np.sin(bands*w)],-1).astype(F)
    zTf=np.ascontiguousarray(zf.T)
    zTr=np.zeros_like(zTf); zTr[:,1:]=zTf[:,:0:-1]
    tr=np.zeros(L,F); tr[1:]=t[:0:-1,0]
    trow=np.stack([t[:,0],tr],0).astype(F)
    mind=math.log(1e-2)/1.5; maxd=math.log(1e-2)/0.3
    deltas=np.abs(np.linspace(mind,maxd,512,dtype=F)).astype(F)
    return dict(FA=FA,Fd=Fd,Gd=Gd,tw=tw,zTf=zTf,zTr=zTr,trow=trow,deltas=deltas)
def l1b_inputs(d,b,g,H):
    W=d['e_w_in'][0]; cs=slice(g*128,(g+1)*128)
    w_hy=np.ascontiguousarray(np.concatenate([W[:,1280:1792][:,cs],W[:,1792:2304][:,cs],W[:,2304:2816][:,cs],W[:,2816:3328][:,cs]],1))
    cw=d['e_conv_w'][0]; cb=d['e_conv_b'][0]
    convw=np.zeros((128,12),F)
    for gi in range(3):
        ch=gi*512+np.arange(g*128,(g+1)*128)
        convw[:,gi*4+0]=cw[0,ch]; convw[:,gi*4+1]=cw[1,ch]; convw[:,gi*4+2]=cw[2,ch]; convw[:,gi*4+3]=cb[ch]
    fparams=np.stack([d['e_filt_f1'][0],d['e_filt_b1'][0],d['e_filt_f2'][0],d['e_filt_b2'][0]],1).astype(F)
    w3=d['e_filt_w3'][0]
    w3d=np.ascontiguousarray(np.concatenate([w3[:,cs],w3[:,512:][:,cs]],1))
    return dict(xT=np.ascontiguousarray(d['x'][b].T), w_hy=w_hy, enorm=np.ascontiguousarray(d['e_norm'][0].reshape(8,128).T),
                convw=convw, fparams=fparams, w1d=d['e_filt_w1'][0], w2d=d['e_filt_w2'][0], w3d=w3d,
                zTf=H['zTf'], zTr=H['zTr'], trow=H['trow'], negdel=(-H['deltas'][cs])[:,None].astype(F), hyd=d['e_hy_d'][0][cs][:,None].astype(F),
                FAd=H['FA'], Fd=H['Fd'], Gd=H['Gd'], twd=H['tw'])

def tm16(t):
    return np.ascontiguousarray(t.reshape(16,128,-1).transpose(1,0,2).reshape(128,-1))
def l2_inputs(d, b, ts, y0T_b, C32):
    sl=slice(ts*2048,(ts+1)*2048)
    return dict(y0T=np.ascontiguousarray(y0T_b[:, sl]), xs=np.ascontiguousarray(d['x'][b, sl]), wout=d['e_w_out'][0], win=d['o_w_in'][0],
                onorm=np.ascontiguousarray(d['o_norm'][0].reshape(8,128).T), cos2=tm16(C32[0][sl]), sin2=tm16(C32[1][sl]),
                identd=np.eye(128,dtype=F).astype(BF))
def l4_inputs(d, b, ts, ocT_b, gcT_bts, x1_bts):
    sl=slice(ts*2048,(ts+1)*2048)
    return dict(ocT=np.ascontiguousarray(ocT_b[:, sl]), gcT=gcT_bts, x1=x1_bts, wout=d['o_w_out'][0], fnorm=d['final_norm'][None,:].astype(F))

def l3_consts():
    c,s=angles(32)
    return dict(cosT=np.ascontiguousarray(np.tile(c.T,(2,1))), sinT=np.ascontiguousarray(np.tile(s.T,(2,1))))
def l3_inputs(d, b, g, latT_b, C3, swap):
    wqb=d['o_w_qb'][0]; wkvb=d['o_w_kvb'][0]
    wq=np.zeros((384,4,96),F); wk=np.zeros((256,4,64),F); wv=np.zeros((256,4,64),F)
    for i in range(4):
        h=4*g+i
        wq[:,i,:]=wqb[:,h*96:h*96+96]
        wk[:,i,:]=wkvb[:,h*128:h*128+64]; wv[:,i,:]=wkvb[:,h*128+64:h*128+128]
    return dict(latT=latT_b, wq=wq, wk=wk, wv=wv, qgain=np.ascontiguousarray(d['o_q_a_norm'][0].reshape(3,128).T),
                kvgain=np.ascontiguousarray(d['o_kv_a_norm'][0].reshape(2,128).T), cosT=C3['cosT'], sinT=C3['sinT'], swapd=swap)

import numpy as np

L = 8192
D = 1024
NBLK = 16
TB = 512
EPS = 1e-6


def load_weights_scaled(c, wd, gain_t, gkey, wb, ncols, tag):
    stg = [c.sb("wstg%s%d" % (tag, i), [128, ncols], F32) for i in range(2)]
    for ch in range(8):
        s = stg[ch % 2]
        k = "wstg%s%d" % (tag, ch % 2)
        c.dma('sp', [], [k], s[:], wd[ch * 128:(ch + 1) * 128, :])
        c.op('dve', [k, gkey], ['wb' + tag],
             lambda e, s=s, ch=ch: e.tensor_scalar(out=wb[:, ch, :], in0=s[:], scalar1=gain_t[:, ch:ch + 1],
                                                   scalar2=None, op0=ALU.mult))


class Sweep:
    def __init__(self, c, xT, tag, hscr=None, mode='compute'):
        self.c = c
        self.tag = tag
        self.mode = mode
        self.hscr = hscr
        self.hT = [c.sb("hT%s%d" % (tag, i), [128, 8, TB], BF16) for i in range(2)]
        if mode == 'read':
            return
        self.xTv = xT.rearrange("(ch p) t -> p ch t", p=128)
        self.xf = [c.sb("xf%s%d" % (tag, i), [128, 8, TB], F32) for i in range(2)]
        self.sq = c.sb("sq" + tag, [128, 8, TB], BF16)
        self.sd = [c.sb("sd%s%d" % (tag, i), [128, TB], F32) for i in range(2)]
        self.R = [c.sb("R%s%d" % (tag, i), [128, TB], F32) for i in range(2)]
        self.ones = c.sb("ones" + tag, [128, 128], BF16)
        c.op('pool', [], ['ones' + tag], lambda e: e.memset(self.ones[:], 1.0))

    def load(self, j):
        c, t = self.c, self.tag
        s = j % 2
        if self.mode == 'read':
            c.dma('sp', ['hscr%d' % j], ['hT%s%d' % (t, s)], self.hT[s][:], self.hscr[:, :, j * TB:(j + 1) * TB])
            return
        c.dma('sp', [], ['xf%s%d' % (t, s)], self.xf[s][:], self.xTv[:, :, j * TB:(j + 1) * TB])

    def stats(self, j, ss_ps, ss_key):
        self.stats_a(j, ss_ps, ss_key)
        self.stats_b(j)

    def stats_b(self, j):
        if self.mode == 'read':
            return
        c, t = self.c, self.tag
        s = j % 2
        sd, R = self.sd[s], self.R[s]
        c.op('dve', ['sd%s%d' % (t, s)], ['R%s%d' % (t, s)], lambda e: e.reciprocal(out=R[:], in_=sd[:]))

    def stats_a(self, j, ss_ps, ss_key):
        if self.mode == 'read':
            return
        c, t = self.c, self.tag
        s = j % 2
        xf = self.xf[s]
        kx = 'xf%s%d' % (t, s)
        sd, R = self.sd[s], self.R[s]
        c.op('pool', [kx], ['sq' + t], lambda e: e.tensor_tensor(out=self.sq[:], in0=xf[:], in1=xf[:], op=ALU.mult))
        for ch in range(8):
            c.op('pe', ['sq' + t, 'ones' + t], [ss_key],
                 lambda e, ch=ch: e.matmul(ss_ps, lhsT=self.ones[:], rhs=self.sq[:, ch, :], start=(ch == 0), stop=(ch == 7)))
        c.op('act', [ss_key], ['sd%s%d' % (t, s)],
             lambda e: e.activation(out=sd[:], in_=ss_ps, func=AF.Sqrt, scale=1.0 / D, bias=EPS))

    def apply(self, j):
        c, t = self.c, self.tag
        s = j % 2
        if self.mode == 'read':
            return self.hT[s], 'hT%s%d' % (t, s)
        xf, hT, R = self.xf[s], self.hT[s], self.R[s]
        kx, kh = 'xf%s%d' % (t, s), 'hT%s%d' % (t, s)
        c.op('dve', [kx, 'R%s%d' % (t, s)], [kh],
             lambda e: e.tensor_tensor(out=hT[:], in0=xf[:], in1=R[:].unsqueeze(1).broadcast_to([128, 8, TB]), op=ALU.mult))
        if self.mode == 'write':
            c.dma('sp', [kh], ['hscr%d' % j], self.hscr[:, :, j * TB:(j + 1) * TB], hT[:])
        return hT, kh


def emit_attn_L0(c, es, xT, w_tm, w_ga, enorm, qkgain, cos_tm, sin_tm, swapd, identd, ya_out, dbg=None, hscr=None, sweep_mode='compute'):
    nc = c.nc
    QTp = [c.sb("QTA", [128, L], BF16), c.sb("QTB", [128, L], BF16)]
    c.op('pool', [], ['QTA'], lambda e: e.memset(QTp[0][64:128, :], 0.0))
    c.op('pool', [], ['QTB'], lambda e: e.memset(QTp[1][0:64, :], 0.0))
    KT = c.sb("KT", [128, L], BF16)
    VA = c.sb("VA", [128, 64, 128], BF16)
    VB = c.sb("VB", [128, 64, 128], BF16)
    gaT = c.sb("gaT", [128, L], BF16)
    c.op('pool', [], ['VA'], lambda e: e.memset(VA[:, :, 64:128], 1.0))
    c.op('pool', [], ['VB'], lambda e: e.memset(VB[:, :, 0:64], 1.0))
    ident = c.sb("ident", [128, 128], BF16)
    c.dma('sp', [], ['ident'], ident[:], identd)
    with ExitStack() as es1:
        c.es = es1
        gain = c.sb("gainA", [128, 8], F32)
        c.dma('sp', [], ['gainA'], gain[:], enorm)
        wtm = c.sb("wbA", [128, 8, 320], BF16)
        wga = c.sb("wbG", [128, 8, 128], BF16)
        load_weights_scaled(c, w_tm, gain, 'gainA', wtm, 320, 'A')
        load_weights_scaled(c, w_ga, gain, 'gainA', wga, 128, 'G')
        G = c.sb("G", [128, 256], F32)
        c.dma('sp', [], ['G'], G[:], qkgain.partition_broadcast(128))
        cosS = c.sb("cosS", [128, 64, 32], F32)
        sinS = c.sb("sinS", [128, 64, 32], F32)
        c.dma('sp', [], ['cosS'], cosS[:], cos_tm.rearrange("p (t j) -> p t j", j=32))
        c.dma('sp', [], ['sinS'], sinS[:], sin_tm.rearrange("p (t j) -> p t j", j=32))
        sw = Sweep(c, xT, 'A', hscr, sweep_mode)
        ss = c.ps("ssA", [128, TB])
        pga = c.ps("pga", [128, TB])
        pt = c.ps("ptA", [128, 4, 512])
        ptr = c.ps("ptrA", [128, 4, 2, 128], BF16)
        sqq = c.sb("sqq", [128, 4, 256], F32)
        ssq = c.sb("ssq", [128, 4, 4], F32)
        sdq = c.sb("sdq", [128, 4, 4], F32)
        rsq = c.sb("rsq", [128, 4, 4], F32)
        qn = c.sb("qn", [128, 4, 256], F32)
        t1 = c.sb("t1", [128, 4, 256], F32)
        t2 = c.sb("t2", [128, 4, 256], F32)
        qk = c.sb("qk", [128, 4, 4, 64], BF16)
        sw.load(0)
        sw.load(1)
        sw.stats(0, ss[:], 'ssA')
        cur = sw.apply(0)
        for j in range(NBLK):
            hT, kh = cur
            if j + 1 < NBLK:
                sw.stats_a(j + 1, ss[:], 'ssA')
            for ch in range(8):
                c.op('pe', ['wbG', kh], ['pga'],
                     lambda e, ch=ch: e.matmul(pga[:], lhsT=wga[:, ch, :], rhs=hT[:, ch, :], start=(ch == 0), stop=(ch == 7)))
            c.op('act', ['pga'], ['gaT'],
                 lambda e: e.activation(out=gaT[:, j * TB:(j + 1) * TB], in_=pga[:], func=AF.Silu))
            for tt in range(4):
                for ch in range(8):
                    c.op('pe', ['wbA', kh], ['ptA'],
                         lambda e, ch=ch, tt=tt: e.matmul(pt[:, tt, 0:320], lhsT=hT[:, ch, tt * 128:(tt + 1) * 128],
                                                          rhs=wtm[:, ch, :], start=(ch == 0), stop=(ch == 7)))
            if j + 1 < NBLK:
                sw.stats_b(j + 1)
                cur = sw.apply(j + 1)
            if j + 2 < NBLK:
                sw.load(j + 2)
            c.op('act', ['ptA'], ['sqq'], lambda e: e.activation(out=sqq[:], in_=pt[:, :, 0:256], func=AF.Square))
            c.op('dve', ['sqq'], ['ssq'],
                 lambda e: e.tensor_reduce(out=ssq[:], in_=sqq[:].rearrange("p t (a h j) -> p t h a j", a=2, h=4),
                                           axis=AX.XY, op=ALU.add))
            c.op('act', ['ssq'], ['sdq'], lambda e: e.activation(out=sdq[:], in_=ssq[:], func=AF.Sqrt, scale=1.0 / 64, bias=EPS))
            c.op('dve', ['sdq'], ['rsq'], lambda e: e.reciprocal(out=rsq[:], in_=sdq[:]))
            for a in range(2):
                c.op('dve', ['ptA', 'rsq'], ['qn'],
                     lambda e, a=a: e.tensor_tensor(
                         out=qn[:, :, a * 128:(a + 1) * 128].rearrange("p t (h j) -> p t h j", h=4),
                         in0=pt[:, :, a * 128:(a + 1) * 128].rearrange("p t (h j) -> p t h j", h=4),
                         in1=rsq[:].unsqueeze(3).broadcast_to([128, 4, 4, 32]), op=ALU.mult))
            c.op('dve', ['qn', 'G'], ['qn'],
                 lambda e: e.tensor_tensor(out=qn[:], in0=qn[:], in1=G[:].unsqueeze(1).broadcast_to([128, 4, 256]), op=ALU.mult))
            qv = qn[:].rearrange("p t (g j) -> p t g j", j=32)
            cb = cosS[:, j * 4:(j + 1) * 4, :].unsqueeze(2).broadcast_to([128, 4, 8, 32])
            sb_ = sinS[:, j * 4:(j + 1) * 4, :].unsqueeze(2).broadcast_to([128, 4, 8, 32])
            c.op('dve', ['qn', 'cosS'], ['t1'],
                 lambda e: e.tensor_tensor(out=t1[:].rearrange("p t (g j) -> p t g j", j=32), in0=qv, in1=cb, op=ALU.mult))
            c.op('pool', ['qn', 'sinS'], ['t2'],
                 lambda e: e.tensor_tensor(out=t2[:].rearrange("p t (g j) -> p t g j", j=32), in0=qv, in1=sb_, op=ALU.mult))
            t1v = t1[:].rearrange("p t (a h j) -> p t a h j", a=2, h=4)
            t2v = t2[:].rearrange("p t (a h j) -> p t a h j", a=2, h=4)
            c.op('dve', ['t1', 't2'], ['qk'],
                 lambda e: e.tensor_tensor(out=qk[:, :, :, 0:32], in0=t1v[:, :, 0, :, :], in1=t2v[:, :, 1, :, :], op=ALU.subtract))
            c.op('pool', ['t1', 't2'], ['qk'],
                 lambda e: e.tensor_tensor(out=qk[:, :, :, 32:64], in0=t2v[:, :, 0, :, :], in1=t1v[:, :, 1, :, :], op=ALU.add))
            for tt in range(4):
                c.op('pe', ['qk', 'ident'], ['ptrA'],
                     lambda e, tt=tt: e.transpose(out=ptr[:, tt, 0, :], in_=qk[:, tt, 0:2, :].rearrange("p h j -> p (h j)"), identity=ident[:]))
                c.op('pe', ['qk', 'ident'], ['ptrA'],
                     lambda e, tt=tt: e.transpose(out=ptr[:, tt, 1, :], in_=qk[:, tt, 2:4, :].rearrange("p h j -> p (h j)"), identity=ident[:]))
            c.op('act', ['ptrA'], ['QTA'],
                 lambda e: e.copy(out=QTp[0][0:64, j * TB:(j + 1) * TB].rearrange("p (t q) -> p t q", t=4), in_=ptr[0:64, :, 0, :]))
            c.op('act', ['ptrA'], ['QTB'],
                 lambda e: e.copy(out=QTp[1][64:128, j * TB:(j + 1) * TB].rearrange("p (t q) -> p t q", t=4), in_=ptr[64:128, :, 0, :]))
            c.op('dve', ['ptrA'], ['KT'],
                 lambda e: e.tensor_copy(out=KT[:, j * TB:(j + 1) * TB].rearrange("p (t q) -> p t q", t=4), in_=ptr[:, :, 1, :]))
            c.op('act', ['ptA'], ['VA'], lambda e: e.copy(out=VA[:, j * 4:(j + 1) * 4, 0:64], in_=pt[:, :, 256:320]))
            c.op('pool', ['VA'], ['VB'], lambda e: e.tensor_copy(out=VB[:, j * 4:(j + 1) * 4, 64:128], in_=VA[:, j * 4:(j + 1) * 4, 0:64]))
    c.barrier()
    c.es = es
    if dbg is not None:
        c.dma('sp', ['QTA'], ['dbgQ'], dbg['QT'][0:64, :], QTp[0][0:64, :])
        c.dma('sp', ['QTB'], ['dbgQ2'], dbg['QT'][64:128, :], QTp[1][64:128, :])
        c.dma('sp', ['KT'], ['dbgK'], dbg['KT'], KT[:])
        c.dma('sp', ['gaT'], ['dbgG'], dbg['gaT'], gaT[:])
        c.dma('sp', ['VA'], ['dbgV'], dbg['VA'], VA[:].rearrange("p t j -> p (t j)"))
    with ExitStack() as es2:
        c.es = es2
        swp = c.sb("swp", [128, 128], F32)
        c.dma('sp', [], ['swp'], swp[:], swapd)
        PS = [c.ps("PS%d" % i, [128, 2, 512]) for i in range(3)]
        PT = [c.sb("PT%d" % i, [128, 2, 512], BF16) for i in range(3)]
        PO = c.ps("PO", [128, 2, 512])
        PS2 = PS[2][:, 0, :]
        S_sb = c.sb("S_sb", [128, 512], F32)
        O_sb = c.sb("O_sb", [128, 512], F32)
        rinv = c.sb("rinv", [128, 512], F32)
        yo = [c.sb("yo%d" % i, [128, 512], BF16) for i in range(2)]

        def S(qb, kt):
            s = kt % 3
            for h in range(2):
                c.op('pe', ['KT', 'QT' + 'AB'[h]], ['PS%d' % s],
                     lambda e, h=h: e.matmul(PS[s][:, h, :], lhsT=KT[:, kt * 128:(kt + 1) * 128],
                                             rhs=QTp[h][:, qb * 512:(qb + 1) * 512], start=True, stop=True))

        S(0, 0)
        S(0, 1)
        for qb in range(NBLK):
            for kt in range(64):
                s = kt % 3
                if kt + 2 < 64:
                    S(qb, kt + 2)
                c.op('act', ['PS%d' % s], ['PT%d' % s],
                     lambda e: e.activation(out=PT[s][:], in_=PS[s][:], func=AF.Exp, scale=0.125))
                c.op('pe', ['VA', 'PT%d' % s], ['PO'],
                     lambda e: e.matmul(PO[:, 0, :], lhsT=VA[:, kt, :], rhs=PT[s][:, 0, :], start=(kt == 0), stop=(kt == 63)))
                c.op('pe', ['VB', 'PT%d' % s], ['PO'],
                     lambda e: e.matmul(PO[:, 1, :], lhsT=VB[:, kt, :], rhs=PT[s][:, 1, :], start=(kt == 0), stop=(kt == 63)))
            if qb + 1 < NBLK:
                S(qb + 1, 0)
                S(qb + 1, 1)
            c.op('dve', ['PO'], ['S_sb'], lambda e: e.tensor_copy(out=S_sb[64:128, :], in_=PO[64:128, 0, :]))
            c.op('dve', ['PO'], ['S_sb'], lambda e: e.tensor_copy(out=S_sb[0:64, :], in_=PO[0:64, 1, :]))
            c.op('dve', ['PO'], ['O_sb'], lambda e: e.tensor_copy(out=O_sb[0:64, :], in_=PO[0:64, 0, :]))
            c.op('dve', ['PO'], ['O_sb'], lambda e: e.tensor_copy(out=O_sb[64:128, :], in_=PO[64:128, 1, :]))
            c.op('pe', ['swp', 'S_sb'], ['PS2'], lambda e: e.matmul(PS2, lhsT=swp[:], rhs=S_sb[:], start=True, stop=True))
            c.op('dve', ['PS2'], ['rinv'], lambda e: e.reciprocal(out=rinv[:], in_=PS2))
            c.op('dve', ['O_sb', 'rinv'], ['O_sb'], lambda e: e.tensor_tensor(out=O_sb[:], in0=O_sb[:], in1=rinv[:], op=ALU.mult))
            y = yo[qb % 2]
            ky = 'yo%d' % (qb % 2)
            c.op('pool', ['O_sb', 'gaT'], [ky],
                 lambda e: e.tensor_tensor(out=y[:], in0=O_sb[:], in1=gaT[:, qb * 512:(qb + 1) * 512], op=ALU.mult))
            c.dma('sp', [ky], ['ya_out'], ya_out[:, qb * 512:(qb + 1) * 512], y[:])
    c.barrier()
    c.es = es

import numpy as np

I32 = mybir.dt.int32
PI_LO = 3.1415925
TWO_PI = 2.0 * np.pi
CG = 4
NG = 128 // CG


def lockstep(*chains):
    n = max(len(ch) for ch in chains)
    for i in range(n):
        for ch in chains:
            if i < len(ch):
                ch[i]()


def emit_hyena_L0(c, es, xT, w_hy, enorm, convw, fparams, w1d, w2d, w3d, zTf, zTr, trow, negdel, hyd,
                  FAd, Fd, Gd, twd, kscr, zscr, yscr, yb_out, dbg=None, hscr=None, sweep_mode='compute'):
    nc = c.nc
    zT = c.sb("zT", [128, L], BF16)
    g0 = c.sb("g0", [128, L], BF16)
    with ExitStack() as es1:
        c.es = es1
        gain = c.sb("gainH", [128, 8], F32)
        c.dma('sp', [], ['gainH'], gain[:], enorm)
        wb = c.sb("wbH", [128, 8, 512], BF16)
        load_weights_scaled(c, w_hy, gain, 'gainH', wb, 512, 'H')
        cw = c.sb("cw", [128, 12], F32)
        c.dma('sp', [], ['cw'], cw[:], convw)
        sw = Sweep(c, xT, 'H', hscr, sweep_mode)
        ss = c.ps("ssH", [128, TB])
        pf = c.ps("pfH", [128, 4, 512])
        raw = [c.sb("raw%d" % i, [128, 3, 514], F32) for i in range(3)]
        sg = [c.sb("sg%d" % i, [128, 512], F32) for i in range(3)]
        u = c.sb("u", [128, 3, 512], F32)
        c.op('pool', [], ['raw0'], lambda e: e.memset(raw[0][:, :, 0:1], 0.0))

        def conv(jj):
            r = raw[jj % 3]
            kr = 'raw%d' % (jj % 3)
            for g in range(3):
                c.op('dve', [kr, 'cw'], ['u'],
                     lambda e, g=g: e.tensor_scalar(out=u[:, g, :], in0=r[:, g, 1:513], scalar1=cw[:, g * 4 + 1:g * 4 + 2],
                                                    scalar2=cw[:, g * 4 + 3:g * 4 + 4], op0=ALU.mult, op1=ALU.add))
                c.op('dve', [kr, 'cw', 'u'], ['u'],
                     lambda e, g=g: e.scalar_tensor_tensor(out=u[:, g, :], in0=r[:, g, 0:512], scalar=cw[:, g * 4:g * 4 + 1],
                                                           in1=u[:, g, :], op0=ALU.mult, op1=ALU.add))
                c.op('dve', [kr, 'cw', 'u'], ['u'],
                     lambda e, g=g: e.scalar_tensor_tensor(out=u[:, g, :], in0=r[:, g, 2:514], scalar=cw[:, g * 4 + 2:g * 4 + 3],
                                                           in1=u[:, g, :], op0=ALU.mult, op1=ALU.add))
            c.op('dve', ['u'], ['zT'],
                 lambda e: e.tensor_tensor(out=zT[:, jj * TB:(jj + 1) * TB], in0=u[:, 2, :], in1=u[:, 1, :], op=ALU.mult))
            c.op('pool', ['u', 'sg%d' % (jj % 3)], ['g0'],
                 lambda e: e.tensor_tensor(out=g0[:, jj * TB:(jj + 1) * TB], in0=u[:, 0, :], in1=sg[jj % 3][:], op=ALU.mult))

        sw.load(0)
        sw.load(1)
        sw.stats(0, ss[:], 'ssH')
        cur = sw.apply(0)
        for j in range(NBLK):
            hT, kh = cur
            if j + 1 < NBLK:
                sw.stats_a(j + 1, ss[:], 'ssH')
            for m in range(4):
                for ch in range(8):
                    c.op('pe', ['wbH', kh], ['pfH'],
                         lambda e, m=m, ch=ch: e.matmul(pf[:, m, :], lhsT=wb[:, ch, m * 128:(m + 1) * 128], rhs=hT[:, ch, :],
                                                        start=(ch == 0), stop=(ch == 7)))
            if j + 1 < NBLK:
                sw.stats_b(j + 1)
                cur = sw.apply(j + 1)
            if j + 2 < NBLK:
                sw.load(j + 2)
            r = raw[j % 3]
            kr = 'raw%d' % (j % 3)
            c.op('act', ['pfH'], [kr], lambda e: e.copy(out=r[:, :, 1:513], in_=pf[:, 0:3, :]))
            c.op('act', ['pfH'], ['sg%d' % (j % 3)], lambda e: e.activation(out=sg[j % 3][:], in_=pf[:, 3, :], func=AF.Silu))
            if j > 0:
                rp = raw[(j - 1) % 3]
                kp = 'raw%d' % ((j - 1) % 3)
                c.op('pool', [kp], [kr], lambda e: e.tensor_copy(out=r[:, :, 0:1], in_=rp[:, :, 512:513]))
                c.op('pool', [kr], [kp], lambda e: e.tensor_copy(out=rp[:, :, 513:514], in_=r[:, :, 1:2]))
                conv(j - 1)
        c.op('pool', [], ['raw%d' % ((NBLK - 1) % 3)], lambda e: e.memset(raw[(NBLK - 1) % 3][:, :, 513:514], 0.0))
        conv(NBLK - 1)
    c.barrier()
    c.es = es
    if dbg is not None:
        c.dma('sp', ['zT'], ['dbgz'], dbg['zT'], zT[:])
        c.dma('sp', ['g0'], ['dbgg'], dbg['g0'], g0[:])
    c.dma('sp', ['zT'], ['zscr'], zscr, zT[:])
    kT = c.sb("kT", [128, 2 * L], BF16)
    l1p = c.sb("l1p", [128, 32], F32)
    rl1 = c.sb("rl1", [128, 1], F32)
    with ExitStack() as es2:
        c.es = es2
        fp = c.sb("fp", [64, 4], F32)
        c.dma('sp', [], ['fp'], fp[:], fparams)
        fb = c.sb("fb", [64, 2], F32)
        c.op('dve', ['fp'], ['fb'], lambda e: e.tensor_tensor(out=fb[:, 0:1], in0=fp[:, 0:1], in1=fp[:, 1:2], op=ALU.mult))
        c.op('dve', ['fp'], ['fb'], lambda e: e.tensor_tensor(out=fb[:, 1:2], in0=fp[:, 2:3], in1=fp[:, 3:4], op=ALU.mult))
        w1 = c.sb("w1", [33, 64], F32)
        w2 = c.sb("w2", [64, 64], F32)
        w3 = c.sb("w3", [64, 256], F32)
        c.dma('sp', [], ['w1'], w1[:], w1d)
        c.dma('sp', [], ['w2'], w2[:], w2d)
        c.dma('sp', [], ['w3'], w3[:], w3d)
        nd = c.sb("nd", [128, 1], F32)
        c.dma('sp', [], ['nd'], nd[:], negdel)
        def mk(name, shape, dt=F32):
            return [c.sb("%s_%d" % (name, dr), shape, dt) for dr in range(2)]
        zb = [[c.sb("zb%d_%d" % (dr, i), [33, 512], F32) for i in range(2)] for dr in range(2)]
        tb = [[c.sb("tb%d_%d" % (dr, i), [128, 512], F32) for i in range(2)] for dr in range(2)]
        P1 = [c.ps("P1_%d" % dr, [64, 512]) for dr in range(2)]
        P2 = [c.ps("P2_%d" % dr, [64, 512]) for dr in range(2)]
        P3 = [c.ps("P3_%d" % dr, [128, 512]) for dr in range(2)]
        a1 = mk("a1", [64, 512]); ki = mk("ki", [64, 512], I32); rr = mk("rr", [64, 512])
        h1 = mk("h1", [64, 512]); h2 = mk("h2", [64, 512]); dec = mk("dec", [128, 512]); hk = mk("hk", [128, 512])

        def sin_stages(dr, P, pk, col, hout, hk_):
            A, KI, RR = a1[dr], ki[dr], rr[dr]
            ka, kk, kr_ = 'a1_%d' % dr, 'ki_%d' % dr, 'rr_%d' % dr
            return [
                lambda: c.op('act', [pk, 'fp', 'fb'], [ka],
                             lambda e: e.activation(out=A[:], in_=P[:], func=AF.Identity, scale=fp[:, 2 * col:2 * col + 1], bias=fb[:, col:col + 1])),
                lambda: c.op('dve', [ka], [kk], lambda e: e.tensor_scalar(out=KI[:], in0=A[:], scalar1=1.0 / TWO_PI, scalar2=None, op0=ALU.mult)),
                lambda: c.op('dve', [kk, ka], [kr_],
                             lambda e: e.scalar_tensor_tensor(out=RR[:], in0=KI[:], scalar=-TWO_PI, in1=A[:], op0=ALU.mult, op1=ALU.add)),
                lambda: c.op('dve', [kr_], [kr_],
                             lambda e: e.tensor_scalar(out=RR[:], in0=RR[:], scalar1=-PI_LO, scalar2=PI_LO, op0=ALU.max, op1=ALU.min)),
                lambda: c.op('act', [kr_], [hk_], lambda e: e.activation(out=hout[:], in_=RR[:], func=AF.Sin)),
            ]

        def filt_chain(dr, j):
            s = j % 2
            zsrc = zTf if dr == 0 else zTr
            Z, T = zb[dr][s], tb[dr][s]
            kz, kt_ = 'zb%d_%d' % (dr, s), 'tb%d_%d' % (dr, s)
            p1, p2, p3 = P1[dr], P2[dr], P3[dr]
            H1, H2, DEC, HK = h1[dr], h2[dr], dec[dr], hk[dr]
            it = dr * NBLK + j
            st = [
                lambda: (c.dma('sp', [], [kz], Z[:], zsrc[:, j * TB:(j + 1) * TB]),
                         c.dma('sp', [], [kt_], T[:], trow[dr:dr + 1, j * TB:(j + 1) * TB].partition_broadcast(128))),
                lambda: c.op('pe', ['w1', kz], ['P1_%d' % dr], lambda e: e.matmul(p1[:], lhsT=w1[:], rhs=Z[:], start=True, stop=True)),
            ]
            st += sin_stages(dr, p1, 'P1_%d' % dr, 0, H1, 'h1_%d' % dr)
            st += [lambda: c.op('pe', ['w2', 'h1_%d' % dr], ['P2_%d' % dr], lambda e: e.matmul(p2[:], lhsT=w2[:], rhs=H1[:], start=True, stop=True))]
            st += sin_stages(dr, p2, 'P2_%d' % dr, 1, H2, 'h2_%d' % dr)
            st += [
                lambda: c.op('pe', ['w3', 'h2_%d' % dr], ['P3_%d' % dr],
                             lambda e: e.matmul(p3[:], lhsT=w3[:, dr * 128:(dr + 1) * 128], rhs=H2[:], start=True, stop=True)),
                lambda: c.op('act', [kt_, 'nd'], ['dec_%d' % dr], lambda e: e.activation(out=DEC[:], in_=T[:], func=AF.Exp, scale=nd[:, 0:1])),
                lambda: c.op('dve', ['P3_%d' % dr, 'dec_%d' % dr], ['hk_%d' % dr], lambda e: e.tensor_tensor(out=HK[:], in0=p3[:], in1=DEC[:], op=ALU.mult)),
            ]
            if dr == 1 and j == 0:
                st += [lambda: c.op('dve', ['hk_%d' % dr], ['hk_%d' % dr], lambda e: e.memset(HK[:, 0:1], 0.0))]
            st += [
                lambda: c.op('dve', ['hk_%d' % dr], ['l1p'],
                             lambda e: e.tensor_reduce(out=l1p[:, it:it + 1], in_=HK[:], axis=AX.X, op=ALU.add, apply_absolute_value=True)),
                lambda: c.op('pool', ['hk_%d' % dr], ['kT'], lambda e: e.tensor_copy(out=kT[:, dr * L + j * TB: dr * L + (j + 1) * TB], in_=HK[:])),
            ]
            return st

        for j in range(NBLK):
            lockstep(filt_chain(0, j), filt_chain(1, j))
        l1s = c.sb("l1s", [128, 1], F32)
        c.op('dve', ['l1p'], ['l1s'], lambda e: e.tensor_reduce(out=l1s[:], in_=l1p[:], axis=AX.X, op=ALU.add))
        c.op('dve', ['l1s'], ['rl1'], lambda e: e.reciprocal(out=rl1[:], in_=l1s[:]))
    c.barrier()
    c.es = es
    if dbg is not None:
        c.dma('sp', ['kT'], ['dbgk'], dbg['kT'], kT[:])
    c.dma('sp', ['kT'], ['kscr'], kscr, kT[:])
    with ExitStack() as es3:
        c.es = es3
        Kc = c.sb("Kc", [128, 128, 128], BF16)
        Xc = c.sb("Xc", [128, 128, 128], BF16)
        kv = kscr.rearrange("c (a b) -> a c b", b=128)
        zv = zscr.rearrange("c (a b) -> a c b", b=128)
        for i in range(16):
            c.dma('sp', ['kscr'], ['Kc'], Kc[:, i * 8:(i + 1) * 8, :], kv[:, i * 8:(i + 1) * 8, :])
        for i in range(16):
            c.dma('sp', ['zscr'], ['Xc'], Xc[0:64, i * 8:(i + 1) * 8, :], zv[:, i * 8:(i + 1) * 8, :])
        FA = c.sb("FA", [128, 256], BF16)
        Fm = c.sb("Fm", [128, 3, 128], BF16)
        Gm = c.sb("Gm", [128, 2, 256], BF16)
        tw = c.sb("tw", [128, 2, 128], F32)
        c.dma('sp', [], ['FA'], FA[:], FAd)
        c.dma('sp', [], ['Fm'], Fm[:], Fd)
        c.dma('sp', [], ['Gm'], Gm[:], Gd)
        c.dma('sp', [], ['tw'], tw[:], twd)
        PA_f = c.ps("PA_f", [128, CG, 256]); PXr_f = c.ps("PXr_f", [128, CG, 128]); PXi_f = c.ps("PXi_f", [128, CG, 128])
        PA_d = c.ps("PA_d", [128, CG, 256]); PXr_d = c.ps("PXr_d", [128, CG, 128]); PXi_d = c.ps("PXi_d", [128, CG, 128])
        mf = [c.sb("mf%d" % i, [128, CG, 128], F32) for i in range(4)]
        md = [c.sb("md%d" % i, [128, CG, 128], F32) for i in range(4)]
        Br_f = c.sb("Br_f", [128, CG, 128], BF16); Bi_f = c.sb("Bi_f", [128, CG, 128], BF16)
        Br_d = c.sb("Br_d", [128, CG, 128], BF16); Bi_d = c.sb("Bi_d", [128, CG, 128], BF16)
        Kr = [c.sb("Kr%d" % i, [128, CG, 128], F32) for i in range(2)]
        Ki = [c.sb("Ki%d" % i, [128, CG, 128], F32) for i in range(2)]
        Yr = c.sb("Yr", [128, CG, 128], BF16); Yi = c.sb("Yi", [128, CG, 128], BF16)
        Dr = c.sb("Dr", [128, CG, 128], BF16); Di = c.sb("Di", [128, CG, 128], BF16)
        Yo = [c.sb("Yo%d" % i, [64, CG, 128], F32) for i in range(2)]
        twc = tw[:, 0, :].unsqueeze(1).broadcast_to([128, CG, 128])
        tws = tw[:, 1, :].unsqueeze(1).broadcast_to([128, CG, 128])

        def cmul_st(m, mk_, ar, ai, ak, br, bi, bk, outr, outi, okr, oki, conj):
            m1, m2, m3, m4 = m
            k1, k2, k3, k4 = [mk_ + str(i) for i in range(4)]
            st = [
                lambda: c.op('dve', ak + bk, [k1], lambda e: e.tensor_tensor(out=m1[:], in0=ar, in1=br, op=ALU.mult)),
                lambda: c.op('dve', ak + bk, [k2], lambda e: e.tensor_tensor(out=m2[:], in0=ai, in1=bi, op=ALU.mult)),
                lambda: c.op('dve', ak + bk, [k3], lambda e: e.tensor_tensor(out=m3[:], in0=ar, in1=bi, op=ALU.mult)),
                lambda: c.op('dve', ak + bk, [k4], lambda e: e.tensor_tensor(out=m4[:], in0=ai, in1=br, op=ALU.mult)),
            ]
            if not conj:
                st += [lambda: c.op('pool', [k1, k2], okr, lambda e: e.tensor_tensor(out=outr, in0=m1[:], in1=m2[:], op=ALU.subtract)),
                       lambda: c.op('pool', [k3, k4], oki, lambda e: e.tensor_tensor(out=outi, in0=m3[:], in1=m4[:], op=ALU.add))]
            else:
                st += [lambda: c.op('pool', [k1, k2], okr, lambda e: e.tensor_tensor(out=outr, in0=m1[:], in1=m2[:], op=ALU.add)),
                       lambda: c.op('pool', [k3, k4], oki, lambda e: e.tensor_tensor(out=outi, in0=m4[:], in1=m3[:], op=ALU.subtract))]
            return st

        def fwd_st(src, ksrc, K, gi, PA, kpa, PXr, kxr, PXi, kxi, m, mk_, Br, kbr, Bi, kbi):
            def stageA():
                for cc in range(CG):
                    ch = gi * CG + cc
                    c.op('pe', [ksrc, 'FA'], [kpa],
                         lambda e, cc=cc, ch=ch: e.matmul(PA[:, cc, :], lhsT=src[0:K, ch, :], rhs=FA[0:K, :], start=True, stop=True))
            Bf = Br[:].rearrange("p c k -> p (c k)")
            Bg = Bi[:].rearrange("p c k -> p (c k)")
            xr = PXr[:].rearrange("p c k -> p (c k)")
            xi = PXi[:].rearrange("p c k -> p (c k)")

            def stageB():
                c.op('pe', ['Fm', kbr], [kxr], lambda e: e.matmul(xr, lhsT=Fm[:, 0, :], rhs=Bf, start=True, stop=False))
                c.op('pe', ['Fm', kbi], [kxr], lambda e: e.matmul(xr, lhsT=Fm[:, 1, :], rhs=Bg, start=False, stop=True))
                c.op('pe', ['Fm', kbi], [kxi], lambda e: e.matmul(xi, lhsT=Fm[:, 0, :], rhs=Bg, start=True, stop=False))
                c.op('pe', ['Fm', kbr], [kxi], lambda e: e.matmul(xi, lhsT=Fm[:, 2, :], rhs=Bf, start=False, stop=True))
            return ([stageA] + cmul_st(m, mk_, PA[:, :, 0:128], PA[:, :, 128:256], [kpa], twc, tws, ['tw'], Br[:], Bi[:], [kbr], [kbi], True)
                    + [stageB])

        def filt_fft(gi):
            s = gi % 2
            st = fwd_st(Kc, 'Kc', 128, gi, PA_f, 'PA_f', PXr_f, 'PXr_f', PXi_f, 'PXi_f', mf, 'mf', Br_f, 'Br_f', Bi_f, 'Bi_f')
            st += [lambda: c.op('act', ['PXr_f'], ['Kr%d' % s], lambda e: e.copy(out=Kr[s][:], in_=PXr_f[:])),
                   lambda: c.op('act', ['PXi_f'], ['Ki%d' % s], lambda e: e.copy(out=Ki[s][:], in_=PXi_f[:]))]
            return st

        yv = yscr.rearrange("c (a b) -> a c b", b=128)

        def data_fft(gi):
            s = gi % 2
            st = fwd_st(Xc, 'Xc', 64, gi, PA_d, 'PA_d', PXr_d, 'PXr_d', PXi_d, 'PXi_d', md, 'md', Br_d, 'Br_d', Bi_d, 'Bi_d')
            st += cmul_st(md, 'md', PXr_d[:], PXi_d[:], ['PXr_d', 'PXi_d'], Kr[s][:], Ki[s][:], ['Kr%d' % s, 'Ki%d' % s], Yr[:], Yi[:], ['Yr'], ['Yi'], False)
            PC = PA_d

            def stageC():
                for cc in range(CG):
                    c.op('pe', ['Yr', 'Gm'], ['PA_d'],
                         lambda e, cc=cc: e.matmul(PC[:, cc, :], lhsT=Yr[:, cc, :], rhs=Gm[:, 0, :], start=True, stop=False))
                    c.op('pe', ['Yi', 'Gm'], ['PA_d'],
                         lambda e, cc=cc: e.matmul(PC[:, cc, :], lhsT=Yi[:, cc, :], rhs=Gm[:, 1, :], start=False, stop=True))
            st += [stageC]
            st += cmul_st(md, 'md', PC[:, :, 0:128], PC[:, :, 128:256], ['PA_d'], twc, tws, ['tw'], Dr[:], Di[:], ['Dr'], ['Di'], False)
            pd = PXr_d[0:64].rearrange("p c k -> p (c k)")
            yo = Yo[s]
            ky = 'Yo%d' % s

            def stageD():
                c.op('pe', ['Fm', 'Dr'], ['PXr_d'],
                     lambda e: e.matmul(pd, lhsT=Fm[:, 0, 0:64], rhs=Dr[:].rearrange("p c k -> p (c k)"), start=True, stop=False))
                c.op('pe', ['Fm', 'Di'], ['PXr_d'],
                     lambda e: e.matmul(pd, lhsT=Fm[:, 2, 0:64], rhs=Di[:].rearrange("p c k -> p (c k)"), start=False, stop=True))
            st += [stageD,
                   lambda: c.op('act', ['PXr_d'], [ky], lambda e: e.activation(out=yo[:], in_=PXr_d[0:64], func=AF.Copy, scale=1.0 / 16384.0)),
                   lambda: c.dma('sp', [ky], ['yscr%d' % gi], yv[:, gi * CG:(gi + 1) * CG, :], yo[:])]
            return st

        lockstep(filt_fft(0))
        for gi in range(NG):
            if gi + 1 < NG:
                lockstep(data_fft(gi), filt_fft(gi + 1))
            else:
                lockstep(data_fft(gi))
    c.barrier()
    c.es = es
    with ExitStack() as es4:
        c.es = es4
        dd = c.sb("dd", [128, 1], F32)
        c.dma('sp', [], ['dd'], dd[:], hyd)
        yt = [c.sb("yt%d" % i, [128, 2048], F32) for i in range(2)]
        zd = c.sb("zd", [128, 2048], F32)
        ob = [c.sb("ob%d" % i, [128, 2048], BF16) for i in range(2)]
        allscr = ['yscr%d' % gi for gi in range(NG)]
        for q in range(4):
            s = q % 2
            sl = slice(q * 2048, (q + 1) * 2048)
            c.dma('sp', allscr, ['yt%d' % s], yt[s][:], yscr[:, sl])
            if dbg is not None:
                c.dma('sp', ['yt%d' % s], ['dbgy%d' % q], dbg['yc'][:, sl], yt[s][:])
            c.op('dve', ['zT', 'dd'], ['zd'], lambda e: e.tensor_scalar(out=zd[:], in0=zT[:, sl], scalar1=dd[:, 0:1], scalar2=None, op0=ALU.mult))
            c.op('dve', ['yt%d' % s, 'rl1', 'zd'], ['yt%d' % s],
                 lambda e: e.scalar_tensor_tensor(out=yt[s][:], in0=yt[s][:], scalar=rl1[:, 0:1], in1=zd[:], op0=ALU.mult, op1=ALU.add))
            c.op('dve', ['yt%d' % s, 'g0'], ['ob%d' % s], lambda e: e.tensor_tensor(out=ob[s][:], in0=yt[s][:], in1=g0[:, sl], op=ALU.mult))
            c.dma('sp', ['ob%d' % s], ['yb_out'], yb_out[:, sl], ob[s][:])
    c.barrier()
    c.es = es

import numpy as np

NT = 16
TS = 2048


def load_weights_plain(c, wd, wb, ncols, tag):
    stg = [c.sb("wstg%s%d" % (tag, i), [128, ncols], F32) for i in range(2)]
    for ch in range(8):
        s = stg[ch % 2]
        k = "wstg%s%d" % (tag, ch % 2)
        c.dma('sp', [], [k], s[:], wd[ch * 128:(ch + 1) * 128, :])
        c.op('dve', [k], ['wb' + tag], lambda e, s=s, ch=ch: e.tensor_copy(out=wb[:, ch, :], in_=s[:]))


def outproj_tile(c, i, yT, ykey, wo, po, xin_d, xt, kx):
    for half in range(2):
        for ech in range(8):
            c.op('pe', [ykey, 'wbO'], ['po'],
                 lambda e, half=half, ech=ech: e.matmul(po[:, half * 512:(half + 1) * 512], lhsT=yT[:, ech, i * 128:(i + 1) * 128],
                                                        rhs=wo[:, ech, half * 512:(half + 1) * 512], start=(ech == 0), stop=(ech == 7)))
    c.dma('sp', [], [kx], xt[:], xin_d[i * 128:(i + 1) * 128, :])
    c.op('dve', ['po', kx], [kx], lambda e: e.tensor_tensor(out=xt[:], in0=po[:], in1=xt[:], op=ALU.add))


def rstd_tile(c, xt, kx, junk, ssq, sd, rs, n):
    c.op('act', [kx], ['junk', 'ssq'], lambda e: e.activation(out=junk[:], in_=xt[:], func=AF.Square, accum_out=ssq[:, 0:1]))
    c.op('act', ['ssq'], ['sd'], lambda e: e.activation(out=sd[:], in_=ssq[:], func=AF.Sqrt, scale=1.0 / n, bias=EPS))
    c.op('dve', ['sd'], ['rs'], lambda e: e.reciprocal(out=rs[:], in_=sd[:]))


def emit_L2(c, es, y0T_d, xs_d, wout_d, win_d, onorm_d, cos_d, sin_d, identd, x1_d, latT_d, gcT_d, do_gate=True):
    with ExitStack() as es1:
        c.es = es1
        ident = c.sb("ident2", [128, 128], BF16)
        c.dma('sp', [], ['ident2'], ident[:], identd)
        yT = c.sb("y0T", [128, 8, TS], BF16)
        c.dma('sp', [], ['y0T'], yT[:], y0T_d.rearrange("(ch p) t -> p ch t", p=128))
        wo = c.sb("wbO", [128, 8, 1024], BF16)
        load_weights_plain(c, wout_d, wo, 1024, 'O')
        gain = c.sb("gainI", [128, 8], F32)
        c.dma('sp', [], ['gainI'], gain[:], onorm_d)
        wi = c.sb("wbI", [128, 8, 1696], BF16)
        load_weights_scaled(c, win_d, gain, 'gainI', wi, 1696, 'I')
        cosS = c.sb("cos2", [128, NT, 16], F32)
        sinS = c.sb("sin2", [128, NT, 16], F32)
        c.dma('sp', [], ['cos2'], cosS[:], cos_d.rearrange("p (t j) -> p t j", j=16))
        c.dma('sp', [], ['sin2'], sinS[:], sin_d.rearrange("p (t j) -> p t j", j=16))
        h1T = c.sb("h1T", [128, 8, TS], BF16)
        po = c.ps("po", [128, 1024])
        ptr = c.ps("ptr2", [128, 8, 128], BF16)
        pl = c.ps("pl", [128, 1024])
        pg = c.ps("pg", [128, 512])
        plt = c.ps("plt", [128, 6, 128], BF16)
        ltT = [c.sb("ltT%d" % i, [128, 6, 128], BF16) for i in range(2)]
        xt = [c.sb("xt%d" % i, [128, 1024], F32) for i in range(2)]
        junk = c.sb("junk", [128, 1024], F32)
        ssq = c.sb("ssq", [128, 1], F32)
        sd = c.sb("sd", [128, 1], F32)
        rs = c.sb("rs", [128, 1], F32)
        h1 = c.sb("h1", [128, 1024], BF16)
        ss2 = c.sb("ss2", [128, 2], F32)
        sd2 = c.sb("sd2", [128, 2], F32)
        rs2 = c.sb("rs2", [128, 2], F32)
        latn = [c.sb("latn%d" % i, [128, 672], BF16) for i in range(2)]
        k1 = c.sb("k1", [128, 32], F32)
        k2 = c.sb("k2", [128, 32], F32)
        gt = [c.sb("gt%d" % i, [128, 512], BF16) for i in range(2)]
        for i in range(NT):
            s = i % 2
            kx = 'xt%d' % s
            outproj_tile(c, i, yT, 'y0T', wo, po, xs_d, xt[s], kx)
            c.dma('sp', [kx], ['x1_d'], x1_d[i * 128:(i + 1) * 128, :], xt[s][:])
            rstd_tile(c, xt[s], kx, junk, ssq, sd, rs, D)
            c.op('act', [kx, 'rs'], ['h1'], lambda e: e.activation(out=h1[:], in_=xt[s][:], func=AF.Identity, scale=rs[:, 0:1]))
            for ch in range(8):
                c.op('pe', ['h1', 'ident2'], ['ptr2'],
                     lambda e, ch=ch: e.transpose(out=ptr[:, ch, :], in_=h1[:, ch * 128:(ch + 1) * 128], identity=ident[:]))
            c.op('dve', ['ptr2'], ['h1T'], lambda e: e.tensor_copy(out=h1T[:, :, i * 128:(i + 1) * 128], in_=ptr[:]))
            for (lo, hi) in ((0, 512), (512, 672)):
                for ch in range(8):
                    c.op('pe', ['h1T', 'wbI'], ['pl'],
                         lambda e, ch=ch, lo=lo, hi=hi: e.matmul(pl[:, lo:hi], lhsT=h1T[:, ch, i * 128:(i + 1) * 128], rhs=wi[:, ch, lo:hi],
                                                                 start=(ch == 0), stop=(ch == 7)))
            c.op('act', ['pl'], ['junk', 'ss2'], lambda e: e.activation(out=junk[:, 0:384], in_=pl[:, 0:384], func=AF.Square, accum_out=ss2[:, 0:1]))
            c.op('act', ['pl'], ['junk', 'ss2'], lambda e: e.activation(out=junk[:, 384:640], in_=pl[:, 384:640], func=AF.Square, accum_out=ss2[:, 1:2]))
            c.op('act', ['ss2'], ['sd2'], lambda e: e.activation(out=sd2[:, 0:1], in_=ss2[:, 0:1], func=AF.Sqrt, scale=1.0 / 384, bias=EPS))
            c.op('act', ['ss2'], ['sd2'], lambda e: e.activation(out=sd2[:, 1:2], in_=ss2[:, 1:2], func=AF.Sqrt, scale=1.0 / 256, bias=EPS))
            c.op('dve', ['sd2'], ['rs2'], lambda e: e.reciprocal(out=rs2[:], in_=sd2[:]))
            ln = latn[s]
            kl = 'latn%d' % s
            c.op('act', ['pl', 'rs2'], [kl], lambda e: e.activation(out=ln[:, 0:384], in_=pl[:, 0:384], func=AF.Identity, scale=rs2[:, 0:1]))
            c.op('act', ['pl', 'rs2'], [kl], lambda e: e.activation(out=ln[:, 384:640], in_=pl[:, 384:640], func=AF.Identity, scale=rs2[:, 1:2]))
            krv = pl[:, 640:672].rearrange("p (a j) -> p a j", a=2)
            cb = cosS[:, i, :].unsqueeze(1).broadcast_to([128, 2, 16])
            sb_ = sinS[:, i, :].unsqueeze(1).broadcast_to([128, 2, 16])
            c.op('dve', ['pl', 'cos2'], ['k1'], lambda e: e.tensor_tensor(out=k1[:].rearrange("p (a j) -> p a j", a=2), in0=krv, in1=cb, op=ALU.mult))
            c.op('dve', ['pl', 'sin2'], ['k2'], lambda e: e.tensor_tensor(out=k2[:].rearrange("p (a j) -> p a j", a=2), in0=krv, in1=sb_, op=ALU.mult))
            c.op('dve', ['k1', 'k2'], [kl], lambda e: e.tensor_tensor(out=ln[:, 640:656], in0=k1[:, 0:16], in1=k2[:, 16:32], op=ALU.subtract))
            c.op('dve', ['k1', 'k2'], [kl], lambda e: e.tensor_tensor(out=ln[:, 656:672], in0=k2[:, 0:16], in1=k1[:, 16:32], op=ALU.add))
            for ch in range(6):
                wdt = 128 if ch < 5 else 32
                c.op('pe', [kl, 'ident2'], ['plt'],
                     lambda e, ch=ch, wdt=wdt: e.transpose(out=plt[0:wdt, ch, :], in_=ln[:, ch * 128:ch * 128 + wdt], identity=ident[:]))
            lt = ltT[s]
            klt = 'ltT%d' % s
            c.op('dve', ['plt'], [klt], lambda e: e.tensor_copy(out=lt[:, 0:5, :], in_=plt[:, 0:5, :]))
            c.op('dve', ['plt'], [klt], lambda e: e.tensor_copy(out=lt[0:32, 5, :], in_=plt[0:32, 5, :]))
            c.dma('sp', [klt], ['latT_d'], latT_d[0:640, i * 128:(i + 1) * 128].rearrange("(ch p) t -> p ch t", p=128), lt[:, 0:5, :])
            c.dma('sp', [klt], ['latT_d'], latT_d[640:672, i * 128:(i + 1) * 128], lt[0:32, 5, :])
        n = 0
        for ec in (range(8) if do_gate else []):
            for blk in range(4):
                for ch in range(8):
                    c.op('pe', ['h1T', 'wbI'], ['pg'],
                         lambda e, ch=ch: e.matmul(pg[:], lhsT=wi[:, ch, 672 + ec * 128:672 + (ec + 1) * 128], rhs=h1T[:, ch, blk * 512:(blk + 1) * 512],
                                                   start=(ch == 0), stop=(ch == 7)))
                g = gt[n % 2]
                kg = 'gt%d' % (n % 2)
                c.op('act', ['pg'], [kg], lambda e: e.activation(out=g[:], in_=pg[:], func=AF.Silu))
                c.dma('sp', [kg], ['gcT_d'], gcT_d[ec * 128:(ec + 1) * 128, blk * 512:(blk + 1) * 512], g[:])
                n += 1
    c.barrier()
    c.es = es


def emit_L4(c, es, ocT_d, gcT_d, x1_d, wout_d, fnorm_d, out_d):
    with ExitStack() as es1:
        c.es = es1
        yT = c.sb("ycT", [128, 8, TS], BF16)
        gT = c.sb("gcT", [128, 8, TS], BF16)
        c.dma('sp', [], ['ycT'], yT[:], ocT_d.rearrange("(ch p) t -> p ch t", p=128))
        c.dma('sp', [], ['gcT'], gT[:], gcT_d.rearrange("(ch p) t -> p ch t", p=128))
        for ch in range(8):
            eng = 'dve' if ch % 2 == 0 else 'pool'
            c.op(eng, ['ycT', 'gcT'], ['ycT'], lambda e, ch=ch: e.tensor_tensor(out=yT[:, ch, :], in0=yT[:, ch, :], in1=gT[:, ch, :], op=ALU.mult))
        wo = c.sb("wbO", [128, 8, 1024], BF16)
        load_weights_plain(c, wout_d, wo, 1024, 'O')
        fn = c.sb("fn", [128, 1024], F32)
        c.dma('sp', [], ['fn'], fn[:], fnorm_d.partition_broadcast(128))
        po = c.ps("po", [128, 1024])
        xt = [c.sb("xt%d" % i, [128, 1024], F32) for i in range(2)]
        ot = [c.sb("ot%d" % i, [128, 1024], F32) for i in range(2)]
        junk = c.sb("junk", [128, 1024], F32)
        ssq = c.sb("ssq", [128, 1], F32)
        sd = c.sb("sd", [128, 1], F32)
        rs = c.sb("rs", [128, 1], F32)
        for i in range(NT):
            s = i % 2
            kx = 'xt%d' % s
            outproj_tile(c, i, yT, 'ycT', wo, po, x1_d, xt[s], kx)
            rstd_tile(c, xt[s], kx, junk, ssq, sd, rs, D)
            c.op('dve', [kx, 'rs', 'fn'], ['ot%d' % s],
                 lambda e: e.scalar_tensor_tensor(out=ot[s][:], in0=xt[s][:], scalar=rs[:, 0:1], in1=fn[:], op0=ALU.mult, op1=ALU.mult))
            c.dma('sp', ['ot%d' % s], ['out_d'], out_d[i * 128:(i + 1) * 128, :], ot[s][:])
    c.barrier()
    c.es = es

import numpy as np

L = 8192
NQB = 16
SCALE3 = 96.0 ** -0.5


def emit_attn_L1(c, es, latT_d, wq_d, wk_d, wv_d, qgain_d, kvgain_d, cosT_d, sinT_d, swapd, oc_d, nqb=NQB, dbg=None, cq_d=None):
    if cq_d is None:
        cq_d = latT_d[0:384, :]
    ckT = c.sb("ckT", [128, 2, L], BF16)
    c.dma('sp', [], ['ckT'], ckT[:], latT_d[384:640, :].rearrange("(ch p) t -> p ch t", p=128))
    qg = c.sb("qg", [128, 3], F32)
    kg = c.sb("kg", [128, 2], F32)
    c.dma('sp', [], ['qg'], qg[:], qgain_d)
    c.dma('sp', [], ['kg'], kg[:], kvgain_d)
    wq = c.sb("wq", [128, 3, 4, 96], BF16)
    wqr = c.sb("wqr", [128, 3, 4, 96], BF16)
    c.op('pool', [], ['wqr'], lambda e: e.memset(wqr[:], 0.0))
    wk = c.sb("wk", [128, 2, 4, 64], BF16)
    wv = c.sb("wv", [128, 2, 4, 64], BF16)
    swp = c.sb("swp3", [128, 128], F32)
    c.dma('sp', [], ['swp3'], swp[:], swapd)
    with ExitStack() as es0:
        c.es = es0
        st = c.sb("wst3", [128, 4 * 96], F32)
        for ch in range(3):
            c.dma('sp', [], ['wst3'], st[:], wq_d[ch * 128:(ch + 1) * 128].rearrange("p h j -> p (h j)"))
            sv = st[:].rearrange("p (h j) -> p h j", h=4)
            c.op('dve', ['wst3', 'qg'], ['wq'], lambda e, ch=ch: e.tensor_scalar(out=wq[:, ch], in0=sv, scalar1=qg[:, ch:ch + 1], scalar2=None, op0=ALU.mult))
            c.op('dve', ['wst3', 'qg'], ['wqr'],
                 lambda e, ch=ch: e.tensor_scalar(out=wqr[:, ch, :, 64:80], in0=sv[:, :, 80:96], scalar1=qg[:, ch:ch + 1], scalar2=-1.0, op0=ALU.mult, op1=ALU.mult))
            c.op('dve', ['wst3', 'qg'], ['wqr'],
                 lambda e, ch=ch: e.tensor_scalar(out=wqr[:, ch, :, 80:96], in0=sv[:, :, 64:80], scalar1=qg[:, ch:ch + 1], scalar2=None, op0=ALU.mult))
        for ch in range(2):
            c.dma('sp', [], ['wst3'], st[:, 0:256], wk_d[ch * 128:(ch + 1) * 128].rearrange("p h j -> p (h j)"))
            sv = st[:, 0:256].rearrange("p (h j) -> p h j", h=4)
            c.op('dve', ['wst3', 'kg'], ['wk'], lambda e, ch=ch: e.tensor_scalar(out=wk[:, ch], in0=sv, scalar1=kg[:, ch:ch + 1], scalar2=None, op0=ALU.mult))
        for ch in range(2):
            c.dma('sp', [], ['wst3'], st[:, 0:256], wv_d[ch * 128:(ch + 1) * 128].rearrange("p h j -> p (h j)"))
            sv = st[:, 0:256].rearrange("p (h j) -> p h j", h=4)
            c.op('dve', ['wst3', 'kg'], ['wv'], lambda e, ch=ch: e.tensor_scalar(out=wv[:, ch], in0=sv, scalar1=kg[:, ch:ch + 1], scalar2=None, op0=ALU.mult))
    c.barrier()
    c.es = es
    KT = [c.sb("KT3%d" % h, [96, L], BF16) for h in range(2)]
    VA = c.sb("VA3", [128, 64, 128], BF16)
    VB = c.sb("VB3", [128, 64, 128], BF16)
    c.op('pool', [], ['VA3'], lambda e: e.memset(VA[:, :, 64:128], 1.0))
    c.op('pool', [], ['VB3'], lambda e: e.memset(VB[:, :, 0:64], 1.0))
    for h in range(2):
        c.dma('sp', [], ['KT3%d' % h], KT[h][64:96, :], latT_d[640:672, :])
    for pair in range(2):
        with ExitStack() as esA:
            c.es = esA
            pk = c.ps("pk", [128, 512])
            pv = c.ps("pv", [128, 4, 128])
            for h in range(2):
                hh = pair * 2 + h
                for blk in range(16):
                    for ch in range(2):
                        c.op('pe', ['wk', 'ckT'], ['pk'],
                             lambda e, ch=ch: e.matmul(pk[0:64, :], lhsT=wk[:, ch, hh, :], rhs=ckT[:, ch, blk * 512:(blk + 1) * 512],
                                                       start=(ch == 0), stop=(ch == 1)))
                    eng = 'act' if blk % 2 == 0 else 'dve'
                    if eng == 'act':
                        c.op('act', ['pk'], ['KT3%d' % h], lambda e: e.copy(out=KT[h][0:64, blk * 512:(blk + 1) * 512], in_=pk[0:64, :]))
                    else:
                        c.op('dve', ['pk'], ['KT3%d' % h], lambda e: e.tensor_copy(out=KT[h][0:64, blk * 512:(blk + 1) * 512], in_=pk[0:64, :]))
            for g4 in range(16):
                for tt in range(4):
                    kt = g4 * 4 + tt
                    for ch in range(2):
                        c.op('pe', ['wv', 'ckT'], ['pv'],
                             lambda e, ch=ch, tt=tt, kt=kt: e.matmul(pv[:, tt, :], lhsT=ckT[:, ch, kt * 128:(kt + 1) * 128],
                                                                     rhs=wv[:, ch, pair * 2:pair * 2 + 2, :].rearrange("p h j -> p (h j)"),
                                                                     start=(ch == 0), stop=(ch == 1)))
                c.op('act', ['pv'], ['VA3'], lambda e: e.copy(out=VA[:, g4 * 4:(g4 + 1) * 4, 0:64], in_=pv[:, :, 0:64]))
                c.op('dve', ['pv'], ['VB3'], lambda e: e.tensor_copy(out=VB[:, g4 * 4:(g4 + 1) * 4, 64:128], in_=pv[:, :, 64:128]))
        c.barrier()
        c.es = es
        if dbg is not None and pair == 0:
            c.dma('sp', ['KT30'], ['dbgK'], dbg['KT'], KT[0][:])
            c.dma('sp', ['VB3'], ['dbgV'], dbg['VB'], VB[:].rearrange("p t j -> p (t j)"))
        with ExitStack() as esB:
            c.es = esB
            PS = [c.ps("PS%d" % i, [128, 2, 512]) for i in range(3)]
            PO = c.ps("PO", [128, 2, 512])
            PQ = PS[0][:, 0, :]
            PQR = PS[1][:, 0, :]
            PS2 = PS[2][:, 0, :]
            PT = [c.sb("PT%d" % i, [128, 2, 512], BF16) for i in range(3)]
            S_sb = c.sb("S_sb", [128, 512], F32)
            O_sb = c.sb("O_sb", [128, 512], F32)
            rinv = c.sb("rinv", [128, 512], F32)
            yo = [c.sb("yo%d" % i, [128, 512], BF16) for i in range(2)]
            cq = [c.sb("cq%d" % i, [128, 3, 512], BF16) for i in range(2)]
            cs = [c.sb("cs%d" % i, [128, 2, 512], F32) for i in range(2)]
            QT = [c.sb("QT3%d" % h, [96, nqb * 512], BF16) for h in range(2)]
            m1 = c.sb("m1", [128, 512], F32)
            m2 = c.sb("m2", [128, 512], F32)

            def loadq(qb):
                s = qb % 2
                c.dma('sp', [], ['cq%d' % s], cq[s][:], cq_d[:, qb * 512:(qb + 1) * 512].rearrange("(ch p) t -> p ch t", p=128))
                c.dma('sp', [], ['cs%d' % s], cs[s][64:96, 0, :], cosT_d[:, qb * 512:(qb + 1) * 512])
                c.dma('sp', [], ['cs%d' % s], cs[s][64:96, 1, :], sinT_d[:, qb * 512:(qb + 1) * 512])

            def projq(qb):
                s = qb % 2
                for h in range(2):
                    hh = pair * 2 + h
                    q = QT[h][:, qb * 512:(qb + 1) * 512]
                    kq = 'QT3%d' % h
                    for ch in range(3):
                        c.op('pe', ['wq', 'cq%d' % s], ['PS0'],
                             lambda e, ch=ch: e.matmul(PQ[0:96, :], lhsT=wq[:, ch, hh, :], rhs=cq[s][:, ch, :], start=(ch == 0), stop=(ch == 2)))
                    for ch in range(3):
                        c.op('pe', ['wqr', 'cq%d' % s], ['PS1'],
                             lambda e, ch=ch: e.matmul(PQR[0:96, :], lhsT=wqr[:, ch, hh, :], rhs=cq[s][:, ch, :], start=(ch == 0), stop=(ch == 2)))
                    c.op('dve', ['PS0'], [kq], lambda e: e.tensor_copy(out=q[0:64, :], in_=PQ[0:64, :]))
                    c.op('dve', ['PS0', 'cs%d' % s], ['m1'], lambda e: e.tensor_tensor(out=m1[64:96, :], in0=PQ[64:96, :], in1=cs[s][64:96, 0, :], op=ALU.mult))
                    c.op('dve', ['PS1', 'cs%d' % s], ['m2'], lambda e: e.tensor_tensor(out=m2[64:96, :], in0=PQR[64:96, :], in1=cs[s][64:96, 1, :], op=ALU.mult))
                    c.op('pool', ['m1', 'm2'], [kq], lambda e: e.tensor_tensor(out=q[64:96, :], in0=m1[64:96, :], in1=m2[64:96, :], op=ALU.add))

            def S(qb, kt):
                s = kt % 3
                for h in range(2):
                    c.op('pe', ['KT3%d' % h, 'QT3%d' % h], ['PS%d' % s],
                         lambda e, h=h: e.matmul(PS[s][:, h, :], lhsT=KT[h][:, kt * 128:(kt + 1) * 128], rhs=QT[h][:, qb * 512:(qb + 1) * 512],
                                                 start=True, stop=True))

            loadq(0)
            for qb in range(nqb):
                if qb + 1 < nqb:
                    loadq(qb + 1)
                projq(qb)
            S(0, 0)
            S(0, 1)
            for qb in range(nqb):
                for kt in range(64):
                    s = kt % 3
                    if kt + 2 < 64:
                        S(qb, kt + 2)
                    c.op('act', ['PS%d' % s], ['PT%d' % s], lambda e: e.activation(out=PT[s][:], in_=PS[s][:], func=AF.Exp, scale=SCALE3))
                    c.op('pe', ['VA3', 'PT%d' % s], ['PO'],
                         lambda e: e.matmul(PO[:, 0, :], lhsT=VA[:, kt, :], rhs=PT[s][:, 0, :], start=(kt == 0), stop=(kt == 63)))
                    c.op('pe', ['VB3', 'PT%d' % s], ['PO'],
                         lambda e: e.matmul(PO[:, 1, :], lhsT=VB[:, kt, :], rhs=PT[s][:, 1, :], start=(kt == 0), stop=(kt == 63)))
                if qb + 1 < nqb:
                    S(qb + 1, 0)
                    S(qb + 1, 1)
                c.op('dve', ['PO'], ['S_sb'], lambda e: e.tensor_copy(out=S_sb[64:128, :], in_=PO[64:128, 0, :]))
                c.op('dve', ['PO'], ['S_sb'], lambda e: e.tensor_copy(out=S_sb[0:64, :], in_=PO[0:64, 1, :]))
                c.op('dve', ['PO'], ['O_sb'], lambda e: e.tensor_copy(out=O_sb[0:64, :], in_=PO[0:64, 0, :]))
                c.op('dve', ['PO'], ['O_sb'], lambda e: e.tensor_copy(out=O_sb[64:128, :], in_=PO[64:128, 1, :]))
                c.op('pe', ['swp3', 'S_sb'], ['PS2'], lambda e: e.matmul(PS2, lhsT=swp[:], rhs=S_sb[:], start=True, stop=True))
                c.op('dve', ['PS2'], ['rinv'], lambda e: e.reciprocal(out=rinv[:], in_=PS2))
                y = yo[qb % 2]
                ky = 'yo%d' % (qb % 2)
                c.op('dve', ['O_sb', 'rinv'], [ky], lambda e: e.tensor_tensor(out=y[:], in0=O_sb[:], in1=rinv[:], op=ALU.mult))
                c.dma('sp', [ky], ['oc_d'], oc_d[pair, :, qb * 512:(qb + 1) * 512], y[:])
        c.barrier()
        c.es = es

import numpy as np


def emit_select(c, es, sel_d, x1s, gcT, latT, x1_own, gc_own, cq_own):
    with ExitStack() as es1:
        c.es = es1
        sel = c.sb("sel", [128, 4], F32)
        c.dma('sp', [], ['sel'], sel[:], sel_d)
        xa = [c.sb("xa%d" % i, [128, 4, 1024], F32) for i in range(2)]
        xo = [c.sb("xo%d" % i, [128, 1024], F32) for i in range(2)]
        xv = x1s.rearrange("(ts i p) d -> i p ts d", ts=4, p=128)
        for i in range(16):
            s = i % 2
            ka, ko = 'xa%d' % s, 'xo%d' % s
            c.dma('sp', [], [ka], xa[s][:], xv[i])
            c.op('dve', [ka, 'sel'], [ko], lambda e: e.tensor_scalar(out=xo[s][:], in0=xa[s][:, 0, :], scalar1=sel[:, 0:1], scalar2=None, op0=ALU.mult))
            for ts in range(1, 4):
                c.op('dve', [ka, 'sel', ko], [ko],
                     lambda e, ts=ts: e.scalar_tensor_tensor(out=xo[s][:], in0=xa[s][:, ts, :], scalar=sel[:, ts:ts + 1], in1=xo[s][:], op0=ALU.mult, op1=ALU.add))
            c.dma('sp', [ko], ['x1_own'], x1_own[i * 128:(i + 1) * 128, :], xo[s][:])
        ga = [c.sb("ga%d" % i, [128, 4, 2048], BF16) for i in range(2)]
        go = [c.sb("go%d" % i, [128, 2048], BF16) for i in range(2)]
        jobs = [(gcT[ch * 128:(ch + 1) * 128, :], gc_own[ch * 128:(ch + 1) * 128, :]) for ch in range(8)]
        jobs += [(latT[ch * 128:(ch + 1) * 128, :], cq_own[ch * 128:(ch + 1) * 128, :]) for ch in range(3)]
        for n, (src, dst) in enumerate(jobs):
            s = n % 2
            ka, ko = 'ga%d' % s, 'go%d' % s
            c.dma('sp', [], [ka], ga[s][:], src.rearrange("p (ts t) -> p ts t", ts=4))
            c.op('dve', [ka, 'sel'], [ko], lambda e: e.tensor_scalar(out=go[s][:], in0=ga[s][:, 0, :], scalar1=sel[:, 0:1], scalar2=None, op0=ALU.mult))
            for ts in range(1, 4):
                c.op('dve', [ka, 'sel', ko], [ko],
                     lambda e, ts=ts: e.scalar_tensor_tensor(out=go[s][:], in0=ga[s][:, ts, :], scalar=sel[:, ts:ts + 1], in1=go[s][:], op0=ALU.mult, op1=ALU.add))
            c.dma('sp', [ko], ['own%d' % n], dst, go[s][:])
    c.barrier()
    c.es = es


_CACHE = {}


def _dram(nc, kind, n, shp, dt=F32):
    return nc.dram_tensor(n, list(shp), dt, kind=kind).ap()


def build_fused():
    nc = bass.Bass("TRN2", target_bir_lowering=False)
    I = lambda n, s, dt=F32: _dram(nc, "ExternalInput", n, s, dt)
    O = lambda n, s, dt=F32: _dram(nc, "ExternalOutput", n, s, dt)
    S = lambda n, s, dt=F32: _dram(nc, "Internal", n, s, dt)
    xT = I("xT", [1024, 8192]); xtm = I("xtm", [8192, 1024]); enorm = I("enorm", [128, 8])
    w_tm = I("w_tm", [4, 1024, 320]); w_ga = I("w_ga", [4, 1024, 128]); qkgain = I("qkgain", [1, 256])
    cos_tm = I("cos_tm", [128, 2048]); sin_tm = I("sin_tm", [128, 2048]); swapd = I("swapd", [128, 128]); identd = I("identd", [128, 128], BF16)
    w_hy = I("w_hy", [4, 1024, 512]); convw = I("convw", [4, 128, 12]); fparams = I("fparams", [64, 4])
    w1d = I("w1d", [33, 64]); w2d = I("w2d", [64, 64]); w3d = I("w3d", [4, 64, 256]); zTf = I("zTf", [33, 8192]); zTr = I("zTr", [33, 8192]); trow = I("trow", [2, 8192])
    negdel = I("negdel", [4, 128, 1]); hyd = I("hyd", [4, 128, 1]); FAd = I("FAd", [128, 256], BF16); Fd = I("Fd", [128, 3, 128], BF16)
    Gd = I("Gd", [128, 2, 256], BF16); twd = I("twd", [128, 2, 128])
    wout = I("wout", [1024, 1024]); win = I("win", [1024, 1696]); onorm = I("onorm", [128, 8]); cos2 = I("cos2", [4, 128, 256]); sin2 = I("sin2", [4, 128, 256])
    wq = I("wq", [4, 384, 4, 96]); wk = I("wk", [4, 256, 4, 64]); wv = I("wv", [4, 256, 4, 64]); qgain = I("qgain", [128, 3]); kvgain = I("kvgain", [128, 2])
    cosT = I("cosT", [32, 2048]); sinT = I("sinT", [32, 2048]); sel = I("sel", [128, 4])
    wout2 = I("wout2", [1024, 1024]); fnorm = I("fnorm", [1, 1024])
    hscr = S("hscr", [128, 8, 8192], BF16); kscr = S("kscr", [128, 16384], BF16); zscr = S("zscr", [128, 8192], BF16); yscr = S("yscr", [128, 8192])
    y0T = S("y0Ts", [1024, 8192], BF16); x1s = S("x1s", [8192, 1024]); latT = S("latTs", [672, 8192], BF16)
    gcT = S("gcTs", [1024, 8192], BF16)
    x1o = S("x1own", [2048, 1024]); gco = S("gcown", [1024, 2048], BF16); cqo = S("cqown", [384, 2048], BF16); oco = S("ocown", [1024, 2048], BF16)
    out = O("out", [2048, 1024])
    with ExitStack() as es:
        c = Ctx(nc, es)

        def scoped(fn):
            with ExitStack() as e1:
                c.es = e1
                fn(e1)
            c.barrier()
            c.es = es

        for g in range(4):
            scoped(lambda e1: emit_attn_L0(c, e1, xT, w_tm[g], w_ga[g], enorm, qkgain, cos_tm, sin_tm, swapd, identd, y0T[g * 128:(g + 1) * 128, :],
                                           hscr=hscr, sweep_mode=('write' if g == 0 else 'read')))
            scoped(lambda e1: emit_hyena_L0(c, e1, xT, w_hy[g], enorm, convw[g], fparams, w1d, w2d, w3d[g], zTf, zTr, trow, negdel[g], hyd[g],
                                            FAd, Fd, Gd, twd, kscr, zscr, yscr, y0T[512 + g * 128:512 + (g + 1) * 128, :], hscr=hscr, sweep_mode='read'))
        for ts in range(4):
            sl = slice(ts * 2048, (ts + 1) * 2048)
            scoped(lambda e1: emit_L2(c, e1, y0T[:, sl], xtm[sl, :], wout, win, onorm, cos2[ts], sin2[ts], identd, x1s[sl, :], latT[:, sl], gcT[:, sl]))
        emit_select(c, es, sel, x1s, gcT, latT, x1o, gco, cqo)
        for g in range(4):
            scoped(lambda e1: emit_attn_L1(c, e1, latT, wq[g], wk[g], wv[g], qgain, kvgain, cosT, sinT, swapd,
                                           oco[g * 256:(g + 1) * 256, :].rearrange("(a p) t -> a p t", p=128), nqb=4, cq_d=cqo))
        scoped(lambda e1: emit_L4(c, e1, oco, gco, x1o, wout2, fnorm, out))
        c.finish()
        print("fused program: inst", c.ninst, "waits", c.nwaits, dict(c.cnt))
    return nc


def _get(name, fn):
    if name not in _CACHE:
        _CACHE[name] = fn()
    return _CACHE[name]


def fused_inputs(d, b, C, H, C32, C3):
    m = {}
    a = [l1a_inputs(d, b, g, C) for g in range(4)]
    h = [l1b_inputs(d, b, g, H) for g in range(4)]
    m.update(xT=a[0]['xT'], xtm=np.ascontiguousarray(d['x'][b]), enorm=a[0]['enorm'], qkgain=a[0]['qkgain'],
             cos_tm=C['cos64'], sin_tm=C['sin64'], swapd=C['swap'], identd=C['ident'],
             w_tm=np.stack([x['w_tm'] for x in a]), w_ga=np.stack([x['w_ga'] for x in a]),
             w_hy=np.stack([x['w_hy'] for x in h]), convw=np.stack([x['convw'] for x in h]), fparams=h[0]['fparams'],
             w1d=h[0]['w1d'], w2d=h[0]['w2d'], w3d=np.stack([x['w3d'] for x in h]), zTf=H['zTf'], zTr=H['zTr'], trow=H['trow'],
             negdel=np.stack([x['negdel'] for x in h]), hyd=np.stack([x['hyd'] for x in h]), FAd=H['FA'], Fd=H['Fd'], Gd=H['Gd'], twd=H['tw'],
             wout=d['e_w_out'][0], win=d['o_w_in'][0], onorm=np.ascontiguousarray(d['o_norm'][0].reshape(8, 128).T),
             cos2=np.stack([tm16(C32[0][ts * 2048:(ts + 1) * 2048]) for ts in range(4)]),
             sin2=np.stack([tm16(C32[1][ts * 2048:(ts + 1) * 2048]) for ts in range(4)]))
    l3 = [l3_inputs(d, b, g, None, C3, C['swap']) for g in range(4)]
    m.update(wq=np.stack([x['wq'] for x in l3]), wk=np.stack([x['wk'] for x in l3]), wv=np.stack([x['wv'] for x in l3]),
             qgain=l3[0]['qgain'], kvgain=l3[0]['kvgain'], cosT=C3['cosT'], sinT=C3['sinT'],
             wout2=d['o_w_out'][0], fnorm=d['final_norm'][None, :].astype(np.float32))
    return m


def kernel(**inputs):
    d = {k: np.asarray(v) for k, v in inputs.items()}
    C = consts(); H = hy_consts(); C32 = angles(32); C3 = l3_consts()
    per_batch = [fused_inputs(d, b, C, H, C32, C3) for b in range(2)]
    ins = []
    for i in range(8):
        q = i % 4
        m = dict(per_batch[i // 4])
        onehot = np.zeros((128, 4), np.float32); onehot[:, q] = 1.0
        m.update(sel=onehot, cosT=np.ascontiguousarray(C3['cosT'][:, q * 2048:(q + 1) * 2048]), sinT=np.ascontiguousarray(C3['sinT'][:, q * 2048:(q + 1) * 2048]))
        ins.append(m)
    res = run_bass_kernel_spmd(_get('F', build_fused), ins, core_ids=list(range(8))).results
    out = np.empty((2, 8192, 1024), np.float32)
    for i in range(8):
        b, q = i // 4, i % 4
        out[b, q * 2048:(q + 1) * 2048] = np.asarray(res[i]['out'])
    return out
```

```python
import math
import ml_dtypes

import numpy as np
from contextlib import ExitStack
import concourse.bass as bass
import concourse.mybir as mybir
from concourse.bass_utils import run_bass_kernel_spmd

F32 = mybir.dt.float32
BF16 = mybir.dt.bfloat16
AF = mybir.ActivationFunctionType
ALU = mybir.AluOpType
AX = mybir.AxisListType

N_DMA_SEMS = 24


class Ctx:
    def __init__(self, nc, es):
        self.nc = nc
        self.es = es
        self.es_root = es
        self.eng = {'pe': nc.tensor, 'act': nc.scalar, 'dve': nc.vector,
                    'pool': nc.gpsimd, 'sp': nc.sync}
        self.semobj = {}
        for e in ['pe', 'act', 'dve', 'pool']:
            self.semobj[e] = es.enter_context(nc.semaphore("s_" + e))
        self.cnt = {e: 0 for e in ['pe', 'act', 'dve', 'pool']}
        self.seen = {e: {} for e in self.eng}
        self.dma_use = []
        for i in range(N_DMA_SEMS):
            k = "d%d" % i
            self.semobj[k] = es.enter_context(nc.semaphore("s_" + k))
            self.dma_use.append(0)
        self.dma_rr = 0
        self.last_w = {}
        self.readers = {}
        self.nwaits = 0
        self.ninst = 0
        self.excl = set()
        self.last_real_w = {}
        self.uid = 0

    def sb(self, name, shape, dt):
        self.uid += 1
        return self.es.enter_context(self.nc.sbuf_tensor("%s_u%d" % (name, self.uid), list(shape), dt))

    def ps(self, name, shape, dt=F32):
        self.excl.add(name)
        self.uid += 1
        return self.es.enter_context(self.nc.psum_tensor("%s_u%d" % (name, self.uid), list(shape), dt))

    def _wait(self, e, sk, val):
        if self.seen[e].get(sk, 0) < val:
            self.eng[e].wait_ge(self.semobj[sk], val)
            self.seen[e][sk] = val
            self.nwaits += 1

    def _deps(self, e, reads, writes):
        need = {}
        ex = [k for k in reads if k in self.excl]
        if ex:
            reads = [k for k in reads if k not in self.excl]

        def add(tok, kind):
            sk, val = tok
            if sk == 'pe' and e == 'pe':
                return
            if need.get(sk, 0) < val:
                need[sk] = val

        for k in ex:
            if k in self.last_real_w:
                add(self.last_real_w[k], 'raw')
            if k in self.last_w and self.last_w[k][0] != e:
                add(self.last_w[k], 'x')
        for k in reads:
            if k in self.last_w:
                add(self.last_w[k], 'raw')
        for k in writes:
            if k in self.last_w:
                add(self.last_w[k], 'waw')
            for sk, val in self.readers.get(k, {}).items():
                add((sk, val), 'war')
        for sk, val in need.items():
            self._wait(e, sk, val)

    def _done(self, tok, reads, writes):
        ex = [k for k in reads if k in self.excl]
        reads = [k for k in reads if k not in self.excl]
        for k in ex:
            self.last_w[k] = tok
            self.readers[k] = {}
        for k in writes:
            self.last_w[k] = tok
            self.readers[k] = {}
            if k in self.excl:
                self.last_real_w[k] = tok
        for k in reads:
            r = self.readers.setdefault(k, {})
            if r.get(tok[0], 0) < tok[1]:
                r[tok[0]] = tok[1]

    def op(self, e, reads, writes, build):
        self._deps(e, reads, writes)
        inst = build(self.eng[e])
        self.cnt[e] += 1
        inst.then_inc(self.semobj[e], 1)
        self._done((e, self.cnt[e]), reads, writes)
        self.ninst += 1
        return inst

    def dma(self, q, reads, writes, out, in_, **kw):
        self._deps(q, reads, writes)
        i = self.dma_rr
        self.dma_rr = (self.dma_rr + 1) % N_DMA_SEMS
        sk = "d%d" % i
        self._wait(q, sk, 16 * self.dma_use[i])
        self.dma_use[i] += 1
        inst = self.eng[q].dma_start(out=out, in_=in_, **kw)
        inst.then_inc(self.semobj[sk], 16)
        self._done((sk, 16 * self.dma_use[i]), reads, writes)
        self.ninst += 1
        return inst

    def collective(self, kind, reads, writes, ins, outs, groups):
        self._deps('pool', reads, writes)
        self.ncoll = getattr(self, 'ncoll', 0) + 1
        sk = "cc%d" % self.ncoll
        self.semobj[sk] = self.es_root.enter_context(self.nc.semaphore("s_" + sk))
        inst = self.nc.gpsimd.collective_compute(kind, ALU.bypass, replica_groups=groups, ins=ins, outs=outs)
        inst.then_inc(self.semobj[sk], 16)
        self._done((sk, 16), reads, writes)
        self.ninst += 1
        return inst

    def barrier(self):
        for e in self.eng:
            for sk in ['pe', 'act', 'dve', 'pool']:
                if sk != e and self.cnt[sk]:
                    self._wait(e, sk, self.cnt[sk])
            for i in range(N_DMA_SEMS):
                if self.dma_use[i]:
                    self._wait(e, "d%d" % i, 16 * self.dma_use[i])
        self.nbar = getattr(self, 'nbar', 0) + 1
        for sk in ['pe', 'act', 'dve', 'pool']:
            if self.cnt[sk] > 20000:
                self.semobj[sk] = self.es_root.enter_context(self.nc.semaphore("s_%s_b%d" % (sk, self.nbar)))
                self.cnt[sk] = 0
                for e in self.eng:
                    self.seen[e].pop(sk, None)
                for k in list(self.last_w):
                    if self.last_w[k][0] == sk:
                        del self.last_w[k]
                for k in list(self.last_real_w):
                    if self.last_real_w[k][0] == sk:
                        del self.last_real_w[k]
                for k in self.readers:
                    self.readers[k].pop(sk, None)

    def finish(self, keys=(), e='sp'):
        for i in range(N_DMA_SEMS):
            if self.dma_use[i]:
                self._wait(e, "d%d" % i, 16 * self.dma_use[i])
        for k in keys:
            if k in self.last_w:
                sk, val = self.last_w[k]
                self._wait(e, sk, val)

import numpy as np
F=np.float32
BF=ml_dtypes.bfloat16
L=8192
def angles(dim):
    rows=L//64; row=np.repeat(np.arange(rows),64).astype(F); col=np.tile(np.arange(64),rows).astype(F)
    n=dim//4; inv=(10000.0**(-np.arange(n,dtype=F)/n)).astype(F)
    ang=np.concatenate([row[:,None]*inv,col[:,None]*inv],-1)
    return np.cos(ang).astype(F),np.sin(ang).astype(F)
def consts():
    sw=np.zeros((128,128),F)
    for p in range(128): sw[p,(p+64)%128]=1
    c64,s64=angles(64)
    tm=lambda t: np.ascontiguousarray(t.reshape(64,128,-1).transpose(1,0,2).reshape(128,-1))
    return dict(swap=sw, ident=np.eye(128,dtype=F).astype(BF), cos64=tm(c64), sin64=tm(s64))
def l1a_inputs(d, b, g, C):
    W=d['e_w_in'][0]
    hA=2*g; hB=2*g+1; kv=g//2
    heads=[W[:,hA*64:(hA+1)*64], W[:,hB*64:(hB+1)*64], W[:,512+kv*64:512+(kv+1)*64], W[:,512+kv*64:512+(kv+1)*64]]
    cols=[]
    for a in range(2):
        for h in range(4):
            cols.append(heads[h][:,a*32:(a+1)*32])
    cols.append(W[:,640+kv*64:640+(kv+1)*64])
    w_tm=np.ascontiguousarray(np.concatenate(cols,1))
    w_ga=np.ascontiguousarray(W[:,768+hA*64:768+hA*64+128])
    gq=d['e_q_norm'][0]; gk=d['e_k_norm'][0]; gs=[gq,gq,gk,gk]
    G=np.concatenate([gs[h][a*32:(a+1)*32] for a in range(2) for h in range(4)])[None,:].astype(F)
    return dict(xT=np.ascontiguousarray(d['x'][b].T), w_tm=w_tm, w_ga=w_ga,
                enorm=np.ascontiguousarray(d['e_norm'][0].reshape(8,128).T), qkgain=G,
                cos_tm=C['cos64'], sin_tm=C['sin64'], swapd=C['swap'], identd=C['ident'])

def hy_consts():
    n=np.arange(128)
    ang=2*np.pi*np.outer(n,n)/128.0
    cs,sn=np.cos(ang),np.sin(ang)
    FA=np.concatenate([cs,-sn],1).astype(BF)
    Fd=np.stack([cs,sn,-sn],1).astype(BF)
    Gd=np.stack([np.concatenate([cs,sn],1),np.concatenate([-sn,cs],1)],1).astype(BF)
    a2=2*np.pi*np.outer(n,n)/16384.0
    tw=np.stack([np.cos(a2),np.sin(a2)],1).astype(F)
    t=np.linspace(0,1,L,dtype=F)[:,None]; w=(2*math.pi*np.arange(L,dtype=F)[:,None]/L).astype(F)
    bands=np.linspace(1e-4,15,16,dtype=F)
    zf=np.concatenate([t,np.cos(bands*w),-np.sin(bands*w)],-1).astype(F)
    zTf=np.ascontiguousarray(zf.T)
    zTr=np.zeros_like(zTf); zTr[:,1:]=zTf[:,:0:-1]
    tr=np.zeros(L,F); tr[1:]=t[:0:-1,0]
    trow=np.stack([t[:,0],tr],0).astype(F)
    mind=math.log(1e-2)/1.5; maxd=math.log(1e-2)/0.3
    deltas=np.abs(np.linspace(mind,maxd,512,dtype=F)).astype(F)
    return dict(FA=FA,Fd=Fd,Gd=Gd,tw=tw,zTf=zTf,zTr=zTr,trow=trow,deltas=deltas)
def l1b_inputs(d,b,g,H):
    W=d['e_w_in'][0]; cs=slice(g*128,(g+1)*128)
    w_hy=np.ascontiguousarray(np.concatenate([W[:,1280:1792][:,cs],W[:,1792:2304][:,cs],W[:,2304:2816][:,cs],W[:,2816:3328][:,cs]],1))
    cw=d['e_conv_w'][0]; cb=d['e_conv_b'][0]
    convw=np.zeros((128,12),F)
    for gi in range(3):
        ch=gi*512+np.arange(g*128,(g+1)*128)
        convw[:,gi*4+0]=cw[0,ch]; convw[:,gi*4+1]=cw[1,ch]; convw[:,gi*4+2]=cw[2,ch]; convw[:,gi*4+3]=cb[ch]
    fparams=np.stack([d['e_filt_f1'][0],d['e_filt_b1'][0],d['e_filt_f2'][0],d['e_filt_b2'][0]],1).astype(F)
    w3=d['e_filt_w3'][0]
    w3d=np.ascontiguousarray(np.concatenate([w3[:,cs],w3[:,512:][:,cs]],1))
    return dict(xT=np.ascontiguousarray(d['x'][b].T), w_hy=w_hy, enorm=np.ascontiguousarray(d['e_norm'][0].reshape(8,128).T),
                convw=convw, fparams=fparams, w1d=d['e_filt_w1'][0], w2d=d['e_filt_w2'][0], w3d=w3d,
                zTf=H['zTf'], zTr=H['zTr'], trow=H['trow'], negdel=(-H['deltas'][cs])[:,None].astype(F), hyd=d['e_hy_d'][0][cs][:,None].astype(F),
                FAd=H['FA'], Fd=H['Fd'], Gd=H['Gd'], twd=H['tw'])

def tm16(t):
    return np.ascontiguousarray(t.reshape(16,128,-1).transpose(1,0,2).reshape(128,-1))
def l2_inputs(d, b, ts, y0T_b, C32):
    sl=slice(ts*2048,(ts+1)*2048)
    return dict(y0T=np.ascontiguousarray(y0T_b[:, sl]), xs=np.ascontiguousarray(d['x'][b, sl]), wout=d['e_w_out'][0], win=d['o_w_in'][0],
                onorm=np.ascontiguousarray(d['o_norm'][0].reshape(8,128).T), cos2=tm16(C32[0][sl]), sin2=tm16(C32[1][sl]),
                identd=np.eye(128,dtype=F).astype(BF))
def l4_inputs(d, b, ts, ocT_b, gcT_bts, x1_bts):
    sl=slice(ts*2048,(ts+1)*2048)
    return dict(ocT=np.ascontiguousarray(ocT_b[:, sl]), gcT=gcT_bts, x1=x1_bts, wout=d['o_w_out'][0], fnorm=d['final_norm'][None,:].astype(F))

def l3_consts():
    c,s=angles(32)
    return dict(cosT=np.ascontiguousarray(np.tile(c.T,(2,1))), sinT=np.ascontiguousarray(np.tile(s.T,(2,1))))
def l3_inputs(d, b, g, latT_b, C3, swap):
    wqb=d['o_w_qb'][0]; wkvb=d['o_w_kvb'][0]
    wq=np.zeros((384,4,96),F); wk=np.zeros((256,4,64),F); wv=np.zeros((256,4,64),F)
    for i in range(4):
        h=4*g+i
        wq[:,i,:]=wqb[:,h*96:h*96+96]
        wk[:,i,:]=wkvb[:,h*128:h*128+64]; wv[:,i,:]=wkvb[:,h*128+64:h*128+128]
    return dict(latT=latT_b, wq=wq, wk=wk, wv=wv, qgain=np.ascontiguousarray(d['o_q_a_norm'][0].reshape(3,128).T),
                kvgain=np.ascontiguousarray(d['o_kv_a_norm'][0].reshape(2,128).T), cosT=C3['cosT'], sinT=C3['sinT'], swapd=swap)

import numpy as np

L = 8192
D = 1024
NBLK = 16
TB = 512
EPS = 1e-6


def load_weights_scaled(c, wd, gain_t, gkey, wb, ncols, tag):
    stg = [c.sb("wstg%s%d" % (tag, i), [128, ncols], F32) for i in range(2)]
    for ch in range(8):
        s = stg[ch % 2]
        k = "wstg%s%d" % (tag, ch % 2)
        c.dma('sp', [], [k], s[:], wd[ch * 128:(ch + 1) * 128, :])
        c.op('dve', [k, gkey], ['wb' + tag],
             lambda e, s=s, ch=ch: e.tensor_scalar(out=wb[:, ch, :], in0=s[:], scalar1=gain_t[:, ch:ch + 1],
                                                   scalar2=None, op0=ALU.mult))


class Sweep:
    def __init__(self, c, xT, tag, hscr=None, mode='compute'):
        self.c = c
        self.tag = tag
        self.mode = mode
        self.hscr = hscr
        self.hT = [c.sb("hT%s%d" % (tag, i), [128, 8, TB], BF16) for i in range(2)]
        if mode == 'read':
            return
        self.xTv = xT.rearrange("(ch p) t -> p ch t", p=128)
        self.xf = [c.sb("xf%s%d" % (tag, i), [128, 8, TB], F32) for i in range(2)]
        self.sq = c.sb("sq" + tag, [128, 8, TB], BF16)
        self.sd = [c.sb("sd%s%d" % (tag, i), [128, TB], F32) for i in range(2)]
        self.R = [c.sb("R%s%d" % (tag, i), [128, TB], F32) for i in range(2)]
        self.ones = c.sb("ones" + tag, [128, 128], BF16)
        c.op('pool', [], ['ones' + tag], lambda e: e.memset(self.ones[:], 1.0))

    def load(self, j):
        c, t = self.c, self.tag
        s = j % 2
        if self.mode == 'read':
            c.dma('sp', ['hscr%d' % j], ['hT%s%d' % (t, s)], self.hT[s][:], self.hscr[:, :, j * TB:(j + 1) * TB])
            return
        c.dma('sp', [], ['xf%s%d' % (t, s)], self.xf[s][:], self.xTv[:, :, j * TB:(j + 1) * TB])

    def stats(self, j, ss_ps, ss_key):
        self.stats_a(j, ss_ps, ss_key)
        self.stats_b(j)

    def stats_b(self, j):
        if self.mode == 'read':
            return
        c, t = self.c, self.tag
        s = j % 2
        sd, R = self.sd[s], self.R[s]
        c.op('dve', ['sd%s%d' % (t, s)], ['R%s%d' % (t, s)], lambda e: e.reciprocal(out=R[:], in_=sd[:]))

    def stats_a(self, j, ss_ps, ss_key):
        if self.mode == 'read':
            return
        c, t = self.c, self.tag
        s = j % 2
        xf = self.xf[s]
        kx = 'xf%s%d' % (t, s)
        sd, R = self.sd[s], self.R[s]
        c.op('pool', [kx], ['sq' + t], lambda e: e.tensor_tensor(out=self.sq[:], in0=xf[:], in1=xf[:], op=ALU.mult))
        for ch in range(8):
            c.op('pe', ['sq' + t, 'ones' + t], [ss_key],
                 lambda e, ch=ch: e.matmul(ss_ps, lhsT=self.ones[:], rhs=self.sq[:, ch, :], start=(ch == 0), stop=(ch == 7)))
        c.op('act', [ss_key], ['sd%s%d' % (t, s)],
             lambda e: e.activation(out=sd[:], in_=ss_ps, func=AF.Sqrt, scale=1.0 / D, bias=EPS))

    def apply(self, j):
        c, t = self.c, self.tag
        s = j % 2
        if self.mode == 'read':
            return self.hT[s], 'hT%s%d' % (t, s)
        xf, hT, R = self.xf[s], self.hT[s], self.R[s]
        kx, kh = 'xf%s%d' % (t, s), 'hT%s%d' % (t, s)
        c.op('dve', [kx, 'R%s%d' % (t, s)], [kh],
             lambda e: e.tensor_tensor(out=hT[:], in0=xf[:], in1=R[:].unsqueeze(1).broadcast_to([128, 8, TB]), op=ALU.mult))
        if self.mode == 'write':
            c.dma('sp', [kh], ['hscr%d' % j], self.hscr[:, :, j * TB:(j + 1) * TB], hT[:])
        return hT, kh


def emit_attn_L0(c, es, xT, w_tm, w_ga, enorm, qkgain, cos_tm, sin_tm, swapd, identd, ya_out, dbg=None, hscr=None, sweep_mode='compute'):
    nc = c.nc
    QTp = [c.sb("QTA", [128, L], BF16), c.sb("QTB", [128, L], BF16)]
    c.op('pool', [], ['QTA'], lambda e: e.memset(QTp[0][64:128, :], 0.0))
    c.op('pool', [], ['QTB'], lambda e: e.memset(QTp[1][0:64, :], 0.0))
    KT = c.sb("KT", [128, L], BF16)
    VA = c.sb("VA", [128, 64, 128], BF16)
    VB = c.sb("VB", [128, 64, 128], BF16)
    gaT = c.sb("gaT", [128, L], BF16)
    c.op('pool', [], ['VA'], lambda e: e.memset(VA[:, :, 64:128], 1.0))
    c.op('pool', [], ['VB'], lambda e: e.memset(VB[:, :, 0:64], 1.0))
    ident = c.sb("ident", [128, 128], BF16)
    c.dma('sp', [], ['ident'], ident[:], identd)
    with ExitStack() as es1:
        c.es = es1
        gain = c.sb("gainA", [128, 8], F32)
        c.dma('sp', [], ['gainA'], gain[:], enorm)
        wtm = c.sb("wbA", [128, 8, 320], BF16)
        wga = c.sb("wbG", [128, 8, 128], BF16)
        load_weights_scaled(c, w_tm, gain, 'gainA', wtm, 320, 'A')
        load_weights_scaled(c, w_ga, gain, 'gainA', wga, 128, 'G')
        G = c.sb("G", [128, 256], F32)
        c.dma('sp', [], ['G'], G[:], qkgain.partition_broadcast(128))
        cosS = c.sb("cosS", [128, 64, 32], F32)
        sinS = c.sb("sinS", [128, 64, 32], F32)
        c.dma('sp', [], ['cosS'], cosS[:], cos_tm.rearrange("p (t j) -> p t j", j=32))
        c.dma('sp', [], ['sinS'], sinS[:], sin_tm.rearrange("p (t j) -> p t j", j=32))
        sw = Sweep(c, xT, 'A', hscr, sweep_mode)
        ss = c.ps("ssA", [128, TB])
        pga = c.ps("pga", [128, TB])
        pt = c.ps("ptA", [128, 4, 512])
        ptr = c.ps("ptrA", [128, 4, 2, 128], BF16)
        sqq = c.sb("sqq", [128, 4, 256], F32)
        ssq = c.sb("ssq", [128, 4, 4], F32)
        sdq = c.sb("sdq", [128, 4, 4], F32)
        rsq = c.sb("rsq", [128, 4, 4], F32)
        qn = c.sb("qn", [128, 4, 256], F32)
        t1 = c.sb("t1", [128, 4, 256], F32)
        t2 = c.sb("t2", [128, 4, 256], F32)
        qk = c.sb("qk", [128, 4, 4, 64], BF16)
        sw.load(0)
        sw.load(1)
        sw.stats(0, ss[:], 'ssA')
        cur = sw.apply(0)
        for j in range(NBLK):
            hT, kh = cur
            if j + 1 < NBLK:
                sw.stats_a(j + 1, ss[:], 'ssA')
            for ch in range(8):
                c.op('pe', ['wbG', kh], ['pga'],
                     lambda e, ch=ch: e.matmul(pga[:], lhsT=wga[:, ch, :], rhs=hT[:, ch, :], start=(ch == 0), stop=(ch == 7)))
            c.op('act', ['pga'], ['gaT'],
                 lambda e: e.activation(out=gaT[:, j * TB:(j + 1) * TB], in_=pga[:], func=AF.Silu))
            for tt in range(4):
                for ch in range(8):
                    c.op('pe', ['wbA', kh], ['ptA'],
                         lambda e, ch=ch, tt=tt: e.matmul(pt[:, tt, 0:320], lhsT=hT[:, ch, tt * 128:(tt + 1) * 128],
                                                          rhs=wtm[:, ch, :], start=(ch == 0), stop=(ch == 7)))
            if j + 1 < NBLK:
                sw.stats_b(j + 1)
                cur = sw.apply(j + 1)
            if j + 2 < NBLK:
                sw.load(j + 2)
            c.op('act', ['ptA'], ['sqq'], lambda e: e.activation(out=sqq[:], in_=pt[:, :, 0:256], func=AF.Square))
            c.op('dve', ['sqq'], ['ssq'],
                 lambda e: e.tensor_reduce(out=ssq[:], in_=sqq[:].rearrange("p t (a h j) -> p t h a j", a=2, h=4),
                                           axis=AX.XY, op=ALU.add))
            c.op('act', ['ssq'], ['sdq'], lambda e: e.activation(out=sdq[:], in_=ssq[:], func=AF.Sqrt, scale=1.0 / 64, bias=EPS))
            c.op('dve', ['sdq'], ['rsq'], lambda e: e.reciprocal(out=rsq[:], in_=sdq[:]))
            for a in range(2):
                c.op('dve', ['ptA', 'rsq'], ['qn'],
                     lambda e, a=a: e.tensor_tensor(
                         out=qn[:, :, a * 128:(a + 1) * 128].rearrange("p t (h j) -> p t h j", h=4),
                         in0=pt[:, :, a * 128:(a + 1) * 128].rearrange("p t (h j) -> p t h j", h=4),
                         in1=rsq[:].unsqueeze(3).broadcast_to([128, 4, 4, 32]), op=ALU.mult))
            c.op('dve', ['qn', 'G'], ['qn'],
                 lambda e: e.tensor_tensor(out=qn[:], in0=qn[:], in1=G[:].unsqueeze(1).broadcast_to([128, 4, 256]), op=ALU.mult))
            qv = qn[:].rearrange("p t (g j) -> p t g j", j=32)
            cb = cosS[:, j * 4:(j + 1) * 4, :].unsqueeze(2).broadcast_to([128, 4, 8, 32])
            sb_ = sinS[:, j * 4:(j + 1) * 4, :].unsqueeze(2).broadcast_to([128, 4, 8, 32])
            c.op('dve', ['qn', 'cosS'], ['t1'],
                 lambda e: e.tensor_tensor(out=t1[:].rearrange("p t (g j) -> p t g j", j=32), in0=qv, in1=cb, op=ALU.mult))
            c.op('pool', ['qn', 'sinS'], ['t2'],
                 lambda e: e.tensor_tensor(out=t2[:].rearrange("p t (g j) -> p t g j", j=32), in0=qv, in1=sb_, op=ALU.mult))
            t1v = t1[:].rearrange("p t (a h j) -> p t a h j", a=2, h=4)
            t2v = t2[:].rearrange("p t (a h j) -> p t a h j", a=2, h=4)
            c.op('dve', ['t1', 't2'], ['qk'],
                 lambda e: e.tensor_tensor(out=qk[:, :, :, 0:32], in0=t1v[:, :, 0, :, :], in1=t2v[:, :, 1, :, :], op=ALU.subtract))
            c.op('pool', ['t1', 't2'], ['qk'],
                 lambda e: e.tensor_tensor(out=qk[:, :, :, 32:64], in0=t2v[:, :, 0, :, :], in1=t1v[:, :, 1, :, :], op=ALU.add))
            for tt in range(4):
                c.op('pe', ['qk', 'ident'], ['ptrA'],
                     lambda e, tt=tt: e.transpose(out=ptr[:, tt, 0, :], in_=qk[:, tt, 0:2, :].rearrange("p h j -> p (h j)"), identity=ident[:]))
                c.op('pe', ['qk', 'ident'], ['ptrA'],
                     lambda e, tt=tt: e.transpose(out=ptr[:, tt, 1, :], in_=qk[:, tt, 2:4, :].rearrange("p h j -> p (h j)"), identity=ident[:]))
            c.op('act', ['ptrA'], ['QTA'],
                 lambda e: e.copy(out=QTp[0][0:64, j * TB:(j + 1) * TB].rearrange("p (t q) -> p t q", t=4), in_=ptr[0:64, :, 0, :]))
            c.op('act', ['ptrA'], ['QTB'],
                 lambda e: e.copy(out=QTp[1][64:128, j * TB:(j + 1) * TB].rearrange("p (t q) -> p t q", t=4), in_=ptr[64:128, :, 0, :]))
            c.op('dve', ['ptrA'], ['KT'],
                 lambda e: e.tensor_copy(out=KT[:, j * TB:(j + 1) * TB].rearrange("p (t q) -> p t q", t=4), in_=ptr[:, :, 1, :]))
            c.op('act', ['ptA'], ['VA'], lambda e: e.copy(out=VA[:, j * 4:(j + 1) * 4, 0:64], in_=pt[:, :, 256:320]))
            c.op('pool', ['VA'], ['VB'], lambda e: e.tensor_copy(out=VB[:, j * 4:(j + 1) * 4, 64:128], in_=VA[:, j * 4:(j + 1) * 4, 0:64]))
    c.barrier()
    c.es = es
    if dbg is not None:
        c.dma('sp', ['QTA'], ['dbgQ'], dbg['QT'][0:64, :], QTp[0][0:64, :])
        c.dma('sp', ['QTB'], ['dbgQ2'], dbg['QT'][64:128, :], QTp[1][64:128, :])
        c.dma('sp', ['KT'], ['dbgK'], dbg['KT'], KT[:])
        c.dma('sp', ['gaT'], ['dbgG'], dbg['gaT'], gaT[:])
        c.dma('sp', ['VA'], ['dbgV'], dbg['VA'], VA[:].rearrange("p t j -> p (t j)"))
    with ExitStack() as es2:
        c.es = es2
        swp = c.sb("swp", [128, 128], F32)
        c.dma('sp', [], ['swp'], swp[:], swapd)
        PS = [c.ps("PS%d" % i, [128, 2, 512]) for i in range(3)]
        PT = [c.sb("PT%d" % i, [128, 2, 512], BF16) for i in range(3)]
        PO = c.ps("PO", [128, 2, 512])
        PS2 = PS[2][:, 0, :]
        S_sb = c.sb("S_sb", [128, 512], F32)
        O_sb = c.sb("O_sb", [128, 512], F32)
        rinv = c.sb("rinv", [128, 512], F32)
        yo = [c.sb("yo%d" % i, [128, 512], BF16) for i in range(2)]

        def S(qb, kt):
            s = kt % 3
            for h in range(2):
                c.op('pe', ['KT', 'QT' + 'AB'[h]], ['PS%d' % s],
                     lambda e, h=h: e.matmul(PS[s][:, h, :], lhsT=KT[:, kt * 128:(kt + 1) * 128],
                                             rhs=QTp[h][:, qb * 512:(qb + 1) * 512], start=True, stop=True))

        S(0, 0)
        S(0, 1)
        for qb in range(NBLK):
            for kt in range(64):
                s = kt % 3
                if kt + 2 < 64:
                    S(qb, kt + 2)
                c.op('act', ['PS%d' % s], ['PT%d' % s],
                     lambda e: e.activation(out=PT[s][:], in_=PS[s][:], func=AF.Exp, scale=0.125))
                c.op('pe', ['VA', 'PT%d' % s], ['PO'],
                     lambda e: e.matmul(PO[:, 0, :], lhsT=VA[:, kt, :], rhs=PT[s][:, 0, :], start=(kt == 0), stop=(kt == 63)))
                c.op('pe', ['VB', 'PT%d' % s], ['PO'],
                     lambda e: e.matmul(PO[:, 1, :], lhsT=VB[:, kt, :], rhs=PT[s][:, 1, :], start=(kt == 0), stop=(kt == 63)))
            if qb + 1 < NBLK:
                S(qb + 1, 0)
                S(qb + 1, 1)
            c.op('dve', ['PO'], ['S_sb'], lambda e: e.tensor_copy(out=S_sb[64:128, :], in_=PO[64:128, 0, :]))
            c.op('dve', ['PO'], ['S_sb'], lambda e: e.tensor_copy(out=S_sb[0:64, :], in_=PO[0:64, 1, :]))
            c.op('dve', ['PO'], ['O_sb'], lambda e: e.tensor_copy(out=O_sb[0:64, :], in_=PO[0:64, 0, :]))
            c.op('dve', ['PO'], ['O_sb'], lambda e: e.tensor_copy(out=O_sb[64:128, :], in_=PO[64:128, 1, :]))
            c.op('pe', ['swp', 'S_sb'], ['PS2'], lambda e: e.matmul(PS2, lhsT=swp[:], rhs=S_sb[:], start=True, stop=True))
            c.op('dve', ['PS2'], ['rinv'], lambda e: e.reciprocal(out=rinv[:], in_=PS2))
            c.op('dve', ['O_sb', 'rinv'], ['O_sb'], lambda e: e.tensor_tensor(out=O_sb[:], in0=O_sb[:], in1=rinv[:], op=ALU.mult))
            y = yo[qb % 2]
            ky = 'yo%d' % (qb % 2)
            c.op('pool', ['O_sb', 'gaT'], [ky],
                 lambda e: e.tensor_tensor(out=y[:], in0=O_sb[:], in1=gaT[:, qb * 512:(qb + 1) * 512], op=ALU.mult))
            c.dma('sp', [ky], ['ya_out'], ya_out[:, qb * 512:(qb + 1) * 512], y[:])
    c.barrier()
    c.es = es

import numpy as np

I32 = mybir.dt.int32
PI_LO = 3.1415925
TWO_PI = 2.0 * np.pi
CG = 4
NG = 128 // CG


def lockstep(*chains):
    n = max(len(ch) for ch in chains)
    for i in range(n):
        for ch in chains:
            if i < len(ch):
                ch[i]()


def emit_hyena_L0(c, es, xT, w_hy, enorm, convw, fparams, w1d, w2d, w3d, zTf, zTr, trow, negdel, hyd,
                  FAd, Fd, Gd, twd, kscr, zscr, yscr, yb_out, dbg=None, hscr=None, sweep_mode='compute', h2scr=None, filt_mode='full'):
    nc = c.nc
    zT = c.sb("zT", [128, L], BF16)
    g0 = c.sb("g0", [128, L], BF16)
    with ExitStack() as es1:
        c.es = es1
        gain = c.sb("gainH", [128, 8], F32)
        c.dma('sp', [], ['gainH'], gain[:], enorm)
        wb = c.sb("wbH", [128, 8, 512], BF16)
        load_weights_scaled(c, w_hy, gain, 'gainH', wb, 512, 'H')
        cw = c.sb("cw", [128, 12], F32)
        c.dma('sp', [], ['cw'], cw[:], convw)
        sw = Sweep(c, xT, 'H', hscr, sweep_mode)
        ss = c.ps("ssH", [128, TB])
        pf = c.ps("pfH", [128, 4, 512])
        raw = [c.sb("raw%d" % i, [128, 3, 514], F32) for i in range(3)]
        sg = [c.sb("sg%d" % i, [128, 512], F32) for i in range(3)]
        u = c.sb("u", [128, 3, 512], F32)
        c.op('pool', [], ['raw0'], lambda e: e.memset(raw[0][:, :, 0:1], 0.0))

        def conv(jj):
            r = raw[jj % 3]
            kr = 'raw%d' % (jj % 3)
            for g in range(3):
                c.op('dve', [kr, 'cw'], ['u'],
                     lambda e, g=g: e.tensor_scalar(out=u[:, g, :], in0=r[:, g, 1:513], scalar1=cw[:, g * 4 + 1:g * 4 + 2],
                                                    scalar2=cw[:, g * 4 + 3:g * 4 + 4], op0=ALU.mult, op1=ALU.add))
                c.op('dve', [kr, 'cw', 'u'], ['u'],
                     lambda e, g=g: e.scalar_tensor_tensor(out=u[:, g, :], in0=r[:, g, 0:512], scalar=cw[:, g * 4:g * 4 + 1],
                                                           in1=u[:, g, :], op0=ALU.mult, op1=ALU.add))
                c.op('dve', [kr, 'cw', 'u'], ['u'],
                     lambda e, g=g: e.scalar_tensor_tensor(out=u[:, g, :], in0=r[:, g, 2:514], scalar=cw[:, g * 4 + 2:g * 4 + 3],
                                                           in1=u[:, g, :], op0=ALU.mult, op1=ALU.add))
            c.op('dve', ['u'], ['zT'],
                 lambda e: e.tensor_tensor(out=zT[:, jj * TB:(jj + 1) * TB], in0=u[:, 2, :], in1=u[:, 1, :], op=ALU.mult))
            c.op('pool', ['u', 'sg%d' % (jj % 3)], ['g0'],
                 lambda e: e.tensor_tensor(out=g0[:, jj * TB:(jj + 1) * TB], in0=u[:, 0, :], in1=sg[jj % 3][:], op=ALU.mult))

        sw.load(0)
        sw.load(1)
        sw.stats(0, ss[:], 'ssH')
        cur = sw.apply(0)
        for j in range(NBLK):
            hT, kh = cur
            if j + 1 < NBLK:
                sw.stats_a(j + 1, ss[:], 'ssH')
            for m in range(4):
                for ch in range(8):
                    c.op('pe', ['wbH', kh], ['pfH'],
                         lambda e, m=m, ch=ch: e.matmul(pf[:, m, :], lhsT=wb[:, ch, m * 128:(m + 1) * 128], rhs=hT[:, ch, :],
                                                        start=(ch == 0), stop=(ch == 7)))
            if j + 1 < NBLK:
                sw.stats_b(j + 1)
                cur = sw.apply(j + 1)
            if j + 2 < NBLK:
                sw.load(j + 2)
            r = raw[j % 3]
            kr = 'raw%d' % (j % 3)
            c.op('act', ['pfH'], [kr], lambda e: e.copy(out=r[:, :, 1:513], in_=pf[:, 0:3, :]))
            c.op('act', ['pfH'], ['sg%d' % (j % 3)], lambda e: e.activation(out=sg[j % 3][:], in_=pf[:, 3, :], func=AF.Silu))
            if j > 0:
                rp = raw[(j - 1) % 3]
                kp = 'raw%d' % ((j - 1) % 3)
                c.op('pool', [kp], [kr], lambda e: e.tensor_copy(out=r[:, :, 0:1], in_=rp[:, :, 512:513]))
                c.op('pool', [kr], [kp], lambda e: e.tensor_copy(out=rp[:, :, 513:514], in_=r[:, :, 1:2]))
                conv(j - 1)
        c.op('pool', [], ['raw%d' % ((NBLK - 1) % 3)], lambda e: e.memset(raw[(NBLK - 1) % 3][:, :, 513:514], 0.0))
        conv(NBLK - 1)
    c.barrier()
    c.es = es
    if dbg is not None:
        c.dma('sp', ['zT'], ['dbgz'], dbg['zT'], zT[:])
        c.dma('sp', ['g0'], ['dbgg'], dbg['g0'], g0[:])
    c.dma('sp', ['zT'], ['zscr'], zscr, zT[:])
    kT = c.sb("kT", [128, 2 * L], BF16)
    l1p = c.sb("l1p", [128, 32], F32)
    rl1 = c.sb("rl1", [128, 1], F32)
    with ExitStack() as es2:
        c.es = es2
        fp = c.sb("fp", [64, 4], F32)
        c.dma('sp', [], ['fp'], fp[:], fparams)
        fb = c.sb("fb", [64, 2], F32)
        c.op('dve', ['fp'], ['fb'], lambda e: e.tensor_tensor(out=fb[:, 0:1], in0=fp[:, 0:1], in1=fp[:, 1:2], op=ALU.mult))
        c.op('dve', ['fp'], ['fb'], lambda e: e.tensor_tensor(out=fb[:, 1:2], in0=fp[:, 2:3], in1=fp[:, 3:4], op=ALU.mult))
        w1 = c.sb("w1", [33, 64], F32)
        w2 = c.sb("w2", [64, 64], F32)
        w3 = c.sb("w3", [64, 256], F32)
        c.dma('sp', [], ['w1'], w1[:], w1d)
        c.dma('sp', [], ['w2'], w2[:], w2d)
        c.dma('sp', [], ['w3'], w3[:], w3d)
        nd = c.sb("nd", [128, 1], F32)
        c.dma('sp', [], ['nd'], nd[:], negdel)
        def mk(name, shape, dt=F32):
            return [c.sb("%s_%d" % (name, dr), shape, dt) for dr in range(2)]
        zb = [[c.sb("zb%d_%d" % (dr, i), [33, 512], F32) for i in range(2)] for dr in range(2)]
        tb = [[c.sb("tb%d_%d" % (dr, i), [128, 512], F32) for i in range(2)] for dr in range(2)]
        P1 = [c.ps("P1_%d" % dr, [64, 512]) for dr in range(2)]
        P2 = [c.ps("P2_%d" % dr, [64, 512]) for dr in range(2)]
        P3 = [c.ps("P3_%d" % dr, [128, 512]) for dr in range(2)]
        a1 = mk("a1", [64, 512]); ki = mk("ki", [64, 512], I32); rr = mk("rr", [64, 512])
        h1 = mk("h1", [64, 512]); h2 = mk("h2", [64, 512]); dec = mk("dec", [128, 512]); hk = mk("hk", [128, 512])
        h2r = [[c.sb("h2r%d_%d" % (dr, i), [64, 512], F32) for i in range(2)] for dr in range(2)]

        def sin_stages(dr, P, pk, col, hout, hk_):
            A, KI, RR = a1[dr], ki[dr], rr[dr]
            ka, kk, kr_ = 'a1_%d' % dr, 'ki_%d' % dr, 'rr_%d' % dr
            return [
                lambda: c.op('act', [pk, 'fp', 'fb'], [ka],
                             lambda e: e.activation(out=A[:], in_=P[:], func=AF.Identity, scale=fp[:, 2 * col:2 * col + 1], bias=fb[:, col:col + 1])),
                lambda: c.op('dve', [ka], [kk], lambda e: e.tensor_scalar(out=KI[:], in0=A[:], scalar1=1.0 / TWO_PI, scalar2=None, op0=ALU.mult)),
                lambda: c.op('dve', [kk, ka], [kr_],
                             lambda e: e.scalar_tensor_tensor(out=RR[:], in0=KI[:], scalar=-TWO_PI, in1=A[:], op0=ALU.mult, op1=ALU.add)),
                lambda: c.op('dve', [kr_], [kr_],
                             lambda e: e.tensor_scalar(out=RR[:], in0=RR[:], scalar1=-PI_LO, scalar2=PI_LO, op0=ALU.max, op1=ALU.min)),
                lambda: c.op('act', [kr_], [hk_], lambda e: e.activation(out=hout[:], in_=RR[:], func=AF.Sin)),
            ]

        def filt_chain(dr, j):
            s = j % 2
            zsrc = zTf if dr == 0 else zTr
            Z, T = zb[dr][s], tb[dr][s]
            kz, kt_ = 'zb%d_%d' % (dr, s), 'tb%d_%d' % (dr, s)
            p1, p2, p3 = P1[dr], P2[dr], P3[dr]
            H1, H2, DEC, HK = h1[dr], h2[dr], dec[dr], hk[dr]
            it = dr * NBLK + j
            kh2 = 'h2_%d' % dr
            if filt_mode == 'h2read':
                H2 = h2r[dr][s]
                kh2 = 'h2r%d_%d' % (dr, s)
                st = [
                    lambda: (c.dma('sp', ['h2scr%d_%d' % (dr, j)], [kh2], H2[:], h2scr[dr, :, j * TB:(j + 1) * TB]),
                             c.dma('sp', [], [kt_], T[:], trow[dr:dr + 1, j * TB:(j + 1) * TB].partition_broadcast(128))),
                ]
            else:
                st = [
                    lambda: (c.dma('sp', [], [kz], Z[:], zsrc[:, j * TB:(j + 1) * TB]),
                             c.dma('sp', [], [kt_], T[:], trow[dr:dr + 1, j * TB:(j + 1) * TB].partition_broadcast(128))),
                    lambda: c.op('pe', ['w1', kz], ['P1_%d' % dr], lambda e: e.matmul(p1[:], lhsT=w1[:], rhs=Z[:], start=True, stop=True)),
                ]
                st += sin_stages(dr, p1, 'P1_%d' % dr, 0, H1, 'h1_%d' % dr)
                st += [lambda: c.op('pe', ['w2', 'h1_%d' % dr], ['P2_%d' % dr], lambda e: e.matmul(p2[:], lhsT=w2[:], rhs=H1[:], start=True, stop=True))]
                st += sin_stages(dr, p2, 'P2_%d' % dr, 1, H2, 'h2_%d' % dr)
                if filt_mode == 'h2write':
                    st += [lambda: c.dma('sp', ['h2_%d' % dr], ['h2scr%d_%d' % (dr, j)], h2scr[dr, :, j * TB:(j + 1) * TB], H2[:])]
            st += [
                lambda: c.op('pe', ['w3', kh2], ['P3_%d' % dr],
                             lambda e: e.matmul(p3[:], lhsT=w3[:, dr * 128:(dr + 1) * 128], rhs=H2[:], start=True, stop=True)),
                lambda: c.op('act', [kt_, 'nd'], ['dec_%d' % dr], lambda e: e.activation(out=DEC[:], in_=T[:], func=AF.Exp, scale=nd[:, 0:1])),
                lambda: c.op('dve', ['P3_%d' % dr, 'dec_%d' % dr], ['hk_%d' % dr], lambda e: e.tensor_tensor(out=HK[:], in0=p3[:], in1=DEC[:], op=ALU.mult)),
            ]
            if dr == 1 and j == 0:
                st += [lambda: c.op('dve', ['hk_%d' % dr], ['hk_%d' % dr], lambda e: e.memset(HK[:, 0:1], 0.0))]
            st += [
                lambda: c.op('dve', ['hk_%d' % dr], ['l1p'],
                             lambda e: e.tensor_reduce(out=l1p[:, it:it + 1], in_=HK[:], axis=AX.X, op=ALU.add, apply_absolute_value=True)),
                lambda: c.op('pool', ['hk_%d' % dr], ['kT'], lambda e: e.tensor_copy(out=kT[:, dr * L + j * TB: dr * L + (j + 1) * TB], in_=HK[:])),
            ]
            return st

        for j in range(NBLK):
            lockstep(filt_chain(0, j), filt_chain(1, j))
        l1s = c.sb("l1s", [128, 1], F32)
        c.op('dve', ['l1p'], ['l1s'], lambda e: e.tensor_reduce(out=l1s[:], in_=l1p[:], axis=AX.X, op=ALU.add))
        c.op('dve', ['l1s'], ['rl1'], lambda e: e.reciprocal(out=rl1[:], in_=l1s[:]))
    c.barrier()
    c.es = es
    if dbg is not None:
        c.dma('sp', ['kT'], ['dbgk'], dbg['kT'], kT[:])
    c.dma('sp', ['kT'], ['kscr'], kscr, kT[:])
    with ExitStack() as es3:
        c.es = es3
        Kc = c.sb("Kc", [128, 128, 128], BF16)
        Xc = c.sb("Xc", [128, 128, 128], BF16)
        kv = kscr.rearrange("c (a b) -> a c b", b=128)
        zv = zscr.rearrange("c (a b) -> a c b", b=128)
        for i in range(16):
            c.dma('sp', ['kscr'], ['Kc'], Kc[:, i * 8:(i + 1) * 8, :], kv[:, i * 8:(i + 1) * 8, :])
        for i in range(16):
            c.dma('sp', ['zscr'], ['Xc'], Xc[0:64, i * 8:(i + 1) * 8, :], zv[:, i * 8:(i + 1) * 8, :])
        FA = c.sb("FA", [128, 256], BF16)
        Fm = c.sb("Fm", [128, 3, 128], BF16)
        Gm = c.sb("Gm", [128, 2, 256], BF16)
        tw = c.sb("tw", [128, 2, 128], F32)
        c.dma('sp', [], ['FA'], FA[:], FAd)
        c.dma('sp', [], ['Fm'], Fm[:], Fd)
        c.dma('sp', [], ['Gm'], Gm[:], Gd)
        c.dma('sp', [], ['tw'], tw[:], twd)
        PA_f = c.ps("PA_f", [128, CG, 256]); PXr_f = c.ps("PXr_f", [128, CG, 128]); PXi_f = c.ps("PXi_f", [128, CG, 128])
        PA_d = c.ps("PA_d", [128, CG, 256]); PXr_d = c.ps("PXr_d", [128, CG, 128]); PXi_d = c.ps("PXi_d", [128, CG, 128])
        mf = [c.sb("mf%d" % i, [128, CG, 128], F32) for i in range(4)]
        md = [c.sb("md%d" % i, [128, CG, 128], F32) for i in range(4)]
        Br_f = c.sb("Br_f", [128, CG, 128], BF16); Bi_f = c.sb("Bi_f", [128, CG, 128], BF16)
        Br_d = c.sb("Br_d", [128, CG, 128], BF16); Bi_d = c.sb("Bi_d", [128, CG, 128], BF16)
        Kr = [c.sb("Kr%d" % i, [128, CG, 128], F32) for i in range(2)]
        Ki = [c.sb("Ki%d" % i, [128, CG, 128], F32) for i in range(2)]
        Yr = c.sb("Yr", [128, CG, 128], BF16); Yi = c.sb("Yi", [128, CG, 128], BF16)
        Dr = c.sb("Dr", [128, CG, 128], BF16); Di = c.sb("Di", [128, CG, 128], BF16)
        Yo = [c.sb("Yo%d" % i, [64, CG, 128], F32) for i in range(2)]
        twc = tw[:, 0, :].unsqueeze(1).broadcast_to([128, CG, 128])
        tws = tw[:, 1, :].unsqueeze(1).broadcast_to([128, CG, 128])

        def cmul_st(m, mk_, ar, ai, ak, br, bi, bk, outr, outi, okr, oki, conj):
            m1, m2, m3, m4 = m
            k1, k2, k3, k4 = [mk_ + str(i) for i in range(4)]
            st = [
                lambda: c.op('dve', ak + bk, [k1], lambda e: e.tensor_tensor(out=m1[:], in0=ar, in1=br, op=ALU.mult)),
                lambda: c.op('dve', ak + bk, [k2], lambda e: e.tensor_tensor(out=m2[:], in0=ai, in1=bi, op=ALU.mult)),
                lambda: c.op('dve', ak + bk, [k3], lambda e: e.tensor_tensor(out=m3[:], in0=ar, in1=bi, op=ALU.mult)),
                lambda: c.op('dve', ak + bk, [k4], lambda e: e.tensor_tensor(out=m4[:], in0=ai, in1=br, op=ALU.mult)),
            ]
            if not conj:
                st += [lambda: c.op('pool', [k1, k2], okr, lambda e: e.tensor_tensor(out=outr, in0=m1[:], in1=m2[:], op=ALU.subtract)),
                       lambda: c.op('pool', [k3, k4], oki, lambda e: e.tensor_tensor(out=outi, in0=m3[:], in1=m4[:], op=ALU.add))]
            else:
                st += [lambda: c.op('pool', [k1, k2], okr, lambda e: e.tensor_tensor(out=outr, in0=m1[:], in1=m2[:], op=ALU.add)),
                       lambda: c.op('pool', [k3, k4], oki, lambda e: e.tensor_tensor(out=outi, in0=m4[:], in1=m3[:], op=ALU.subtract))]
            return st

        def fwd_st(src, ksrc, K, gi, PA, kpa, PXr, kxr, PXi, kxi, m, mk_, Br, kbr, Bi, kbi):
            def stageA():
                for cc in range(CG):
                    ch = gi * CG + cc
                    c.op('pe', [ksrc, 'FA'], [kpa],
                         lambda e, cc=cc, ch=ch: e.matmul(PA[:, cc, :], lhsT=src[0:K, ch, :], rhs=FA[0:K, :], start=True, stop=True))
            Bf = Br[:].rearrange("p c k -> p (c k)")
            Bg = Bi[:].rearrange("p c k -> p (c k)")
            xr = PXr[:].rearrange("p c k -> p (c k)")
            xi = PXi[:].rearrange("p c k -> p (c k)")

            def stageB():
                c.op('pe', ['Fm', kbr], [kxr], lambda e: e.matmul(xr, lhsT=Fm[:, 0, :], rhs=Bf, start=True, stop=False))
                c.op('pe', ['Fm', kbi], [kxr], lambda e: e.matmul(xr, lhsT=Fm[:, 1, :], rhs=Bg, start=False, stop=True))
                c.op('pe', ['Fm', kbi], [kxi], lambda e: e.matmul(xi, lhsT=Fm[:, 0, :], rhs=Bg, start=True, stop=False))
                c.op('pe', ['Fm', kbr], [kxi], lambda e: e.matmul(xi, lhsT=Fm[:, 2, :], rhs=Bf, start=False, stop=True))
            return ([stageA] + cmul_st(m, mk_, PA[:, :, 0:128], PA[:, :, 128:256], [kpa], twc, tws, ['tw'], Br[:], Bi[:], [kbr], [kbi], True)
                    + [stageB])

        def filt_fft(gi):
            s = gi % 2
            st = fwd_st(Kc, 'Kc', 128, gi, PA_f, 'PA_f', PXr_f, 'PXr_f', PXi_f, 'PXi_f', mf, 'mf', Br_f, 'Br_f', Bi_f, 'Bi_f')
            st += [lambda: c.op('act', ['PXr_f'], ['Kr%d' % s], lambda e: e.copy(out=Kr[s][:], in_=PXr_f[:])),
                   lambda: c.op('act', ['PXi_f'], ['Ki%d' % s], lambda e: e.copy(out=Ki[s][:], in_=PXi_f[:]))]
            return st

        yv = yscr.rearrange("c (a b) -> a c b", b=128)

        def data_fft(gi):
            s = gi % 2
            st = fwd_st(Xc, 'Xc', 64, gi, PA_d, 'PA_d', PXr_d, 'PXr_d', PXi_d, 'PXi_d', md, 'md', Br_d, 'Br_d', Bi_d, 'Bi_d')
            st += cmul_st(md, 'md', PXr_d[:], PXi_d[:], ['PXr_d', 'PXi_d'], Kr[s][:], Ki[s][:], ['Kr%d' % s, 'Ki%d' % s], Yr[:], Yi[:], ['Yr'], ['Yi'], False)
            PC = PA_d

            def stageC():
                for cc in range(CG):
                    c.op('pe', ['Yr', 'Gm'], ['PA_d'],
                         lambda e, cc=cc: e.matmul(PC[:, cc, :], lhsT=Yr[:, cc, :], rhs=Gm[:, 0, :], start=True, stop=False))
                    c.op('pe', ['Yi', 'Gm'], ['PA_d'],
                         lambda e, cc=cc: e.matmul(PC[:, cc, :], lhsT=Yi[:, cc, :], rhs=Gm[:, 1, :], start=False, stop=True))
            st += [stageC]
            st += cmul_st(md, 'md', PC[:, :, 0:128], PC[:, :, 128:256], ['PA_d'], twc, tws, ['tw'], Dr[:], Di[:], ['Dr'], ['Di'], False)
            pd = PXr_d[0:64].rearrange("p c k -> p (c k)")
            yo = Yo[s]
            ky = 'Yo%d' % s

            def stageD():
                c.op('pe', ['Fm', 'Dr'], ['PXr_d'],
                     lambda e: e.matmul(pd, lhsT=Fm[:, 0, 0:64], rhs=Dr[:].rearrange("p c k -> p (c k)"), start=True, stop=False))
                c.op('pe', ['Fm', 'Di'], ['PXr_d'],
                     lambda e: e.matmul(pd, lhsT=Fm[:, 2, 0:64], rhs=Di[:].rearrange("p c k -> p (c k)"), start=False, stop=True))
            st += [stageD,
                   lambda: c.op('act', ['PXr_d'], [ky], lambda e: e.activation(out=yo[:], in_=PXr_d[0:64], func=AF.Copy, scale=1.0 / 16384.0)),
                   lambda: c.dma('sp', [ky], ['yscr%d' % gi], yv[:, gi * CG:(gi + 1) * CG, :], yo[:])]
            return st

        lockstep(filt_fft(0))
        for gi in range(NG):
            if gi + 1 < NG:
                lockstep(data_fft(gi), filt_fft(gi + 1))
            else:
                lockstep(data_fft(gi))
    c.barrier()
    c.es = es
    with ExitStack() as es4:
        c.es = es4
        dd = c.sb("dd", [128, 1], F32)
        c.dma('sp', [], ['dd'], dd[:], hyd)
        yt = [c.sb("yt%d" % i, [128, 2048], F32) for i in range(2)]
        zd = c.sb("zd", [128, 2048], F32)
        ob = [c.sb("ob%d" % i, [128, 2048], BF16) for i in range(2)]
        allscr = ['yscr%d' % gi for gi in range(NG)]
        for q in range(4):
            s = q % 2
            sl = slice(q * 2048, (q + 1) * 2048)
            c.dma('sp', allscr, ['yt%d' % s], yt[s][:], yscr[:, sl])
            if dbg is not None:
                c.dma('sp', ['yt%d' % s], ['dbgy%d' % q], dbg['yc'][:, sl], yt[s][:])
            c.op('dve', ['zT', 'dd'], ['zd'], lambda e: e.tensor_scalar(out=zd[:], in0=zT[:, sl], scalar1=dd[:, 0:1], scalar2=None, op0=ALU.mult))
            c.op('dve', ['yt%d' % s, 'rl1', 'zd'], ['yt%d' % s],
                 lambda e: e.scalar_tensor_tensor(out=yt[s][:], in0=yt[s][:], scalar=rl1[:, 0:1], in1=zd[:], op0=ALU.mult, op1=ALU.add))
            c.op('dve', ['yt%d' % s, 'g0'], ['ob%d' % s], lambda e: e.tensor_tensor(out=ob[s][:], in0=yt[s][:], in1=g0[:, sl], op=ALU.mult))
            c.dma('sp', ['ob%d' % s], ['yb_out'], yb_out[:, sl], ob[s][:])
    c.barrier()
    c.es = es

import numpy as np

NT = 16
TS = 2048


def load_weights_plain(c, wd, wb, ncols, tag):
    stg = [c.sb("wstg%s%d" % (tag, i), [128, ncols], F32) for i in range(2)]
    for ch in range(8):
        s = stg[ch % 2]
        k = "wstg%s%d" % (tag, ch % 2)
        c.dma('sp', [], [k], s[:], wd[ch * 128:(ch + 1) * 128, :])
        c.op('dve', [k], ['wb' + tag], lambda e, s=s, ch=ch: e.tensor_copy(out=wb[:, ch, :], in_=s[:]))


def outproj_tile(c, i, yT, ykey, wo, po, xin_d, xt, kx):
    for half in range(2):
        for ech in range(8):
            c.op('pe', [ykey, 'wbO'], ['po'],
                 lambda e, half=half, ech=ech: e.matmul(po[:, half * 512:(half + 1) * 512], lhsT=yT[:, ech, i * 128:(i + 1) * 128],
                                                        rhs=wo[:, ech, half * 512:(half + 1) * 512], start=(ech == 0), stop=(ech == 7)))
    c.dma('sp', [], [kx], xt[:], xin_d[i * 128:(i + 1) * 128, :])
    c.op('dve', ['po', kx], [kx], lambda e: e.tensor_tensor(out=xt[:], in0=po[:], in1=xt[:], op=ALU.add))


def rstd_tile(c, xt, kx, junk, ssq, sd, rs, n):
    c.op('act', [kx], ['junk', 'ssq'], lambda e: e.activation(out=junk[:], in_=xt[:], func=AF.Square, accum_out=ssq[:, 0:1]))
    c.op('act', ['ssq'], ['sd'], lambda e: e.activation(out=sd[:], in_=ssq[:], func=AF.Sqrt, scale=1.0 / n, bias=EPS))
    c.op('dve', ['sd'], ['rs'], lambda e: e.reciprocal(out=rs[:], in_=sd[:]))


def emit_L2(c, es, y0T_d, xs_d, wout_d, win_d, onorm_d, cos_d, sin_d, identd, x1_d, latT_d, gcT_d, do_gate=True):
    with ExitStack() as es1:
        c.es = es1
        ident = c.sb("ident2", [128, 128], BF16)
        c.dma('sp', [], ['ident2'], ident[:], identd)
        yT = c.sb("y0T", [128, 8, TS], BF16)
        c.dma('sp', [], ['y0T'], yT[:], y0T_d.rearrange("(ch p) t -> p ch t", p=128))
        wo = c.sb("wbO", [128, 8, 1024], BF16)
        load_weights_plain(c, wout_d, wo, 1024, 'O')
        gain = c.sb("gainI", [128, 8], F32)
        c.dma('sp', [], ['gainI'], gain[:], onorm_d)
        wi = c.sb("wbI", [128, 8, 1696], BF16)
        load_weights_scaled(c, win_d, gain, 'gainI', wi, 1696, 'I')
        cosS = c.sb("cos2", [128, NT, 16], F32)
        sinS = c.sb("sin2", [128, NT, 16], F32)
        c.dma('sp', [], ['cos2'], cosS[:], cos_d.rearrange("p (t j) -> p t j", j=16))
        c.dma('sp', [], ['sin2'], sinS[:], sin_d.rearrange("p (t j) -> p t j", j=16))
        h1T = c.sb("h1T", [128, 8, TS], BF16)
        po = c.ps("po", [128, 1024])
        ptr = c.ps("ptr2", [128, 8, 128], BF16)
        pl = c.ps("pl", [128, 1024])
        pg = c.ps("pg", [128, 512])
        plt = c.ps("plt", [128, 6, 128], BF16)
        ltT = [c.sb("ltT%d" % i, [128, 6, 128], BF16) for i in range(2)]
        xt = [c.sb("xt%d" % i, [128, 1024], F32) for i in range(2)]
        junk = c.sb("junk", [128, 1024], F32)
        ssq = c.sb("ssq", [128, 1], F32)
        sd = c.sb("sd", [128, 1], F32)
        rs = c.sb("rs", [128, 1], F32)
        h1 = c.sb("h1", [128, 1024], BF16)
        ss2 = c.sb("ss2", [128, 2], F32)
        sd2 = c.sb("sd2", [128, 2], F32)
        rs2 = c.sb("rs2", [128, 2], F32)
        latn = [c.sb("latn%d" % i, [128, 672], BF16) for i in range(2)]
        k1 = c.sb("k1", [128, 32], F32)
        k2 = c.sb("k2", [128, 32], F32)
        gt = [c.sb("gt%d" % i, [128, 512], BF16) for i in range(2)]
        for i in range(NT):
            s = i % 2
            kx = 'xt%d' % s
            outproj_tile(c, i, yT, 'y0T', wo, po, xs_d, xt[s], kx)
            c.dma('sp', [kx], ['x1_d'], x1_d[i * 128:(i + 1) * 128, :], xt[s][:])
            rstd_tile(c, xt[s], kx, junk, ssq, sd, rs, D)
            c.op('act', [kx, 'rs'], ['h1'], lambda e: e.activation(out=h1[:], in_=xt[s][:], func=AF.Identity, scale=rs[:, 0:1]))
            for ch in range(8):
                c.op('pe', ['h1', 'ident2'], ['ptr2'],
                     lambda e, ch=ch: e.transpose(out=ptr[:, ch, :], in_=h1[:, ch * 128:(ch + 1) * 128], identity=ident[:]))
            c.op('dve', ['ptr2'], ['h1T'], lambda e: e.tensor_copy(out=h1T[:, :, i * 128:(i + 1) * 128], in_=ptr[:]))
            for (lo, hi) in ((0, 512), (512, 672)):
                for ch in range(8):
                    c.op('pe', ['h1T', 'wbI'], ['pl'],
                         lambda e, ch=ch, lo=lo, hi=hi: e.matmul(pl[:, lo:hi], lhsT=h1T[:, ch, i * 128:(i + 1) * 128], rhs=wi[:, ch, lo:hi],
                                                                 start=(ch == 0), stop=(ch == 7)))
            c.op('act', ['pl'], ['junk', 'ss2'], lambda e: e.activation(out=junk[:, 0:384], in_=pl[:, 0:384], func=AF.Square, accum_out=ss2[:, 0:1]))
            c.op('act', ['pl'], ['junk', 'ss2'], lambda e: e.activation(out=junk[:, 384:640], in_=pl[:, 384:640], func=AF.Square, accum_out=ss2[:, 1:2]))
            c.op('act', ['ss2'], ['sd2'], lambda e: e.activation(out=sd2[:, 0:1], in_=ss2[:, 0:1], func=AF.Sqrt, scale=1.0 / 384, bias=EPS))
            c.op('act', ['ss2'], ['sd2'], lambda e: e.activation(out=sd2[:, 1:2], in_=ss2[:, 1:2], func=AF.Sqrt, scale=1.0 / 256, bias=EPS))
            c.op('dve', ['sd2'], ['rs2'], lambda e: e.reciprocal(out=rs2[:], in_=sd2[:]))
            ln = latn[s]
            kl = 'latn%d' % s
            c.op('act', ['pl', 'rs2'], [kl], lambda e: e.activation(out=ln[:, 0:384], in_=pl[:, 0:384], func=AF.Identity, scale=rs2[:, 0:1]))
            c.op('act', ['pl', 'rs2'], [kl], lambda e: e.activation(out=ln[:, 384:640], in_=pl[:, 384:640], func=AF.Identity, scale=rs2[:, 1:2]))
            krv = pl[:, 640:672].rearrange("p (a j) -> p a j", a=2)
            cb = cosS[:, i, :].unsqueeze(1).broadcast_to([128, 2, 16])
            sb_ = sinS[:, i, :].unsqueeze(1).broadcast_to([128, 2, 16])
            c.op('dve', ['pl', 'cos2'], ['k1'], lambda e: e.tensor_tensor(out=k1[:].rearrange("p (a j) -> p a j", a=2), in0=krv, in1=cb, op=ALU.mult))
            c.op('dve', ['pl', 'sin2'], ['k2'], lambda e: e.tensor_tensor(out=k2[:].rearrange("p (a j) -> p a j", a=2), in0=krv, in1=sb_, op=ALU.mult))
            c.op('dve', ['k1', 'k2'], [kl], lambda e: e.tensor_tensor(out=ln[:, 640:656], in0=k1[:, 0:16], in1=k2[:, 16:32], op=ALU.subtract))
            c.op('dve', ['k1', 'k2'], [kl], lambda e: e.tensor_tensor(out=ln[:, 656:672], in0=k2[:, 0:16], in1=k1[:, 16:32], op=ALU.add))
            for ch in range(6):
                wdt = 128 if ch < 5 else 32
                c.op('pe', [kl, 'ident2'], ['plt'],
                     lambda e, ch=ch, wdt=wdt: e.transpose(out=plt[0:wdt, ch, :], in_=ln[:, ch * 128:ch * 128 + wdt], identity=ident[:]))
            lt = ltT[s]
            klt = 'ltT%d' % s
            c.op('dve', ['plt'], [klt], lambda e: e.tensor_copy(out=lt[:, 0:5, :], in_=plt[:, 0:5, :]))
            c.op('dve', ['plt'], [klt], lambda e: e.tensor_copy(out=lt[0:32, 5, :], in_=plt[0:32, 5, :]))
            c.dma('sp', [klt], ['latT_d'], latT_d[0:640, i * 128:(i + 1) * 128].rearrange("(ch p) t -> p ch t", p=128), lt[:, 0:5, :])
            c.dma('sp', [klt], ['latT_d'], latT_d[640:672, i * 128:(i + 1) * 128], lt[0:32, 5, :])
        n = 0
        for ec in (range(8) if do_gate else []):
            for blk in range(4):
                for ch in range(8):
                    c.op('pe', ['h1T', 'wbI'], ['pg'],
                         lambda e, ch=ch: e.matmul(pg[:], lhsT=wi[:, ch, 672 + ec * 128:672 + (ec + 1) * 128], rhs=h1T[:, ch, blk * 512:(blk + 1) * 512],
                                                   start=(ch == 0), stop=(ch == 7)))
                g = gt[n % 2]
                kg = 'gt%d' % (n % 2)
                c.op('act', ['pg'], [kg], lambda e: e.activation(out=g[:], in_=pg[:], func=AF.Silu))
                c.dma('sp', [kg], ['gcT_d'], gcT_d[ec * 128:(ec + 1) * 128, blk * 512:(blk + 1) * 512], g[:])
                n += 1
    c.barrier()
    c.es = es


def emit_L4(c, es, ocT_d, gcT_d, x1_d, wout_d, fnorm_d, out_d):
    with ExitStack() as es1:
        c.es = es1
        yT = c.sb("ycT", [128, 8, TS], BF16)
        gT = c.sb("gcT", [128, 8, TS], BF16)
        c.dma('sp', [], ['ycT'], yT[:], ocT_d.rearrange("(ch p) t -> p ch t", p=128))
        c.dma('sp', [], ['gcT'], gT[:], gcT_d.rearrange("(ch p) t -> p ch t", p=128))
        for ch in range(8):
            eng = 'dve' if ch % 2 == 0 else 'pool'
            c.op(eng, ['ycT', 'gcT'], ['ycT'], lambda e, ch=ch: e.tensor_tensor(out=yT[:, ch, :], in0=yT[:, ch, :], in1=gT[:, ch, :], op=ALU.mult))
        wo = c.sb("wbO", [128, 8, 1024], BF16)
        load_weights_plain(c, wout_d, wo, 1024, 'O')
        fn = c.sb("fn", [128, 1024], F32)
        c.dma('sp', [], ['fn'], fn[:], fnorm_d.partition_broadcast(128))
        po = c.ps("po", [128, 1024])
        xt = [c.sb("xt%d" % i, [128, 1024], F32) for i in range(2)]
        ot = [c.sb("ot%d" % i, [128, 1024], F32) for i in range(2)]
        junk = c.sb("junk", [128, 1024], F32)
        ssq = c.sb("ssq", [128, 1], F32)
        sd = c.sb("sd", [128, 1], F32)
        rs = c.sb("rs", [128, 1], F32)
        for i in range(NT):
            s = i % 2
            kx = 'xt%d' % s
            outproj_tile(c, i, yT, 'ycT', wo, po, x1_d, xt[s], kx)
            rstd_tile(c, xt[s], kx, junk, ssq, sd, rs, D)
            c.op('dve', [kx, 'rs', 'fn'], ['ot%d' % s],
                 lambda e: e.scalar_tensor_tensor(out=ot[s][:], in0=xt[s][:], scalar=rs[:, 0:1], in1=fn[:], op0=ALU.mult, op1=ALU.mult))
            c.dma('sp', ['ot%d' % s], ['out_d'], out_d[i * 128:(i + 1) * 128, :], ot[s][:])
    c.barrier()
    c.es = es

import numpy as np

L = 8192
NQB = 16
SCALE3 = 96.0 ** -0.5


def emit_attn_L1(c, es, latT_d, wq_d, wk_d, wv_d, qgain_d, kvgain_d, cosT_d, sinT_d, swapd, oc_d, nqb=NQB, dbg=None, cq_d=None):
    if cq_d is None:
        cq_d = latT_d[0:384, :]
    ckT = c.sb("ckT", [128, 2, L], BF16)
    c.dma('sp', [], ['ckT'], ckT[:], latT_d[384:640, :].rearrange("(ch p) t -> p ch t", p=128))
    qg = c.sb("qg", [128, 3], F32)
    kg = c.sb("kg", [128, 2], F32)
    c.dma('sp', [], ['qg'], qg[:], qgain_d)
    c.dma('sp', [], ['kg'], kg[:], kvgain_d)
    wq = c.sb("wq", [128, 3, 4, 96], BF16)
    wqr = c.sb("wqr", [128, 3, 4, 96], BF16)
    c.op('pool', [], ['wqr'], lambda e: e.memset(wqr[:], 0.0))
    wk = c.sb("wk", [128, 2, 4, 64], BF16)
    wv = c.sb("wv", [128, 2, 4, 64], BF16)
    swp = c.sb("swp3", [128, 128], F32)
    c.dma('sp', [], ['swp3'], swp[:], swapd)
    with ExitStack() as es0:
        c.es = es0
        st = c.sb("wst3", [128, 4 * 96], F32)
        for ch in range(3):
            c.dma('sp', [], ['wst3'], st[:], wq_d[ch * 128:(ch + 1) * 128].rearrange("p h j -> p (h j)"))
            sv = st[:].rearrange("p (h j) -> p h j", h=4)
            c.op('dve', ['wst3', 'qg'], ['wq'], lambda e, ch=ch: e.tensor_scalar(out=wq[:, ch], in0=sv, scalar1=qg[:, ch:ch + 1], scalar2=None, op0=ALU.mult))
            c.op('dve', ['wst3', 'qg'], ['wqr'],
                 lambda e, ch=ch: e.tensor_scalar(out=wqr[:, ch, :, 64:80], in0=sv[:, :, 80:96], scalar1=qg[:, ch:ch + 1], scalar2=-1.0, op0=ALU.mult, op1=ALU.mult))
            c.op('dve', ['wst3', 'qg'], ['wqr'],
                 lambda e, ch=ch: e.tensor_scalar(out=wqr[:, ch, :, 80:96], in0=sv[:, :, 64:80], scalar1=qg[:, ch:ch + 1], scalar2=None, op0=ALU.mult))
        for ch in range(2):
            c.dma('sp', [], ['wst3'], st[:, 0:256], wk_d[ch * 128:(ch + 1) * 128].rearrange("p h j -> p (h j)"))
            sv = st[:, 0:256].rearrange("p (h j) -> p h j", h=4)
            c.op('dve', ['wst3', 'kg'], ['wk'], lambda e, ch=ch: e.tensor_scalar(out=wk[:, ch], in0=sv, scalar1=kg[:, ch:ch + 1], scalar2=None, op0=ALU.mult))
        for ch in range(2):
            c.dma('sp', [], ['wst3'], st[:, 0:256], wv_d[ch * 128:(ch + 1) * 128].rearrange("p h j -> p (h j)"))
            sv = st[:, 0:256].rearrange("p (h j) -> p h j", h=4)
            c.op('dve', ['wst3', 'kg'], ['wv'], lambda e, ch=ch: e.tensor_scalar(out=wv[:, ch], in0=sv, scalar1=kg[:, ch:ch + 1], scalar2=None, op0=ALU.mult))
    c.barrier()
    c.es = es
    KT = [c.sb("KT3%d" % h, [96, L], BF16) for h in range(2)]
    VA = c.sb("VA3", [128, 64, 128], BF16)
    VB = c.sb("VB3", [128, 64, 128], BF16)
    c.op('pool', [], ['VA3'], lambda e: e.memset(VA[:, :, 64:128], 1.0))
    c.op('pool', [], ['VB3'], lambda e: e.memset(VB[:, :, 0:64], 1.0))
    for h in range(2):
        c.dma('sp', [], ['KT3%d' % h], KT[h][64:96, :], latT_d[640:672, :])
    for pair in range(2):
        with ExitStack() as esA:
            c.es = esA
            pk = c.ps("pk", [128, 512])
            pv = c.ps("pv", [128, 4, 128])
            for h in range(2):
                hh = pair * 2 + h
                for blk in range(16):
                    for ch in range(2):
                        c.op('pe', ['wk', 'ckT'], ['pk'],
                             lambda e, ch=ch: e.matmul(pk[0:64, :], lhsT=wk[:, ch, hh, :], rhs=ckT[:, ch, blk * 512:(blk + 1) * 512],
                                                       start=(ch == 0), stop=(ch == 1)))
                    eng = 'act' if blk % 2 == 0 else 'dve'
                    if eng == 'act':
                        c.op('act', ['pk'], ['KT3%d' % h], lambda e: e.copy(out=KT[h][0:64, blk * 512:(blk + 1) * 512], in_=pk[0:64, :]))
                    else:
                        c.op('dve', ['pk'], ['KT3%d' % h], lambda e: e.tensor_copy(out=KT[h][0:64, blk * 512:(blk + 1) * 512], in_=pk[0:64, :]))
            for g4 in range(16):
                for tt in range(4):
                    kt = g4 * 4 + tt
                    for ch in range(2):
                        c.op('pe', ['wv', 'ckT'], ['pv'],
                             lambda e, ch=ch, tt=tt, kt=kt: e.matmul(pv[:, tt, :], lhsT=ckT[:, ch, kt * 128:(kt + 1) * 128],
                                                                     rhs=wv[:, ch, pair * 2:pair * 2 + 2, :].rearrange("p h j -> p (h j)"),
                                                                     start=(ch == 0), stop=(ch == 1)))
                c.op('act', ['pv'], ['VA3'], lambda e: e.copy(out=VA[:, g4 * 4:(g4 + 1) * 4, 0:64], in_=pv[:, :, 0:64]))
                c.op('dve', ['pv'], ['VB3'], lambda e: e.tensor_copy(out=VB[:, g4 * 4:(g4 + 1) * 4, 64:128], in_=pv[:, :, 64:128]))
        c.barrier()
        c.es = es
        if dbg is not None and pair == 0:
            c.dma('sp', ['KT30'], ['dbgK'], dbg['KT'], KT[0][:])
            c.dma('sp', ['VB3'], ['dbgV'], dbg['VB'], VB[:].rearrange("p t j -> p (t j)"))
        with ExitStack() as esB:
            c.es = esB
            PS = [c.ps("PS%d" % i, [128, 2, 512]) for i in range(3)]
            PO = c.ps("PO", [128, 2, 512])
            PQ = PS[0][:, 0, :]
            PQR = PS[1][:, 0, :]
            PS2 = PS[2][:, 0, :]
            PT = [c.sb("PT%d" % i, [128, 2, 512], BF16) for i in range(3)]
            S_sb = c.sb("S_sb", [128, 512], F32)
            O_sb = c.sb("O_sb", [128, 512], F32)
            rinv = c.sb("rinv", [128, 512], F32)
            yo = [c.sb("yo%d" % i, [128, 512], BF16) for i in range(2)]
            cq = [c.sb("cq%d" % i, [128, 3, 512], BF16) for i in range(2)]
            cs = [c.sb("cs%d" % i, [128, 2, 512], F32) for i in range(2)]
            QT = [c.sb("QT3%d" % h, [96, nqb * 512], BF16) for h in range(2)]
            m1 = c.sb("m1", [128, 512], F32)
            m2 = c.sb("m2", [128, 512], F32)

            def loadq(qb):
                s = qb % 2
                c.dma('sp', [], ['cq%d' % s], cq[s][:], cq_d[:, qb * 512:(qb + 1) * 512].rearrange("(ch p) t -> p ch t", p=128))
                c.dma('sp', [], ['cs%d' % s], cs[s][64:96, 0, :], cosT_d[:, qb * 512:(qb + 1) * 512])
                c.dma('sp', [], ['cs%d' % s], cs[s][64:96, 1, :], sinT_d[:, qb * 512:(qb + 1) * 512])

            def projq(qb):
                s = qb % 2
                for h in range(2):
                    hh = pair * 2 + h
                    q = QT[h][:, qb * 512:(qb + 1) * 512]
                    kq = 'QT3%d' % h
                    for ch in range(3):
                        c.op('pe', ['wq', 'cq%d' % s], ['PS0'],
                             lambda e, ch=ch: e.matmul(PQ[0:96, :], lhsT=wq[:, ch, hh, :], rhs=cq[s][:, ch, :], start=(ch == 0), stop=(ch == 2)))
                    for ch in range(3):
                        c.op('pe', ['wqr', 'cq%d' % s], ['PS1'],
                             lambda e, ch=ch: e.matmul(PQR[0:96, :], lhsT=wqr[:, ch, hh, :], rhs=cq[s][:, ch, :], start=(ch == 0), stop=(ch == 2)))
                    c.op('dve', ['PS0'], [kq], lambda e: e.tensor_copy(out=q[0:64, :], in_=PQ[0:64, :]))
                    c.op('dve', ['PS0', 'cs%d' % s], ['m1'], lambda e: e.tensor_tensor(out=m1[64:96, :], in0=PQ[64:96, :], in1=cs[s][64:96, 0, :], op=ALU.mult))
                    c.op('dve', ['PS1', 'cs%d' % s], ['m2'], lambda e: e.tensor_tensor(out=m2[64:96, :], in0=PQR[64:96, :], in1=cs[s][64:96, 1, :], op=ALU.mult))
                    c.op('pool', ['m1', 'm2'], [kq], lambda e: e.tensor_tensor(out=q[64:96, :], in0=m1[64:96, :], in1=m2[64:96, :], op=ALU.add))

            def S(qb, kt):
                s = kt % 3
                for h in range(2):
                    c.op('pe', ['KT3%d' % h, 'QT3%d' % h], ['PS%d' % s],
                         lambda e, h=h: e.matmul(PS[s][:, h, :], lhsT=KT[h][:, kt * 128:(kt + 1) * 128], rhs=QT[h][:, qb * 512:(qb + 1) * 512],
                                                 start=True, stop=True))

            loadq(0)
            for qb in range(nqb):
                if qb + 1 < nqb:
                    loadq(qb + 1)
                projq(qb)
            S(0, 0)
            S(0, 1)
            for qb in range(nqb):
                for kt in range(64):
                    s = kt % 3
                    if kt + 2 < 64:
                        S(qb, kt + 2)
                    c.op('act', ['PS%d' % s], ['PT%d' % s], lambda e: e.activation(out=PT[s][:], in_=PS[s][:], func=AF.Exp, scale=SCALE3))
                    c.op('pe', ['VA3', 'PT%d' % s], ['PO'],
                         lambda e: e.matmul(PO[:, 0, :], lhsT=VA[:, kt, :], rhs=PT[s][:, 0, :], start=(kt == 0), stop=(kt == 63)))
                    c.op('pe', ['VB3', 'PT%d' % s], ['PO'],
                         lambda e: e.matmul(PO[:, 1, :], lhsT=VB[:, kt, :], rhs=PT[s][:, 1, :], start=(kt == 0), stop=(kt == 63)))
                if qb + 1 < nqb:
                    S(qb + 1, 0)
                    S(qb + 1, 1)
                c.op('dve', ['PO'], ['S_sb'], lambda e: e.tensor_copy(out=S_sb[64:128, :], in_=PO[64:128, 0, :]))
                c.op('dve', ['PO'], ['S_sb'], lambda e: e.tensor_copy(out=S_sb[0:64, :], in_=PO[0:64, 1, :]))
                c.op('dve', ['PO'], ['O_sb'], lambda e: e.tensor_copy(out=O_sb[0:64, :], in_=PO[0:64, 0, :]))
                c.op('dve', ['PO'], ['O_sb'], lambda e: e.tensor_copy(out=O_sb[64:128, :], in_=PO[64:128, 1, :]))
                c.op('pe', ['swp3', 'S_sb'], ['PS2'], lambda e: e.matmul(PS2, lhsT=swp[:], rhs=S_sb[:], start=True, stop=True))
                c.op('dve', ['PS2'], ['rinv'], lambda e: e.reciprocal(out=rinv[:], in_=PS2))
                y = yo[qb % 2]
                ky = 'yo%d' % (qb % 2)
                c.op('dve', ['O_sb', 'rinv'], [ky], lambda e: e.tensor_tensor(out=y[:], in0=O_sb[:], in1=rinv[:], op=ALU.mult))
                c.dma('sp', [ky], ['oc_d'], oc_d[pair, :, qb * 512:(qb + 1) * 512], y[:])
        c.barrier()
        c.es = es

import numpy as np


def emit_select(c, es, sel_d, x1s, gcT, latT, x1_own, gc_own, cq_own):
    with ExitStack() as es1:
        c.es = es1
        sel = c.sb("sel", [128, 4], F32)
        c.dma('sp', [], ['sel'], sel[:], sel_d)
        xa = [c.sb("xa%d" % i, [128, 4, 1024], F32) for i in range(2)]
        xo = [c.sb("xo%d" % i, [128, 1024], F32) for i in range(2)]
        xv = x1s.rearrange("(ts i p) d -> i p ts d", ts=4, p=128)
        for i in range(16):
            s = i % 2
            ka, ko = 'xa%d' % s, 'xo%d' % s
            c.dma('sp', [], [ka], xa[s][:], xv[i])
            c.op('dve', [ka, 'sel'], [ko], lambda e: e.tensor_scalar(out=xo[s][:], in0=xa[s][:, 0, :], scalar1=sel[:, 0:1], scalar2=None, op0=ALU.mult))
            for ts in range(1, 4):
                c.op('dve', [ka, 'sel', ko], [ko],
                     lambda e, ts=ts: e.scalar_tensor_tensor(out=xo[s][:], in0=xa[s][:, ts, :], scalar=sel[:, ts:ts + 1], in1=xo[s][:], op0=ALU.mult, op1=ALU.add))
            c.dma('sp', [ko], ['x1_own'], x1_own[i * 128:(i + 1) * 128, :], xo[s][:])
        ga = [c.sb("ga%d" % i, [128, 4, 2048], BF16) for i in range(2)]
        go = [c.sb("go%d" % i, [128, 2048], BF16) for i in range(2)]
        jobs = [(gcT[ch * 128:(ch + 1) * 128, :], gc_own[ch * 128:(ch + 1) * 128, :]) for ch in range(8)]
        jobs += [(latT[ch * 128:(ch + 1) * 128, :], cq_own[ch * 128:(ch + 1) * 128, :]) for ch in range(3)]
        for n, (src, dst) in enumerate(jobs):
            s = n % 2
            ka, ko = 'ga%d' % s, 'go%d' % s
            c.dma('sp', [], [ka], ga[s][:], src.rearrange("p (ts t) -> p ts t", ts=4))
            c.op('dve', [ka, 'sel'], [ko], lambda e: e.tensor_scalar(out=go[s][:], in0=ga[s][:, 0, :], scalar1=sel[:, 0:1], scalar2=None, op0=ALU.mult))
            for ts in range(1, 4):
                c.op('dve', [ka, 'sel', ko], [ko],
                     lambda e, ts=ts: e.scalar_tensor_tensor(out=go[s][:], in0=ga[s][:, ts, :], scalar=sel[:, ts:ts + 1], in1=go[s][:], op0=ALU.mult, op1=ALU.add))
            c.dma('sp', [ko], ['own%d' % n], dst, go[s][:])
    c.barrier()
    c.es = es


_CACHE = {}


def _dram(nc, kind, n, shp, dt=F32):
    return nc.dram_tensor(n, list(shp), dt, kind=kind).ap()


def build_fused():
    nc = bass.Bass("TRN2", target_bir_lowering=False)
    I = lambda n, s, dt=F32: _dram(nc, "ExternalInput", n, s, dt)
    O = lambda n, s, dt=F32: _dram(nc, "ExternalOutput", n, s, dt)
    S = lambda n, s, dt=F32: _dram(nc, "Internal", n, s, dt)
    xT = I("xT", [1024, 8192]); xtm = I("xtm", [8192, 1024]); enorm = I("enorm", [128, 8])
    w_tm = I("w_tm", [4, 1024, 320]); w_ga = I("w_ga", [4, 1024, 128]); qkgain = I("qkgain", [1, 256])
    cos_tm = I("cos_tm", [128, 2048]); sin_tm = I("sin_tm", [128, 2048]); swapd = I("swapd", [128, 128]); identd = I("identd", [128, 128], BF16)
    w_hy = I("w_hy", [4, 1024, 512]); convw = I("convw", [4, 128, 12]); fparams = I("fparams", [64, 4])
    w1d = I("w1d", [33, 64]); w2d = I("w2d", [64, 64]); w3d = I("w3d", [4, 64, 256]); zTf = I("zTf", [33, 8192]); zTr = I("zTr", [33, 8192]); trow = I("trow", [2, 8192])
    negdel = I("negdel", [4, 128, 1]); hyd = I("hyd", [4, 128, 1]); FAd = I("FAd", [128, 256], BF16); Fd = I("Fd", [128, 3, 128], BF16)
    Gd = I("Gd", [128, 2, 256], BF16); twd = I("twd", [128, 2, 128])
    wout = I("wout", [1024, 1024]); win = I("win", [1024, 1696]); onorm = I("onorm", [128, 8]); cos2 = I("cos2", [4, 128, 256]); sin2 = I("sin2", [4, 128, 256])
    wq = I("wq", [4, 384, 4, 96]); wk = I("wk", [4, 256, 4, 64]); wv = I("wv", [4, 256, 4, 64]); qgain = I("qgain", [128, 3]); kvgain = I("kvgain", [128, 2])
    cosT = I("cosT", [32, 2048]); sinT = I("sinT", [32, 2048]); sel = I("sel", [128, 4])
    wout2 = I("wout2", [1024, 1024]); fnorm = I("fnorm", [1, 1024])
    hscr = S("hscr", [128, 8, 8192], BF16); h2scr = S("h2scr", [2, 64, 8192]); kscr = S("kscr", [128, 16384], BF16); zscr = S("zscr", [128, 8192], BF16); yscr = S("yscr", [128, 8192])
    y0T = S("y0Ts", [1024, 8192], BF16); x1s = S("x1s", [8192, 1024]); latT = S("latTs", [672, 8192], BF16)
    gcT = S("gcTs", [1024, 8192], BF16)
    x1o = S("x1own", [2048, 1024]); gco = S("gcown", [1024, 2048], BF16); cqo = S("cqown", [384, 2048], BF16); oco = S("ocown", [1024, 2048], BF16)
    out = O("out", [2048, 1024])
    with ExitStack() as es:
        c = Ctx(nc, es)

        def scoped(fn):
            with ExitStack() as e1:
                c.es = e1
                fn(e1)
            c.barrier()
            c.es = es

        for g in range(4):
            scoped(lambda e1: emit_attn_L0(c, e1, xT, w_tm[g], w_ga[g], enorm, qkgain, cos_tm, sin_tm, swapd, identd, y0T[g * 128:(g + 1) * 128, :],
                                           hscr=hscr, sweep_mode=('write' if g == 0 else 'read')))
            scoped(lambda e1: emit_hyena_L0(c, e1, xT, w_hy[g], enorm, convw[g], fparams, w1d, w2d, w3d[g], zTf, zTr, trow, negdel[g], hyd[g],
                                            FAd, Fd, Gd, twd, kscr, zscr, yscr, y0T[512 + g * 128:512 + (g + 1) * 128, :], hscr=hscr, sweep_mode='read', h2scr=h2scr, filt_mode=('h2write' if g == 0 else 'h2read')))
        for ts in range(4):
            sl = slice(ts * 2048, (ts + 1) * 2048)
            scoped(lambda e1: emit_L2(c, e1, y0T[:, sl], xtm[sl, :], wout, win, onorm, cos2[ts], sin2[ts], identd, x1s[sl, :], latT[:, sl], gcT[:, sl]))
        emit_select(c, es, sel, x1s, gcT, latT, x1o, gco, cqo)
        for g in range(4):
            scoped(lambda e1: emit_attn_L1(c, e1, latT, wq[g], wk[g], wv[g], qgain, kvgain, cosT, sinT, swapd,
                                           oco[g * 256:(g + 1) * 256, :].rearrange("(a p) t -> a p t", p=128), nqb=4, cq_d=cqo))
        scoped(lambda e1: emit_L4(c, e1, oco, gco, x1o, wout2, fnorm, out))
        c.finish()
        print("fused program: inst", c.ninst, "waits", c.nwaits, dict(c.cnt))
    return nc


def _get(name, fn):
    if name not in _CACHE:
        _CACHE[name] = fn()
    return _CACHE[name]


def fused_inputs(d, b, C, H, C32, C3):
    m = {}
    a = [l1a_inputs(d, b, g, C) for g in range(4)]
    h = [l1b_inputs(d, b, g, H) for g in range(4)]
    m.update(xT=a[0]['xT'], xtm=np.ascontiguousarray(d['x'][b]), enorm=a[0]['enorm'], qkgain=a[0]['qkgain'],
             cos_tm=C['cos64'], sin_tm=C['sin64'], swapd=C['swap'], identd=C['ident'],
             w_tm=np.stack([x['w_tm'] for x in a]), w_ga=np.stack([x['w_ga'] for x in a]),
             w_hy=np.stack([x['w_hy'] for x in h]), convw=np.stack([x['convw'] for x in h]), fparams=h[0]['fparams'],
             w1d=h[0]['w1d'], w2d=h[0]['w2d'], w3d=np.stack([x['w3d'] for x in h]), zTf=H['zTf'], zTr=H['zTr'], trow=H['trow'],
             negdel=np.stack([x['negdel'] for x in h]), hyd=np.stack([x['hyd'] for x in h]), FAd=H['FA'], Fd=H['Fd'], Gd=H['Gd'], twd=H['tw'],
             wout=d['e_w_out'][0], win=d['o_w_in'][0], onorm=np.ascontiguousarray(d['o_norm'][0].reshape(8, 128).T),
             cos2=np.stack([tm16(C32[0][ts * 2048:(ts + 1) * 2048]) for ts in range(4)]),
             sin2=np.stack([tm16(C32[1][ts * 2048:(ts + 1) * 2048]) for ts in range(4)]))
    l3 = [l3_inputs(d, b, g, None, C3, C['swap']) for g in range(4)]
    m.update(wq=np.stack([x['wq'] for x in l3]), wk=np.stack([x['wk'] for x in l3]), wv=np.stack([x['wv'] for x in l3]),
             qgain=l3[0]['qgain'], kvgain=l3[0]['kvgain'], cosT=C3['cosT'], sinT=C3['sinT'],
             wout2=d['o_w_out'][0], fnorm=d['final_norm'][None, :].astype(np.float32))
    return m


def kernel(**inputs):
    d = {k: np.asarray(v) for k, v in inputs.items()}
    C = consts(); H = hy_consts(); C32 = angles(32); C3 = l3_consts()
    per_batch = [fused_inputs(d, b, C, H, C32, C3) for b in range(2)]
    ins = []
    for i in range(8):
        q = i % 4
        m = dict(per_batch[i // 4])
        onehot = np.zeros((128, 4), np.float32); onehot[:, q] = 1.0
        m.update(sel=onehot, cosT=np.ascontiguousarray(C3['cosT'][:, q * 2048:(q + 1) * 2048]), sinT=np.ascontiguousarray(C3['sinT'][:, q * 2048:(q + 1) * 2048]))
        ins.append(m)
    res = run_bass_kernel_spmd(_get('F', build_fused), ins, core_ids=list(range(8))).results
    out = np.empty((2, 8192, 1024), np.float32)
    for i in range(8):
        b, q = i // 4, i % 4
        out[b, q * 2048:(q + 1) * 2048] = np.asarray(res[i]['out'])
    return out
```

```python
import math
import ml_dtypes

import numpy as np
from contextlib import ExitStack
import concourse.bass as bass
import concourse.mybir as mybir
from concourse.bass_utils import run_bass_kernel_spmd

F32 = mybir.dt.float32
BF16 = mybir.dt.bfloat16
AF = mybir.ActivationFunctionType
ALU = mybir.AluOpType
AX = mybir.AxisListType

N_DMA_SEMS = 24


class Ctx:
    def __init__(self, nc, es):
        self.nc = nc
        self.es = es
        self.es_root = es
        self.eng = {'pe': nc.tensor, 'act': nc.scalar, 'dve': nc.vector,
                    'pool': nc.gpsimd, 'sp': nc.sync}
        self.semobj = {}
        for e in ['pe', 'act', 'dve', 'pool']:
            self.semobj[e] = es.enter_context(nc.semaphore("s_" + e))
        self.cnt = {e: 0 for e in ['pe', 'act', 'dve', 'pool']}
        self.seen = {e: {} for e in self.eng}
        self.dma_use = []
        for i in range(N_DMA_SEMS):
            k = "d%d" % i
            self.semobj[k] = es.enter_context(nc.semaphore("s_" + k))
            self.dma_use.append(0)
        self.dma_rr = 0
        self.last_w = {}
        self.readers = {}
        self.nwaits = 0
        self.ninst = 0
        self.excl = set()
        self.last_real_w = {}
        self.uid = 0

    def sb(self, name, shape, dt):
        self.uid += 1
        return self.es.enter_context(self.nc.sbuf_tensor("%s_u%d" % (name, self.uid), list(shape), dt))

    def ps(self, name, shape, dt=F32):
        self.excl.add(name)
        self.uid += 1
        return self.es.enter_context(self.nc.psum_tensor("%s_u%d" % (name, self.uid), list(shape), dt))

    def _wait(self, e, sk, val):
        if self.seen[e].get(sk, 0) < val:
            self.eng[e].wait_ge(self.semobj[sk], val)
            self.seen[e][sk] = val
            self.nwaits += 1

    def _deps(self, e, reads, writes):
        need = {}
        ex = [k for k in reads if k in self.excl]
        if ex:
            reads = [k for k in reads if k not in self.excl]

        def add(tok, kind):
            sk, val = tok
            if sk == 'pe' and e == 'pe':
                return
            if need.get(sk, 0) < val:
                need[sk] = val

        for k in ex:
            if k in self.last_real_w:
                add(self.last_real_w[k], 'raw')
            if k in self.last_w and self.last_w[k][0] != e:
                add(self.last_w[k], 'x')
        for k in reads:
            if k in self.last_w:
                add(self.last_w[k], 'raw')
        for k in writes:
            if k in self.last_w:
                add(self.last_w[k], 'waw')
            for sk, val in self.readers.get(k, {}).items():
                add((sk, val), 'war')
        for sk, val in need.items():
            self._wait(e, sk, val)

    def _done(self, tok, reads, writes):
        ex = [k for k in reads if k in self.excl]
        reads = [k for k in reads if k not in self.excl]
        for k in ex:
            self.last_w[k] = tok
            self.readers[k] = {}
        for k in writes:
            self.last_w[k] = tok
            self.readers[k] = {}
            if k in self.excl:
                self.last_real_w[k] = tok
        for k in reads:
            r = self.readers.setdefault(k, {})
            if r.get(tok[0], 0) < tok[1]:
                r[tok[0]] = tok[1]

    def op(self, e, reads, writes, build):
        self._deps(e, reads, writes)
        inst = build(self.eng[e])
        self.cnt[e] += 1
        inst.then_inc(self.semobj[e], 1)
        self._done((e, self.cnt[e]), reads, writes)
        self.ninst += 1
        return inst

    def dma(self, q, reads, writes, out, in_, **kw):
        self._deps(q, reads, writes)
        i = self.dma_rr
        self.dma_rr = (self.dma_rr + 1) % N_DMA_SEMS
        sk = "d%d" % i
        self._wait(q, sk, 16 * self.dma_use[i])
        self.dma_use[i] += 1
        inst = self.eng[q].dma_start(out=out, in_=in_, **kw)
        inst.then_inc(self.semobj[sk], 16)
        self._done((sk, 16 * self.dma_use[i]), reads, writes)
        self.ninst += 1
        return inst

    def collective(self, kind, reads, writes, ins, outs, groups):
        self._deps('pool', reads, writes)
        self.ncoll = getattr(self, 'ncoll', 0) + 1
        sk = "cc%d" % self.ncoll
        self.semobj[sk] = self.es_root.enter_context(self.nc.semaphore("s_" + sk))
        inst = self.nc.gpsimd.collective_compute(kind, ALU.bypass, replica_groups=groups, ins=ins, outs=outs)
        inst.then_inc(self.semobj[sk], 16)
        self._done((sk, 16), reads, writes)
        self.ninst += 1
        return inst

    def barrier(self):
        for e in self.eng:
            for sk in ['pe', 'act', 'dve', 'pool']:
                if sk != e and self.cnt[sk]:
                    self._wait(e, sk, self.cnt[sk])
            for i in range(N_DMA_SEMS):
                if self.dma_use[i]:
                    self._wait(e, "d%d" % i, 16 * self.dma_use[i])
        self.nbar = getattr(self, 'nbar', 0) + 1
        for sk in ['pe', 'act', 'dve', 'pool']:
            if self.cnt[sk] > 20000:
                self.semobj[sk] = self.es_root.enter_context(self.nc.semaphore("s_%s_b%d" % (sk, self.nbar)))
                self.cnt[sk] = 0
                for e in self.eng:
                    self.seen[e].pop(sk, None)
                for k in list(self.last_w):
                    if self.last_w[k][0] == sk:
                        del self.last_w[k]
                for k in list(self.last_real_w):
                    if self.last_real_w[k][0] == sk:
                        del self.last_real_w[k]
                for k in self.readers:
                    self.readers[k].pop(sk, None)

    def finish(self, keys=(), e='sp'):
        for i in range(N_DMA_SEMS):
            if self.dma_use[i]:
                self._wait(e, "d%d" % i, 16 * self.dma_use[i])
        for k in keys:
            if k in self.last_w:
                sk, val = self.last_w[k]
                self._wait(e, sk, val)

import numpy as np
F=np.float32
BF=ml_dtypes.bfloat16
L=8192
def angles(dim):
    rows=L//64; row=np.repeat(np.arange(rows),64).astype(F); col=np.tile(np.arange(64),rows).astype(F)
    n=dim//4; inv=(10000.0**(-np.arange(n,dtype=F)/n)).astype(F)
    ang=np.concatenate([row[:,None]*inv,col[:,None]*inv],-1)
    return np.cos(ang).astype(F),np.sin(ang).astype(F)
def consts():
    sw=np.zeros((128,128),F)
    for p in range(128): sw[p,(p+64)%128]=1
    c64,s64=angles(64)
    tm=lambda t: np.ascontiguousarray(t.reshape(64,128,-1).transpose(1,0,2).reshape(128,-1))
    return dict(swap=sw, ident=np.eye(128,dtype=F).astype(BF), cos64=tm(c64), sin64=tm(s64))
def l1a_inputs(d, b, g, C):
    W=d['e_w_in'][0]
    hA=2*g; hB=2*g+1; kv=g//2
    heads=[W[:,hA*64:(hA+1)*64], W[:,hB*64:(hB+1)*64], W[:,512+kv*64:512+(kv+1)*64], W[:,512+kv*64:512+(kv+1)*64]]
    cols=[]
    for a in range(2):
        for h in range(4):
            cols.append(heads[h][:,a*32:(a+1)*32])
    cols.append(W[:,640+kv*64:640+(kv+1)*64])
    w_tm=np.ascontiguousarray(np.concatenate(cols,1))
    w_ga=np.ascontiguousarray(W[:,768+hA*64:768+hA*64+128])
    gq=d['e_q_norm'][0]; gk=d['e_k_norm'][0]; gs=[gq,gq,gk,gk]
    G=np.concatenate([gs[h][a*32:(a+1)*32] for a in range(2) for h in range(4)])[None,:].astype(F)
    return dict(xT=np.ascontiguousarray(d['x'][b].T), w_tm=w_tm, w_ga=w_ga,
                enorm=np.ascontiguousarray(d['e_norm'][0].reshape(8,128).T), qkgain=G,
                cos_tm=C['cos64'], sin_tm=C['sin64'], swapd=C['swap'], identd=C['ident'])

def hy_consts():
    n=np.arange(128)
    ang=2*np.pi*np.outer(n,n)/128.0
    cs,sn=np.cos(ang),np.sin(ang)
    FA=np.concatenate([cs,-sn],1).astype(BF)
    Fd=np.stack([cs,sn,-sn],1).astype(BF)
    Gd=np.stack([np.concatenate([cs,sn],1),np.concatenate([-sn,cs],1)],1).astype(BF)
    a2=2*np.pi*np.outer(n,n)/16384.0
    tw=np.stack([np.cos(a2),np.sin(a2)],1).astype(F)
    t=np.linspace(0,1,L,dtype=F)[:,None]; w=(2*math.pi*np.arange(L,dtype=F)[:,None]/L).astype(F)
    bands=np.linspace(1e-4,15,16,dtype=F)
    zf=np.concatenate([t,np.cos(bands*w),-np.sin(bands*w)],-1).astype(F)
    zTf=np.ascontiguousarray(zf.T)
    zTr=np.zeros_like(zTf); zTr[:,1:]=zTf[:,:0:-1]
    tr=np.zeros(L,F); tr[1:]=t[:0:-1,0]
    trow=np.stack([t[:,0],tr],0).astype(F)
    mind=math.log(1e-2)/1.5; maxd=math.log(1e-2)/0.3
    deltas=np.abs(np.linspace(mind,maxd,512,dtype=F)).astype(F)
    return dict(FA=FA,Fd=Fd,Gd=Gd,tw=tw,zTf=zTf,zTr=zTr,trow=trow,deltas=deltas)
def l1b_inputs(d,b,g,H):
    W=d['e_w_in'][0]; cs=slice(g*128,(g+1)*128)
    w_hy=np.ascontiguousarray(np.concatenate([W[:,1280:1792][:,cs],W[:,1792:2304][:,cs],W[:,2304:2816][:,cs],W[:,2816:3328][:,cs]],1))
    cw=d['e_conv_w'][0]; cb=d['e_conv_b'][0]
    convw=np.zeros((128,12),F)
    for gi in range(3):
        ch=gi*512+np.arange(g*128,(g+1)*128)
        convw[:,gi*4+0]=cw[0,ch]; convw[:,gi*4+1]=cw[1,ch]; convw[:,gi*4+2]=cw[2,ch]; convw[:,gi*4+3]=cb[ch]
    fparams=np.stack([d['e_filt_f1'][0],d['e_filt_b1'][0],d['e_filt_f2'][0],d['e_filt_b2'][0]],1).astype(F)
    w3=d['e_filt_w3'][0]
    w3d=np.ascontiguousarray(np.concatenate([w3[:,cs],w3[:,512:][:,cs]],1))
    return dict(xT=np.ascontiguousarray(d['x'][b].T), w_hy=w_hy, enorm=np.ascontiguousarray(d['e_norm'][0].reshape(8,128).T),
                convw=convw, fparams=fparams, w1d=d['e_filt_w1'][0], w2d=d['e_filt_w2'][0], w3d=w3d,
                zTf=H['zTf'], zTr=H['zTr'], trow=H['trow'], negdel=(-H['deltas'][cs])[:,None].astype(F), hyd=d['e_hy_d'][0][cs][:,None].astype(F),
                FAd=H['FA'], Fd=H['Fd'], Gd=H['Gd'], twd=H['tw'])

def tm16(t):
    return np.ascontiguousarray(t.reshape(16,128,-1).transpose(1,0,2).reshape(128,-1))
def l2_inputs(d, b, ts, y0T_b, C32):
    sl=slice(ts*2048,(ts+1)*2048)
    return dict(y0T=np.ascontiguousarray(y0T_b[:, sl]), xs=np.ascontiguousarray(d['x'][b, sl]), wout=d['e_w_out'][0], win=d['o_w_in'][0],
                onorm=np.ascontiguousarray(d['o_norm'][0].reshape(8,128).T), cos2=tm16(C32[0][sl]), sin2=tm16(C32[1][sl]),
                identd=np.eye(128,dtype=F).astype(BF))
def l4_inputs(d, b, ts, ocT_b, gcT_bts, x1_bts):
    sl=slice(ts*2048,(ts+1)*2048)
    return dict(ocT=np.ascontiguousarray(ocT_b[:, sl]), gcT=gcT_bts, x1=x1_bts, wout=d['o_w_out'][0], fnorm=d['final_norm'][None,:].astype(F))

def l3_consts():
    c,s=angles(32)
    return dict(cosT=np.ascontiguousarray(np.tile(c.T,(2,1))), sinT=np.ascontiguousarray(np.tile(s.T,(2,1))))
def l3_inputs(d, b, g, latT_b, C3, swap):
    wqb=d['o_w_qb'][0]; wkvb=d['o_w_kvb'][0]
    wq=np.zeros((384,4,96),F); wk=np.zeros((256,4,64),F); wv=np.zeros((256,4,64),F)
    for i in range(4):
        h=4*g+i
        wq[:,i,:]=wqb[:,h*96:h*96+96]
        wk[:,i,:]=wkvb[:,h*128:h*128+64]; wv[:,i,:]=wkvb[:,h*128+64:h*128+128]
    return dict(latT=latT_b, wq=wq, wk=wk, wv=wv, qgain=np.ascontiguousarray(d['o_q_a_norm'][0].reshape(3,128).T),
                kvgain=np.ascontiguousarray(d['o_kv_a_norm'][0].reshape(2,128).T), cosT=C3['cosT'], sinT=C3['sinT'], swapd=swap)

import numpy as np

L = 8192
D = 1024
NBLK = 16
TB = 512
EPS = 1e-6


def load_weights_scaled(c, wd, gain_t, gkey, wb, ncols, tag):
    stg = [c.sb("wstg%s%d" % (tag, i), [128, ncols], F32) for i in range(2)]
    for ch in range(8):
        s = stg[ch % 2]
        k = "wstg%s%d" % (tag, ch % 2)
        c.dma('sp', [], [k], s[:], wd[ch * 128:(ch + 1) * 128, :])
        c.op('dve', [k, gkey], ['wb' + tag],
             lambda e, s=s, ch=ch: e.tensor_scalar(out=wb[:, ch, :], in0=s[:], scalar1=gain_t[:, ch:ch + 1],
                                                   scalar2=None, op0=ALU.mult))


class Sweep:
    def __init__(self, c, xT, tag, hscr=None, mode='compute'):
        self.c = c
        self.tag = tag
        self.mode = mode
        self.hscr = hscr
        self.hT = [c.sb("hT%s%d" % (tag, i), [128, 8, TB], BF16) for i in range(2)]
        if mode == 'read':
            return
        self.xTv = xT.rearrange("(ch p) t -> p ch t", p=128)
        self.xf = [c.sb("xf%s%d" % (tag, i), [128, 8, TB], F32) for i in range(2)]
        self.sq = c.sb("sq" + tag, [128, 8, TB], BF16)
        self.sd = [c.sb("sd%s%d" % (tag, i), [128, TB], F32) for i in range(2)]
        self.R = [c.sb("R%s%d" % (tag, i), [128, TB], F32) for i in range(2)]
        self.ones = c.sb("ones" + tag, [128, 128], BF16)
        c.op('pool', [], ['ones' + tag], lambda e: e.memset(self.ones[:], 1.0))

    def load(self, j):
        c, t = self.c, self.tag
        s = j % 2
        if self.mode == 'read':
            c.dma('sp', ['hscr%d' % j], ['hT%s%d' % (t, s)], self.hT[s][:], self.hscr[:, :, j * TB:(j + 1) * TB])
            return
        c.dma('sp', [], ['xf%s%d' % (t, s)], self.xf[s][:], self.xTv[:, :, j * TB:(j + 1) * TB])

    def stats(self, j, ss_ps, ss_key):
        self.stats_a(j, ss_ps, ss_key)
        self.stats_b(j)

    def stats_b(self, j):
        if self.mode == 'read':
            return
        c, t = self.c, self.tag
        s = j % 2
        sd, R = self.sd[s], self.R[s]
        c.op('dve', ['sd%s%d' % (t, s)], ['R%s%d' % (t, s)], lambda e: e.reciprocal(out=R[:], in_=sd[:]))

    def stats_a(self, j, ss_ps, ss_key):
        if self.mode == 'read':
            return
        c, t = self.c, self.tag
        s = j % 2
        xf = self.xf[s]
        kx = 'xf%s%d' % (t, s)
        sd, R = self.sd[s], self.R[s]
        c.op('pool', [kx], ['sq' + t], lambda e: e.tensor_tensor(out=self.sq[:], in0=xf[:], in1=xf[:], op=ALU.mult))
        for ch in range(8):
            c.op('pe', ['sq' + t, 'ones' + t], [ss_key],
                 lambda e, ch=ch: e.matmul(ss_ps, lhsT=self.ones[:], rhs=self.sq[:, ch, :], start=(ch == 0), stop=(ch == 7)))
        c.op('act', [ss_key], ['sd%s%d' % (t, s)],
             lambda e: e.activation(out=sd[:], in_=ss_ps, func=AF.Sqrt, scale=1.0 / D, bias=EPS))

    def apply(self, j):
        c, t = self.c, self.tag
        s = j % 2
        if self.mode == 'read':
            return self.hT[s], 'hT%s%d' % (t, s)
        xf, hT, R = self.xf[s], self.hT[s], self.R[s]
        kx, kh = 'xf%s%d' % (t, s), 'hT%s%d' % (t, s)
        c.op('dve', [kx, 'R%s%d' % (t, s)], [kh],
             lambda e: e.tensor_tensor(out=hT[:], in0=xf[:], in1=R[:].unsqueeze(1).broadcast_to([128, 8, TB]), op=ALU.mult))
        if self.mode == 'write':
            c.dma('sp', [kh], ['hscr%d' % j], self.hscr[:, :, j * TB:(j + 1) * TB], hT[:])
        return hT, kh


def emit_attn_L0(c, es, xT, w_tm, w_ga, enorm, qkgain, cos_tm, sin_tm, swapd, identd, ya_out, dbg=None, hscr=None, sweep_mode='compute'):
    nc = c.nc
    QTp = [c.sb("QTA", [128, L], BF16), c.sb("QTB", [128, L], BF16)]
    c.op('pool', [], ['QTA'], lambda e: e.memset(QTp[0][64:128, :], 0.0))
    c.op('pool', [], ['QTB'], lambda e: e.memset(QTp[1][0:64, :], 0.0))
    KT = c.sb("KT", [128, L], BF16)
    VA = c.sb("VA", [128, 64, 128], BF16)
    VB = c.sb("VB", [128, 64, 128], BF16)
    gaT = c.sb("gaT", [128, L], BF16)
    c.op('pool', [], ['VA'], lambda e: e.memset(VA[:, :, 64:128], 1.0))
    c.op('pool', [], ['VB'], lambda e: e.memset(VB[:, :, 0:64], 1.0))
    ident = c.sb("ident", [128, 128], BF16)
    c.dma('sp', [], ['ident'], ident[:], identd)
    with ExitStack() as es1:
        c.es = es1
        gain = c.sb("gainA", [128, 8], F32)
        c.dma('sp', [], ['gainA'], gain[:], enorm)
        wtm = c.sb("wbA", [128, 8, 320], BF16)
        wga = c.sb("wbG", [128, 8, 128], BF16)
        load_weights_scaled(c, w_tm, gain, 'gainA', wtm, 320, 'A')
        load_weights_scaled(c, w_ga, gain, 'gainA', wga, 128, 'G')
        G = c.sb("G", [128, 256], F32)
        c.dma('sp', [], ['G'], G[:], qkgain.partition_broadcast(128))
        cosS = c.sb("cosS", [128, 64, 32], F32)
        sinS = c.sb("sinS", [128, 64, 32], F32)
        c.dma('sp', [], ['cosS'], cosS[:], cos_tm.rearrange("p (t j) -> p t j", j=32))
        c.dma('sp', [], ['sinS'], sinS[:], sin_tm.rearrange("p (t j) -> p t j", j=32))
        sw = Sweep(c, xT, 'A', hscr, sweep_mode)
        ss = c.ps("ssA", [128, TB])
        pga = c.ps("pga", [128, TB])
        pt = c.ps("ptA", [128, 4, 512])
        ptr = c.ps("ptrA", [128, 4, 2, 128], BF16)
        sqq = c.sb("sqq", [128, 4, 256], F32)
        ssq = c.sb("ssq", [128, 4, 4], F32)
        sdq = c.sb("sdq", [128, 4, 4], F32)
        rsq = c.sb("rsq", [128, 4, 4], F32)
        qn = c.sb("qn", [128, 4, 256], F32)
        t1 = c.sb("t1", [128, 4, 256], F32)
        t2 = c.sb("t2", [128, 4, 256], F32)
        qk = c.sb("qk", [128, 4, 4, 64], BF16)
        sw.load(0)
        sw.load(1)
        sw.stats(0, ss[:], 'ssA')
        cur = sw.apply(0)
        for j in range(NBLK):
            hT, kh = cur
            if j + 1 < NBLK:
                sw.stats_a(j + 1, ss[:], 'ssA')
            for ch in range(8):
                c.op('pe', ['wbG', kh], ['pga'],
                     lambda e, ch=ch: e.matmul(pga[:], lhsT=wga[:, ch, :], rhs=hT[:, ch, :], start=(ch == 0), stop=(ch == 7)))
            c.op('act', ['pga'], ['gaT'],
                 lambda e: e.activation(out=gaT[:, j * TB:(j + 1) * TB], in_=pga[:], func=AF.Silu))
            for tt in range(4):
                for ch in range(8):
                    c.op('pe', ['wbA', kh], ['ptA'],
                         lambda e, ch=ch, tt=tt: e.matmul(pt[:, tt, 0:320], lhsT=hT[:, ch, tt * 128:(tt + 1) * 128],
                                                          rhs=wtm[:, ch, :], start=(ch == 0), stop=(ch == 7)))
            if j + 1 < NBLK:
                sw.stats_b(j + 1)
                cur = sw.apply(j + 1)
            if j + 2 < NBLK:
                sw.load(j + 2)
            c.op('act', ['ptA'], ['sqq'], lambda e: e.activation(out=sqq[:], in_=pt[:, :, 0:256], func=AF.Square))
            c.op('dve', ['sqq'], ['ssq'],
                 lambda e: e.tensor_reduce(out=ssq[:], in_=sqq[:].rearrange("p t (a h j) -> p t h a j", a=2, h=4),
                                           axis=AX.XY, op=ALU.add))
            c.op('act', ['ssq'], ['sdq'], lambda e: e.activation(out=sdq[:], in_=ssq[:], func=AF.Sqrt, scale=1.0 / 64, bias=EPS))
            c.op('dve', ['sdq'], ['rsq'], lambda e: e.reciprocal(out=rsq[:], in_=sdq[:]))
            for a in range(2):
                c.op('dve', ['ptA', 'rsq'], ['qn'],
                     lambda e, a=a: e.tensor_tensor(
                         out=qn[:, :, a * 128:(a + 1) * 128].rearrange("p t (h j) -> p t h j", h=4),
                         in0=pt[:, :, a * 128:(a + 1) * 128].rearrange("p t (h j) -> p t h j", h=4),
                         in1=rsq[:].unsqueeze(3).broadcast_to([128, 4, 4, 32]), op=ALU.mult))
            c.op('dve', ['qn', 'G'], ['qn'],
                 lambda e: e.tensor_tensor(out=qn[:], in0=qn[:], in1=G[:].unsqueeze(1).broadcast_to([128, 4, 256]), op=ALU.mult))
            qv = qn[:].rearrange("p t (g j) -> p t g j", j=32)
            cb = cosS[:, j * 4:(j + 1) * 4, :].unsqueeze(2).broadcast_to([128, 4, 8, 32])
            sb_ = sinS[:, j * 4:(j + 1) * 4, :].unsqueeze(2).broadcast_to([128, 4, 8, 32])
            c.op('dve', ['qn', 'cosS'], ['t1'],
                 lambda e: e.tensor_tensor(out=t1[:].rearrange("p t (g j) -> p t g j", j=32), in0=qv, in1=cb, op=ALU.mult))
            c.op('pool', ['qn', 'sinS'], ['t2'],
                 lambda e: e.tensor_tensor(out=t2[:].rearrange("p t (g j) -> p t g j", j=32), in0=qv, in1=sb_, op=ALU.mult))
            t1v = t1[:].rearrange("p t (a h j) -> p t a h j", a=2, h=4)
            t2v = t2[:].rearrange("p t (a h j) -> p t a h j", a=2, h=4)
            c.op('dve', ['t1', 't2'], ['qk'],
                 lambda e: e.tensor_tensor(out=qk[:, :, :, 0:32], in0=t1v[:, :, 0, :, :], in1=t2v[:, :, 1, :, :], op=ALU.subtract))
            c.op('pool', ['t1', 't2'], ['qk'],
                 lambda e: e.tensor_tensor(out=qk[:, :, :, 32:64], in0=t2v[:, :, 0, :, :], in1=t1v[:, :, 1, :, :], op=ALU.add))
            for tt in range(4):
                c.op('pe', ['qk', 'ident'], ['ptrA'],
                     lambda e, tt=tt: e.transpose(out=ptr[:, tt, 0, :], in_=qk[:, tt, 0:2, :].rearrange("p h j -> p (h j)"), identity=ident[:]))
                c.op('pe', ['qk', 'ident'], ['ptrA'],
                     lambda e, tt=tt: e.transpose(out=ptr[:, tt, 1, :], in_=qk[:, tt, 2:4, :].rearrange("p h j -> p (h j)"), identity=ident[:]))
            c.op('act', ['ptrA'], ['QTA'],
                 lambda e: e.copy(out=QTp[0][0:64, j * TB:(j + 1) * TB].rearrange("p (t q) -> p t q", t=4), in_=ptr[0:64, :, 0, :]))
            c.op('act', ['ptrA'], ['QTB'],
                 lambda e: e.copy(out=QTp[1][64:128, j * TB:(j + 1) * TB].rearrange("p (t q) -> p t q", t=4), in_=ptr[64:128, :, 0, :]))
            c.op('dve', ['ptrA'], ['KT'],
                 lambda e: e.tensor_copy(out=KT[:, j * TB:(j + 1) * TB].rearrange("p (t q) -> p t q", t=4), in_=ptr[:, :, 1, :]))
            c.op('act', ['ptA'], ['VA'], lambda e: e.copy(out=VA[:, j * 4:(j + 1) * 4, 0:64], in_=pt[:, :, 256:320]))
            c.op('pool', ['VA'], ['VB'], lambda e: e.tensor_copy(out=VB[:, j * 4:(j + 1) * 4, 64:128], in_=VA[:, j * 4:(j + 1) * 4, 0:64]))
    c.barrier()
    c.es = es
    if dbg is not None:
        c.dma('sp', ['QTA'], ['dbgQ'], dbg['QT'][0:64, :], QTp[0][0:64, :])
        c.dma('sp', ['QTB'], ['dbgQ2'], dbg['QT'][64:128, :], QTp[1][64:128, :])
        c.dma('sp', ['KT'], ['dbgK'], dbg['KT'], KT[:])
        c.dma('sp', ['gaT'], ['dbgG'], dbg['gaT'], gaT[:])
        c.dma('sp', ['VA'], ['dbgV'], dbg['VA'], VA[:].rearrange("p t j -> p (t j)"))
    with ExitStack() as es2:
        c.es = es2
        swp = c.sb("swp", [128, 128], F32)
        c.dma('sp', [], ['swp'], swp[:], swapd)
        PS = [c.ps("PS%d" % i, [128, 2, 512]) for i in range(3)]
        PT = [c.sb("PT%d" % i, [128, 2, 512], BF16) for i in range(3)]
        PO = c.ps("PO", [128, 2, 512])
        PS2 = PS[2][:, 0, :]
        S_sb = c.sb("S_sb", [128, 512], F32)
        O_sb = c.sb("O_sb", [128, 512], F32)
        rinv = c.sb("rinv", [128, 512], F32)
        yo = [c.sb("yo%d" % i, [128, 512], BF16) for i in range(2)]

        def S(qb, kt):
            s = kt % 3
            for h in range(2):
                c.op('pe', ['KT', 'QT' + 'AB'[h]], ['PS%d' % s],
                     lambda e, h=h: e.matmul(PS[s][:, h, :], lhsT=KT[:, kt * 128:(kt + 1) * 128],
                                             rhs=QTp[h][:, qb * 512:(qb + 1) * 512], start=True, stop=True))

        S(0, 0)
        S(0, 1)
        for qb in range(NBLK):
            for kt in range(64):
                s = kt % 3
                if kt + 2 < 64:
                    S(qb, kt + 2)
                c.op('act', ['PS%d' % s], ['PT%d' % s],
                     lambda e: e.activation(out=PT[s][:], in_=PS[s][:], func=AF.Exp, scale=0.125))
                c.op('pe', ['VA', 'PT%d' % s], ['PO'],
                     lambda e: e.matmul(PO[:, 0, :], lhsT=VA[:, kt, :], rhs=PT[s][:, 0, :], start=(kt == 0), stop=(kt == 63)))
                c.op('pe', ['VB', 'PT%d' % s], ['PO'],
                     lambda e: e.matmul(PO[:, 1, :], lhsT=VB[:, kt, :], rhs=PT[s][:, 1, :], start=(kt == 0), stop=(kt == 63)))
            if qb + 1 < NBLK:
                S(qb + 1, 0)
                S(qb + 1, 1)
            c.op('dve', ['PO'], ['S_sb'], lambda e: e.tensor_copy(out=S_sb[64:128, :], in_=PO[64:128, 0, :]))
            c.op('dve', ['PO'], ['S_sb'], lambda e: e.tensor_copy(out=S_sb[0:64, :], in_=PO[0:64, 1, :]))
            c.op('dve', ['PO'], ['O_sb'], lambda e: e.tensor_copy(out=O_sb[0:64, :], in_=PO[0:64, 0, :]))
            c.op('dve', ['PO'], ['O_sb'], lambda e: e.tensor_copy(out=O_sb[64:128, :], in_=PO[64:128, 1, :]))
            c.op('pe', ['swp', 'S_sb'], ['PS2'], lambda e: e.matmul(PS2, lhsT=swp[:], rhs=S_sb[:], start=True, stop=True))
            c.op('dve', ['PS2'], ['rinv'], lambda e: e.reciprocal(out=rinv[:], in_=PS2))
            c.op('dve', ['O_sb', 'rinv'], ['O_sb'], lambda e: e.tensor_tensor(out=O_sb[:], in0=O_sb[:], in1=rinv[:], op=ALU.mult))
            y = yo[qb % 2]
            ky = 'yo%d' % (qb % 2)
            c.op('pool', ['O_sb', 'gaT'], [ky],
                 lambda e: e.tensor_tensor(out=y[:], in0=O_sb[:], in1=gaT[:, qb * 512:(qb + 1) * 512], op=ALU.mult))
            c.dma('sp', [ky], ['ya_out'], ya_out[:, qb * 512:(qb + 1) * 512], y[:])
    c.barrier()
    c.es = es

import numpy as np

I32 = mybir.dt.int32
PI_LO = 3.1415925
TWO_PI = 2.0 * np.pi
CG = 4
NG = 128 // CG


def lockstep(*chains):
    n = max(len(ch) for ch in chains)
    for i in range(n):
        for ch in chains:
            if i < len(ch):
                ch[i]()


def emit_hyena_L0(c, es, xT, w_hy, enorm, convw, fparams, w1d, w2d, w3d, zTf, zTr, trow, negdel, hyd,
                  FAd, Fd, Gd, twd, kscr, zscr, yscr, yb_out, dbg=None, hscr=None, sweep_mode='compute', h2scr=None, filt_mode='full'):
    nc = c.nc
    zT = c.sb("zT", [128, L], BF16)
    g0 = c.sb("g0", [128, L], BF16)
    with ExitStack() as es1:
        c.es = es1
        gain = c.sb("gainH", [128, 8], F32)
        c.dma('sp', [], ['gainH'], gain[:], enorm)
        wb = c.sb("wbH", [128, 8, 512], BF16)
        load_weights_scaled(c, w_hy, gain, 'gainH', wb, 512, 'H')
        cw = c.sb("cw", [128, 12], F32)
        c.dma('sp', [], ['cw'], cw[:], convw)
        sw = Sweep(c, xT, 'H', hscr, sweep_mode)
        ss = c.ps("ssH", [128, TB])
        pf = c.ps("pfH", [128, 4, 512])
        raw = [c.sb("raw%d" % i, [128, 3, 514], F32) for i in range(3)]
        sg = [c.sb("sg%d" % i, [128, 512], F32) for i in range(3)]
        u = c.sb("u", [128, 3, 512], F32)
        c.op('pool', [], ['raw0'], lambda e: e.memset(raw[0][:, :, 0:1], 0.0))

        def conv(jj):
            r = raw[jj % 3]
            kr = 'raw%d' % (jj % 3)
            for g in range(3):
                c.op('dve', [kr, 'cw'], ['u'],
                     lambda e, g=g: e.tensor_scalar(out=u[:, g, :], in0=r[:, g, 1:513], scalar1=cw[:, g * 4 + 1:g * 4 + 2],
                                                    scalar2=cw[:, g * 4 + 3:g * 4 + 4], op0=ALU.mult, op1=ALU.add))
                c.op('dve', [kr, 'cw', 'u'], ['u'],
                     lambda e, g=g: e.scalar_tensor_tensor(out=u[:, g, :], in0=r[:, g, 0:512], scalar=cw[:, g * 4:g * 4 + 1],
                                                           in1=u[:, g, :], op0=ALU.mult, op1=ALU.add))
                c.op('dve', [kr, 'cw', 'u'], ['u'],
                     lambda e, g=g: e.scalar_tensor_tensor(out=u[:, g, :], in0=r[:, g, 2:514], scalar=cw[:, g * 4 + 2:g * 4 + 3],
                                                           in1=u[:, g, :], op0=ALU.mult, op1=ALU.add))
            c.op('dve', ['u'], ['zT'],
                 lambda e: e.tensor_tensor(out=zT[:, jj * TB:(jj + 1) * TB], in0=u[:, 2, :], in1=u[:, 1, :], op=ALU.mult))
            c.op('pool', ['u', 'sg%d' % (jj % 3)], ['g0'],
                 lambda e: e.tensor_tensor(out=g0[:, jj * TB:(jj + 1) * TB], in0=u[:, 0, :], in1=sg[jj % 3][:], op=ALU.mult))

        sw.load(0)
        sw.load(1)
        sw.stats(0, ss[:], 'ssH')
        cur = sw.apply(0)
        for j in range(NBLK):
            hT, kh = cur
            if j + 1 < NBLK:
                sw.stats_a(j + 1, ss[:], 'ssH')
            for m in range(4):
                for ch in range(8):
                    c.op('pe', ['wbH', kh], ['pfH'],
                         lambda e, m=m, ch=ch: e.matmul(pf[:, m, :], lhsT=wb[:, ch, m * 128:(m + 1) * 128], rhs=hT[:, ch, :],
                                                        start=(ch == 0), stop=(ch == 7)))
            if j + 1 < NBLK:
                sw.stats_b(j + 1)
                cur = sw.apply(j + 1)
            if j + 2 < NBLK:
                sw.load(j + 2)
            r = raw[j % 3]
            kr = 'raw%d' % (j % 3)
            c.op('act', ['pfH'], [kr], lambda e: e.copy(out=r[:, :, 1:513], in_=pf[:, 0:3, :]))
            c.op('act', ['pfH'], ['sg%d' % (j % 3)], lambda e: e.activation(out=sg[j % 3][:], in_=pf[:, 3, :], func=AF.Silu))
            if j > 0:
                rp = raw[(j - 1) % 3]
                kp = 'raw%d' % ((j - 1) % 3)
                c.op('pool', [kp], [kr], lambda e: e.tensor_copy(out=r[:, :, 0:1], in_=rp[:, :, 512:513]))
                c.op('pool', [kr], [kp], lambda e: e.tensor_copy(out=rp[:, :, 513:514], in_=r[:, :, 1:2]))
                conv(j - 1)
        c.op('pool', [], ['raw%d' % ((NBLK - 1) % 3)], lambda e: e.memset(raw[(NBLK - 1) % 3][:, :, 513:514], 0.0))
        conv(NBLK - 1)
    c.barrier()
    c.es = es
    if dbg is not None:
        c.dma('sp', ['zT'], ['dbgz'], dbg['zT'], zT[:])
        c.dma('sp', ['g0'], ['dbgg'], dbg['g0'], g0[:])
    c.dma('sp', ['zT'], ['zscr'], zscr, zT[:])
    kT = c.sb("kT", [128, 2 * L], BF16)
    l1p = c.sb("l1p", [128, 32], F32)
    rl1 = c.sb("rl1", [128, 1], F32)
    with ExitStack() as es2:
        c.es = es2
        fp = c.sb("fp", [64, 4], F32)
        c.dma('sp', [], ['fp'], fp[:], fparams)
        fb = c.sb("fb", [64, 2], F32)
        c.op('dve', ['fp'], ['fb'], lambda e: e.tensor_tensor(out=fb[:, 0:1], in0=fp[:, 0:1], in1=fp[:, 1:2], op=ALU.mult))
        c.op('dve', ['fp'], ['fb'], lambda e: e.tensor_tensor(out=fb[:, 1:2], in0=fp[:, 2:3], in1=fp[:, 3:4], op=ALU.mult))
        w1 = c.sb("w1", [33, 64], F32)
        w2 = c.sb("w2", [64, 64], F32)
        w3 = c.sb("w3", [64, 256], F32)
        c.dma('sp', [], ['w1'], w1[:], w1d)
        c.dma('sp', [], ['w2'], w2[:], w2d)
        c.dma('sp', [], ['w3'], w3[:], w3d)
        nd = c.sb("nd", [128, 1], F32)
        c.dma('sp', [], ['nd'], nd[:], negdel)
        def mk(name, shape, dt=F32):
            return [c.sb("%s_%d" % (name, dr), shape, dt) for dr in range(2)]
        zb = [[c.sb("zb%d_%d" % (dr, i), [33, 512], F32) for i in range(2)] for dr in range(2)]
        tb = [[c.sb("tb%d_%d" % (dr, i), [128, 512], F32) for i in range(2)] for dr in range(2)]
        P1 = [c.ps("P1_%d" % dr, [64, 512]) for dr in range(2)]
        P2 = [c.ps("P2_%d" % dr, [64, 512]) for dr in range(2)]
        P3 = [c.ps("P3_%d" % dr, [128, 512]) for dr in range(2)]
        a1 = mk("a1", [64, 512]); ki = mk("ki", [64, 512], I32); rr = mk("rr", [64, 512])
        h1 = mk("h1", [64, 512]); h2 = mk("h2", [64, 512]); dec = mk("dec", [128, 512]); hk = mk("hk", [128, 512])
        h2r = [[c.sb("h2r%d_%d" % (dr, i), [64, 512], F32) for i in range(2)] for dr in range(2)]

        def sin_stages(dr, P, pk, col, hout, hk_):
            A, KI, RR = a1[dr], ki[dr], rr[dr]
            ka, kk, kr_ = 'a1_%d' % dr, 'ki_%d' % dr, 'rr_%d' % dr
            return [
                lambda: c.op('act', [pk, 'fp', 'fb'], [ka],
                             lambda e: e.activation(out=A[:], in_=P[:], func=AF.Identity, scale=fp[:, 2 * col:2 * col + 1], bias=fb[:, col:col + 1])),
                lambda: c.op('dve', [ka], [kk], lambda e: e.tensor_scalar(out=KI[:], in0=A[:], scalar1=1.0 / TWO_PI, scalar2=None, op0=ALU.mult)),
                lambda: c.op('dve', [kk, ka], [kr_],
                             lambda e: e.scalar_tensor_tensor(out=RR[:], in0=KI[:], scalar=-TWO_PI, in1=A[:], op0=ALU.mult, op1=ALU.add)),
                lambda: c.op('dve', [kr_], [kr_],
                             lambda e: e.tensor_scalar(out=RR[:], in0=RR[:], scalar1=-PI_LO, scalar2=PI_LO, op0=ALU.max, op1=ALU.min)),
                lambda: c.op('act', [kr_], [hk_], lambda e: e.activation(out=hout[:], in_=RR[:], func=AF.Sin)),
            ]

        def filt_chain(dr, j):
            s = j % 2
            zsrc = zTf if dr == 0 else zTr
            Z, T = zb[dr][s], tb[dr][s]
            kz, kt_ = 'zb%d_%d' % (dr, s), 'tb%d_%d' % (dr, s)
            p1, p2, p3 = P1[dr], P2[dr], P3[dr]
            H1, H2, DEC, HK = h1[dr], h2[dr], dec[dr], hk[dr]
            it = dr * NBLK + j
            kh2 = 'h2_%d' % dr
            if filt_mode == 'h2read':
                H2 = h2r[dr][s]
                kh2 = 'h2r%d_%d' % (dr, s)
                st = [
                    lambda: (c.dma('sp', ['h2scr%d_%d' % (dr, j)], [kh2], H2[:], h2scr[dr, :, j * TB:(j + 1) * TB]),
                             c.dma('sp', [], [kt_], T[:], trow[dr:dr + 1, j * TB:(j + 1) * TB].partition_broadcast(128))),
                ]
            else:
                st = [
                    lambda: (c.dma('sp', [], [kz], Z[:], zsrc[:, j * TB:(j + 1) * TB]),
                             c.dma('sp', [], [kt_], T[:], trow[dr:dr + 1, j * TB:(j + 1) * TB].partition_broadcast(128))),
                    lambda: c.op('pe', ['w1', kz], ['P1_%d' % dr], lambda e: e.matmul(p1[:], lhsT=w1[:], rhs=Z[:], start=True, stop=True)),
                ]
                st += sin_stages(dr, p1, 'P1_%d' % dr, 0, H1, 'h1_%d' % dr)
                st += [lambda: c.op('pe', ['w2', 'h1_%d' % dr], ['P2_%d' % dr], lambda e: e.matmul(p2[:], lhsT=w2[:], rhs=H1[:], start=True, stop=True))]
                st += sin_stages(dr, p2, 'P2_%d' % dr, 1, H2, 'h2_%d' % dr)
                if filt_mode == 'h2write':
                    st += [lambda: c.dma('sp', ['h2_%d' % dr], ['h2scr%d_%d' % (dr, j)], h2scr[dr, :, j * TB:(j + 1) * TB], H2[:])]
            st += [
                lambda: c.op('pe', ['w3', kh2], ['P3_%d' % dr],
                             lambda e: e.matmul(p3[:], lhsT=w3[:, dr * 128:(dr + 1) * 128], rhs=H2[:], start=True, stop=True)),
                lambda: c.op('act', [kt_, 'nd'], ['dec_%d' % dr], lambda e: e.activation(out=DEC[:], in_=T[:], func=AF.Exp, scale=nd[:, 0:1])),
                lambda: c.op('dve', ['P3_%d' % dr, 'dec_%d' % dr], ['hk_%d' % dr], lambda e: e.tensor_tensor(out=HK[:], in0=p3[:], in1=DEC[:], op=ALU.mult)),
            ]
            if dr == 1 and j == 0:
                st += [lambda: c.op('dve', ['hk_%d' % dr], ['hk_%d' % dr], lambda e: e.memset(HK[:, 0:1], 0.0))]
            st += [
                lambda: c.op('dve', ['hk_%d' % dr], ['l1p'],
                             lambda e: e.tensor_reduce(out=l1p[:, it:it + 1], in_=HK[:], axis=AX.X, op=ALU.add, apply_absolute_value=True)),
                lambda: c.op('pool', ['hk_%d' % dr], ['kT'], lambda e: e.tensor_copy(out=kT[:, dr * L + j * TB: dr * L + (j + 1) * TB], in_=HK[:])),
            ]
            return st

        for j in range(NBLK):
            lockstep(filt_chain(0, j), filt_chain(1, j))
        l1s = c.sb("l1s", [128, 1], F32)
        c.op('dve', ['l1p'], ['l1s'], lambda e: e.tensor_reduce(out=l1s[:], in_=l1p[:], axis=AX.X, op=ALU.add))
        c.op('dve', ['l1s'], ['rl1'], lambda e: e.reciprocal(out=rl1[:], in_=l1s[:]))
    c.barrier()
    c.es = es
    if dbg is not None:
        c.dma('sp', ['kT'], ['dbgk'], dbg['kT'], kT[:])
    c.dma('sp', ['kT'], ['kscr'], kscr, kT[:])
    with ExitStack() as es3:
        c.es = es3
        Kc = c.sb("Kc", [128, 128, 128], BF16)
        Xc = c.sb("Xc", [128, 128, 128], BF16)
        kv = kscr.rearrange("c (a b) -> a c b", b=128)
        zv = zscr.rearrange("c (a b) -> a c b", b=128)
        for i in range(16):
            c.dma('sp', ['kscr'], ['Kc'], Kc[:, i * 8:(i + 1) * 8, :], kv[:, i * 8:(i + 1) * 8, :])
        for i in range(16):
            c.dma('sp', ['zscr'], ['Xc'], Xc[0:64, i * 8:(i + 1) * 8, :], zv[:, i * 8:(i + 1) * 8, :])
        FA = c.sb("FA", [128, 256], BF16)
        Fm = c.sb("Fm", [128, 3, 128], BF16)
        Gm = c.sb("Gm", [128, 2, 256], BF16)
        tw = c.sb("tw", [128, 2, 128], F32)
        c.dma('sp', [], ['FA'], FA[:], FAd)
        c.dma('sp', [], ['Fm'], Fm[:], Fd)
        c.dma('sp', [], ['Gm'], Gm[:], Gd)
        c.dma('sp', [], ['tw'], tw[:], twd)
        PA_f = c.ps("PA_f", [128, CG, 256]); PXr_f = c.ps("PXr_f", [128, CG, 128]); PXi_f = c.ps("PXi_f", [128, CG, 128])
        PA_d = c.ps("PA_d", [128, CG, 256]); PXr_d = c.ps("PXr_d", [128, CG, 128]); PXi_d = c.ps("PXi_d", [128, CG, 128])
        mf = [c.sb("mf%d" % i, [128, CG, 128], F32) for i in range(4)]
        md = [c.sb("md%d" % i, [128, CG, 128], F32) for i in range(4)]
        Br_f = c.sb("Br_f", [128, CG, 128], BF16); Bi_f = c.sb("Bi_f", [128, CG, 128], BF16)
        Br_d = c.sb("Br_d", [128, CG, 128], BF16); Bi_d = c.sb("Bi_d", [128, CG, 128], BF16)
        Kr = [c.sb("Kr%d" % i, [128, CG, 128], F32) for i in range(2)]
        Ki = [c.sb("Ki%d" % i, [128, CG, 128], F32) for i in range(2)]
        Yr = c.sb("Yr", [128, CG, 128], BF16); Yi = c.sb("Yi", [128, CG, 128], BF16)
        Dr = c.sb("Dr", [128, CG, 128], BF16); Di = c.sb("Di", [128, CG, 128], BF16)
        Yo = [c.sb("Yo%d" % i, [64, CG, 128], F32) for i in range(2)]
        twc = tw[:, 0, :].unsqueeze(1).broadcast_to([128, CG, 128])
        tws = tw[:, 1, :].unsqueeze(1).broadcast_to([128, CG, 128])

        def cmul_st(m, mk_, ar, ai, ak, br, bi, bk, outr, outi, okr, oki, conj):
            m1, m2, m3, m4 = m
            k1, k2, k3, k4 = [mk_ + str(i) for i in range(4)]
            st = [
                lambda: c.op('dve', ak + bk, [k1], lambda e: e.tensor_tensor(out=m1[:], in0=ar, in1=br, op=ALU.mult)),
                lambda: c.op('dve', ak + bk, [k2], lambda e: e.tensor_tensor(out=m2[:], in0=ai, in1=bi, op=ALU.mult)),
                lambda: c.op('dve', ak + bk, [k3], lambda e: e.tensor_tensor(out=m3[:], in0=ar, in1=bi, op=ALU.mult)),
                lambda: c.op('dve', ak + bk, [k4], lambda e: e.tensor_tensor(out=m4[:], in0=ai, in1=br, op=ALU.mult)),
            ]
            if not conj:
                st += [lambda: c.op('pool', [k1, k2], okr, lambda e: e.tensor_tensor(out=outr, in0=m1[:], in1=m2[:], op=ALU.subtract)),
                       lambda: c.op('pool', [k3, k4], oki, lambda e: e.tensor_tensor(out=outi, in0=m3[:], in1=m4[:], op=ALU.add))]
            else:
                st += [lambda: c.op('pool', [k1, k2], okr, lambda e: e.tensor_tensor(out=outr, in0=m1[:], in1=m2[:], op=ALU.add)),
                       lambda: c.op('pool', [k3, k4], oki, lambda e: e.tensor_tensor(out=outi, in0=m4[:], in1=m3[:], op=ALU.subtract))]
            return st

        def fwd_st(src, ksrc, K, gi, PA, kpa, PXr, kxr, PXi, kxi, m, mk_, Br, kbr, Bi, kbi):
            def stageA():
                for cc in range(CG):
                    ch = gi * CG + cc
                    c.op('pe', [ksrc, 'FA'], [kpa],
                         lambda e, cc=cc, ch=ch: e.matmul(PA[:, cc, :], lhsT=src[0:K, ch, :], rhs=FA[0:K, :], start=True, stop=True))
            Bf = Br[:].rearrange("p c k -> p (c k)")
            Bg = Bi[:].rearrange("p c k -> p (c k)")
            xr = PXr[:].rearrange("p c k -> p (c k)")
            xi = PXi[:].rearrange("p c k -> p (c k)")

            def stageB():
                c.op('pe', ['Fm', kbr], [kxr], lambda e: e.matmul(xr, lhsT=Fm[:, 0, :], rhs=Bf, start=True, stop=False))
                c.op('pe', ['Fm', kbi], [kxr], lambda e: e.matmul(xr, lhsT=Fm[:, 1, :], rhs=Bg, start=False, stop=True))
                c.op('pe', ['Fm', kbi], [kxi], lambda e: e.matmul(xi, lhsT=Fm[:, 0, :], rhs=Bg, start=True, stop=False))
                c.op('pe', ['Fm', kbr], [kxi], lambda e: e.matmul(xi, lhsT=Fm[:, 2, :], rhs=Bf, start=False, stop=True))
            return ([stageA] + cmul_st(m, mk_, PA[:, :, 0:128], PA[:, :, 128:256], [kpa], twc, tws, ['tw'], Br[:], Bi[:], [kbr], [kbi], True)
                    + [stageB])

        def filt_fft(gi):
            s = gi % 2
            st = fwd_st(Kc, 'Kc', 128, gi, PA_f, 'PA_f', PXr_f, 'PXr_f', PXi_f, 'PXi_f', mf, 'mf', Br_f, 'Br_f', Bi_f, 'Bi_f')
            st += [lambda: c.op('act', ['PXr_f'], ['Kr%d' % s], lambda e: e.copy(out=Kr[s][:], in_=PXr_f[:])),
                   lambda: c.op('act', ['PXi_f'], ['Ki%d' % s], lambda e: e.copy(out=Ki[s][:], in_=PXi_f[:]))]
            return st

        yv = yscr.rearrange("c (a b) -> a c b", b=128)

        def data_fft(gi):
            s = gi % 2
            st = fwd_st(Xc, 'Xc', 64, gi, PA_d, 'PA_d', PXr_d, 'PXr_d', PXi_d, 'PXi_d', md, 'md', Br_d, 'Br_d', Bi_d, 'Bi_d')
            st += cmul_st(md, 'md', PXr_d[:], PXi_d[:], ['PXr_d', 'PXi_d'], Kr[s][:], Ki[s][:], ['Kr%d' % s, 'Ki%d' % s], Yr[:], Yi[:], ['Yr'], ['Yi'], False)
            PC = PA_d

            def stageC():
                for cc in range(CG):
                    c.op('pe', ['Yr', 'Gm'], ['PA_d'],
                         lambda e, cc=cc: e.matmul(PC[:, cc, :], lhsT=Yr[:, cc, :], rhs=Gm[:, 0, :], start=True, stop=False))
                    c.op('pe', ['Yi', 'Gm'], ['PA_d'],
                         lambda e, cc=cc: e.matmul(PC[:, cc, :], lhsT=Yi[:, cc, :], rhs=Gm[:, 1, :], start=False, stop=True))
            st += [stageC]
            st += cmul_st(md, 'md', PC[:, :, 0:128], PC[:, :, 128:256], ['PA_d'], twc, tws, ['tw'], Dr[:], Di[:], ['Dr'], ['Di'], False)
            pd = PXr_d[0:64].rearrange("p c k -> p (c k)")
            yo = Yo[s]
            ky = 'Yo%d' % s

            def stageD():
                c.op('pe', ['Fm', 'Dr'], ['PXr_d'],
                     lambda e: e.matmul(pd, lhsT=Fm[:, 0, 0:64], rhs=Dr[:].rearrange("p c k -> p (c k)"), start=True, stop=False))
                c.op('pe', ['Fm', 'Di'], ['PXr_d'],
                     lambda e: e.matmul(pd, lhsT=Fm[:, 2, 0:64], rhs=Di[:].rearrange("p c k -> p (c k)"), start=False, stop=True))
            st += [stageD,
                   lambda: c.op('act', ['PXr_d'], [ky], lambda e: e.activation(out=yo[:], in_=PXr_d[0:64], func=AF.Copy, scale=1.0 / 16384.0)),
                   lambda: c.dma('sp', [ky], ['yscr%d' % gi], yv[:, gi * CG:(gi + 1) * CG, :], yo[:])]
            return st

        lockstep(filt_fft(0))
        for gi in range(NG):
            if gi + 1 < NG:
                lockstep(data_fft(gi), filt_fft(gi + 1))
            else:
                lockstep(data_fft(gi))
    c.barrier()
    c.es = es
    with ExitStack() as es4:
        c.es = es4
        dd = c.sb("dd", [128, 1], F32)
        c.dma('sp', [], ['dd'], dd[:], hyd)
        yt = [c.sb("yt%d" % i, [128, 2048], F32) for i in range(2)]
        zd = c.sb("zd", [128, 2048], F32)
        ob = [c.sb("ob%d" % i, [128, 2048], BF16) for i in range(2)]
        allscr = ['yscr%d' % gi for gi in range(NG)]
        for q in range(4):
            s = q % 2
            sl = slice(q * 2048, (q + 1) * 2048)
            c.dma('sp', allscr, ['yt%d' % s], yt[s][:], yscr[:, sl])
            if dbg is not None:
                c.dma('sp', ['yt%d' % s], ['dbgy%d' % q], dbg['yc'][:, sl], yt[s][:])
            c.op('dve', ['zT', 'dd'], ['zd'], lambda e: e.tensor_scalar(out=zd[:], in0=zT[:, sl], scalar1=dd[:, 0:1], scalar2=None, op0=ALU.mult))
            c.op('dve', ['yt%d' % s, 'rl1', 'zd'], ['yt%d' % s],
                 lambda e: e.scalar_tensor_tensor(out=yt[s][:], in0=yt[s][:], scalar=rl1[:, 0:1], in1=zd[:], op0=ALU.mult, op1=ALU.add))
            c.op('dve', ['yt%d' % s, 'g0'], ['ob%d' % s], lambda e: e.tensor_tensor(out=ob[s][:], in0=yt[s][:], in1=g0[:, sl], op=ALU.mult))
            c.dma('sp', ['ob%d' % s], ['yb_out'], yb_out[:, sl], ob[s][:])
    c.barrier()
    c.es = es

import numpy as np

NT = 16
TS = 2048


def load_weights_plain(c, wd, wb, ncols, tag):
    stg = [c.sb("wstg%s%d" % (tag, i), [128, ncols], F32) for i in range(2)]
    for ch in range(8):
        s = stg[ch % 2]
        k = "wstg%s%d" % (tag, ch % 2)
        c.dma('sp', [], [k], s[:], wd[ch * 128:(ch + 1) * 128, :])
        c.op('dve', [k], ['wb' + tag], lambda e, s=s, ch=ch: e.tensor_copy(out=wb[:, ch, :], in_=s[:]))


def outproj_tile(c, i, yT, ykey, wo, po, xin_d, xt, kx):
    for half in range(2):
        for ech in range(8):
            c.op('pe', [ykey, 'wbO'], ['po'],
                 lambda e, half=half, ech=ech: e.matmul(po[:, half * 512:(half + 1) * 512], lhsT=yT[:, ech, i * 128:(i + 1) * 128],
                                                        rhs=wo[:, ech, half * 512:(half + 1) * 512], start=(ech == 0), stop=(ech == 7)))
    c.dma('sp', [], [kx], xt[:], xin_d[i * 128:(i + 1) * 128, :])
    c.op('dve', ['po', kx], [kx], lambda e: e.tensor_tensor(out=xt[:], in0=po[:], in1=xt[:], op=ALU.add))


def rstd_tile(c, xt, kx, junk, ssq, sd, rs, n):
    c.op('act', [kx], ['junk', 'ssq'], lambda e: e.activation(out=junk[:], in_=xt[:], func=AF.Square, accum_out=ssq[:, 0:1]))
    c.op('act', ['ssq'], ['sd'], lambda e: e.activation(out=sd[:], in_=ssq[:], func=AF.Sqrt, scale=1.0 / n, bias=EPS))
    c.op('dve', ['sd'], ['rs'], lambda e: e.reciprocal(out=rs[:], in_=sd[:]))


def emit_L2(c, es, y0T_d, xs_d, wout_d, win_d, onorm_d, cos_d, sin_d, identd, x1_d, latT_d, gcT_d, do_gate=True):
    with ExitStack() as es1:
        c.es = es1
        ident = c.sb("ident2", [128, 128], BF16)
        c.dma('sp', [], ['ident2'], ident[:], identd)
        yT = c.sb("y0T", [128, 8, TS], BF16)
        c.dma('sp', [], ['y0T'], yT[:], y0T_d.rearrange("(ch p) t -> p ch t", p=128))
        wo = c.sb("wbO", [128, 8, 1024], BF16)
        load_weights_plain(c, wout_d, wo, 1024, 'O')
        gain = c.sb("gainI", [128, 8], F32)
        c.dma('sp', [], ['gainI'], gain[:], onorm_d)
        wi = c.sb("wbI", [128, 8, 1696], BF16)
        load_weights_scaled(c, win_d, gain, 'gainI', wi, 1696, 'I')
        cosS = c.sb("cos2", [128, NT, 16], F32)
        sinS = c.sb("sin2", [128, NT, 16], F32)
        c.dma('sp', [], ['cos2'], cosS[:], cos_d.rearrange("p (t j) -> p t j", j=16))
        c.dma('sp', [], ['sin2'], sinS[:], sin_d.rearrange("p (t j) -> p t j", j=16))
        h1T = c.sb("h1T", [128, 8, TS], BF16)
        po = c.ps("po", [128, 1024])
        ptr = c.ps("ptr2", [128, 8, 128], BF16)
        pl = c.ps("pl", [128, 1024])
        pg = c.ps("pg", [128, 512])
        plt = c.ps("plt", [128, 6, 128], BF16)
        ltT = [c.sb("ltT%d" % i, [128, 6, 128], BF16) for i in range(2)]
        xt = [c.sb("xt%d" % i, [128, 1024], F32) for i in range(2)]
        junk = c.sb("junk", [128, 1024], F32)
        ssq = c.sb("ssq", [128, 1], F32)
        sd = c.sb("sd", [128, 1], F32)
        rs = c.sb("rs", [128, 1], F32)
        h1 = c.sb("h1", [128, 1024], BF16)
        ss2 = c.sb("ss2", [128, 2], F32)
        sd2 = c.sb("sd2", [128, 2], F32)
        rs2 = c.sb("rs2", [128, 2], F32)
        latn = [c.sb("latn%d" % i, [128, 672], BF16) for i in range(2)]
        k1 = c.sb("k1", [128, 32], F32)
        k2 = c.sb("k2", [128, 32], F32)
        gt = [c.sb("gt%d" % i, [128, 512], BF16) for i in range(2)]
        for i in range(NT):
            s = i % 2
            kx = 'xt%d' % s
            outproj_tile(c, i, yT, 'y0T', wo, po, xs_d, xt[s], kx)
            c.dma('sp', [kx], ['x1_d'], x1_d[i * 128:(i + 1) * 128, :], xt[s][:])
            rstd_tile(c, xt[s], kx, junk, ssq, sd, rs, D)
            c.op('act', [kx, 'rs'], ['h1'], lambda e: e.activation(out=h1[:], in_=xt[s][:], func=AF.Identity, scale=rs[:, 0:1]))
            for ch in range(8):
                c.op('pe', ['h1', 'ident2'], ['ptr2'],
                     lambda e, ch=ch: e.transpose(out=ptr[:, ch, :], in_=h1[:, ch * 128:(ch + 1) * 128], identity=ident[:]))
            c.op('dve', ['ptr2'], ['h1T'], lambda e: e.tensor_copy(out=h1T[:, :, i * 128:(i + 1) * 128], in_=ptr[:]))
            for (lo, hi) in ((0, 512), (512, 672)):
                for ch in range(8):
                    c.op('pe', ['h1T', 'wbI'], ['pl'],
                         lambda e, ch=ch, lo=lo, hi=hi: e.matmul(pl[:, lo:hi], lhsT=h1T[:, ch, i * 128:(i + 1) * 128], rhs=wi[:, ch, lo:hi],
                                                                 start=(ch == 0), stop=(ch == 7)))
            c.op('act', ['pl'], ['junk', 'ss2'], lambda e: e.activation(out=junk[:, 0:384], in_=pl[:, 0:384], func=AF.Square, accum_out=ss2[:, 0:1]))
            c.op('act', ['pl'], ['junk', 'ss2'], lambda e: e.activation(out=junk[:, 384:640], in_=pl[:, 384:640], func=AF.Square, accum_out=ss2[:, 1:2]))
            c.op('act', ['ss2'], ['sd2'], lambda e: e.activation(out=sd2[:, 0:1], in_=ss2[:, 0:1], func=AF.Sqrt, scale=1.0 / 384, bias=EPS))
            c.op('act', ['ss2'], ['sd2'], lambda e: e.activation(out=sd2[:, 1:2], in_=ss2[:, 1:2], func=AF.Sqrt, scale=1.0 / 256, bias=EPS))
            c.op('dve', ['sd2'], ['rs2'], lambda e: e.reciprocal(out=rs2[:], in_=sd2[:]))
            ln = latn[s]
            kl = 'latn%d' % s
            c.op('act', ['pl', 'rs2'], [kl], lambda e: e.activation(out=ln[:, 0:384], in_=pl[:, 0:384], func=AF.Identity, scale=rs2[:, 0:1]))
            c.op('act', ['pl', 'rs2'], [kl], lambda e: e.activation(out=ln[:, 384:640], in_=pl[:, 384:640], func=AF.Identity, scale=rs2[:, 1:2]))
            krv = pl[:, 640:672].rearrange("p (a j) -> p a j", a=2)
            cb = cosS[:, i, :].unsqueeze(1).broadcast_to([128, 2, 16])
            sb_ = sinS[:, i, :].unsqueeze(1).broadcast_to([128, 2, 16])
            c.op('dve', ['pl', 'cos2'], ['k1'], lambda e: e.tensor_tensor(out=k1[:].rearrange("p (a j) -> p a j", a=2), in0=krv, in1=cb, op=ALU.mult))
            c.op('dve', ['pl', 'sin2'], ['k2'], lambda e: e.tensor_tensor(out=k2[:].rearrange("p (a j) -> p a j", a=2), in0=krv, in1=sb_, op=ALU.mult))
            c.op('dve', ['k1', 'k2'], [kl], lambda e: e.tensor_tensor(out=ln[:, 640:656], in0=k1[:, 0:16], in1=k2[:, 16:32], op=ALU.subtract))
            c.op('dve', ['k1', 'k2'], [kl], lambda e: e.tensor_tensor(out=ln[:, 656:672], in0=k2[:, 0:16], in1=k1[:, 16:32], op=ALU.add))
            for ch in range(6):
                wdt = 128 if ch < 5 else 32
                c.op('pe', [kl, 'ident2'], ['plt'],
                     lambda e, ch=ch, wdt=wdt: e.transpose(out=plt[0:wdt, ch, :], in_=ln[:, ch * 128:ch * 128 + wdt], identity=ident[:]))
            lt = ltT[s]
            klt = 'ltT%d' % s
            c.op('dve', ['plt'], [klt], lambda e: e.tensor_copy(out=lt[:, 0:5, :], in_=plt[:, 0:5, :]))
            c.op('dve', ['plt'], [klt], lambda e: e.tensor_copy(out=lt[0:32, 5, :], in_=plt[0:32, 5, :]))
            c.dma('sp', [klt], ['latT_d'], latT_d[0:640, i * 128:(i + 1) * 128].rearrange("(ch p) t -> p ch t", p=128), lt[:, 0:5, :])
            c.dma('sp', [klt], ['latT_d'], latT_d[640:672, i * 128:(i + 1) * 128], lt[0:32, 5, :])
        n = 0
        for ec in (range(8) if do_gate else []):
            for blk in range(4):
                for ch in range(8):
                    c.op('pe', ['h1T', 'wbI'], ['pg'],
                         lambda e, ch=ch: e.matmul(pg[:], lhsT=wi[:, ch, 672 + ec * 128:672 + (ec + 1) * 128], rhs=h1T[:, ch, blk * 512:(blk + 1) * 512],
                                                   start=(ch == 0), stop=(ch == 7)))
                g = gt[n % 2]
                kg = 'gt%d' % (n % 2)
                c.op('act', ['pg'], [kg], lambda e: e.activation(out=g[:], in_=pg[:], func=AF.Silu))
                c.dma('sp', [kg], ['gcT_d'], gcT_d[ec * 128:(ec + 1) * 128, blk * 512:(blk + 1) * 512], g[:])
                n += 1
    c.barrier()
    c.es = es


def emit_L4(c, es, ocT_d, gcT_d, x1_d, wout_d, fnorm_d, out_d):
    with ExitStack() as es1:
        c.es = es1
        yT = c.sb("ycT", [128, 8, TS], BF16)
        gT = c.sb("gcT", [128, 8, TS], BF16)
        c.dma('sp', [], ['ycT'], yT[:], ocT_d.rearrange("(ch p) t -> p ch t", p=128))
        c.dma('sp', [], ['gcT'], gT[:], gcT_d.rearrange("(ch p) t -> p ch t", p=128))
        for ch in range(8):
            eng = 'dve' if ch % 2 == 0 else 'pool'
            c.op(eng, ['ycT', 'gcT'], ['ycT'], lambda e, ch=ch: e.tensor_tensor(out=yT[:, ch, :], in0=yT[:, ch, :], in1=gT[:, ch, :], op=ALU.mult))
        wo = c.sb("wbO", [128, 8, 1024], BF16)
        load_weights_plain(c, wout_d, wo, 1024, 'O')
        fn = c.sb("fn", [128, 1024], F32)
        c.dma('sp', [], ['fn'], fn[:], fnorm_d.partition_broadcast(128))
        po = c.ps("po", [128, 1024])
        xt = [c.sb("xt%d" % i, [128, 1024], F32) for i in range(2)]
        ot = [c.sb("ot%d" % i, [128, 1024], F32) for i in range(2)]
        junk = c.sb("junk", [128, 1024], F32)
        ssq = c.sb("ssq", [128, 1], F32)
        sd = c.sb("sd", [128, 1], F32)
        rs = c.sb("rs", [128, 1], F32)
        for i in range(NT):
            s = i % 2
            kx = 'xt%d' % s
            outproj_tile(c, i, yT, 'ycT', wo, po, x1_d, xt[s], kx)
            rstd_tile(c, xt[s], kx, junk, ssq, sd, rs, D)
            c.op('dve', [kx, 'rs', 'fn'], ['ot%d' % s],
                 lambda e: e.scalar_tensor_tensor(out=ot[s][:], in0=xt[s][:], scalar=rs[:, 0:1], in1=fn[:], op0=ALU.mult, op1=ALU.mult))
            c.dma('sp', ['ot%d' % s], ['out_d'], out_d[i * 128:(i + 1) * 128, :], ot[s][:])
    c.barrier()
    c.es = es

import numpy as np

L = 8192
NQB = 16
SCALE3 = 96.0 ** -0.5


def emit_attn_L1(c, es, latT_d, wq_d, wk_d, wv_d, qgain_d, kvgain_d, cosT_d, sinT_d, swapd, oc_d, nqb=NQB, dbg=None, cq_d=None):
    if cq_d is None:
        cq_d = latT_d[0:384, :]
    ckT = c.sb("ckT", [128, 2, L], BF16)
    c.dma('sp', [], ['ckT'], ckT[:], latT_d[384:640, :].rearrange("(ch p) t -> p ch t", p=128))
    qg = c.sb("qg", [128, 3], F32)
    kg = c.sb("kg", [128, 2], F32)
    c.dma('sp', [], ['qg'], qg[:], qgain_d)
    c.dma('sp', [], ['kg'], kg[:], kvgain_d)
    wq = c.sb("wq", [128, 3, 4, 96], BF16)
    wqr = c.sb("wqr", [128, 3, 4, 96], BF16)
    c.op('pool', [], ['wqr'], lambda e: e.memset(wqr[:], 0.0))
    wk = c.sb("wk", [128, 2, 4, 64], BF16)
    wv = c.sb("wv", [128, 2, 4, 64], BF16)
    swp = c.sb("swp3", [128, 128], F32)
    c.dma('sp', [], ['swp3'], swp[:], swapd)
    with ExitStack() as es0:
        c.es = es0
        st = c.sb("wst3", [128, 4 * 96], F32)
        for ch in range(3):
            c.dma('sp', [], ['wst3'], st[:], wq_d[ch * 128:(ch + 1) * 128].rearrange("p h j -> p (h j)"))
            sv = st[:].rearrange("p (h j) -> p h j", h=4)
            c.op('dve', ['wst3', 'qg'], ['wq'], lambda e, ch=ch: e.tensor_scalar(out=wq[:, ch], in0=sv, scalar1=qg[:, ch:ch + 1], scalar2=None, op0=ALU.mult))
            c.op('dve', ['wst3', 'qg'], ['wqr'],
                 lambda e, ch=ch: e.tensor_scalar(out=wqr[:, ch, :, 64:80], in0=sv[:, :, 80:96], scalar1=qg[:, ch:ch + 1], scalar2=-1.0, op0=ALU.mult, op1=ALU.mult))
            c.op('dve', ['wst3', 'qg'], ['wqr'],
                 lambda e, ch=ch: e.tensor_scalar(out=wqr[:, ch, :, 80:96], in0=sv[:, :, 64:80], scalar1=qg[:, ch:ch + 1], scalar2=None, op0=ALU.mult))
        for ch in range(2):
            c.dma('sp', [], ['wst3'], st[:, 0:256], wk_d[ch * 128:(ch + 1) * 128].rearrange("p h j -> p (h j)"))
            sv = st[:, 0:256].rearrange("p (h j) -> p h j", h=4)
            c.op('dve', ['wst3', 'kg'], ['wk'], lambda e, ch=ch: e.tensor_scalar(out=wk[:, ch], in0=sv, scalar1=kg[:, ch:ch + 1], scalar2=None, op0=ALU.mult))
        for ch in range(2):
            c.dma('sp', [], ['wst3'], st[:, 0:256], wv_d[ch * 128:(ch + 1) * 128].rearrange("p h j -> p (h j)"))
            sv = st[:, 0:256].rearrange("p (h j) -> p h j", h=4)
            c.op('dve', ['wst3', 'kg'], ['wv'], lambda e, ch=ch: e.tensor_scalar(out=wv[:, ch], in0=sv, scalar1=kg[:, ch:ch + 1], scalar2=None, op0=ALU.mult))
    c.barrier()
    c.es = es
    KT = [c.sb("KT3%d" % h, [96, L], BF16) for h in range(2)]
    VA = c.sb("VA3", [128, 64, 128], BF16)
    VB = c.sb("VB3", [128, 64, 128], BF16)
    c.op('pool', [], ['VA3'], lambda e: e.memset(VA[:, :, 64:128], 1.0))
    c.op('pool', [], ['VB3'], lambda e: e.memset(VB[:, :, 0:64], 1.0))
    for h in range(2):
        c.dma('sp', [], ['KT3%d' % h], KT[h][64:96, :], latT_d[640:672, :])
    for pair in range(2):
        with ExitStack() as esA:
            c.es = esA
            pk = c.ps("pk", [128, 512])
            pv = c.ps("pv", [128, 4, 128])
            for h in range(2):
                hh = pair * 2 + h
                for blk in range(16):
                    for ch in range(2):
                        c.op('pe', ['wk', 'ckT'], ['pk'],
                             lambda e, ch=ch: e.matmul(pk[0:64, :], lhsT=wk[:, ch, hh, :], rhs=ckT[:, ch, blk * 512:(blk + 1) * 512],
                                                       start=(ch == 0), stop=(ch == 1)))
                    eng = 'act' if blk % 2 == 0 else 'dve'
                    if eng == 'act':
                        c.op('act', ['pk'], ['KT3%d' % h], lambda e: e.copy(out=KT[h][0:64, blk * 512:(blk + 1) * 512], in_=pk[0:64, :]))
                    else:
                        c.op('dve', ['pk'], ['KT3%d' % h], lambda e: e.tensor_copy(out=KT[h][0:64, blk * 512:(blk + 1) * 512], in_=pk[0:64, :]))
            for g4 in range(16):
                for tt in range(4):
                    kt = g4 * 4 + tt
                    for ch in range(2):
                        c.op('pe', ['wv', 'ckT'], ['pv'],
                             lambda e, ch=ch, tt=tt, kt=kt: e.matmul(pv[:, tt, :], lhsT=ckT[:, ch, kt * 128:(kt + 1) * 128],
                                                                     rhs=wv[:, ch, pair * 2:pair * 2 + 2, :].rearrange("p h j -> p (h j)"),
                                                                     start=(ch == 0), stop=(ch == 1)))
                c.op('act', ['pv'], ['VA3'], lambda e: e.copy(out=VA[:, g4 * 4:(g4 + 1) * 4, 0:64], in_=pv[:, :, 0:64]))
                c.op('dve', ['pv'], ['VB3'], lambda e: e.tensor_copy(out=VB[:, g4 * 4:(g4 + 1) * 4, 64:128], in_=pv[:, :, 64:128]))
        c.barrier()
        c.es = es
        if dbg is not None and pair == 0:
            c.dma('sp', ['KT30'], ['dbgK'], dbg['KT'], KT[0][:])
            c.dma('sp', ['VB3'], ['dbgV'], dbg['VB'], VB[:].rearrange("p t j -> p (t j)"))
        with ExitStack() as esB:
            c.es = esB
            PS = [c.ps("PS%d" % i, [128, 2, 512]) for i in range(3)]
            PO = c.ps("PO", [128, 2, 512])
            PQ = PS[0][:, 0, :]
            PQR = PS[1][:, 0, :]
            PS2 = PS[2][:, 0, :]
            PT = [c.sb("PT%d" % i, [128, 2, 512], BF16) for i in range(3)]
            S_sb = c.sb("S_sb", [128, 512], F32)
            O_sb = c.sb("O_sb", [128, 512], F32)
            rinv = c.sb("rinv", [128, 512], F32)
            yo = [c.sb("yo%d" % i, [128, 512], BF16) for i in range(2)]
            cq = [c.sb("cq%d" % i, [128, 3, 512], BF16) for i in range(2)]
            cs = [c.sb("cs%d" % i, [128, 2, 512], F32) for i in range(2)]
            QT = [c.sb("QT3%d" % h, [96, nqb * 512], BF16) for h in range(2)]
            m1 = c.sb("m1", [128, 512], F32)
            m2 = c.sb("m2", [128, 512], F32)

            def loadq(qb):
                s = qb % 2
                c.dma('sp', [], ['cq%d' % s], cq[s][:], cq_d[:, qb * 512:(qb + 1) * 512].rearrange("(ch p) t -> p ch t", p=128))
                c.dma('sp', [], ['cs%d' % s], cs[s][64:96, 0, :], cosT_d[:, qb * 512:(qb + 1) * 512])
                c.dma('sp', [], ['cs%d' % s], cs[s][64:96, 1, :], sinT_d[:, qb * 512:(qb + 1) * 512])

            def projq(qb):
                s = qb % 2
                for h in range(2):
                    hh = pair * 2 + h
                    q = QT[h][:, qb * 512:(qb + 1) * 512]
                    kq = 'QT3%d' % h
                    for ch in range(3):
                        c.op('pe', ['wq', 'cq%d' % s], ['PS0'],
                             lambda e, ch=ch: e.matmul(PQ[0:96, :], lhsT=wq[:, ch, hh, :], rhs=cq[s][:, ch, :], start=(ch == 0), stop=(ch == 2)))
                    for ch in range(3):
                        c.op('pe', ['wqr', 'cq%d' % s], ['PS1'],
                             lambda e, ch=ch: e.matmul(PQR[0:96, :], lhsT=wqr[:, ch, hh, :], rhs=cq[s][:, ch, :], start=(ch == 0), stop=(ch == 2)))
                    c.op('dve', ['PS0'], [kq], lambda e: e.tensor_copy(out=q[0:64, :], in_=PQ[0:64, :]))
                    c.op('dve', ['PS0', 'cs%d' % s], ['m1'], lambda e: e.tensor_tensor(out=m1[64:96, :], in0=PQ[64:96, :], in1=cs[s][64:96, 0, :], op=ALU.mult))
                    c.op('dve', ['PS1', 'cs%d' % s], ['m2'], lambda e: e.tensor_tensor(out=m2[64:96, :], in0=PQR[64:96, :], in1=cs[s][64:96, 1, :], op=ALU.mult))
                    c.op('pool', ['m1', 'm2'], [kq], lambda e: e.tensor_tensor(out=q[64:96, :], in0=m1[64:96, :], in1=m2[64:96, :], op=ALU.add))

            def S(qb, kt):
                s = kt % 3
                for h in range(2):
                    c.op('pe', ['KT3%d' % h, 'QT3%d' % h], ['PS%d' % s],
                         lambda e, h=h: e.matmul(PS[s][:, h, :], lhsT=KT[h][:, kt * 128:(kt + 1) * 128], rhs=QT[h][:, qb * 512:(qb + 1) * 512],
                                                 start=True, stop=True))

            loadq(0)
            for qb in range(nqb):
                if qb + 1 < nqb:
                    loadq(qb + 1)
                projq(qb)
            S(0, 0)
            S(0, 1)
            for qb in range(nqb):
                for kt in range(64):
                    s = kt % 3
                    if kt + 2 < 64:
                        S(qb, kt + 2)
                    c.op('act', ['PS%d' % s], ['PT%d' % s], lambda e: e.activation(out=PT[s][:], in_=PS[s][:], func=AF.Exp, scale=SCALE3))
                    c.op('pe', ['VA3', 'PT%d' % s], ['PO'],
                         lambda e: e.matmul(PO[:, 0, :], lhsT=VA[:, kt, :], rhs=PT[s][:, 0, :], start=(kt == 0), stop=(kt == 63)))
                    c.op('pe', ['VB3', 'PT%d' % s], ['PO'],
                         lambda e: e.matmul(PO[:, 1, :], lhsT=VB[:, kt, :], rhs=PT[s][:, 1, :], start=(kt == 0), stop=(kt == 63)))
                if qb + 1 < nqb:
                    S(qb + 1, 0)
                    S(qb + 1, 1)
                c.op('dve', ['PO'], ['S_sb'], lambda e: e.tensor_copy(out=S_sb[64:128, :], in_=PO[64:128, 0, :]))
                c.op('dve', ['PO'], ['S_sb'], lambda e: e.tensor_copy(out=S_sb[0:64, :], in_=PO[0:64, 1, :]))
                c.op('dve', ['PO'], ['O_sb'], lambda e: e.tensor_copy(out=O_sb[0:64, :], in_=PO[0:64, 0, :]))
                c.op('dve', ['PO'], ['O_sb'], lambda e: e.tensor_copy(out=O_sb[64:128, :], in_=PO[64:128, 1, :]))
                c.op('pe', ['swp3', 'S_sb'], ['PS2'], lambda e: e.matmul(PS2, lhsT=swp[:], rhs=S_sb[:], start=True, stop=True))
                c.op('dve', ['PS2'], ['rinv'], lambda e: e.reciprocal(out=rinv[:], in_=PS2))
                y = yo[qb % 2]
                ky = 'yo%d' % (qb % 2)
                c.op('dve', ['O_sb', 'rinv'], [ky], lambda e: e.tensor_tensor(out=y[:], in0=O_sb[:], in1=rinv[:], op=ALU.mult))
                c.dma('sp', [ky], ['oc_d'], oc_d[pair, :, qb * 512:(qb + 1) * 512], y[:])
        c.barrier()
        c.es = es

import numpy as np


def emit_select(c, es, sel_d, x1s, gcT, latT, x1_own, gc_own, cq_own):
    with ExitStack() as es1:
        c.es = es1
        sel = c.sb("sel", [128, 4], F32)
        c.dma('sp', [], ['sel'], sel[:], sel_d)
        xa = [c.sb("xa%d" % i, [128, 4, 1024], F32) for i in range(2)]
        xo = [c.sb("xo%d" % i, [128, 1024], F32) for i in range(2)]
        xv = x1s.rearrange("(ts i p) d -> i p ts d", ts=4, p=128)
        for i in range(16):
            s = i % 2
            ka, ko = 'xa%d' % s, 'xo%d' % s
            c.dma('sp', [], [ka], xa[s][:], xv[i])
            c.op('dve', [ka, 'sel'], [ko], lambda e: e.tensor_scalar(out=xo[s][:], in0=xa[s][:, 0, :], scalar1=sel[:, 0:1], scalar2=None, op0=ALU.mult))
            for ts in range(1, 4):
                c.op('dve', [ka, 'sel', ko], [ko],
                     lambda e, ts=ts: e.scalar_tensor_tensor(out=xo[s][:], in0=xa[s][:, ts, :], scalar=sel[:, ts:ts + 1], in1=xo[s][:], op0=ALU.mult, op1=ALU.add))
            c.dma('sp', [ko], ['x1_own'], x1_own[i * 128:(i + 1) * 128, :], xo[s][:])
        ga = [c.sb("ga%d" % i, [128, 4, 2048], BF16) for i in range(2)]
        go = [c.sb("go%d" % i, [128, 2048], BF16) for i in range(2)]
        jobs = [(gcT[ch * 128:(ch + 1) * 128, :], gc_own[ch * 128:(ch + 1) * 128, :]) for ch in range(8)]
        jobs += [(latT[ch * 128:(ch + 1) * 128, :], cq_own[ch * 128:(ch + 1) * 128, :]) for ch in range(3)]
        for n, (src, dst) in enumerate(jobs):
            s = n % 2
            ka, ko = 'ga%d' % s, 'go%d' % s
            c.dma('sp', [], [ka], ga[s][:], src.rearrange("p (ts t) -> p ts t", ts=4))
            c.op('dve', [ka, 'sel'], [ko], lambda e: e.tensor_scalar(out=go[s][:], in0=ga[s][:, 0, :], scalar1=sel[:, 0:1], scalar2=None, op0=ALU.mult))
            for ts in range(1, 4):
                c.op('dve', [ka, 'sel', ko], [ko],
                     lambda e, ts=ts: e.scalar_tensor_tensor(out=go[s][:], in0=ga[s][:, ts, :], scalar=sel[:, ts:ts + 1], in1=go[s][:], op0=ALU.mult, op1=ALU.add))
            c.dma('sp', [ko], ['own%d' % n], dst, go[s][:])
    c.barrier()
    c.es = es


_CACHE = {}


def _dram(nc, kind, n, shp, dt=F32):
    return nc.dram_tensor(n, list(shp), dt, kind=kind).ap()


def build_L1():
    nc = bass.Bass("TRN2", target_bir_lowering=False)
    I = lambda n, s, dt=F32: _dram(nc, "ExternalInput", n, s, dt)
    O = lambda n, s, dt=F32: _dram(nc, "ExternalOutput", n, s, dt)
    S = lambda n, s, dt=F32: _dram(nc, "Internal", n, s, dt)
    xT = I("xT", [1024, 8192]); enorm = I("enorm", [128, 8])
    w_tm = I("w_tm", [1024, 320]); w_ga = I("w_ga", [1024, 128]); qkgain = I("qkgain", [1, 256])
    cos_tm = I("cos_tm", [128, 2048]); sin_tm = I("sin_tm", [128, 2048]); swapd = I("swapd", [128, 128]); identd = I("identd", [128, 128], BF16)
    w_hy = I("w_hy", [1024, 512]); convw = I("convw", [128, 12]); fparams = I("fparams", [64, 4])
    w1d = I("w1d", [33, 64]); w2d = I("w2d", [64, 64]); w3d = I("w3d", [64, 256]); zTf = I("zTf", [33, 8192]); zTr = I("zTr", [33, 8192]); trow = I("trow", [2, 8192])
    negdel = I("negdel", [128, 1]); hyd = I("hyd", [128, 1]); FAd = I("FAd", [128, 256], BF16); Fd = I("Fd", [128, 3, 128], BF16)
    Gd = I("Gd", [128, 2, 256], BF16); twd = I("twd", [128, 2, 128])
    hscr = S("hscr", [128, 8, 8192], BF16); kscr = S("kscr", [128, 16384], BF16); zscr = S("zscr", [128, 8192], BF16); yscr = S("yscr", [128, 8192])
    ya = O("ya", [128, 8192], BF16); yb = O("yb", [128, 8192], BF16)
    with ExitStack() as es:
        c = Ctx(nc, es)
        with ExitStack() as esA:
            c.es = esA
            emit_attn_L0(c, esA, xT, w_tm, w_ga, enorm, qkgain, cos_tm, sin_tm, swapd, identd, ya, hscr=hscr, sweep_mode='write')
        c.barrier()
        with ExitStack() as esB:
            c.es = esB
            emit_hyena_L0(c, esB, xT, w_hy, enorm, convw, fparams, w1d, w2d, w3d, zTf, zTr, trow, negdel, hyd, FAd, Fd, Gd, twd, kscr, zscr, yscr, yb, hscr=hscr, sweep_mode='read')
        c.barrier()
        c.es = es
        c.finish()
    return nc


def build_L2():
    nc = bass.Bass("TRN2", target_bir_lowering=False)
    I = lambda n, s, dt=F32: _dram(nc, "ExternalInput", n, s, dt)
    O = lambda n, s, dt=F32: _dram(nc, "ExternalOutput", n, s, dt)
    y0T = I("y0T", [1024, 2048], BF16); xs = I("xs", [2048, 1024]); wout = I("wout", [1024, 1024]); win = I("win", [1024, 1696]); onorm = I("onorm", [128, 8])
    cos2 = I("cos2", [128, 256]); sin2 = I("sin2", [128, 256]); identd = I("identd", [128, 128], BF16)
    x1 = O("x1", [2048, 1024]); lat = O("lat", [672, 2048], BF16); gcT = O("gcT", [1024, 2048], BF16)
    with ExitStack() as es:
        c = Ctx(nc, es)
        emit_L2(c, es, y0T, xs, wout, win, onorm, cos2, sin2, identd, x1, lat, gcT)
        c.finish()
    return nc


def build_L3():
    nc = bass.Bass("TRN2", target_bir_lowering=False)
    I = lambda n, s, dt=F32: _dram(nc, "ExternalInput", n, s, dt)
    O = lambda n, s, dt=F32: _dram(nc, "ExternalOutput", n, s, dt)
    latT = I("latT", [672, 8192], BF16); wq = I("wq", [384, 4, 96]); wk = I("wk", [256, 4, 64]); wv = I("wv", [256, 4, 64]); qgain = I("qgain", [128, 3]); kvgain = I("kvgain", [128, 2])
    cosT = I("cosT", [32, 8192]); sinT = I("sinT", [32, 8192]); swapd = I("swapd", [128, 128])
    oc = O("oc", [2, 128, 8192], BF16)
    with ExitStack() as es:
        c = Ctx(nc, es)
        emit_attn_L1(c, es, latT, wq, wk, wv, qgain, kvgain, cosT, sinT, swapd, oc)
        c.finish()
    return nc


def build_L4():
    nc = bass.Bass("TRN2", target_bir_lowering=False)
    I = lambda n, s, dt=F32: _dram(nc, "ExternalInput", n, s, dt)
    O = lambda n, s, dt=F32: _dram(nc, "ExternalOutput", n, s, dt)
    ocT = I("ocT", [1024, 2048], BF16); gcT = I("gcT", [1024, 2048], BF16); x1 = I("x1", [2048, 1024]); wout = I("wout", [1024, 1024]); fnorm = I("fnorm", [1, 1024])
    out = O("out", [2048, 1024])
    with ExitStack() as es:
        c = Ctx(nc, es)
        emit_L4(c, es, ocT, gcT, x1, wout, fnorm, out)
        c.finish()
    return nc


def _get(name, fn):
    if name not in _CACHE:
        _CACHE[name] = fn()
    return _CACHE[name]


def kernel(**inputs):
    d = {k: np.asarray(v) for k, v in inputs.items()}
    C = consts(); H = hy_consts(); C32 = angles(32); C3 = l3_consts()
    cores = [(b, g) for b in range(2) for g in range(4)]
    ids = list(range(8))
    ins = []
    for (b, g) in cores:
        m = l1a_inputs(d, b, g, C)
        m.update(l1b_inputs(d, b, g, H))
        ins.append(m)
    r1 = run_bass_kernel_spmd(_get('L1', build_L1), ins, core_ids=ids).results
    y0T = []
    for b in range(2):
        rows = [np.asarray(r1[b * 4 + g]['ya']) for g in range(4)] + [np.asarray(r1[b * 4 + g]['yb']) for g in range(4)]
        y0T.append(np.concatenate(rows, 0))
    ins = [l2_inputs(d, b, ts, y0T[b], C32) for (b, ts) in cores]
    r2 = run_bass_kernel_spmd(_get('L2', build_L2), ins, core_ids=ids).results
    latT = []
    for b in range(2):
        latT.append(np.concatenate([np.asarray(r2[b * 4 + ts]['lat']) for ts in range(4)], 1))
    ins = [l3_inputs(d, b, g, latT[b], C3, C['swap']) for (b, g) in cores]
    r3 = run_bass_kernel_spmd(_get('L3', build_L3), ins, core_ids=ids).results
    ocT = []
    for b in range(2):
        ocT.append(np.concatenate([np.asarray(r3[b * 4 + g]['oc']).reshape(256, 8192) for g in range(4)], 0))
    ins = [l4_inputs(d, b, ts, ocT[b], np.asarray(r2[b * 4 + ts]['gcT']), np.asarray(r2[b * 4 + ts]['x1'])) for (b, ts) in cores]
    r4 = run_bass_kernel_spmd(_get('L4', build_L4), ins, core_ids=ids).results
    out = np.empty((2, 8192, 1024), np.float32)
    for i, (b, ts) in enumerate(cores):
        out[b, ts * 2048:(ts + 1) * 2048] = np.asarray(r4[i]['out'])
    return out
```
